# Optimizing a Trainium2 kernel written in Bass

```python
import math
import jax, jax.numpy as jnp
from jax import lax
import numpy as np

D_MODEL = 1024
BATCH = 4
SEQ = 8192
DEPTH = 1

HEAD_DIM = 64
N_Q_HEADS = 8
N_KV_HEADS = 2
Q_PER_KV = N_Q_HEADS // N_KV_HEADS
ATTN_WIDTH = N_Q_HEADS * HEAD_DIM
KV_WIDTH = N_KV_HEADS * HEAD_DIM
WINDOW = 128
ATTN_BLOCK = 128
ROPE_THETA = 10000.0
SSM_HEADS = 8
SSM_HEAD_DIM = 64
SSM_WIDTH = SSM_HEADS * SSM_HEAD_DIM
SSM_GROUPS = 2
HEADS_PER_GROUP = SSM_HEADS // SSM_GROUPS
D_STATE = 128
CONV_WIDTH = 4
CHUNK = 128
CONV_CH = SSM_WIDTH + 2 * SSM_GROUPS * D_STATE
MIX_WIDTH = ATTN_WIDTH + SSM_WIDTH
IN_PROJ = ATTN_WIDTH + 2 * KV_WIDTH + SSM_WIDTH + CONV_CH + SSM_HEADS
D_FF = -(-8 * D_MODEL // (3 * 256)) * 256
N_MOD = 6
EPS = 1e-6

kernel_name = "hymba_swa_sink_ssd_adaln_layer"


def rmsnorm(x, w):
    xf = x.astype(jnp.float32)
    y = xf * lax.rsqrt(jnp.mean(xf * xf, axis=-1, keepdims=True) + EPS)
    return (y * w.astype(jnp.float32)).astype(x.dtype)


def modulate(h, shift, scale):
    return h * (1.0 + scale[:, None, :]) + shift[:, None, :]


def rope(t, positions):
    half = HEAD_DIM // 2
    inv_freq = ROPE_THETA ** (-jnp.arange(half, dtype=jnp.float32) / half)
    ang = positions.astype(jnp.float32)[..., None] * inv_freq
    cos = jnp.cos(ang)[:, :, None, :]
    sin = jnp.sin(ang)[:, :, None, :]
    tf = t.astype(jnp.float32)
    t1, t2 = tf[..., :half], tf[..., half:]
    out = jnp.concatenate([t1 * cos - t2 * sin, t2 * cos + t1 * sin], axis=-1)
    return out.astype(t.dtype)


def sliding_window_attention(q, k, v, sinks):
    b, s = q.shape[0], q.shape[1]
    nb = s // ATTN_BLOCK
    qb = q.reshape(b, nb, ATTN_BLOCK, N_KV_HEADS, Q_PER_KV, HEAD_DIM)

    def band(t):
        tb = t.reshape(b, nb, ATTN_BLOCK, N_KV_HEADS, HEAD_DIM)
        prev = jnp.pad(tb, ((0, 0), (1, 0), (0, 0), (0, 0), (0, 0)))[:, :-1]
        return jnp.concatenate([prev, tb], axis=2)

    kb, vb = band(k), band(v)
    scores = jnp.einsum('bnqhgd,bnkhd->bnhgqk', qb, kb).astype(jnp.float32)
    scores = scores * (1.0 / math.sqrt(HEAD_DIM))
    blk = jnp.arange(nb)[:, None] * ATTN_BLOCK
    qpos = blk + jnp.arange(ATTN_BLOCK)[None, :]
    kpos = blk - ATTN_BLOCK + jnp.arange(2 * ATTN_BLOCK)[None, :]
    diff = qpos[:, :, None] - kpos[:, None, :]
    mask = (diff >= 0) & (diff < WINDOW) & (kpos[:, None, :] >= 0)
    scores = jnp.where(mask[None, :, None, None], scores, -jnp.inf)
    sink = sinks.astype(jnp.float32).reshape(N_KV_HEADS, Q_PER_KV)[None, None, :, :, None, None]
    m = jnp.maximum(jnp.max(scores, axis=-1, keepdims=True), sink)
    p = jnp.exp(scores - m)
    denom = jnp.sum(p, axis=-1, keepdims=True) + jnp.exp(sink - m)
    probs = (p / denom).astype(v.dtype)
    out = jnp.einsum('bnhgqk,bnkhd->bnqhgd', probs, vb)
    return out.reshape(b, s, ATTN_WIDTH)


def causal_depthwise_conv(u, w, bias):
    out = lax.conv_general_dilated(
        u, w[:, None, :].astype(u.dtype), window_strides=(1,),
        padding=[(CONV_WIDTH - 1, 0)], dimension_numbers=('NWC', 'WIO', 'NWC'),
        feature_group_count=u.shape[-1])
    return out + bias.astype(u.dtype)


def ssd_chunked_scan(xs, dt, A, Bm, Cm, d_skip):
    b, s = xs.shape[0], xs.shape[1]
    nc = s // CHUNK
    xf = xs.astype(jnp.float32)
    xdt = (xf * dt[..., None]).reshape(b, nc, CHUNK, SSM_GROUPS, HEADS_PER_GROUP, SSM_HEAD_DIM)
    a = (dt * A).reshape(b, nc, CHUNK, SSM_GROUPS, HEADS_PER_GROUP)
    a_cs = jnp.cumsum(a, axis=2)
    Bc = Bm.astype(jnp.float32).reshape(b, nc, CHUNK, SSM_GROUPS, D_STATE)
    Cc = Cm.astype(jnp.float32).reshape(b, nc, CHUNK, SSM_GROUPS, D_STATE)
    causal = jnp.tril(jnp.ones((CHUNK, CHUNK), dtype=bool))[None, None, :, :, None, None]
    seg = a_cs[:, :, :, None] - a_cs[:, :, None, :]
    decay = jnp.exp(jnp.where(causal, seg, -jnp.inf))
    cb = jnp.einsum('bclgn,bcsgn->bclsg', Cc, Bc)
    y_diag = jnp.einsum('bclsg,bclsgj,bcsgjp->bclgjp', cb, decay, xdt)
    decay_to_end = jnp.exp(a_cs[:, :, -1:] - a_cs)
    states = jnp.einsum('bclgn,bclgj,bclgjp->bcgjpn', Bc, decay_to_end, xdt)
    chunk_decay = jnp.exp(a_cs[:, :, -1])

    def step(h, inp):
        st, dec = inp
        return h * dec[..., None, None] + st, h

    init = jnp.zeros((b, SSM_GROUPS, HEADS_PER_GROUP, SSM_HEAD_DIM, D_STATE), jnp.float32)
    _, prev = lax.scan(step, init, (jnp.moveaxis(states, 1, 0), jnp.moveaxis(chunk_decay, 1, 0)))
    prev = jnp.moveaxis(prev, 0, 1)
    y_off = jnp.einsum('bclgn,bcgjpn,bclgj->bclgjp', Cc, prev, jnp.exp(a_cs))
    y = (y_diag + y_off).reshape(b, s, SSM_HEADS, SSM_HEAD_DIM)
    return y + xf * d_skip.astype(jnp.float32)[:, None]


def hybrid_mixer(h, positions, w_in, conv_w, conv_b, dt_bias, a_log, d_skip, sinks, ssm_norm_w, w_out):
    b, s = h.shape[0], h.shape[1]
    proj = h @ w_in
    o1 = ATTN_WIDTH
    o2 = o1 + KV_WIDTH
    o3 = o2 + KV_WIDTH
    o4 = o3 + SSM_WIDTH
    o5 = o4 + CONV_CH
    q, k, v, z, xbc, dt_raw = jnp.split(proj, [o1, o2, o3, o4, o5], axis=-1)
    q = rope(q.reshape(b, s, N_Q_HEADS, HEAD_DIM), positions)
    k = rope(k.reshape(b, s, N_KV_HEADS, HEAD_DIM), positions)
    v = v.reshape(b, s, N_KV_HEADS, HEAD_DIM)
    attn = sliding_window_attention(q, k, v, sinks)
    xbc = jax.nn.silu(causal_depthwise_conv(xbc, conv_w, conv_b))
    xs, Bm, Cm = jnp.split(xbc, [SSM_WIDTH, SSM_WIDTH + SSM_GROUPS * D_STATE], axis=-1)
    xs = xs.reshape(b, s, SSM_HEADS, SSM_HEAD_DIM)
    Bm = Bm.reshape(b, s, SSM_GROUPS, D_STATE)
    Cm = Cm.reshape(b, s, SSM_GROUPS, D_STATE)
    dt = jax.nn.softplus(dt_raw.astype(jnp.float32) + dt_bias.astype(jnp.float32))
    A = -jnp.exp(a_log.astype(jnp.float32))
    y = ssd_chunked_scan(xs, dt, A, Bm, Cm, d_skip).reshape(b, s, SSM_WIDTH)
    y = y * jax.nn.silu(z.astype(jnp.float32))
    yg = y.reshape(b, s, SSM_GROUPS, SSM_WIDTH // SSM_GROUPS)
    yg = yg * lax.rsqrt(jnp.mean(yg * yg, axis=-1, keepdims=True) + EPS)
    y = (yg.reshape(b, s, SSM_WIDTH) * ssm_norm_w.astype(jnp.float32)).astype(h.dtype)
    return jnp.concatenate([attn.astype(h.dtype), y], axis=-1) @ w_out


def swiglu(h, w_gate_up, w_down):
    gu = h @ w_gate_up
    g, u = jnp.split(gu, 2, axis=-1)
    return (jax.nn.silu(g) * u) @ w_down


def setup_inputs(seed: int = 0) -> dict:
    key = jax.random.key(seed)
    ks = jax.random.split(key, 20)
    f32 = jnp.float32
    L = DEPTH
    x = jax.random.normal(ks[0], (BATCH, SEQ, D_MODEL), f32)
    c = jax.random.normal(ks[1], (BATCH, D_MODEL), f32)
    offset = jax.random.randint(ks[2], (BATCH, 1), 0, 4096, dtype=jnp.int32)
    positions = (jnp.arange(SEQ, dtype=jnp.int32)[None, :] + offset).astype(jnp.int32)
    w_ada = jax.random.normal(ks[3], (L, D_MODEL, N_MOD * D_MODEL), f32) * (0.5 * D_MODEL ** -0.5)
    b_ada = 0.01 * jax.random.normal(ks[4], (L, N_MOD * D_MODEL), f32)
    norm1_w = 1.0 + 0.02 * jax.random.normal(ks[5], (L, D_MODEL), f32)
    w_in = jax.random.normal(ks[6], (L, D_MODEL, IN_PROJ), f32) * D_MODEL ** -0.5
    conv_w = jax.random.normal(ks[7], (L, CONV_WIDTH, CONV_CH), f32) * CONV_WIDTH ** -0.5
    conv_b = 0.01 * jax.random.normal(ks[8], (L, CONV_CH), f32)
    dt0 = jnp.exp(jax.random.uniform(ks[9], (L, SSM_HEADS), f32, math.log(1e-3), math.log(1e-1)))
    dt_bias = dt0 + jnp.log(-jnp.expm1(-dt0))
    a_log = jnp.log(jax.random.uniform(ks[10], (L, SSM_HEADS), f32, 1.0, 16.0))
    d_skip = 1.0 + 0.1 * jax.random.normal(ks[11], (L, SSM_HEADS), f32)
    attn_sinks = jax.random.normal(ks[12], (L, N_Q_HEADS), f32)
    ssm_norm_w = 1.0 + 0.02 * jax.random.normal(ks[13], (L, SSM_WIDTH), f32)
    w_out = jax.random.normal(ks[14], (L, MIX_WIDTH, D_MODEL), f32) * MIX_WIDTH ** -0.5
    norm2_w = 1.0 + 0.02 * jax.random.normal(ks[15], (L, D_MODEL), f32)
    w_gate_up = jax.random.normal(ks[16], (L, D_MODEL, 2 * D_FF), f32) * D_MODEL ** -0.5
    w_down = jax.random.normal(ks[17], (L, D_FF, D_MODEL), f32) * D_FF ** -0.5
    final_norm_w = 1.0 + 0.02 * jax.random.normal(ks[18], (D_MODEL,), f32)
    return {"x": x, "c": c, "positions": positions, "w_ada": w_ada, "b_ada": b_ada,
            "norm1_w": norm1_w, "w_in": w_in, "conv_w": conv_w, "conv_b": conv_b,
            "dt_bias": dt_bias, "a_log": a_log, "d_skip": d_skip, "attn_sinks": attn_sinks,
            "ssm_norm_w": ssm_norm_w, "w_out": w_out, "norm2_w": norm2_w,
            "w_gate_up": w_gate_up, "w_down": w_down, "final_norm_w": final_norm_w}


def reference(x, c, positions, w_ada, b_ada, norm1_w, w_in, conv_w, conv_b, dt_bias, a_log,
              d_skip, attn_sinks, ssm_norm_w, w_out, norm2_w, w_gate_up, w_down, final_norm_w):
    for layer in range(DEPTH):
        mod = jax.nn.silu(c) @ w_ada[layer] + b_ada[layer]
        shift1, scale1, gate1, shift2, scale2, gate2 = jnp.split(mod, N_MOD, axis=-1)
        h = modulate(rmsnorm(x, norm1_w[layer]), shift1, scale1)
        x = x + gate1[:, None, :] * hybrid_mixer(
            h, positions, w_in[layer], conv_w[layer], conv_b[layer], dt_bias[layer],
            a_log[layer], d_skip[layer], attn_sinks[layer], ssm_norm_w[layer], w_out[layer])
        h = modulate(rmsnorm(x, norm2_w[layer]), shift2, scale2)
        x = x + gate2[:, None, :] * swiglu(h, w_gate_up[layer], w_down[layer])
    return rmsnorm(x, final_norm_w)
```

```python
import numpy as np
import concourse.bass as bass
import concourse.mybir as mybir
from concourse.bass_utils import run_bass_kernel_spmd

F32 = mybir.dt.float32
BF16 = mybir.dt.bfloat16
I32 = mybir.dt.int32
AF = mybir.ActivationFunctionType
ALU = mybir.AluOpType
AX = mybir.AxisListType

EPOCH = 2000
N_DMA_SEMS = 12

D = 1024
KD = 8
SEQ_HALF = 4096
T = 512
NCH = 4
NSC = SEQ_HALF // T
DFF = 2816
NFF = 22
EPS = 1e-6
CQ, CQS, CK, CKS, CXC, CV, CDT, CZ, NW = 0, 512, 1024, 1280, 1536, 2560, 2688, 2696, 3208
NFM = 20
NTM = NW - CV
NEG = -30000.0


class _Op:
    __slots__ = ("eng", "fn", "reads", "writes", "dma", "deps", "signal", "barrier")

    def __init__(self, eng, fn, reads, writes, dma, barrier=False):
        self.eng, self.fn, self.reads, self.writes, self.dma = eng, fn, reads, writes, dma
        self.deps = ()
        self.signal = False
        self.barrier = barrier


class Emitter:
    def __init__(self, nc):
        self.nc = nc
        self.ops = []
        self.engines = {"pe": nc.tensor, "act": nc.scalar, "dve": nc.vector, "pool": nc.gpsimd, "sp": nc.sync}

    def op(self, eng, fn, reads=(), writes=()):
        self.ops.append(_Op(eng, fn, tuple(reads), tuple(writes), False))

    def dma(self, out, in_, reads=(), writes=(), eng="sp"):
        self.ops.append(_Op(eng, lambda e: e.dma_start(out=out, in_=in_), tuple(reads), tuple(writes), True))

    def barrier(self):
        self.ops.append(_Op("sp", lambda e: e.nop(), (), (), False, barrier=True))

    def finalize(self):
        nc = self.nc
        ops = self.ops
        n = len(ops)
        last_writer, readers = {}, {}
        last_on_eng = {}
        dma_since = []
        cur_barrier = None
        for i, o in enumerate(ops):
            deps = set()
            if o.barrier:
                deps.update(last_on_eng.values())
                deps.update(dma_since)
                dma_since = []
                last_writer, readers = {}, {}
            else:
                for r in o.reads:
                    w = last_writer.get(r)
                    if w is not None:
                        deps.add(w)
                for w_ in o.writes:
                    w = last_writer.get(w_)
                    if w is not None:
                        deps.add(w)
                    deps.update(readers.get(w_, ()))
                if o.eng == "pe":
                    deps = {d for d in deps if not (ops[d].eng == "pe" and not ops[d].dma)}
                if cur_barrier is not None:
                    deps.add(cur_barrier)
            deps.discard(i)
            o.deps = tuple(sorted(deps))
            for d in o.deps:
                ops[d].signal = True
            if o.barrier:
                cur_barrier = i
            for w_ in o.writes:
                last_writer[w_] = i
                readers[w_] = []
            for r in o.reads:
                if r not in o.writes:
                    readers.setdefault(r, []).append(i)
            last_on_eng[o.eng] = i
            if o.dma:
                dma_since.append(i)
        eng_count = {e: 0 for e in self.engines}
        eng_sems = {e: [] for e in self.engines}
        dma_sems = [nc.alloc_semaphore(name=f"dma{i}") for i in range(N_DMA_SEMS)]
        dma_val = [0] * N_DMA_SEMS
        rr = 0
        state = {e: {} for e in self.engines}
        sig = [None] * n
        clock = [None] * n
        nwaits = 0
        for i, o in enumerate(ops):
            E = self.engines[o.eng]
            st = state[o.eng]
            for d in o.deps:
                key, val, sem, semval = sig[d]
                if st.get(key, 0) >= val:
                    continue
                E.wait_ge(sem, semval)
                nwaits += 1
                for k2, v2 in clock[d].items():
                    if st.get(k2, 0) < v2:
                        st[k2] = v2
            if o.dma:
                k = rr
                rr = (rr + 1) % N_DMA_SEMS
                key = ("d", k)
                if st.get(key, 0) < dma_val[k]:
                    E.wait_ge(dma_sems[k], dma_val[k])
                    nwaits += 1
                    st[key] = dma_val[k]
                ins = o.fn(E)
                dma_val[k] += 16
                ins.then_inc(dma_sems[k], 16)
                sig[i] = (key, dma_val[k], dma_sems[k], dma_val[k])
                clk = dict(st)
                clk[key] = dma_val[k]
                clock[i] = clk
            else:
                ins = o.fn(E)
                if o.signal:
                    c = eng_count[o.eng]
                    ep, off = divmod(c, EPOCH)
                    if ep >= len(eng_sems[o.eng]):
                        eng_sems[o.eng].append(nc.alloc_semaphore(name=f"{o.eng}{ep}"))
                    sem = eng_sems[o.eng][ep]
                    ins.then_inc(sem, 1)
                    eng_count[o.eng] = c + 1
                    key = ("e", o.eng)
                    sig[i] = (key, c + 1, sem, off + 1)
                    clk = dict(st)
                    clk[key] = c + 1
                    clock[i] = clk
        return dict(nops=n, nwaits=nwaits, counts=dict(eng_count))


def bcl(ap, n):
    return ap.unsqueeze(2).to_broadcast([ap.shape[0], ap.shape[1], n])


def bcm(ap, n):
    return ap.unsqueeze(1).to_broadcast([ap.shape[0], n, ap.shape[1]])


def build_program(n_pre=NSC, n_main=NSC, dbg=None, stop=None):
    nc = bass.Bass("TRN2", target_bir_lowering=False)

    def din(name, shape, dt=F32):
        return nc.dram_tensor(name, list(shape), dt, kind="ExternalInput").ap()

    xprev = din("xprev", [SEQ_HALF, D])
    xown = din("xown", [SEQ_HALF, D])
    posrep = din("posrep", [128, T + SEQ_HALF], I32)
    colpack = din("colpack", [128, 80])
    rowpack = din("rowpack", [128, 1568])
    bada = din("bada", [128, 6144])
    cpack = din("cpack", [128, 1408])
    w_ada = din("w_ada", [D, 6144])
    w_in = din("w_in", [D, NW])
    w_out = din("w_out", [D, D])
    w_gu = din("w_gu", [D, 2 * DFF])
    w_dn = din("w_dn", [DFF, D])
    out = nc.dram_tensor("out", [SEQ_HALF, D], F32, kind="ExternalOutput").ap()
    wgu_scr = nc.dram_tensor("wgu_scr", [NFF, 128, KD * 256], BF16, kind="Internal").ap()
    wdn_scr = nc.dram_tensor("wdn_scr", [4, 128, NFF * 256], BF16, kind="Internal").ap()
    rope_scr = nc.dram_tensor("rope_scr", [1 + NSC, 128, 2, T], F32, kind="Internal").ap()
    dbg_out = {}
    if dbg:
        for nm, shp in dbg.items():
            dbg_out[nm] = nc.dram_tensor("dbg_" + nm, list(shp), F32, kind="ExternalOutput").ap()

    def sb(name, shape, dt):
        return nc.sbuf_tensor(name, list(shape), dt).__enter__()

    def psum(name, shape, dt):
        return nc.psum_tensor(name, list(shape), dt).__enter__()

    w_in_bf = sb("w_in_bf", [128, KD, NW], BF16)
    w_out_bf = sb("w_out_bf", [128, KD, D], BF16)
    cst = sb("cst", [128, 512], F32)
    identf, triLE, triGT, onesf = cst[:, 0:128], cst[:, 128:256], cst[:, 256:384], cst[:, 384:512]
    cstb = sb("cstb", [128, 256], BF16)
    identb, onesb = cstb[:, 0:128], cstb[:, 128:256]
    maskb = sb("maskb", [128, 1024], BF16)
    cdiag = sb("cdiag", [128, 32, 128], BF16)
    bdiag = sb("bdiag", [128, 8, 128], BF16)
    rows = sb("rows", [128, 1568], F32)
    fnw32, ssmw16 = rows[:, 0:1024], rows[:, 1024:1536]
    dtb, aneg, dskip, esink = rows[:, 1536:1544], rows[:, 1544:1552], rows[:, 1552:1560], rows[:, 1560:1568]
    cols = sb("cols", [128, 80], F32)
    ccol, convw, convb = cols[:, 0:8], cols[:, 8:40], cols[:, 40:48]
    invf, flag, rsign = cols[:, 48:49], cols[:, 49:50], cols[:, 50:51]
    n1wc, n2wc = cols[:, 51:59], cols[:, 59:67]
    hmask = cols[:, 67:69]
    small = sb("small", [128, 256], F32)
    g1c, g2c, sh1c, sh2c = small[:, 0:8], small[:, 8:16], small[:, 16:24], small[:, 24:32]
    nhalf = small[:, 32:33]
    ss1 = small[:, 40:44]
    rs1 = small[:, 44:48]
    ssg = small[:, 48:50]
    rsg = small[:, 50:52]
    den = small[:, 56:64]
    rden = small[:, 64:72]
    bias_fm = small[:, 80:100]
    bias_gu = small[:, 100:144]
    dtraw = small[:, 144:176]
    dtv = small[:, 176:208]
    av = small[:, 208:240]
    tmp32 = sb("tmp32", [128, 128], F32)
    exv = sb("exv", [128, NCH, 24], F32)
    wend = sb("wend", [128, NCH, 8], F32)
    bias_row = sb("bias_row", [1, NTM], BF16)
    browf = sb("browf", [1, 2, 512], F32)
    prevT = sb("prevT", [128, 512], F32)
    prevTb = sb("prevTb", [128, 512], BF16)
    x1buf = sb("x1buf", [128, NCH, D], F32)
    xstage = sb("xstage", [128, 2, D], F32)
    hT = sb("hT", [128, KD, T], BF16)
    wgus = sb("wgus", [128, 3, KD, 256], BF16)
    utail = sb("utail", [128, 8, 3], BF16)
    khalo = sb("khalo", [128, 2, 128], BF16)
    vhalo = sb("vhalo", [128, 130], BF16)
    ARENA = 61440
    arena = sb("arena", [128, ARENA // 2], BF16)

    class Lay:
        def __init__(self, base=0):
            self.off = base

        def take(self, shape, dt):
            nel = int(np.prod(shape))
            nb = nel * (4 if dt == F32 or dt == I32 else 2)
            nb_al = (nb + 63) // 64 * 64
            o = self.off
            self.off += nb_al
            assert self.off <= ARENA, (self.off, ARENA)
            ap = arena[:, o // 2:(o + nb) // 2]
            if dt != BF16:
                ap = ap.bitcast(dt)
            if len(shape) == 2:
                return ap.rearrange("p (a b) -> p a b", a=shape[0])
            if len(shape) == 3:
                return ap.rearrange("p (a b c) -> p a b c", a=shape[0], b=shape[1])
            return ap

    L = Lay()
    qT = L.take([4, T], BF16)
    kT = L.take([2, 128 + T], BF16)
    kTz = L.take([2, 2, 128 + T], BF16)
    Vaug = L.take([5, 130], BF16)
    xdt = L.take([NCH, 512], BF16)
    xdtd = L.take([NCH, 512], BF16)
    xsD = L.take([NCH, 512], BF16)
    Btm = L.take([NCH, 256], BF16)
    BCT = L.take([4, T], BF16)
    sz = L.take([NCH, 512], BF16)
    shared_end = L.off
    LA = Lay(shared_end)
    uT = LA.take([8, T + 3], BF16)
    cosT = LA.take([T], F32)
    sinT = LA.take([T], F32)
    rt1 = LA.take([T], F32)
    rt2 = LA.take([T], F32)
    xs = LA.take([512], F32)
    xn = LA.take([D], BF16)
    posi = LA.take([T], I32)
    LB = Lay(shared_end)
    PT = LB.take([4, 512], BF16)
    aTri = LB.take([1024], F32)
    Ebuf = LB.take([1024], F32)
    MT = LB.take([1024], BF16)
    CBm = LB.take([256], F32)
    yt = LB.take([512], F32)
    xnB = LB.take([D], BF16)
    sqj = LB.take([D], BF16)
    ytm = LB.take([D], BF16)
    jnk = LB.take([256], BF16)
    LC = Lay()
    actT = LC.take([NFF, T], BF16)
    wdns = LC.take([2, NFF, 256], BF16)
    xnC = LC.take([D], BF16)
    sg = LC.take([2, T], BF16)
    xn4C = arena[:, 49152 // 2:(49152 + NCH * D * 2) // 2].rearrange("p (c d) -> p c d", c=NCH)
    assert LC.off <= 49152
    LS = Lay()
    stg = LS.take([2, KD * 512], F32)
    cvo = LS.take([2, KD * 512], BF16)
    scb = LS.take([KD, 128], F32)
    rowst = LS.take([1568], F32)
    mod_lo = x1buf[:].rearrange("p a b -> p (a b)")
    mod_hi = hT[:].rearrange("p a b -> p (a b)").bitcast(F32)

    def modbc(c0, c1):
        if c1 <= 4096:
            return mod_lo[:, c0:c1]
        assert c0 >= 4096
        return mod_hi[:, c0 - 4096:c1 - 4096]

    TP = psum("TP", [128, 8, 128], BF16)
    PA = [psum(f"PA{i}", [128, 512], F32) for i in range(4)]
    PC = [psum(f"PC{i}", [128, 512], F32) for i in range(2)]
    PE0 = psum("PE0", [128, 512], F32)

    def pk(name):
        return [name]

    em = Emitter(nc)
    EO = em.op

    def dump(name, ap, reads):
        if name in dbg_out:
            em.dma(dbg_out[name], ap, reads=reads, writes=["dbg_" + name])

    em.dma(cst[:, 0:384], cpack[:, 0:384], writes=["cst"])
    em.dma(cols[:], colpack, writes=["cols"])
    em.dma(rowst[:], rowpack, writes=["rowst"])
    EO("dve", lambda e: e.memset(onesf, 1.0), writes=["onesf"])
    EO("dve", lambda e: e.memset(onesb, 1.0), writes=["onesb"])
    EO("dve", lambda e: e.memset(nhalf, -0.5), writes=["nhalf"])
    EO("dve", lambda e: e.tensor_copy(out=identb, in_=identf), reads=["cst"], writes=["identb"])
    em.dma(stg[:, 0, 0:1024], cpack[:, 384:1408], writes=["stg0"])
    EO("dve", lambda e: e.tensor_copy(out=maskb[:], in_=stg[:, 0, 0:1024]), reads=["stg0"], writes=["maskb"])
    EO("dve", lambda e: e.tensor_scalar(out=fnw32, in0=rowst[:, 0:1024], scalar1=32.0, scalar2=None, op0=ALU.mult),
       reads=["rowst"], writes=["fnw32"])
    EO("dve", lambda e: e.tensor_scalar(out=ssmw16, in0=rowst[:, 1024:1536], scalar1=16.0, scalar2=None, op0=ALU.mult),
       reads=["rowst"], writes=["ssmw16"])
    EO("dve", lambda e: e.tensor_copy(out=rows[:, 1536:1544], in_=rowst[:, 1536:1544]), reads=["rowst"], writes=["dtb"])
    EO("dve", lambda e: e.tensor_copy(out=dskip, in_=rowst[:, 1552:1560]), reads=["rowst"], writes=["dskip"])
    EO("act", lambda e: e.activation(out=aneg, in_=rowst[:, 1544:1552], func=AF.Exp), reads=["rowst"], writes=["aneg0"])
    EO("dve", lambda e: e.tensor_scalar(out=aneg, in0=aneg, scalar1=-1.0, scalar2=None, op0=ALU.mult),
       reads=["aneg0"], writes=["aneg"])
    EO("act", lambda e: e.activation(out=esink, in_=rowst[:, 1560:1568], func=AF.Exp), reads=["rowst"], writes=["esink"])
    EO("act", lambda e: e.activation(out=small[:, 240:248], in_=ccol, func=AF.Silu), reads=["cols"], writes=["sc"])
    EO("dve", lambda e: e.tensor_copy(out=scb[:], in_=bcl(small[:, 240:248], 128)), reads=["sc"], writes=["scb"])
    w_ada_v = w_ada.rearrange("(k p) c -> p k c", p=128)
    scbb = wgus[:].rearrange("p a k c -> p (a k c)")[:, 4096:5120].rearrange("p (k f) -> p k f", k=KD)
    shb = wgus[:].rearrange("p a k c -> p (a k c)")[:, 5120:5136]
    plainb = wgus[:].rearrange("p a k c -> p (a k c)")[:, 0:4096]
    EO("dve", lambda e: e.tensor_copy(out=scbb, in_=bcl(small[:, 240:248], 128)), reads=["sc"], writes=["scbb"])
    for cg in range(12):
        s = cg % 2
        em.dma(stg[:, s, :].rearrange("p (k c) -> p k c", k=KD), w_ada_v[:, :, cg * 512:(cg + 1) * 512], writes=[f"stg{s}"])
        em.dma(xstage[:, s, 0:512], bada[:, cg * 512:(cg + 1) * 512], writes=[f"xst{s}"])
        EO("act", lambda e, s=s: e.activation(out=cvo[:, s, 0:2048], in_=stg[:, s, 0:2048], func=AF.Copy), reads=[f"stg{s}"], writes=[f"cvo{s}a"])
        EO("dve", lambda e, s=s: e.tensor_copy(out=cvo[:, s, 2048:4096], in_=stg[:, s, 2048:4096]), reads=[f"stg{s}"], writes=[f"cvo{s}b"])
        bank = PA[cg % 4]

        def mmod(e, s=s, bank=bank):
            last = None
            for k in range(KD):
                last = e.matmul(bank[:], lhsT=scbb[:, k, :], rhs=cvo[:, s, k * 512:(k + 1) * 512], start=(k == 0), stop=(k == KD - 1))
            return last
        EO("pe", mmod, reads=["scbb", f"cvo{s}a", f"cvo{s}b"], writes=pk(f"PA{cg % 4}"))
        EO("dve", lambda e, s=s, bank=bank, cg=cg: e.tensor_tensor(out=modbc(cg * 512, (cg + 1) * 512), in0=bank[:],
                                                                 in1=xstage[:, s, 0:512], op=ALU.add),
           reads=pk(f"PA{cg % 4}") + [f"xst{s}"], writes=[f"mod{cg}"])
    modkeys = [f"mod{i}" for i in range(12)]

    def diag_extract(dst, c0, key):
        EO("dve", lambda e: e.tensor_tensor(out=stg[:, 0, 0:1024].rearrange("p (k f) -> p k f", k=KD),
                                            in0=modbc(c0, c0 + 1024).rearrange("p (k f) -> p k f", k=KD),
                                            in1=bcm(identf, KD), op=ALU.mult), reads=modkeys + ["cst", "stg0"], writes=["stg0"])
        EO("dve", lambda e: e.tensor_reduce(out=dst, in_=stg[:, 0, 0:1024].rearrange("p (k f) -> p k f", k=KD),
                                            axis=AX.X, op=ALU.add), reads=["stg0"], writes=[key])
    diag_extract(sh1c, 0, "sh1c")
    diag_extract(g1c, 1024, "g1c0")
    diag_extract(sh2c, 3072, "sh2c")
    diag_extract(g2c, 4096, "g2c0")
    EO("dve", lambda e: e.tensor_copy(out=shb[:, 0:8], in_=sh1c), reads=["sh1c"], writes=["shb1"])
    EO("dve", lambda e: e.tensor_copy(out=shb[:, 8:16], in_=sh2c), reads=["sh2c"], writes=["shb2"])
    for gc, nw, k0, k1 in ((g1c, n1wc, "g1c0", "g1c"), (g2c, n2wc, "g2c0", "g2c")):
        EO("dve", lambda e, gc=gc, nw=nw: e.scalar_tensor_tensor(out=gc, in0=gc, scalar=1.0, in1=nw, op0=ALU.add, op1=ALU.mult),
           reads=[k0, "cols"], writes=[k0 + "x"])
        EO("dve", lambda e, gc=gc: e.tensor_scalar(out=gc, in0=gc, scalar1=32.0, scalar2=None, op0=ALU.mult),
           reads=[k0 + "x"], writes=[k1])

    w_in_v = w_in.rearrange("(k p) c -> p k c", p=128)
    pieces = [(i * 512, 512) for i in range(5)] + [(CV, 136), (CZ, 512)]
    cvt_rr = 0
    for pi, (c0, w) in enumerate(pieces):
        s = pi % 2
        sv = stg[:, s, 0:KD * w].rearrange("p (k c) -> p k c", k=KD)
        em.dma(sv, w_in_v[:, :, c0:c0 + w], writes=[f"stg{s}"])
        pv = plainb[:, 0:KD * w].rearrange("p (k c) -> p k c", k=KD)
        EO("act", lambda e, sv=sv, pv=pv: e.activation(out=pv[:, 0:4, :], in_=sv[:, 0:4, :], func=AF.Copy), reads=[f"stg{s}", "plainb"], writes=["plainb_a"])
        EO("dve", lambda e, sv=sv, pv=pv: e.tensor_copy(out=pv[:, 4:8, :], in_=sv[:, 4:8, :]), reads=[f"stg{s}", "plainb"], writes=["plainb_b"])
        if c0 < CV:
            rb = pi % 2

            def mbr(e, pv=pv):
                last = None
                for k in range(KD):
                    last = e.matmul(PC[0][0:1, 0:512], lhsT=shb[:, k:k + 1], rhs=pv[:, k, :], start=(k == 0), stop=(k == KD - 1))
                return last
            EO("pe", mbr, reads=["plainb_a", "plainb_b", "shb1"], writes=["PC0", "plainb"])
            EO("dve", lambda e, rb=rb: e.tensor_copy(out=browf[0:1, rb, :], in_=PC[0][0:1, 0:512]), reads=["PC0"], writes=[f"browf{rb}"])

            def mbt(e, rb=rb, c0=c0):
                last = None
                for m in range(4):
                    mi = c0 // 128 + m
                    last = e.matmul(PE0[:, mi:mi + 1], lhsT=browf[0:1, rb, m * 128:(m + 1) * 128], rhs=onesf[0:1, 0:1], start=True, stop=True)
                return last
            EO("pe", mbt, reads=[f"browf{rb}", "onesf"], writes=["PE0"])
        else:
            o0 = c0 - CV
            bank = PC[0] if c0 == CV else PC[1]

            def mb2(e, pv=pv, w=w, bank=bank):
                last = None
                for k in range(KD):
                    last = e.matmul(bank[0:1, 0:w], lhsT=shb[:, k:k + 1], rhs=pv[:, k, :], start=(k == 0), stop=(k == KD - 1))
                return last
            bk = "PC0" if c0 == CV else "PC1"
            EO("pe", mb2, reads=["plainb_a", "plainb_b", "shb1"], writes=pk(bk) + ["plainb"])
            EO("dve", lambda e, w=w, bank=bank, o0=o0: e.tensor_copy(out=bias_row[0:1, o0:o0 + w], in_=bank[0:1, 0:w]),
               reads=pk(bk), writes=[f"brow{o0}"])
        for k in range(KD):
            eng = ("act", "dve")[cvt_rr % 2]
            cvt_rr += 1
            if eng == "act":
                EO("act", lambda e, k=k, sv=sv, c0=c0, w=w: e.activation(out=w_in_bf[:, k, c0:c0 + w], in_=sv[:, k, :], func=AF.Identity,
                                                                     scale=g1c[:, k:k + 1]),
                   reads=[f"stg{s}", "g1c"], writes=[f"win{pi}_{k}"])
            else:
                EO(eng, lambda e, k=k, sv=sv, c0=c0, w=w: e.tensor_scalar(out=w_in_bf[:, k, c0:c0 + w], in0=sv[:, k, :],
                                                                       scalar1=g1c[:, k:k + 1], scalar2=None, op0=ALU.mult),
                   reads=[f"stg{s}", "g1c"], writes=[f"win{pi}_{k}"])
    EO("dve", lambda e: e.tensor_copy(out=bias_fm, in_=PE0[:, 0:NFM]), reads=["PE0"], writes=["bias_fm"])

    s_rt1, s_rt2 = xstage[:, 0, 0:T], xstage[:, 0, T:2 * T]
    s_cos, s_sin = xstage[:, 1, 0:T], xstage[:, 1, T:2 * T]
    s_posi = prevT[:].bitcast(I32)
    for blk in range(1 + NSC):
        em.dma(s_posi, posrep[:, blk * T:(blk + 1) * T], writes=["s_posi"])
        EO("dve", lambda e: e.tensor_copy(out=s_rt1, in_=s_posi), reads=["s_posi", "xst0"], writes=["xst0"])
        EO("dve", lambda e: e.tensor_scalar(out=s_rt1, in0=s_rt1, scalar1=invf, scalar2=None, op0=ALU.mult), reads=["xst0", "cols"], writes=["xst0"])
        for which, dst, shift in (("s", s_sin, 0.0), ("c", s_cos, float(np.pi / 2))):
            EO("dve", lambda e, shift=shift: e.tensor_scalar(out=s_rt2, in0=s_rt1, scalar1=shift, scalar2=float(1.0 / (2 * np.pi)),
                                                             op0=ALU.add, op1=ALU.mult), reads=["xst0"], writes=["xst0"])
            EO("dve", lambda e: e.tensor_copy(out=s_posi, in_=s_rt2), reads=["xst0", "s_posi"], writes=["s_posi"])
            EO("dve", lambda e: e.tensor_copy(out=s_rt2, in_=s_posi), reads=["s_posi", "xst0"], writes=["xst0"])
            EO("dve", lambda e: e.tensor_scalar(out=s_rt2, in0=s_rt2, scalar1=float(-2 * np.pi), scalar2=None, op0=ALU.mult), reads=["xst0"], writes=["xst0"])
            EO("dve", lambda e, dst=dst: e.tensor_tensor(out=dst, in0=s_rt1, in1=s_rt2, op=ALU.add), reads=["xst0", "xst1"], writes=["xst1"])
            if shift != 0.0:
                EO("dve", lambda e, dst=dst, shift=shift: e.tensor_scalar(out=dst, in0=dst, scalar1=shift, scalar2=None, op0=ALU.add),
                   reads=["xst1"], writes=["xst1"])
            EO("dve", lambda e, dst=dst: e.tensor_scalar(out=s_rt2, in0=dst, scalar1=float(np.pi), scalar2=float(-2 * np.pi),
                                                         op0=ALU.is_gt, op1=ALU.mult), reads=["xst1", "xst0"], writes=["xst0"])
            EO("dve", lambda e, dst=dst: e.tensor_tensor(out=dst, in0=dst, in1=s_rt2, op=ALU.add), reads=["xst1", "xst0"], writes=["xst1"])
            EO("dve", lambda e, dst=dst: e.tensor_scalar(out=s_rt2, in0=dst, scalar1=float(-np.pi), scalar2=float(2 * np.pi),
                                                         op0=ALU.is_lt, op1=ALU.mult), reads=["xst1", "xst0"], writes=["xst0"])
            EO("dve", lambda e, dst=dst: e.tensor_tensor(out=dst, in0=dst, in1=s_rt2, op=ALU.add), reads=["xst1", "xst0"], writes=["xst1"])
            EO("act", lambda e, dst=dst: e.activation(out=dst, in_=dst, func=AF.Sin), reads=["xst1"], writes=["xst1"])
        EO("dve", lambda e: e.tensor_scalar(out=s_sin, in0=s_sin, scalar1=rsign, scalar2=None, op0=ALU.mult), reads=["xst1", "cols"], writes=["xst1"])
        em.dma(rope_scr[blk].rearrange("p a t -> p (a t)"), xstage[:, 1, :], reads=["xst1"], writes=[f"ropescr{blk}"])

    w_out_v = w_out.rearrange("(k p) c -> p k c", p=128)
    for hh in range(2):
        s = hh
        sv = stg[:, s, :].rearrange("p (k c) -> p k c", k=KD)
        em.dma(sv, w_out_v[:, :, hh * 512:(hh + 1) * 512], writes=[f"stg{s}"])
        EO(("dve", "pool")[hh], lambda e, sv=sv, hh=hh: e.tensor_tensor(out=w_out_bf[:, :, hh * 512:(hh + 1) * 512], in0=sv,
                                                                      in1=bcm(modbc(2048 + hh * 512, 2048 + (hh + 1) * 512), KD), op=ALU.mult),
           reads=[f"stg{s}"] + modkeys, writes=[f"wout{hh}"])

    for j in range(8):
        for k in range(4):
            EO("dve", lambda e, j=j, k=k: e.tensor_scalar(out=cdiag[:, j * 4 + k, :], in0=identf,
                                                                                 scalar1=convw[:, j * 4 + k:j * 4 + k + 1], scalar2=None, op0=ALU.mult),
               reads=["cst", "cols"], writes=[f"cdiag{j}_{k}"])
        EO("dve", lambda e, j=j: e.tensor_scalar(out=bdiag[:, j, :], in0=identf, scalar1=convb[:, j:j + 1], scalar2=None, op0=ALU.mult),
           reads=["cst", "cols"], writes=[f"bdiag{j}"])

    w_gu_v = w_gu.rearrange("(k p) c -> p k c", p=128)
    def gu_load(pc):
        s = pc % 2
        sv = stg[:, s, :].rearrange("p (k c) -> p k c", k=KD)
        em.dma(sv[:, :, 0:256], w_gu_v[:, :, pc * 256:(pc + 1) * 256], writes=[f"stg{s}"])
        em.dma(sv[:, :, 256:512], w_gu_v[:, :, DFF + pc * 256:DFF + (pc + 1) * 256], writes=[f"stg{s}"])
    gu_load(0)
    for pc in range(11):
        s = pc % 2
        sv = stg[:, s, :].rearrange("p (k c) -> p k c", k=KD)
        if pc + 1 < 11:
            gu_load(pc + 1)
        rb = pc % 2
        pv = plainb.rearrange("p (k c) -> p k c", k=KD)
        EO("act", lambda e, sv=sv, pv=pv: e.activation(out=pv[:, 0:4, :], in_=sv[:, 0:4, :], func=AF.Copy), reads=[f"stg{s}", "plainb"], writes=["plainb_a"])
        EO("dve", lambda e, sv=sv, pv=pv: e.tensor_copy(out=pv[:, 4:8, :], in_=sv[:, 4:8, :]), reads=[f"stg{s}", "plainb"], writes=["plainb_b"])

        def mbr3(e, pv=pv):
            last = None
            for k in range(KD):
                last = e.matmul(PC[0][0:1, 0:512], lhsT=shb[:, 8 + k:9 + k], rhs=pv[:, k, :], start=(k == 0), stop=(k == KD - 1))
            return last
        EO("pe", mbr3, reads=["plainb_a", "plainb_b", "shb2"], writes=["PC0", "plainb"])
        EO("dve", lambda e, rb=rb: e.tensor_copy(out=browf[0:1, rb, :], in_=PC[0][0:1, 0:512]), reads=["PC0"], writes=[f"browf{rb}"])

        def mbt3(e, rb=rb, pc=pc):
            last = None
            for m in range(4):
                ffc = 2 * pc + (m % 2)
                bcol = ffc if m < 2 else NFF + ffc
                last = e.matmul(PE0[:, 64 + bcol:64 + bcol + 1], lhsT=browf[0:1, rb, m * 128:(m + 1) * 128], rhs=onesf[0:1, 0:1], start=True, stop=True)
            return last
        EO("pe", mbt3, reads=[f"browf{rb}", "onesf"], writes=["PE0"])
        cv = cvo[:, s, :].rearrange("p (k c) -> p k c", k=KD)
        for k in range(KD):
            eng = ("act", "dve")[cvt_rr % 2]
            cvt_rr += 1
            if eng == "act":
                EO("act", lambda e, k=k, sv=sv, cv=cv: e.activation(out=cv[:, k, :], in_=sv[:, k, :], func=AF.Identity, scale=g2c[:, k:k + 1]),
                   reads=[f"stg{s}", "g2c"], writes=[f"cvo{s}_{k}"])
            else:
                EO(eng, lambda e, k=k, sv=sv, cv=cv: e.tensor_scalar(out=cv[:, k, :], in0=sv[:, k, :], scalar1=g2c[:, k:k + 1], scalar2=None,
                                                                    op0=ALU.mult),
                   reads=[f"stg{s}", "g2c"], writes=[f"cvo{s}_{k}"])
        for half in range(2):
            ffc = 2 * pc + half
            dst = wgu_scr[ffc].rearrange("p (k t c) -> p k t c", k=KD, t=2)
            em.dma(dst[:, :, 0, :], cv[:, :, half * 128:(half + 1) * 128], reads=[f"cvo{s}_{k}" for k in range(KD)], writes=[f"wguscr{ffc}g"])
            em.dma(dst[:, :, 1, :], cv[:, :, 256 + half * 128:256 + (half + 1) * 128], reads=[f"cvo{s}_{k}" for k in range(KD)],
                   writes=[f"wguscr{ffc}u"])
    EO("dve", lambda e: e.tensor_copy(out=bias_gu, in_=PE0[:, 64:64 + 2 * NFF]), reads=["PE0"], writes=["bias_gu"])
    w_dn_v = w_dn.rearrange("(f p) c -> p f c", p=128)
    dn_pieces = [(dq, fh) for dq in range(4) for fh in range(2)]

    def dn_load(i):
        dq, fh = dn_pieces[i]
        s = i % 2
        sv = stg[:, s, 0:11 * 256].rearrange("p (f c) -> p f c", f=11)
        em.dma(sv, w_dn_v[:, fh * 11:(fh + 1) * 11, dq * 256:(dq + 1) * 256], writes=[f"stg{s}"])
    dn_load(0)
    for i, (dq, fh) in enumerate(dn_pieces):
        s = i % 2
        sv = stg[:, s, 0:11 * 256].rearrange("p (f c) -> p f c", f=11)
        cv = cvo[:, s, 0:11 * 256].rearrange("p (f c) -> p f c", f=11)
        if i + 1 < len(dn_pieces):
            dn_load(i + 1)
        EO(("dve", "pool")[i % 2], lambda e, sv=sv, cv=cv, dq=dq: e.tensor_tensor(
            out=cv, in0=sv, in1=bcm(modbc(5120 + dq * 256, 5120 + (dq + 1) * 256), 11), op=ALU.mult),
           reads=[f"stg{s}"] + modkeys, writes=[f"cvo{s}"] + [f"cvo{s}_{k}" for k in range(KD)])
        dst = wdn_scr[dq].rearrange("p (f c) -> p f c", f=NFF)
        em.dma(dst[:, fh * 11:(fh + 1) * 11, :], cv, reads=[f"cvo{s}"], writes=[f"wdnscr{dq}_{fh}"])
    em.barrier()

    EO("dve", lambda e: e.memset(prevT[:], 0.0), writes=["prevT"])
    EO("dve", lambda e: e.memset(prevTb[:], 0.0), writes=["prevTb"])
    EO("pool", lambda e: e.memset(utail[:], 0.0), writes=["utail"])
    EO("pool", lambda e: e.memset(khalo[:], 0.0), writes=["khalo"])
    EO("pool", lambda e: e.memset(vhalo[:], 0.0), writes=["vhalo"])

    SCRKEYS_W = [f"wguscr{f}{t}" for f in range(NFF) for t in "gu"] + [f"wdnscr{q}_{h}" for q in range(4) for h in range(2)]

    def rstd_from_ss(ssap, rsap, n_eps, rk, wk):
        EO("act", lambda e: e.activation(out=rsap, in_=ssap, func=AF.Ln, bias=float(n_eps)), reads=rk, writes=[wk + "t"])
        EO("act", lambda e: e.activation(out=rsap, in_=rsap, func=AF.Exp, scale=-0.5), reads=[wk + "t"], writes=[wk])

    def transpose_to_hT(src, src_keys, c, banks):
        for half in range(2):
            bank, bkey = banks[half]

            def tps(e, half=half, bank=bank):
                last = None
                for f4 in range(4):
                    f = half * 4 + f4
                    last = e.matmul(bank[:, f4 * 128:(f4 + 1) * 128], lhsT=src[:, f * 128:(f + 1) * 128], rhs=identb, start=True, stop=True)
                return last
            EO("pe", tps, reads=src_keys + ["identb"], writes=[bkey])
            dst = hT[:, half * 4:(half + 1) * 4, c * 128:(c + 1) * 128]
            srcv = bank[:].rearrange("p (f t) -> p f t", f=4)
            if half == 0:
                EO("act", lambda e, dst=dst, srcv=srcv: e.activation(out=dst, in_=srcv, func=AF.Copy), reads=[bkey], writes=[f"hT{c}"])
            else:
                EO("dve", lambda e, dst=dst, srcv=srcv: e.tensor_copy(out=dst, in_=srcv), reads=[bkey], writes=[f"hT{c}"])

    S1_BANKS = ((PA[0], "PA0"), (PA[1], "PA1"))

    def norm_to_hT(xsrc, xkeys, xnbuf, c, sskey):
        EO("act", lambda e: e.activation(out=xnbuf[:], in_=xsrc, func=AF.Identity, scale=rs1[:, c:c + 1]),
           reads=xkeys + [sskey, "xn"], writes=["xn"])
        transpose_to_hT(xnbuf, ["xn"], c, S1_BANKS)

    def emit_s1(si, xsrc_dram, main):
        tok0 = si * T
        for c in range(NCH):
            em.dma(xstage[:, c % 2, :], xsrc_dram[tok0 + c * 128: tok0 + (c + 1) * 128, :], writes=[f"xst{c % 2}"])
            EO("act", lambda e, c=c: e.activation(out=xn[:], in_=xstage[:, c % 2, :], func=AF.Square, accum_out=ss1[:, c:c + 1]),
               reads=[f"xst{c % 2}"], writes=["xn", f"ss1_{c}"])
            if c == 1 and main:
                em.dma(x1buf[:], xsrc_dram[tok0:tok0 + T, :].rearrange("(c p) d -> p c d", p=128), writes=[f"x1_{cc}" for cc in range(NCH)])
            if c % 2 == 1:
                cc0 = c - 1
                rstd_from_ss(ss1[:, cc0:c + 1], rs1[:, cc0:c + 1], D * EPS, [f"ss1_{cc0}", f"ss1_{c}"], f"rs1_{cc0}")
                for cc in (cc0, c):
                    norm_to_hT(xstage[:, cc % 2, :], [f"xst{cc % 2}"], xn, cc, f"rs1_{cc0}")

    xn4 = x1buf[:].rearrange("p a b -> p (a b)").bitcast(BF16)[:, 0:NCH * D].rearrange("p (c d) -> p c d", c=NCH)

    def emit_s1a(si, xsrc_dram, junk=None, buf=None):
        junk = xn if junk is None else junk
        buf = xn4 if buf is None else buf
        tok0 = si * T
        for c in range(NCH):
            em.dma(xstage[:, c % 2, :], xsrc_dram[tok0 + c * 128: tok0 + (c + 1) * 128, :], writes=[f"xst{c % 2}"])
            EO("dve", lambda e, c=c: e.scalar_tensor_tensor(out=junk[:], in0=xstage[:, c % 2, :], scalar=1.0, in1=xstage[:, c % 2, :],
                                                           op0=ALU.mult, op1=ALU.mult, accum_out=ss1[:, c:c + 1]),
               reads=[f"xst{c % 2}", "xn"], writes=["xn", f"ss1_{c}"])
            if c % 2 == 1:
                cc0 = c - 1
                rstd_from_ss(ss1[:, cc0:c + 1], rs1[:, cc0:c + 1], D * EPS, [f"ss1_{cc0}", f"ss1_{c}"], f"rs1_{cc0}")
                for cc in (cc0, c):
                    EO("dve", lambda e, cc=cc: e.tensor_scalar(out=buf[:, cc, :], in0=xstage[:, cc % 2, :], scalar1=rs1[:, cc:cc + 1], scalar2=None,
                                                              op0=ALU.mult), reads=[f"xst{cc % 2}", f"rs1_{cc0}"], writes=[f"xn4_{cc}"])

    def emit_s1b(buf=None, banks=None, alias_x1=True):
        buf = xn4 if buf is None else buf
        banks = S1_BANKS if banks is None else banks
        for c in range(NCH):
            keys = [f"xn4_{c}"] + ([f"x1_{c // 2}"] if alias_x1 else [])
            transpose_to_hT(buf[:, c, :], keys, c, banks)

    def emit_sc(kind, si, xsrc_dram, pos0, s1_done=False, next_s1=None, next_s1a=None, next_s1b=None):
        main = kind == "main"
        last_pre = kind == "prelast"
        need_k = main or last_pre
        tok0 = si * T
        hTk = [f"hT{c}" for c in range(NCH)]
        if not s1_done:
            emit_s1(si, xsrc_dram, main)
        elif main and si > 0:
            em.dma(x1buf[:], xsrc_dram[tok0:tok0 + T, :].rearrange("(c p) d -> p c d", p=128), writes=[f"x1_{cc}" for cc in range(NCH)])
        if stop == "A1":
            em.barrier()
            return
        EO("pool", lambda e: e.tensor_copy(out=uT[:, :, 0:3], in_=utail[:]), reads=["utail"], writes=["uTtail"])
        if main:
            EO("pool", lambda e: e.tensor_copy(out=kT[:, :, 0:128], in_=khalo[:]), reads=["khalo"], writes=["kThalo"])
            EO("pool", lambda e: e.memset(Vaug[:], 1.0), writes=["Vaug_all"] + [f"Vaug{i}" for i in range(5)])
            EO("pool", lambda e: e.tensor_copy(out=Vaug[:, 0, :], in_=vhalo[:]), reads=["vhalo", "Vaug_all"], writes=["Vaug0"])
            if si == 0:
                EO("dve", lambda e: e.tensor_scalar(out=uT[:, :, 0:3], in0=uT[:, :, 0:3], scalar1=flag, scalar2=None, op0=ALU.mult),
                   reads=["uTtail", "cols"], writes=["uTtail"])
                EO("dve", lambda e: e.tensor_scalar(out=Vaug[:, 0, :], in0=Vaug[:, 0, :], scalar1=flag, scalar2=None, op0=ALU.mult),
                   reads=["Vaug0", "cols"], writes=["Vaug0"])
                EO("dve", lambda e: e.tensor_scalar(out=prevT[:], in0=prevT[:], scalar1=flag, scalar2=None, op0=ALU.mult),
                   reads=["prevT", "cols"], writes=["prevT"])
                EO("dve", lambda e: e.tensor_copy(out=prevTb[:], in_=prevT[:]), reads=["prevT"], writes=["prevTb"])
        else:
            EO("pool", lambda e: e.memset(Vaug[:], 1.0), writes=["Vaug_all"] + [f"Vaug{i}" for i in range(5)])
        if need_k:
            blk = pos0 // T
            em.dma(cosT[:], rope_scr[blk][:, 0, :], writes=["cT"])
            em.dma(sinT[:], rope_scr[blk][:, 1, :], writes=["sT"])
        if stop == "A2":
            em.barrier()
            return

        def fm_group(m, bank, bkey):
            def f(e):
                last = None
                for k in range(KD):
                    last = e.matmul(bank[:], lhsT=w_in_bf[:, k, m * 128:(m + 1) * 128], rhs=hT[:, k, :], start=(k == 0), stop=(k == KD - 1))
                return last
            EO("pe", f, reads=hTk, writes=pk(bkey))

        def rope_pair(m_plain, m_sw, dst, dkey, par):
            b0, b1 = PA[2 * par], PA[2 * par + 1]
            fm_group(m_plain, b0, f"PA{2 * par}")
            fm_group(m_sw, b1, f"PA{2 * par + 1}")
            EO("dve", lambda e: e.scalar_tensor_tensor(out=rt1[:], in0=b0[:], scalar=bias_fm[:, m_plain:m_plain + 1], in1=cosT[:],
                                                      op0=ALU.add, op1=ALU.mult), reads=pk(f"PA{2 * par}") + ["cT", "rt1"], writes=["rt1"])
            EO("dve", lambda e: e.scalar_tensor_tensor(out=rt2[:], in0=b1[:], scalar=bias_fm[:, m_sw:m_sw + 1], in1=sinT[:],
                                                      op0=ALU.add, op1=ALU.mult), reads=pk(f"PA{2 * par + 1}") + ["sT", "rt2"], writes=["rt2"])
            EO("pool", lambda e: e.tensor_tensor(out=dst, in0=rt1[:], in1=rt2[:], op=ALU.add), reads=["rt1", "rt2"], writes=[dkey])

        nxc = 8 if (main or last_pre) else 6
        xbanks = ((PC[0], "PC0"), (PC[1], "PC1"), (PE0, "PE0"))

        def xgroup(j):
            m = 12 + j
            xb, xk = xbanks[j % 3]
            fm_group(m, xb, xk)
            EO("act", lambda e: e.activation(out=uT[:, j, 3:3 + T], in_=xb[:], func=AF.Identity, bias=bias_fm[:, m:m + 1]),
               reads=[xk], writes=[f"uT{j}"])
        pairs = []
        if main:
            pairs += [(j, 4 + j, qT[:, j, :], f"qT{j}") for j in range(4)]
        if need_k:
            pairs += [(8 + g, 10 + g, kT[:, g, 128:128 + T], f"kT{g}") for g in range(2)]
        xq = list(range(nxc))
        par = 0
        for (mp, ms, dst, dkey) in pairs:
            rope_pair(mp, ms, dst, dkey, par)
            par ^= 1
            for _ in range(2):
                if xq:
                    xgroup(xq.pop(0))
        while xq:
            xgroup(xq.pop(0))
        if need_k:
            EO("pool", lambda e: e.tensor_copy(out=khalo[:], in_=kT[:, :, T:T + 128]), reads=["kT0", "kT1"], writes=["khalo"])
            if main:
                for g in range(2):
                    for hp in range(2):
                        EO("dve", lambda e, g=g, hp=hp: e.tensor_scalar(out=kTz[:, g, hp, :], in0=kT[:, g, :], scalar1=hmask[:, hp:hp + 1],
                                                                         scalar2=None, op0=ALU.mult),
                           reads=[f"kT{g}", "kThalo", "cols"], writes=[f"kTz{g}"])
        uTk = [f"uT{j}" for j in range(nxc)] + ["uTtail"]
        EO("pool", lambda e: e.tensor_copy(out=utail[:], in_=uT[:, :, T:T + 3]), reads=uTk, writes=["utail"])
        if stop == "A3":
            em.barrier()
            return
        for c in range(NCH):
            def fvd(e, c=c):
                for k in range(KD):
                    e.matmul(PE0[:, 0:136], lhsT=hT[:, k, c * 128:(c + 1) * 128], rhs=w_in_bf[:, k, CV:CV + 136], start=(k == 0), stop=False)
                return e.matmul(PE0[:, 0:136], lhsT=onesb[0:1, :], rhs=bias_row[0:1, 0:136], start=False, stop=True)
            EO("pe", fvd, reads=[f"hT{c}"], writes=["PE0"])
            EO("act", lambda e, c=c: e.activation(out=Vaug[:, c + 1, :].rearrange("p (g d) -> p g d", g=2)[:, :, 0:64],
                                                  in_=PE0[:, 0:128].rearrange("p (g d) -> p g d", g=2), func=AF.Copy),
               reads=["PE0", "Vaug_all"], writes=[f"Vaug{c + 1}"])
            EO("dve", lambda e, c=c: e.tensor_tensor(out=dtraw[:, c * 8:(c + 1) * 8], in0=PE0[:, 128:136], in1=dtb, op=ALU.add),
               reads=["PE0", "dtb"], writes=[f"dtraw{c}"])
        EO("pool", lambda e: e.tensor_copy(out=vhalo[:], in_=Vaug[:, 4, :]), reads=["Vaug4"], writes=["vhalo"])
        if next_s1 is not None:
            next_s1()
        if main:
            for c in range(NCH):
                def fz(e, c=c):
                    for k in range(KD):
                        e.matmul(PA[c][:], lhsT=hT[:, k, c * 128:(c + 1) * 128], rhs=w_in_bf[:, k, CZ:CZ + 512], start=(k == 0), stop=False)
                    return e.matmul(PA[c][:], lhsT=onesb[0:1, :], rhs=bias_row[0:1, 136:648], start=False, stop=True)
                EO("pe", fz, reads=[f"hT{c}"], writes=[f"PA{c}"])
        dk = [f"dtraw{c}" for c in range(NCH)]
        EO("dve", lambda e: e.scalar_tensor_tensor(out=dtv, in0=dtraw, scalar=-1.0, in1=dtraw, op0=ALU.mult, op1=ALU.max), reads=dk, writes=["dtv"])
        EO("act", lambda e: e.activation(out=dtv, in_=dtv, func=AF.Exp, scale=-1.0), reads=["dtv"], writes=["dtv"])
        EO("act", lambda e: e.activation(out=dtv, in_=dtv, func=AF.Ln, bias=1.0), reads=["dtv"], writes=["dtv"])
        EO("dve", lambda e: e.scalar_tensor_tensor(out=dtv, in0=dtraw, scalar=0.0, in1=dtv, op0=ALU.max, op1=ALU.add),
           reads=dk + ["dtv"], writes=["dtv"])
        EO("dve", lambda e: e.tensor_tensor(out=av.rearrange("p (c h) -> p c h", c=NCH), in0=dtv.rearrange("p (c h) -> p c h", c=NCH),
                                            in1=bcm(aneg, NCH), op=ALU.mult), reads=["dtv", "aneg"], writes=["av"])
        def fsm(e):
            last = None
            for c in range(NCH):
                a_c = av[:, c * 8:(c + 1) * 8]
                o = 136 + c * 24
                e.matmul(PE0[:, o:o + 8], lhsT=triGT, rhs=a_c, start=True, stop=True)
                e.matmul(PE0[:, o + 8:o + 16], lhsT=triLE, rhs=a_c, start=True, stop=True)
                last = e.matmul(PE0[:, o + 16:o + 24], lhsT=onesf, rhs=a_c, start=True, stop=True)
            return last
        EO("pe", fsm, reads=["av", "cst", "onesf"], writes=["PE0"])
        EO("act", lambda e: e.activation(out=exv[:].rearrange("p c k -> p (c k)"), in_=PE0[:, 136:136 + NCH * 24], func=AF.Exp),
           reads=["PE0"], writes=[f"exv{c}" for c in range(NCH)])
        EO("dve", lambda e: e.tensor_tensor(out=wend[:], in0=dtv.rearrange("p (c h) -> p c h", c=NCH), in1=exv[:, :, 0:8], op=ALU.mult),
           reads=["dtv"] + [f"exv{c}" for c in range(NCH)], writes=[f"wend{c}" for c in range(NCH)])
        if main:
            for c in range(NCH):
                EO("act", lambda e, c=c: e.activation(out=sz[:, c, :], in_=PA[c][:], func=AF.Silu), reads=[f"PA{c}"], writes=[f"sz{c}"])
        if next_s1a is not None:
            next_s1a()
        for c in range(NCH):
            def fcx(e, c=c):
                last = None
                for j in range(4):
                    for k in range(4):
                        e.matmul(PC[1][:, j * 128:(j + 1) * 128], lhsT=uT[:, j, c * 128 + k:c * 128 + k + 128], rhs=cdiag[:, j * 4 + k, :],
                                 start=(k == 0), stop=False)
                    last = e.matmul(PC[1][:, j * 128:(j + 1) * 128], lhsT=onesb, rhs=bdiag[:, j, :], start=False, stop=True)
                return last
            EO("pe", fcx, reads=uTk, writes=pk("PC1"))
            EO("act", lambda e: e.activation(out=xs[:], in_=PC[1][:], func=AF.Silu), reads=pk("PC1") + ["xs"], writes=["xs"])
            xs3 = xs[:].rearrange("p (h d) -> p h d", h=8)
            EO("dve", lambda e, c=c: e.tensor_tensor(out=xdt[:, c, :].rearrange("p (h d) -> p h d", h=8), in0=xs3,
                                                     in1=bcl(dtv[:, c * 8:(c + 1) * 8], 64), op=ALU.mult), reads=["xs", "dtv"], writes=[f"xdt{c}"])
            EO("dve", lambda e, c=c: e.tensor_tensor(out=xdtd[:, c, :].rearrange("p (h d) -> p h d", h=8), in0=xs3,
                                                     in1=bcl(wend[:, c, :], 64), op=ALU.mult), reads=["xs", f"wend{c}"], writes=[f"xdtd{c}"])
            if main:
                EO("pool", lambda e, c=c: e.tensor_tensor(out=xsD[:, c, :].rearrange("p (h d) -> p h d", h=8), in0=xs3,
                                                          in1=bcl(dskip, 64), op=ALU.mult), reads=["xs", "dskip"], writes=[f"xsD{c}"])

            def fcb(e, c=c):
                last = None
                for jj in range(2):
                    j = 4 + jj
                    for k in range(4):
                        e.matmul(PC[0][:, jj * 128:(jj + 1) * 128], lhsT=uT[:, j, c * 128 + k:c * 128 + k + 128], rhs=cdiag[:, j * 4 + k, :],
                                 start=(k == 0), stop=False)
                    last = e.matmul(PC[0][:, jj * 128:(jj + 1) * 128], lhsT=onesb, rhs=bdiag[:, j, :], start=False, stop=True)
                return last
            EO("pe", fcb, reads=uTk, writes=["PC0"])
            EO("act", lambda e, c=c: e.activation(out=Btm[:, c, :], in_=PC[0][:, 0:256], func=AF.Silu), reads=["PC0"], writes=[f"Btm{c}"])
        if main:
            for jj in range(4):
                j = 4 + jj

                def fcf(e, j=j, jj=jj):
                    last = None
                    for k in range(4):
                        last = e.matmul(PA[jj][:], lhsT=cdiag[:, j * 4 + k, :], rhs=uT[:, j, k:k + T], start=(k == 0), stop=(k == 3))
                    return last
                EO("pe", fcf, reads=uTk, writes=pk(f"PA{jj}"))
                EO("act", lambda e, j=j, jj=jj: e.activation(out=BCT[:, jj, :], in_=PA[jj][:], func=AF.Silu, bias=convb[:, j:j + 1]),
                   reads=pk(f"PA{jj}") + ["cols"], writes=[f"BCT{jj}"])
        if main:
            em.barrier()
        if stop == "A":
            return

        TPf = TP[:].rearrange("p a b -> p (a b)").bitcast(F32)

        def state_update(c, bank, bkey):
            def fst(e, c=c):
                last = None
                for g in range(2):
                    last = e.matmul(bank[:, g * 256:(g + 1) * 256], lhsT=Btm[:, c, g * 128:(g + 1) * 128], rhs=xdtd[:, c, g * 256:(g + 1) * 256],
                                    start=True, stop=True)
                return last
            EO("pe", fst, reads=[f"Btm{c}", f"xdtd{c}"], writes=[bkey])
            EO("dve", lambda e, c=c: e.tensor_tensor(out=prevT[:].rearrange("p (h d) -> p h d", h=8), in0=prevT[:].rearrange("p (h d) -> p h d", h=8),
                                                     in1=bcl(exv[:, c, 16:24], 64), op=ALU.mult), reads=["prevT", f"exv{c}"], writes=["prevT"])
            EO("dve", lambda e: e.tensor_tensor(out=prevT[:], in0=bank, in1=prevT[:], op=ALU.add), reads=[bkey, "prevT"], writes=["prevT"])
            EO("act", lambda e: e.activation(out=prevTb[:], in_=prevT[:], func=AF.Copy), reads=["prevT"], writes=["prevTb"])

        if not main:
            for c in range(NCH):
                state_update(c, PC[1][:], "PC1")
            if next_s1b is not None:
                next_s1b()
            return

        def E1(c):
            a_c = av[:, c * 8:(c + 1) * 8]
            EO("dve", lambda e, a_c=a_c: e.tensor_tensor(out=aTri[:].rearrange("p (h l) -> p h l", h=8), in0=bcm(triLE, 8), in1=bcl(a_c, 128),
                                                         op=ALU.mult), reads=["av", "cst"], writes=["aTri"])

            def scores(g):
                for blk in range(2):
                    bank = PA[blk]
                    kc0 = c * 128 + blk * 128

                    def fsc(e, g=g, blk=blk, bank=bank, kc0=kc0):
                        moff = 0 if blk == 1 else 512
                        last = None
                        for hp in range(2):
                            for jj in range(2):
                                r0 = (hp * 2 + jj) * 128
                                e.matmul(bank[:, r0:r0 + 128], lhsT=identb, rhs=maskb[:, moff:moff + 128], start=True, stop=False)
                                last = e.matmul(bank[:, r0:r0 + 128], lhsT=kTz[:, g, hp, kc0:kc0 + 128],
                                                rhs=qT[:, 2 * g + jj, c * 128:(c + 1) * 128], start=False, stop=True)
                        return last
                    EO("pe", fsc, reads=[f"qT{2 * g}", f"qT{2 * g + 1}", f"kTz{g}", "maskb", "identb"], writes=[f"PA{blk}"])
                    EO("act", lambda e, g=g, blk=blk, bank=bank: e.activation(out=PT[:, 2 * g + blk, :], in_=bank[:], func=AF.Exp, scale=0.125),
                       reads=[f"PA{blk}"], writes=[f"PT{2 * g + blk}"])
            scores(0)
            for hh in range(2):
                EO("pe", lambda e, hh=hh: e.matmul(PA[2 + hh][:], lhsT=triGT, rhs=aTri[:, hh * 512:(hh + 1) * 512], start=True, stop=True),
                   reads=["aTri", "cst"], writes=[f"PA{2 + hh}"])
                EO("act", lambda e, hh=hh: e.activation(out=Ebuf[:, hh * 512:(hh + 1) * 512], in_=PA[2 + hh][:], func=AF.Exp),
                   reads=[f"PA{2 + hh}"], writes=[f"E{hh}"])
            scores(1)

            def fcbm(e):
                last = None
                for g in range(2):
                    last = e.matmul(PE0[:, 256 + g * 128:256 + (g + 1) * 128], lhsT=BCT[:, g, c * 128:(c + 1) * 128],
                                    rhs=BCT[:, 2 + g, c * 128:(c + 1) * 128], start=True, stop=True)
                return last
            EO("pe", fcbm, reads=[f"BCT{i}" for i in range(4)], writes=["PE0"])

        def E1b(c):
            EO("dve", lambda e: e.tensor_tensor(out=CBm[:].rearrange("p (g l) -> p g l", g=2),
                                                in0=PE0[:, 256:512].rearrange("p (g l) -> p g l", g=2), in1=bcm(triLE, 2), op=ALU.mult),
               reads=["PE0", "cst"], writes=["CBm"])
            EO("dve", lambda e: e.tensor_tensor(out=MT[:].rearrange("p (g j l) -> p g j l", g=2, j=4),
                                                in0=Ebuf[:].rearrange("p (g j l) -> p g j l", g=2, j=4),
                                                in1=CBm[:].rearrange("p (g l) -> p g l", g=2).unsqueeze(2).to_broadcast([128, 2, 4, 128]),
                                                op=ALU.mult), reads=["E0", "E1", "CBm"], writes=["MT"])

        def E2a(c):
            for g in range(2):
                def fpv(e, g=g):
                    last = None
                    for i in range(4):
                        hp, jj = i % 2, i // 2
                        cb = hp * 256 + jj * 128
                        e.matmul(PC[g][:, i * 128:i * 128 + 65], lhsT=PT[:, 2 * g, cb:cb + 128], rhs=Vaug[:, c, g * 65:(g + 1) * 65],
                                 start=True, stop=False)
                        last = e.matmul(PC[g][:, i * 128:i * 128 + 65], lhsT=PT[:, 2 * g + 1, cb:cb + 128], rhs=Vaug[:, c + 1, g * 65:(g + 1) * 65],
                                        start=False, stop=True)
                    return last
                EO("pe", fpv, reads=[f"PT{2 * g}", f"PT{2 * g + 1}", f"Vaug{c}", f"Vaug{c + 1}", "Vaug_all"], writes=[f"PC{g}"])
                o3 = PC[g][:, :].rearrange("p (i d) -> p i d", i=4)
                EO("dve", lambda e, g=g, o3=o3: e.tensor_tensor(out=den[:, g * 4:(g + 1) * 4], in0=o3[:, :, 64], in1=esink[:, g * 4:(g + 1) * 4],
                                                              op=ALU.add), reads=[f"PC{g}", "esink"], writes=[f"den{g}"])
                EO("dve", lambda e, g=g: e.reciprocal(out=rden[:, g * 4:(g + 1) * 4], in_=den[:, g * 4:(g + 1) * 4]), reads=[f"den{g}"],
                   writes=[f"rden{g}"])
                EO("dve", lambda e, g=g, o3=o3: e.tensor_tensor(out=ytm[:, g * 256:(g + 1) * 256].rearrange("p (i d) -> p i d", i=4),
                                                              in0=o3[:, :, 0:64], in1=bcl(rden[:, g * 4:(g + 1) * 4], 64), op=ALU.mult),
                   reads=[f"PC{g}", f"rden{g}"], writes=[f"ytm_a{g}"])

            def fyd(e):
                last = None
                for h in range(8):
                    e.matmul(PC[0][:, h * 64:(h + 1) * 64], lhsT=identb, rhs=xsD[:, c, h * 64:(h + 1) * 64], start=True, stop=False)
                    last = e.matmul(PC[0][:, h * 64:(h + 1) * 64], lhsT=MT[:, h * 128:(h + 1) * 128], rhs=xdt[:, c, h * 64:(h + 1) * 64],
                                    start=False, stop=True)
                return last
            EO("pe", fyd, reads=["MT", f"xdt{c}", f"xsD{c}", "identb"], writes=["PC0"])

            def fyo(e):
                last = None
                for g in range(2):
                    last = e.matmul(PC[1][:, g * 256:(g + 1) * 256], lhsT=BCT[:, 2 + g, c * 128:(c + 1) * 128],
                                    rhs=prevTb[:, g * 256:(g + 1) * 256], start=True, stop=True)
                return last
            EO("pe", fyo, reads=["BCT2", "BCT3", "prevTb"], writes=["PC1"])
            state_update(c, TPf, "TP")

        def E2b(c):
            EO("dve", lambda e: e.tensor_tensor(out=yt[:].rearrange("p (h d) -> p h d", h=8),
                                                in0=PC[1][:].rearrange("p (h d) -> p h d", h=8), in1=bcl(exv[:, c, 8:16], 64), op=ALU.mult),
               reads=["PC1", f"exv{c}"], writes=["yt"])
            EO("dve", lambda e: e.tensor_tensor(out=yt[:], in0=PC[0][:], in1=yt[:], op=ALU.add), reads=["PC0", "yt"], writes=["yt"])
            EO("dve", lambda e: e.tensor_tensor(out=yt[:], in0=yt[:], in1=sz[:, c, :], op=ALU.mult), reads=["yt", f"sz{c}"], writes=["yt"])
            for g in range(2):
                EO("dve", lambda e, g=g: e.scalar_tensor_tensor(out=sqj[:, g * 256:(g + 1) * 256], in0=yt[:, g * 256:(g + 1) * 256], scalar=1.0,
                                                               in1=yt[:, g * 256:(g + 1) * 256], op0=ALU.mult, op1=ALU.mult,
                                                               accum_out=ssg[:, g:g + 1]),
                   reads=["yt", "sqj"], writes=["sqj", f"ssg{g}"])
            rstd_from_ss(ssg, rsg, 256 * EPS, ["ssg0", "ssg1"], "rsg")
            for g in range(2):
                EO("dve", lambda e, g=g: e.scalar_tensor_tensor(out=ytm[:, 512 + g * 256:512 + (g + 1) * 256], in0=yt[:, g * 256:(g + 1) * 256],
                                                               scalar=rsg[:, g:g + 1], in1=ssmw16[:, g * 256:(g + 1) * 256], op0=ALU.mult,
                                                               op1=ALU.mult), reads=["yt", "rsg", "ssmw16"], writes=[f"ytm_s{g}"])

        def E2c(c):
            def tpy(e):
                last = None
                for f in range(KD):
                    last = e.transpose(out=TP[:, f, :], in_=ytm[:, f * 128:(f + 1) * 128], identity=identb)
                return last
            EO("pe", tpy, reads=["ytm_a0", "ytm_a1", "ytm_s0", "ytm_s1", "identb"], writes=["TP"])
            EO("act", lambda e: e.activation(out=hT[:, :, c * 128:(c + 1) * 128], in_=TP[:], func=AF.Copy), reads=["TP"], writes=[f"hT{c}"])

        E1(0)
        E1b(0)
        for c in range(NCH):
            E2a(c)
            if c + 1 < NCH:
                E1(c + 1)
            E2b(c)
            if c + 1 < NCH:
                E1b(c + 1)
            E2c(c)
        if not main:
            return
        if stop == "B":
            em.barrier()
            return
        def s8a(c):
            EO("act", lambda e: e.activation(out=sqj[:], in_=x1buf[:, c, :], func=AF.Square, accum_out=ss1[:, c:c + 1]),
               reads=[f"x1_{c}", "sqj"], writes=["sqj", f"ss1_{c}"])
            rstd_from_ss(ss1[:, c:c + 1], rs1[:, c:c + 1], D * EPS, [f"ss1_{c}"], f"rs8_{c}")
            EO("act", lambda e: e.activation(out=xnB[:], in_=x1buf[:, c, :], func=AF.Identity, scale=rs1[:, c:c + 1]),
               reads=[f"x1_{c}", f"rs8_{c}", "xnB"], writes=["xnB"])

        def s8b(c):
            transpose_to_hT(xnB, ["xnB"], c, ((PC[0], "PC0"), (PC[1], "PC1")))

        for p in range(3):
            em.dma(wgus[:, p, :, :], wgu_scr[p].rearrange("p (k c) -> p k c", k=KD), writes=[f"wgus{p}"])
        for c in range(NCH):
            for dh in range(2):
                bi = (2 * c + dh) % 4

                def fop(e, c=c, dh=dh, bi=bi):
                    last = None
                    for k in range(KD):
                        last = e.matmul(PA[bi][:], lhsT=hT[:, k, c * 128:(c + 1) * 128], rhs=w_out_bf[:, k, dh * 512:(dh + 1) * 512],
                                        start=(k == 0), stop=(k == KD - 1))
                    return last
                EO("pe", fop, reads=[f"hT{c}"], writes=pk(f"PA{bi}"))
                EO("dve", lambda e, c=c, dh=dh, bi=bi: e.tensor_tensor(out=x1buf[:, c, dh * 512:(dh + 1) * 512], in0=PA[bi][:],
                                                                    in1=x1buf[:, c, dh * 512:(dh + 1) * 512], op=ALU.add),
                   reads=pk(f"PA{bi}") + [f"x1_{c}"], writes=[f"x1_{c}"])
            if c > 0:
                s8b(c - 1)
            s8a(c)
        s8b(NCH - 1)
        em.barrier()
        if stop == "OP":
            return
        for ffc in range(NFF):
            slot = ffc % 3
            bg, bu = PA[2 * (ffc % 2)], PA[2 * (ffc % 2) + 1]
            kg, ku = f"PA{2 * (ffc % 2)}", f"PA{2 * (ffc % 2) + 1}"

            def fup(e, slot=slot, bg=bg, bu=bu):
                last = None
                for t, bank in ((0, bg), (1, bu)):
                    for k in range(KD):
                        last = e.matmul(bank[:], lhsT=wgus[:, slot, k, t * 128:(t + 1) * 128], rhs=hT[:, k, :], start=(k == 0), stop=(k == KD - 1))
                return last
            EO("pe", fup, reads=hTk + [f"wgus{slot}"], writes=pk(kg) + pk(ku))
            if ffc + 3 < NFF:
                em.dma(wgus[:, slot, :, :], wgu_scr[ffc + 3].rearrange("p (k c) -> p k c", k=KD), writes=[f"wgus{slot}"])
            sgs = sg[:, ffc % 2, :]
            EO("act", lambda e, ffc=ffc, bg=bg, sgs=sgs: e.activation(out=sgs, in_=bg[:], func=AF.Silu, bias=bias_gu[:, ffc:ffc + 1]),
               reads=pk(kg) + [f"sg{ffc % 2}"], writes=[f"sg{ffc % 2}"])
            EO("dve", lambda e, ffc=ffc, bu=bu, sgs=sgs: e.scalar_tensor_tensor(out=actT[:, ffc, :], in0=bu[:], scalar=bias_gu[:, NFF + ffc:NFF + ffc + 1],
                                                                             in1=sgs, op0=ALU.add, op1=ALU.mult),
               reads=pk(ku) + [f"sg{ffc % 2}"], writes=[f"actT{ffc}"])
        if si + 1 < n_main:
            emit_s1a(si + 1, xown, junk=xnC, buf=xn4C)
        actk = [f"actT{f}" for f in range(NFF)]
        for dq in range(4):
            s = dq % 2
            em.dma(wdns[:, s, :, :], wdn_scr[dq].rearrange("p (f c) -> p f c", f=NFF), writes=[f"wdns{s}"])
            for c in range(NCH):
                reg = (dq * NCH + c) % 4
                bank = (PC[0], PC[1], PE0, PA[0])[reg][:, 0:256]
                bkey = ("PC0", "PC1", "PE0", "PA0")[reg]

                def fdn(e, s=s, c=c, bank=bank):
                    last = None
                    for f in range(NFF):
                        last = e.matmul(bank, lhsT=actT[:, f, c * 128:(c + 1) * 128], rhs=wdns[:, s, f, :], start=(f == 0), stop=(f == NFF - 1))
                    return last
                EO("pe", fdn, reads=actk + [f"wdns{s}"], writes=[bkey])
                EO("dve", lambda e, c=c, dq=dq, bank=bank: e.tensor_tensor(out=x1buf[:, c, dq * 256:(dq + 1) * 256], in0=bank,
                                                                        in1=x1buf[:, c, dq * 256:(dq + 1) * 256], op=ALU.add),
                   reads=[bkey, f"x1_{c}"], writes=[f"x1_{c}"])
        if si + 1 < n_main:
            emit_s1b(buf=xn4C, banks=((PA[2], "PA2"), (PA[3], "PA3")), alias_x1=False)
        for c in range(NCH):
            EO("act", lambda e, c=c: e.activation(out=xnC[:], in_=x1buf[:, c, :], func=AF.Square, accum_out=ss1[:, c:c + 1]),
               reads=[f"x1_{c}", "xn"], writes=["xn", f"ss1_{c}"])
        rstd_from_ss(ss1, rs1, D * EPS, [f"ss1_{c}" for c in range(NCH)], "rs1_fin")
        for c in range(NCH):
            EO("dve", lambda e, c=c: e.scalar_tensor_tensor(out=x1buf[:, c, :], in0=x1buf[:, c, :], scalar=rs1[:, c:c + 1], in1=fnw32,
                                                           op0=ALU.mult, op1=ALU.mult), reads=[f"x1_{c}", "rs1_fin", "fnw32"], writes=[f"x1_{c}"])
            em.dma(out[tok0 + c * 128:tok0 + (c + 1) * 128, :], x1buf[:, c, :], reads=[f"x1_{c}"], writes=[f"out{si}_{c}"])
        em.barrier()

    seq = [("prelast" if si == NSC - 1 else "pre", si, xprev, 0) for si in range(NSC - n_pre, NSC)]
    seq += [("main", si, xown, T + si * T) for si in range(n_main)]
    for idx, (kind, si, src, pos0) in enumerate(seq):
        s1_done = idx > 0
        nxt = nxa = nxb = None
        if kind != "main" and idx + 1 < len(seq):
            nk, nsi, nsrc, _ = seq[idx + 1]
            if nk == "main":
                nxt = (lambda nsi=nsi, nsrc=nsrc: emit_s1(nsi, nsrc, True))
            else:
                nxa = (lambda nsi=nsi, nsrc=nsrc: emit_s1a(nsi, nsrc))
                nxb = emit_s1b
        emit_sc(kind, si, src, pos0, s1_done, nxt, nxa, nxb)
    em.barrier()
    stats = em.finalize()
    return nc, stats


def _gather_cols():
    idx = []
    idx += list(range(0, 512))
    idx += [64 * h + (d + 32) % 64 for h in range(8) for d in range(64)]
    for g in range(2):
        idx += [512 + 64 * g + d for d in range(64)] * 2
    for g in range(2):
        idx += [512 + 64 * g + (d + 32) % 64 for d in range(64)] * 2
    idx += list(range(1280, 2304))
    idx += list(range(640, 768))
    idx += list(range(2304, 2312))
    idx += list(range(768, 1280))
    assert len(idx) == NW
    return np.array(idx)


def _consts():
    p = np.arange(128)
    ident = (p[:, None] == p[None, :]).astype(np.float32)
    triLE = (p[:, None] <= p[None, :]).astype(np.float32)
    triGT = (p[:, None] > p[None, :]).astype(np.float32)
    mcur = np.where(p[:, None] <= p[None, :], 0.0, NEG).astype(np.float32)
    mprev = np.where(p[:, None] > p[None, :], 0.0, NEG).astype(np.float32)
    cp = np.concatenate([ident, triLE, triGT, np.tile(mcur, (1, 4)), np.tile(mprev, (1, 4))], axis=1)
    return np.ascontiguousarray(cp.astype(np.float32))


_PROG = {}


def kernel(x, c, positions, w_ada, b_ada, norm1_w, w_in, conv_w, conv_b, dt_bias, a_log, d_skip, attn_sinks,
           ssm_norm_w, w_out, norm2_w, w_gate_up, w_down, final_norm_w):
    f32 = np.float32
    x = np.asarray(x, f32)
    if "p" not in _PROG:
        _PROG["p"] = build_program()
    nc, stats = _PROG["p"]
    w_in_g = np.ascontiguousarray(np.asarray(w_in, f32)[0][:, _gather_cols()])
    cpack = _consts()
    half = 32
    inv_freq = (10000.0 ** (-np.arange(half, dtype=np.float32) / np.float32(half))).astype(f32)
    p = np.arange(128)
    rows = np.concatenate([np.asarray(final_norm_w, f32), np.asarray(ssm_norm_w, f32)[0], np.asarray(dt_bias, f32)[0],
                           np.asarray(a_log, f32)[0], np.asarray(d_skip, f32)[0], np.asarray(attn_sinks, f32)[0]])
    rowpack = np.ascontiguousarray(np.tile(rows[None, :], (128, 1)))
    badap = np.ascontiguousarray(np.tile(np.asarray(b_ada, f32)[0][None, :], (128, 1)))
    in_maps = []
    for i in range(8):
        b, hf = i // 2, i % 2
        colp = np.zeros((128, 80), f32)
        colp[:, 0:8] = np.asarray(c, f32)[b].reshape(8, 128).T
        cw = np.asarray(conv_w, f32)[0]
        colp[:, 8:40] = cw.reshape(4, 8, 128).transpose(2, 1, 0).reshape(128, 32)
        colp[:, 40:48] = np.asarray(conv_b, f32)[0].reshape(8, 128).T
        colp[:, 48] = inv_freq[p % 32]
        colp[:, 49] = float(hf)
        colp[:, 50] = np.where((p % 64) < 32, -1.0, 1.0)
        colp[:, 51:59] = np.asarray(norm1_w, f32)[0].reshape(8, 128).T
        colp[:, 59:67] = np.asarray(norm2_w, f32)[0].reshape(8, 128).T
        colp[:, 67] = (p < 64).astype(f32)
        colp[:, 68] = (p >= 64).astype(f32)
        pos_b = np.asarray(positions)[b].astype(np.int32)
        pos_cat = np.concatenate([pos_b[SEQ_HALF - T:SEQ_HALF], pos_b[hf * SEQ_HALF:(hf + 1) * SEQ_HALF]])
        in_maps.append({
            "xprev": np.ascontiguousarray(x[b, 0:SEQ_HALF]),
            "xown": np.ascontiguousarray(x[b, hf * SEQ_HALF:(hf + 1) * SEQ_HALF]),
            "posrep": np.ascontiguousarray(np.tile(pos_cat[None, :], (128, 1))),
            "colpack": colp, "rowpack": rowpack, "bada": badap, "cpack": cpack,
            "w_ada": np.ascontiguousarray(np.asarray(w_ada, f32)[0]), "w_in": w_in_g,
            "w_out": np.ascontiguousarray(np.asarray(w_out, f32)[0]),
            "w_gu": np.ascontiguousarray(np.asarray(w_gate_up, f32)[0]),
            "w_dn": np.ascontiguousarray(np.asarray(w_down, f32)[0]),
        })
    res = run_bass_kernel_spmd(nc, in_maps, core_ids=list(range(8)))
    outp = np.empty((4, 2 * SEQ_HALF, D), f32)
    for i in range(8):
        b, hf = i // 2, i % 2
        outp[b, hf * SEQ_HALF:(hf + 1) * SEQ_HALF] = res.results[i]["out"]
    return outp
```

```python
import numpy as np
import concourse.bass as bass
import concourse.mybir as mybir
from concourse.bass_utils import run_bass_kernel_spmd

F32 = mybir.dt.float32
BF16 = mybir.dt.bfloat16
I32 = mybir.dt.int32
AF = mybir.ActivationFunctionType
ALU = mybir.AluOpType
AX = mybir.AxisListType

EPOCH = 2000
N_DMA_SEMS = 12

D = 1024
KD = 8
SEQ_HALF = 4096
T = 512
NCH = 4
NSC = SEQ_HALF // T
DFF = 2816
NFF = 22
EPS = 1e-6
CQ, CQS, CK, CKS, CXC, CV, CDT, CZ, NW = 0, 512, 1024, 1280, 1536, 2560, 2688, 2696, 3208
NFM = 20
NTM = NW - CV
NEG = -30000.0


class _Op:
    __slots__ = ("eng", "fn", "reads", "writes", "dma", "deps", "signal", "barrier")

    def __init__(self, eng, fn, reads, writes, dma, barrier=False):
        self.eng, self.fn, self.reads, self.writes, self.dma = eng, fn, reads, writes, dma
        self.deps = ()
        self.signal = False
        self.barrier = barrier


class Emitter:
    def __init__(self, nc):
        self.nc = nc
        self.ops = []
        self.engines = {"pe": nc.tensor, "act": nc.scalar, "dve": nc.vector, "pool": nc.gpsimd, "sp": nc.sync}

    def op(self, eng, fn, reads=(), writes=()):
        self.ops.append(_Op(eng, fn, tuple(reads), tuple(writes), False))

    def dma(self, out, in_, reads=(), writes=(), eng="sp"):
        self.ops.append(_Op(eng, lambda e: e.dma_start(out=out, in_=in_), tuple(reads), tuple(writes), True))

    def barrier(self):
        self.ops.append(_Op("sp", lambda e: e.nop(), (), (), False, barrier=True))

    def finalize(self):
        nc = self.nc
        ops = self.ops
        n = len(ops)
        last_writer, readers = {}, {}
        last_on_eng = {}
        dma_since = []
        cur_barrier = None
        for i, o in enumerate(ops):
            deps = set()
            if o.barrier:
                deps.update(last_on_eng.values())
                deps.update(dma_since)
                dma_since = []
                last_writer, readers = {}, {}
            else:
                for r in o.reads:
                    w = last_writer.get(r)
                    if w is not None:
                        deps.add(w)
                for w_ in o.writes:
                    w = last_writer.get(w_)
                    if w is not None:
                        deps.add(w)
                    deps.update(readers.get(w_, ()))
                if o.eng == "pe":
                    deps = {d for d in deps if not (ops[d].eng == "pe" and not ops[d].dma)}
                if cur_barrier is not None:
                    deps.add(cur_barrier)
            deps.discard(i)
            o.deps = tuple(sorted(deps))
            for d in o.deps:
                ops[d].signal = True
            if o.barrier:
                cur_barrier = i
            for w_ in o.writes:
                last_writer[w_] = i
                readers[w_] = []
            for r in o.reads:
                if r not in o.writes:
                    readers.setdefault(r, []).append(i)
            last_on_eng[o.eng] = i
            if o.dma:
                dma_since.append(i)
        eng_count = {e: 0 for e in self.engines}
        eng_sems = {e: [] for e in self.engines}
        dma_sems = [nc.alloc_semaphore(name=f"dma{i}") for i in range(N_DMA_SEMS)]
        dma_val = [0] * N_DMA_SEMS
        rr = 0
        state = {e: {} for e in self.engines}
        sig = [None] * n
        clock = [None] * n
        nwaits = 0
        for i, o in enumerate(ops):
            E = self.engines[o.eng]
            st = state[o.eng]
            for d in o.deps:
                key, val, sem, semval = sig[d]
                if st.get(key, 0) >= val:
                    continue
                E.wait_ge(sem, semval)
                nwaits += 1
                for k2, v2 in clock[d].items():
                    if st.get(k2, 0) < v2:
                        st[k2] = v2
            if o.dma:
                k = rr
                rr = (rr + 1) % N_DMA_SEMS
                key = ("d", k)
                if st.get(key, 0) < dma_val[k]:
                    E.wait_ge(dma_sems[k], dma_val[k])
                    nwaits += 1
                    st[key] = dma_val[k]
                ins = o.fn(E)
                dma_val[k] += 16
                ins.then_inc(dma_sems[k], 16)
                sig[i] = (key, dma_val[k], dma_sems[k], dma_val[k])
                clk = dict(st)
                clk[key] = dma_val[k]
                clock[i] = clk
            else:
                ins = o.fn(E)
                if o.signal:
                    c = eng_count[o.eng]
                    ep, off = divmod(c, EPOCH)
                    if ep >= len(eng_sems[o.eng]):
                        eng_sems[o.eng].append(nc.alloc_semaphore(name=f"{o.eng}{ep}"))
                    sem = eng_sems[o.eng][ep]
                    ins.then_inc(sem, 1)
                    eng_count[o.eng] = c + 1
                    key = ("e", o.eng)
                    sig[i] = (key, c + 1, sem, off + 1)
                    clk = dict(st)
                    clk[key] = c + 1
                    clock[i] = clk
        return dict(nops=n, nwaits=nwaits, counts=dict(eng_count))


def bcl(ap, n):
    return ap.unsqueeze(2).to_broadcast([ap.shape[0], ap.shape[1], n])


def bcm(ap, n):
    return ap.unsqueeze(1).to_broadcast([ap.shape[0], n, ap.shape[1]])


def build_program(n_pre=NSC, n_main=NSC, dbg=None, stop=None):
    nc = bass.Bass("TRN2", target_bir_lowering=False)

    def din(name, shape, dt=F32):
        return nc.dram_tensor(name, list(shape), dt, kind="ExternalInput").ap()

    xprev = din("xprev", [SEQ_HALF, D])
    xown = din("xown", [SEQ_HALF, D])
    posrep = din("posrep", [128, T + SEQ_HALF], I32)
    colpack = din("colpack", [128, 80])
    rowpack = din("rowpack", [128, 1568])
    bada = din("bada", [128, 6144])
    cpack = din("cpack", [128, 1408])
    w_ada = din("w_ada", [D, 6144])
    w_in = din("w_in", [D, NW])
    w_out = din("w_out", [D, D])
    w_gu = din("w_gu", [D, 2 * DFF])
    w_dn = din("w_dn", [DFF, D])
    out = nc.dram_tensor("out", [SEQ_HALF, D], F32, kind="ExternalOutput").ap()
    wgu_scr = nc.dram_tensor("wgu_scr", [NFF, 128, KD * 256], BF16, kind="Internal").ap()
    wdn_scr = nc.dram_tensor("wdn_scr", [4, 128, NFF * 256], BF16, kind="Internal").ap()
    dbg_out = {}
    if dbg:
        for nm, shp in dbg.items():
            dbg_out[nm] = nc.dram_tensor("dbg_" + nm, list(shp), F32, kind="ExternalOutput").ap()

    def sb(name, shape, dt):
        return nc.sbuf_tensor(name, list(shape), dt).__enter__()

    def psum(name, shape, dt):
        return nc.psum_tensor(name, list(shape), dt).__enter__()

    w_in_bf = sb("w_in_bf", [128, KD, NW], BF16)
    w_out_bf = sb("w_out_bf", [128, KD, D], BF16)
    cst = sb("cst", [128, 512], F32)
    identf, triLE, triGT, onesf = cst[:, 0:128], cst[:, 128:256], cst[:, 256:384], cst[:, 384:512]
    cstb = sb("cstb", [128, 256], BF16)
    identb, onesb = cstb[:, 0:128], cstb[:, 128:256]
    maskb = sb("maskb", [128, 1024], BF16)
    cdiag = sb("cdiag", [128, 32, 128], BF16)
    bdiag = sb("bdiag", [128, 8, 128], BF16)
    rows = sb("rows", [128, 1568], F32)
    fnw32, ssmw16 = rows[:, 0:1024], rows[:, 1024:1536]
    dtb, aneg, dskip, esink = rows[:, 1536:1544], rows[:, 1544:1552], rows[:, 1552:1560], rows[:, 1560:1568]
    cols = sb("cols", [128, 80], F32)
    ccol, convw, convb = cols[:, 0:8], cols[:, 8:40], cols[:, 40:48]
    invf, flag, rsign = cols[:, 48:49], cols[:, 49:50], cols[:, 50:51]
    n1wc, n2wc = cols[:, 51:59], cols[:, 59:67]
    hmask = cols[:, 67:69]
    small = sb("small", [128, 256], F32)
    g1c, g2c, sh1c, sh2c = small[:, 0:8], small[:, 8:16], small[:, 16:24], small[:, 24:32]
    nhalf = small[:, 32:33]
    ss1 = small[:, 40:44]
    rs1 = small[:, 44:48]
    ssg = small[:, 48:50]
    rsg = small[:, 50:52]
    den = small[:, 56:64]
    rden = small[:, 64:72]
    bias_fm = small[:, 80:100]
    bias_gu = small[:, 100:144]
    dtraw = small[:, 144:176]
    dtv = small[:, 176:208]
    av = small[:, 208:240]
    tmp32 = sb("tmp32", [128, 128], F32)
    exv = sb("exv", [128, NCH, 24], F32)
    wend = sb("wend", [128, NCH, 8], F32)
    bias_row = sb("bias_row", [1, NTM], BF16)
    browf = sb("browf", [1, 2, 512], F32)
    prevT = sb("prevT", [128, 512], F32)
    prevTb = sb("prevTb", [128, 512], BF16)
    x1buf = sb("x1buf", [128, NCH, D], F32)
    xstage = sb("xstage", [128, 2, D], F32)
    hT = sb("hT", [128, KD, T], BF16)
    wgus = sb("wgus", [128, 3, KD, 256], BF16)
    utail = sb("utail", [128, 8, 3], BF16)
    khalo = sb("khalo", [128, 2, 128], BF16)
    vhalo = sb("vhalo", [128, 130], BF16)
    ARENA = 61440
    arena = sb("arena", [128, ARENA // 2], BF16)

    class Lay:
        def __init__(self, base=0):
            self.off = base

        def take(self, shape, dt):
            nel = int(np.prod(shape))
            nb = nel * (4 if dt == F32 or dt == I32 else 2)
            nb_al = (nb + 63) // 64 * 64
            o = self.off
            self.off += nb_al
            assert self.off <= ARENA, (self.off, ARENA)
            ap = arena[:, o // 2:(o + nb) // 2]
            if dt != BF16:
                ap = ap.bitcast(dt)
            if len(shape) == 2:
                return ap.rearrange("p (a b) -> p a b", a=shape[0])
            if len(shape) == 3:
                return ap.rearrange("p (a b c) -> p a b c", a=shape[0], b=shape[1])
            return ap

    L = Lay()
    qT = L.take([4, T], BF16)
    kT = L.take([2, 128 + T], BF16)
    kTz = L.take([2, 2, 128 + T], BF16)
    Vaug = L.take([5, 130], BF16)
    xdt = L.take([NCH, 512], BF16)
    xdtd = L.take([NCH, 512], BF16)
    xsD = L.take([NCH, 512], BF16)
    Btm = L.take([NCH, 256], BF16)
    BCT = L.take([4, T], BF16)
    sz = L.take([NCH, 512], BF16)
    shared_end = L.off
    LA = Lay(shared_end)
    uT = LA.take([8, T + 3], BF16)
    cosT = LA.take([T], F32)
    sinT = LA.take([T], F32)
    rt1 = LA.take([T], F32)
    rt2 = LA.take([T], F32)
    xs = LA.take([512], F32)
    xn = LA.take([D], BF16)
    posi = LA.take([T], I32)
    LB = Lay(shared_end)
    PT = LB.take([4, 512], BF16)
    aTri = LB.take([1024], F32)
    Ebuf = LB.take([1024], F32)
    MT = LB.take([1024], BF16)
    CBm = LB.take([256], F32)
    yt = LB.take([512], F32)
    xnB = LB.take([D], BF16)
    sqj = LB.take([D], BF16)
    ytm = LB.take([D], BF16)
    jnk = LB.take([256], BF16)
    LC = Lay()
    actT = LC.take([NFF, T], BF16)
    wdns = LC.take([2, NFF, 256], BF16)
    xnC = LC.take([D], BF16)
    sg = LC.take([2, T], BF16)
    xn4C = arena[:, 49152 // 2:(49152 + NCH * D * 2) // 2].rearrange("p (c d) -> p c d", c=NCH)
    assert LC.off <= 49152
    LS = Lay()
    stg = LS.take([2, KD * 512], F32)
    cvo = LS.take([2, KD * 512], BF16)
    scb = LS.take([KD, 128], F32)
    rowst = LS.take([1568], F32)
    mod_lo = x1buf[:].rearrange("p a b -> p (a b)")
    mod_hi = hT[:].rearrange("p a b -> p (a b)").bitcast(F32)

    def modbc(c0, c1):
        if c1 <= 4096:
            return mod_lo[:, c0:c1]
        assert c0 >= 4096
        return mod_hi[:, c0 - 4096:c1 - 4096]

    TP = psum("TP", [128, 8, 128], BF16)
    PA = [psum(f"PA{i}", [128, 512], F32) for i in range(4)]
    PC = [psum(f"PC{i}", [128, 512], F32) for i in range(2)]
    PE0 = psum("PE0", [128, 512], F32)

    def pk(name):
        return [name]

    em = Emitter(nc)
    EO = em.op

    def dump(name, ap, reads):
        if name in dbg_out:
            em.dma(dbg_out[name], ap, reads=reads, writes=["dbg_" + name])

    em.dma(cst[:, 0:384], cpack[:, 0:384], writes=["cst"])
    em.dma(cols[:], colpack, writes=["cols"])
    em.dma(rowst[:], rowpack, writes=["rowst"])
    EO("dve", lambda e: e.memset(onesf, 1.0), writes=["onesf"])
    EO("dve", lambda e: e.memset(onesb, 1.0), writes=["onesb"])
    EO("dve", lambda e: e.memset(nhalf, -0.5), writes=["nhalf"])
    EO("dve", lambda e: e.tensor_copy(out=identb, in_=identf), reads=["cst"], writes=["identb"])
    em.dma(stg[:, 0, 0:1024], cpack[:, 384:1408], writes=["stg0"])
    EO("dve", lambda e: e.tensor_copy(out=maskb[:], in_=stg[:, 0, 0:1024]), reads=["stg0"], writes=["maskb"])
    EO("dve", lambda e: e.tensor_scalar(out=fnw32, in0=rowst[:, 0:1024], scalar1=32.0, scalar2=None, op0=ALU.mult),
       reads=["rowst"], writes=["fnw32"])
    EO("dve", lambda e: e.tensor_scalar(out=ssmw16, in0=rowst[:, 1024:1536], scalar1=16.0, scalar2=None, op0=ALU.mult),
       reads=["rowst"], writes=["ssmw16"])
    EO("dve", lambda e: e.tensor_copy(out=rows[:, 1536:1544], in_=rowst[:, 1536:1544]), reads=["rowst"], writes=["dtb"])
    EO("dve", lambda e: e.tensor_copy(out=dskip, in_=rowst[:, 1552:1560]), reads=["rowst"], writes=["dskip"])
    EO("act", lambda e: e.activation(out=aneg, in_=rowst[:, 1544:1552], func=AF.Exp), reads=["rowst"], writes=["aneg0"])
    EO("dve", lambda e: e.tensor_scalar(out=aneg, in0=aneg, scalar1=-1.0, scalar2=None, op0=ALU.mult),
       reads=["aneg0"], writes=["aneg"])
    EO("act", lambda e: e.activation(out=esink, in_=rowst[:, 1560:1568], func=AF.Exp), reads=["rowst"], writes=["esink"])
    EO("act", lambda e: e.activation(out=small[:, 240:248], in_=ccol, func=AF.Silu), reads=["cols"], writes=["sc"])
    EO("dve", lambda e: e.tensor_copy(out=scb[:], in_=bcl(small[:, 240:248], 128)), reads=["sc"], writes=["scb"])
    w_ada_v = w_ada.rearrange("(k p) c -> p k c", p=128)
    scbb = wgus[:].rearrange("p a k c -> p (a k c)")[:, 4096:5120].rearrange("p (k f) -> p k f", k=KD)
    shb = wgus[:].rearrange("p a k c -> p (a k c)")[:, 5120:5136]
    plainb = wgus[:].rearrange("p a k c -> p (a k c)")[:, 0:4096]
    EO("dve", lambda e: e.tensor_copy(out=scbb, in_=bcl(small[:, 240:248], 128)), reads=["sc"], writes=["scbb"])
    for cg in range(12):
        s = cg % 2
        em.dma(stg[:, s, :].rearrange("p (k c) -> p k c", k=KD), w_ada_v[:, :, cg * 512:(cg + 1) * 512], writes=[f"stg{s}"])
        em.dma(xstage[:, s, 0:512], bada[:, cg * 512:(cg + 1) * 512], writes=[f"xst{s}"])
        EO("act", lambda e, s=s: e.activation(out=cvo[:, s, 0:2048], in_=stg[:, s, 0:2048], func=AF.Copy), reads=[f"stg{s}"], writes=[f"cvo{s}a"])
        EO("dve", lambda e, s=s: e.tensor_copy(out=cvo[:, s, 2048:4096], in_=stg[:, s, 2048:4096]), reads=[f"stg{s}"], writes=[f"cvo{s}b"])
        bank = PA[cg % 4]

        def mmod(e, s=s, bank=bank):
            last = None
            for k in range(KD):
                last = e.matmul(bank[:], lhsT=scbb[:, k, :], rhs=cvo[:, s, k * 512:(k + 1) * 512], start=(k == 0), stop=(k == KD - 1))
            return last
        EO("pe", mmod, reads=["scbb", f"cvo{s}a", f"cvo{s}b"], writes=pk(f"PA{cg % 4}"))
        EO("dve", lambda e, s=s, bank=bank, cg=cg: e.tensor_tensor(out=modbc(cg * 512, (cg + 1) * 512), in0=bank[:],
                                                                 in1=xstage[:, s, 0:512], op=ALU.add),
           reads=pk(f"PA{cg % 4}") + [f"xst{s}"], writes=[f"mod{cg}"])
    modkeys = [f"mod{i}" for i in range(12)]

    def diag_extract(dst, c0, key):
        EO("dve", lambda e: e.tensor_tensor(out=stg[:, 0, 0:1024].rearrange("p (k f) -> p k f", k=KD),
                                            in0=modbc(c0, c0 + 1024).rearrange("p (k f) -> p k f", k=KD),
                                            in1=bcm(identf, KD), op=ALU.mult), reads=modkeys + ["cst", "stg0"], writes=["stg0"])
        EO("dve", lambda e: e.tensor_reduce(out=dst, in_=stg[:, 0, 0:1024].rearrange("p (k f) -> p k f", k=KD),
                                            axis=AX.X, op=ALU.add), reads=["stg0"], writes=[key])
    diag_extract(sh1c, 0, "sh1c")
    diag_extract(g1c, 1024, "g1c0")
    diag_extract(sh2c, 3072, "sh2c")
    diag_extract(g2c, 4096, "g2c0")
    EO("dve", lambda e: e.tensor_copy(out=shb[:, 0:8], in_=sh1c), reads=["sh1c"], writes=["shb1"])
    EO("dve", lambda e: e.tensor_copy(out=shb[:, 8:16], in_=sh2c), reads=["sh2c"], writes=["shb2"])
    for gc, nw, k0, k1 in ((g1c, n1wc, "g1c0", "g1c"), (g2c, n2wc, "g2c0", "g2c")):
        EO("dve", lambda e, gc=gc, nw=nw: e.scalar_tensor_tensor(out=gc, in0=gc, scalar=1.0, in1=nw, op0=ALU.add, op1=ALU.mult),
           reads=[k0, "cols"], writes=[k0 + "x"])
        EO("dve", lambda e, gc=gc: e.tensor_scalar(out=gc, in0=gc, scalar1=32.0, scalar2=None, op0=ALU.mult),
           reads=[k0 + "x"], writes=[k1])

    w_in_v = w_in.rearrange("(k p) c -> p k c", p=128)
    pieces = [(i * 512, 512) for i in range(5)] + [(CV, 136), (CZ, 512)]
    cvt_rr = 0
    for pi, (c0, w) in enumerate(pieces):
        s = pi % 2
        sv = stg[:, s, 0:KD * w].rearrange("p (k c) -> p k c", k=KD)
        em.dma(sv, w_in_v[:, :, c0:c0 + w], writes=[f"stg{s}"])
        pv = plainb[:, 0:KD * w].rearrange("p (k c) -> p k c", k=KD)
        EO("act", lambda e, sv=sv, pv=pv: e.activation(out=pv[:, 0:4, :], in_=sv[:, 0:4, :], func=AF.Copy), reads=[f"stg{s}", "plainb"], writes=["plainb_a"])
        EO("dve", lambda e, sv=sv, pv=pv: e.tensor_copy(out=pv[:, 4:8, :], in_=sv[:, 4:8, :]), reads=[f"stg{s}", "plainb"], writes=["plainb_b"])
        if c0 < CV:
            rb = pi % 2

            def mbr(e, pv=pv):
                last = None
                for k in range(KD):
                    last = e.matmul(PC[0][0:1, 0:512], lhsT=shb[:, k:k + 1], rhs=pv[:, k, :], start=(k == 0), stop=(k == KD - 1))
                return last
            EO("pe", mbr, reads=["plainb_a", "plainb_b", "shb1"], writes=["PC0", "plainb"])
            EO("dve", lambda e, rb=rb: e.tensor_copy(out=browf[0:1, rb, :], in_=PC[0][0:1, 0:512]), reads=["PC0"], writes=[f"browf{rb}"])

            def mbt(e, rb=rb, c0=c0):
                last = None
                for m in range(4):
                    mi = c0 // 128 + m
                    last = e.matmul(PE0[:, mi:mi + 1], lhsT=browf[0:1, rb, m * 128:(m + 1) * 128], rhs=onesf[0:1, 0:1], start=True, stop=True)
                return last
            EO("pe", mbt, reads=[f"browf{rb}", "onesf"], writes=["PE0"])
        else:
            o0 = c0 - CV
            bank = PC[0] if c0 == CV else PC[1]

            def mb2(e, pv=pv, w=w, bank=bank):
                last = None
                for k in range(KD):
                    last = e.matmul(bank[0:1, 0:w], lhsT=shb[:, k:k + 1], rhs=pv[:, k, :], start=(k == 0), stop=(k == KD - 1))
                return last
            bk = "PC0" if c0 == CV else "PC1"
            EO("pe", mb2, reads=["plainb_a", "plainb_b", "shb1"], writes=pk(bk) + ["plainb"])
            EO("dve", lambda e, w=w, bank=bank, o0=o0: e.tensor_copy(out=bias_row[0:1, o0:o0 + w], in_=bank[0:1, 0:w]),
               reads=pk(bk), writes=[f"brow{o0}"])
        for k in range(KD):
            eng = ("act", "dve")[cvt_rr % 2]
            cvt_rr += 1
            if eng == "act":
                EO("act", lambda e, k=k, sv=sv, c0=c0, w=w: e.activation(out=w_in_bf[:, k, c0:c0 + w], in_=sv[:, k, :], func=AF.Identity,
                                                                     scale=g1c[:, k:k + 1]),
                   reads=[f"stg{s}", "g1c"], writes=[f"win{pi}_{k}"])
            else:
                EO(eng, lambda e, k=k, sv=sv, c0=c0, w=w: e.tensor_scalar(out=w_in_bf[:, k, c0:c0 + w], in0=sv[:, k, :],
                                                                       scalar1=g1c[:, k:k + 1], scalar2=None, op0=ALU.mult),
                   reads=[f"stg{s}", "g1c"], writes=[f"win{pi}_{k}"])
    EO("dve", lambda e: e.tensor_copy(out=bias_fm, in_=PE0[:, 0:NFM]), reads=["PE0"], writes=["bias_fm"])

    w_out_v = w_out.rearrange("(k p) c -> p k c", p=128)
    for hh in range(2):
        s = hh
        sv = stg[:, s, :].rearrange("p (k c) -> p k c", k=KD)
        em.dma(sv, w_out_v[:, :, hh * 512:(hh + 1) * 512], writes=[f"stg{s}"])
        EO(("dve", "pool")[hh], lambda e, sv=sv, hh=hh: e.tensor_tensor(out=w_out_bf[:, :, hh * 512:(hh + 1) * 512], in0=sv,
                                                                      in1=bcm(modbc(2048 + hh * 512, 2048 + (hh + 1) * 512), KD), op=ALU.mult),
           reads=[f"stg{s}"] + modkeys, writes=[f"wout{hh}"])

    for j in range(8):
        for k in range(4):
            EO("dve", lambda e, j=j, k=k: e.tensor_scalar(out=cdiag[:, j * 4 + k, :], in0=identf,
                                                                                 scalar1=convw[:, j * 4 + k:j * 4 + k + 1], scalar2=None, op0=ALU.mult),
               reads=["cst", "cols"], writes=[f"cdiag{j}_{k}"])
        EO("dve", lambda e, j=j: e.tensor_scalar(out=bdiag[:, j, :], in0=identf, scalar1=convb[:, j:j + 1], scalar2=None, op0=ALU.mult),
           reads=["cst", "cols"], writes=[f"bdiag{j}"])

    w_gu_v = w_gu.rearrange("(k p) c -> p k c", p=128)
    def gu_load(pc):
        s = pc % 2
        sv = stg[:, s, :].rearrange("p (k c) -> p k c", k=KD)
        em.dma(sv[:, :, 0:256], w_gu_v[:, :, pc * 256:(pc + 1) * 256], writes=[f"stg{s}"])
        em.dma(sv[:, :, 256:512], w_gu_v[:, :, DFF + pc * 256:DFF + (pc + 1) * 256], writes=[f"stg{s}"])
    gu_load(0)
    for pc in range(11):
        s = pc % 2
        sv = stg[:, s, :].rearrange("p (k c) -> p k c", k=KD)
        if pc + 1 < 11:
            gu_load(pc + 1)
        rb = pc % 2
        pv = plainb.rearrange("p (k c) -> p k c", k=KD)
        EO("act", lambda e, sv=sv, pv=pv: e.activation(out=pv[:, 0:4, :], in_=sv[:, 0:4, :], func=AF.Copy), reads=[f"stg{s}", "plainb"], writes=["plainb_a"])
        EO("dve", lambda e, sv=sv, pv=pv: e.tensor_copy(out=pv[:, 4:8, :], in_=sv[:, 4:8, :]), reads=[f"stg{s}", "plainb"], writes=["plainb_b"])

        def mbr3(e, pv=pv):
            last = None
            for k in range(KD):
                last = e.matmul(PC[0][0:1, 0:512], lhsT=shb[:, 8 + k:9 + k], rhs=pv[:, k, :], start=(k == 0), stop=(k == KD - 1))
            return last
        EO("pe", mbr3, reads=["plainb_a", "plainb_b", "shb2"], writes=["PC0", "plainb"])
        EO("dve", lambda e, rb=rb: e.tensor_copy(out=browf[0:1, rb, :], in_=PC[0][0:1, 0:512]), reads=["PC0"], writes=[f"browf{rb}"])

        def mbt3(e, rb=rb, pc=pc):
            last = None
            for m in range(4):
                ffc = 2 * pc + (m % 2)
                bcol = ffc if m < 2 else NFF + ffc
                last = e.matmul(PE0[:, 64 + bcol:64 + bcol + 1], lhsT=browf[0:1, rb, m * 128:(m + 1) * 128], rhs=onesf[0:1, 0:1], start=True, stop=True)
            return last
        EO("pe", mbt3, reads=[f"browf{rb}", "onesf"], writes=["PE0"])
        cv = cvo[:, s, :].rearrange("p (k c) -> p k c", k=KD)
        for k in range(KD):
            eng = ("act", "dve")[cvt_rr % 2]
            cvt_rr += 1
            if eng == "act":
                EO("act", lambda e, k=k, sv=sv, cv=cv: e.activation(out=cv[:, k, :], in_=sv[:, k, :], func=AF.Identity, scale=g2c[:, k:k + 1]),
                   reads=[f"stg{s}", "g2c"], writes=[f"cvo{s}_{k}"])
            else:
                EO(eng, lambda e, k=k, sv=sv, cv=cv: e.tensor_scalar(out=cv[:, k, :], in0=sv[:, k, :], scalar1=g2c[:, k:k + 1], scalar2=None,
                                                                    op0=ALU.mult),
                   reads=[f"stg{s}", "g2c"], writes=[f"cvo{s}_{k}"])
        for half in range(2):
            ffc = 2 * pc + half
            dst = wgu_scr[ffc].rearrange("p (k t c) -> p k t c", k=KD, t=2)
            em.dma(dst[:, :, 0, :], cv[:, :, half * 128:(half + 1) * 128], reads=[f"cvo{s}_{k}" for k in range(KD)], writes=[f"wguscr{ffc}g"])
            em.dma(dst[:, :, 1, :], cv[:, :, 256 + half * 128:256 + (half + 1) * 128], reads=[f"cvo{s}_{k}" for k in range(KD)],
                   writes=[f"wguscr{ffc}u"])
    EO("dve", lambda e: e.tensor_copy(out=bias_gu, in_=PE0[:, 64:64 + 2 * NFF]), reads=["PE0"], writes=["bias_gu"])
    w_dn_v = w_dn.rearrange("(f p) c -> p f c", p=128)
    dn_pieces = [(dq, fh) for dq in range(4) for fh in range(2)]

    def dn_load(i):
        dq, fh = dn_pieces[i]
        s = i % 2
        sv = stg[:, s, 0:11 * 256].rearrange("p (f c) -> p f c", f=11)
        em.dma(sv, w_dn_v[:, fh * 11:(fh + 1) * 11, dq * 256:(dq + 1) * 256], writes=[f"stg{s}"])
    dn_load(0)
    for i, (dq, fh) in enumerate(dn_pieces):
        s = i % 2
        sv = stg[:, s, 0:11 * 256].rearrange("p (f c) -> p f c", f=11)
        cv = cvo[:, s, 0:11 * 256].rearrange("p (f c) -> p f c", f=11)
        if i + 1 < len(dn_pieces):
            dn_load(i + 1)
        EO(("dve", "pool")[i % 2], lambda e, sv=sv, cv=cv, dq=dq: e.tensor_tensor(
            out=cv, in0=sv, in1=bcm(modbc(5120 + dq * 256, 5120 + (dq + 1) * 256), 11), op=ALU.mult),
           reads=[f"stg{s}"] + modkeys, writes=[f"cvo{s}"] + [f"cvo{s}_{k}" for k in range(KD)])
        dst = wdn_scr[dq].rearrange("p (f c) -> p f c", f=NFF)
        em.dma(dst[:, fh * 11:(fh + 1) * 11, :], cv, reads=[f"cvo{s}"], writes=[f"wdnscr{dq}_{fh}"])
    em.barrier()

    EO("dve", lambda e: e.memset(prevT[:], 0.0), writes=["prevT"])
    EO("dve", lambda e: e.memset(prevTb[:], 0.0), writes=["prevTb"])
    EO("pool", lambda e: e.memset(utail[:], 0.0), writes=["utail"])
    EO("pool", lambda e: e.memset(khalo[:], 0.0), writes=["khalo"])
    EO("pool", lambda e: e.memset(vhalo[:], 0.0), writes=["vhalo"])

    SCRKEYS_W = [f"wguscr{f}{t}" for f in range(NFF) for t in "gu"] + [f"wdnscr{q}_{h}" for q in range(4) for h in range(2)]

    def rstd_from_ss(ssap, rsap, n_eps, rk, wk):
        EO("act", lambda e: e.activation(out=rsap, in_=ssap, func=AF.Ln, bias=float(n_eps)), reads=rk, writes=[wk + "t"])
        EO("act", lambda e: e.activation(out=rsap, in_=rsap, func=AF.Exp, scale=-0.5), reads=[wk + "t"], writes=[wk])

    def transpose_to_hT(src, src_keys, c, banks):
        for half in range(2):
            bank, bkey = banks[half]

            def tps(e, half=half, bank=bank):
                last = None
                for f4 in range(4):
                    f = half * 4 + f4
                    last = e.matmul(bank[:, f4 * 128:(f4 + 1) * 128], lhsT=src[:, f * 128:(f + 1) * 128], rhs=identb, start=True, stop=True)
                return last
            EO("pe", tps, reads=src_keys + ["identb"], writes=[bkey])
            dst = hT[:, half * 4:(half + 1) * 4, c * 128:(c + 1) * 128]
            srcv = bank[:].rearrange("p (f t) -> p f t", f=4)
            if half == 0:
                EO("act", lambda e, dst=dst, srcv=srcv: e.activation(out=dst, in_=srcv, func=AF.Copy), reads=[bkey], writes=[f"hT{c}"])
            else:
                EO("dve", lambda e, dst=dst, srcv=srcv: e.tensor_copy(out=dst, in_=srcv), reads=[bkey], writes=[f"hT{c}"])

    S1_BANKS = ((PA[0], "PA0"), (PA[1], "PA1"))

    def norm_to_hT(xsrc, xkeys, xnbuf, c, sskey):
        EO("act", lambda e: e.activation(out=xnbuf[:], in_=xsrc, func=AF.Identity, scale=rs1[:, c:c + 1]),
           reads=xkeys + [sskey, "xn"], writes=["xn"])
        transpose_to_hT(xnbuf, ["xn"], c, S1_BANKS)

    def emit_s1(si, xsrc_dram, main):
        tok0 = si * T
        for c in range(NCH):
            em.dma(xstage[:, c % 2, :], xsrc_dram[tok0 + c * 128: tok0 + (c + 1) * 128, :], writes=[f"xst{c % 2}"])
            EO("act", lambda e, c=c: e.activation(out=xn[:], in_=xstage[:, c % 2, :], func=AF.Square, accum_out=ss1[:, c:c + 1]),
               reads=[f"xst{c % 2}"], writes=["xn", f"ss1_{c}"])
            if c == 1 and main:
                em.dma(x1buf[:], xsrc_dram[tok0:tok0 + T, :].rearrange("(c p) d -> p c d", p=128), writes=[f"x1_{cc}" for cc in range(NCH)])
            if c % 2 == 1:
                cc0 = c - 1
                rstd_from_ss(ss1[:, cc0:c + 1], rs1[:, cc0:c + 1], D * EPS, [f"ss1_{cc0}", f"ss1_{c}"], f"rs1_{cc0}")
                for cc in (cc0, c):
                    norm_to_hT(xstage[:, cc % 2, :], [f"xst{cc % 2}"], xn, cc, f"rs1_{cc0}")

    xn4 = x1buf[:].rearrange("p a b -> p (a b)").bitcast(BF16)[:, 0:NCH * D].rearrange("p (c d) -> p c d", c=NCH)

    def emit_s1a(si, xsrc_dram, junk=None, buf=None):
        junk = xn if junk is None else junk
        buf = xn4 if buf is None else buf
        tok0 = si * T
        for c in range(NCH):
            em.dma(xstage[:, c % 2, :], xsrc_dram[tok0 + c * 128: tok0 + (c + 1) * 128, :], writes=[f"xst{c % 2}"])
            EO("dve", lambda e, c=c: e.scalar_tensor_tensor(out=junk[:], in0=xstage[:, c % 2, :], scalar=1.0, in1=xstage[:, c % 2, :],
                                                           op0=ALU.mult, op1=ALU.mult, accum_out=ss1[:, c:c + 1]),
               reads=[f"xst{c % 2}", "xn"], writes=["xn", f"ss1_{c}"])
            if c % 2 == 1:
                cc0 = c - 1
                rstd_from_ss(ss1[:, cc0:c + 1], rs1[:, cc0:c + 1], D * EPS, [f"ss1_{cc0}", f"ss1_{c}"], f"rs1_{cc0}")
                for cc in (cc0, c):
                    EO("dve", lambda e, cc=cc: e.tensor_scalar(out=buf[:, cc, :], in0=xstage[:, cc % 2, :], scalar1=rs1[:, cc:cc + 1], scalar2=None,
                                                              op0=ALU.mult), reads=[f"xst{cc % 2}", f"rs1_{cc0}"], writes=[f"xn4_{cc}"])

    def emit_s1b(buf=None, banks=None, alias_x1=True):
        buf = xn4 if buf is None else buf
        banks = S1_BANKS if banks is None else banks
        for c in range(NCH):
            keys = [f"xn4_{c}"] + ([f"x1_{c // 2}"] if alias_x1 else [])
            transpose_to_hT(buf[:, c, :], keys, c, banks)

    def emit_sc(kind, si, xsrc_dram, pos0, s1_done=False, next_s1=None, next_s1a=None, next_s1b=None):
        main = kind == "main"
        last_pre = kind == "prelast"
        need_k = main or last_pre
        tok0 = si * T
        hTk = [f"hT{c}" for c in range(NCH)]
        if not s1_done:
            emit_s1(si, xsrc_dram, main)
        elif main and si > 0:
            em.dma(x1buf[:], xsrc_dram[tok0:tok0 + T, :].rearrange("(c p) d -> p c d", p=128), writes=[f"x1_{cc}" for cc in range(NCH)])
        if stop == "A1":
            em.barrier()
            return
        EO("pool", lambda e: e.tensor_copy(out=uT[:, :, 0:3], in_=utail[:]), reads=["utail"], writes=["uTtail"])
        if main:
            EO("pool", lambda e: e.tensor_copy(out=kT[:, :, 0:128], in_=khalo[:]), reads=["khalo"], writes=["kThalo"])
            EO("pool", lambda e: e.memset(Vaug[:], 1.0), writes=["Vaug_all"] + [f"Vaug{i}" for i in range(5)])
            EO("pool", lambda e: e.tensor_copy(out=Vaug[:, 0, :], in_=vhalo[:]), reads=["vhalo", "Vaug_all"], writes=["Vaug0"])
            if si == 0:
                EO("dve", lambda e: e.tensor_scalar(out=uT[:, :, 0:3], in0=uT[:, :, 0:3], scalar1=flag, scalar2=None, op0=ALU.mult),
                   reads=["uTtail", "cols"], writes=["uTtail"])
                EO("dve", lambda e: e.tensor_scalar(out=Vaug[:, 0, :], in0=Vaug[:, 0, :], scalar1=flag, scalar2=None, op0=ALU.mult),
                   reads=["Vaug0", "cols"], writes=["Vaug0"])
                EO("dve", lambda e: e.tensor_scalar(out=prevT[:], in0=prevT[:], scalar1=flag, scalar2=None, op0=ALU.mult),
                   reads=["prevT", "cols"], writes=["prevT"])
                EO("dve", lambda e: e.tensor_copy(out=prevTb[:], in_=prevT[:]), reads=["prevT"], writes=["prevTb"])
        else:
            EO("pool", lambda e: e.memset(Vaug[:], 1.0), writes=["Vaug_all"] + [f"Vaug{i}" for i in range(5)])
        if need_k:
            em.dma(posi[:], posrep[:, pos0:pos0 + T], writes=["posi"])
            EO("dve", lambda e: e.tensor_copy(out=rt1[:], in_=posi[:]), reads=["posi"], writes=["rt1"])
            EO("dve", lambda e: e.tensor_scalar(out=rt1[:], in0=rt1[:], scalar1=invf, scalar2=None, op0=ALU.mult),
               reads=["rt1", "cols"], writes=["rt1"])
            for which, dst, shift in (("s", sinT, 0.0), ("c", cosT, float(np.pi / 2))):
                EO("dve", lambda e, shift=shift: e.tensor_scalar(out=rt2[:], in0=rt1[:], scalar1=shift, scalar2=float(1.0 / (2 * np.pi)),
                                                                 op0=ALU.add, op1=ALU.mult), reads=["rt1", "rt2"], writes=["rt2"])
                EO("dve", lambda e: e.tensor_copy(out=posi[:], in_=rt2[:]), reads=["rt2", "posi"], writes=["posi"])
                EO("dve", lambda e: e.tensor_copy(out=rt2[:], in_=posi[:]), reads=["posi"], writes=["rt2"])
                EO("dve", lambda e: e.tensor_scalar(out=rt2[:], in0=rt2[:], scalar1=float(-2 * np.pi), scalar2=None, op0=ALU.mult),
                   reads=["rt2"], writes=["rt2"])
                EO("dve", lambda e, shift=shift, dst=dst: e.scalar_tensor_tensor(out=dst[:], in0=rt1[:], scalar=shift, in1=rt2[:], op0=ALU.add,
                                                                              op1=ALU.add) if False else
                   e.tensor_tensor(out=dst[:], in0=rt1[:], in1=rt2[:], op=ALU.add), reads=["rt1", "rt2"], writes=[which + "T"])
                if shift != 0.0:
                    EO("dve", lambda e, dst=dst, shift=shift: e.tensor_scalar(out=dst[:], in0=dst[:], scalar1=shift, scalar2=None, op0=ALU.add),
                       reads=[which + "T"], writes=[which + "T"])
                EO("dve", lambda e, dst=dst: e.tensor_scalar(out=rt2[:], in0=dst[:], scalar1=float(np.pi), scalar2=float(-2 * np.pi),
                                                             op0=ALU.is_gt, op1=ALU.mult), reads=[which + "T", "rt2"], writes=["rt2"])
                EO("dve", lambda e, dst=dst: e.tensor_tensor(out=dst[:], in0=dst[:], in1=rt2[:], op=ALU.add), reads=[which + "T", "rt2"],
                   writes=[which + "T"])
                EO("dve", lambda e, dst=dst: e.tensor_scalar(out=rt2[:], in0=dst[:], scalar1=float(-np.pi), scalar2=float(2 * np.pi),
                                                             op0=ALU.is_lt, op1=ALU.mult), reads=[which + "T", "rt2"], writes=["rt2"])
                EO("dve", lambda e, dst=dst: e.tensor_tensor(out=dst[:], in0=dst[:], in1=rt2[:], op=ALU.add), reads=[which + "T", "rt2"],
                   writes=[which + "T"])
                EO("act", lambda e, dst=dst: e.activation(out=dst[:], in_=dst[:], func=AF.Sin), reads=[which + "T"], writes=[which + "T"])
            EO("dve", lambda e: e.tensor_scalar(out=sinT[:], in0=sinT[:], scalar1=rsign, scalar2=None, op0=ALU.mult),
               reads=["sT", "cols"], writes=["sT"])
        if stop == "A2":
            em.barrier()
            return

        def fm_group(m, bank, bkey):
            def f(e):
                last = None
                for k in range(KD):
                    last = e.matmul(bank[:], lhsT=w_in_bf[:, k, m * 128:(m + 1) * 128], rhs=hT[:, k, :], start=(k == 0), stop=(k == KD - 1))
                return last
            EO("pe", f, reads=hTk, writes=pk(bkey))

        def rope_pair(m_plain, m_sw, dst, dkey, par):
            b0, b1 = PA[2 * par], PA[2 * par + 1]
            fm_group(m_plain, b0, f"PA{2 * par}")
            fm_group(m_sw, b1, f"PA{2 * par + 1}")
            EO("dve", lambda e: e.scalar_tensor_tensor(out=rt1[:], in0=b0[:], scalar=bias_fm[:, m_plain:m_plain + 1], in1=cosT[:],
                                                      op0=ALU.add, op1=ALU.mult), reads=pk(f"PA{2 * par}") + ["cT", "rt1"], writes=["rt1"])
            EO("dve", lambda e: e.scalar_tensor_tensor(out=rt2[:], in0=b1[:], scalar=bias_fm[:, m_sw:m_sw + 1], in1=sinT[:],
                                                      op0=ALU.add, op1=ALU.mult), reads=pk(f"PA{2 * par + 1}") + ["sT", "rt2"], writes=["rt2"])
            EO("pool", lambda e: e.tensor_tensor(out=dst, in0=rt1[:], in1=rt2[:], op=ALU.add), reads=["rt1", "rt2"], writes=[dkey])

        nxc = 8 if (main or last_pre) else 6
        xbanks = ((PC[0], "PC0"), (PC[1], "PC1"), (PE0, "PE0"))

        def xgroup(j):
            m = 12 + j
            xb, xk = xbanks[j % 3]
            fm_group(m, xb, xk)
            EO("act", lambda e: e.activation(out=uT[:, j, 3:3 + T], in_=xb[:], func=AF.Identity, bias=bias_fm[:, m:m + 1]),
               reads=[xk], writes=[f"uT{j}"])
        pairs = []
        if main:
            pairs += [(j, 4 + j, qT[:, j, :], f"qT{j}") for j in range(4)]
        if need_k:
            pairs += [(8 + g, 10 + g, kT[:, g, 128:128 + T], f"kT{g}") for g in range(2)]
        for j in range(nxc):
            xgroup(j)
        par = 0
        for (mp, ms, dst, dkey) in pairs:
            rope_pair(mp, ms, dst, dkey, par)
            par ^= 1
        if need_k:
            EO("pool", lambda e: e.tensor_copy(out=khalo[:], in_=kT[:, :, T:T + 128]), reads=["kT0", "kT1"], writes=["khalo"])
            if main:
                for g in range(2):
                    for hp in range(2):
                        EO("dve", lambda e, g=g, hp=hp: e.tensor_scalar(out=kTz[:, g, hp, :], in0=kT[:, g, :], scalar1=hmask[:, hp:hp + 1],
                                                                         scalar2=None, op0=ALU.mult),
                           reads=[f"kT{g}", "kThalo", "cols"], writes=[f"kTz{g}"])
        uTk = [f"uT{j}" for j in range(nxc)] + ["uTtail"]
        EO("pool", lambda e: e.tensor_copy(out=utail[:], in_=uT[:, :, T:T + 3]), reads=uTk, writes=["utail"])
        if stop == "A3":
            em.barrier()
            return
        for c in range(NCH):
            def fvd(e, c=c):
                for k in range(KD):
                    e.matmul(PE0[:, 0:136], lhsT=hT[:, k, c * 128:(c + 1) * 128], rhs=w_in_bf[:, k, CV:CV + 136], start=(k == 0), stop=False)
                return e.matmul(PE0[:, 0:136], lhsT=onesb[0:1, :], rhs=bias_row[0:1, 0:136], start=False, stop=True)
            EO("pe", fvd, reads=[f"hT{c}"], writes=["PE0"])
            EO("act", lambda e, c=c: e.activation(out=Vaug[:, c + 1, :].rearrange("p (g d) -> p g d", g=2)[:, :, 0:64],
                                                  in_=PE0[:, 0:128].rearrange("p (g d) -> p g d", g=2), func=AF.Copy),
               reads=["PE0", "Vaug_all"], writes=[f"Vaug{c + 1}"])
            EO("dve", lambda e, c=c: e.tensor_tensor(out=dtraw[:, c * 8:(c + 1) * 8], in0=PE0[:, 128:136], in1=dtb, op=ALU.add),
               reads=["PE0", "dtb"], writes=[f"dtraw{c}"])
        EO("pool", lambda e: e.tensor_copy(out=vhalo[:], in_=Vaug[:, 4, :]), reads=["Vaug4"], writes=["vhalo"])
        if next_s1 is not None:
            next_s1()
        if main:
            for c in range(NCH):
                def fz(e, c=c):
                    for k in range(KD):
                        e.matmul(PA[c][:], lhsT=hT[:, k, c * 128:(c + 1) * 128], rhs=w_in_bf[:, k, CZ:CZ + 512], start=(k == 0), stop=False)
                    return e.matmul(PA[c][:], lhsT=onesb[0:1, :], rhs=bias_row[0:1, 136:648], start=False, stop=True)
                EO("pe", fz, reads=[f"hT{c}"], writes=[f"PA{c}"])
        dk = [f"dtraw{c}" for c in range(NCH)]
        EO("dve", lambda e: e.scalar_tensor_tensor(out=dtv, in0=dtraw, scalar=-1.0, in1=dtraw, op0=ALU.mult, op1=ALU.max), reads=dk, writes=["dtv"])
        EO("act", lambda e: e.activation(out=dtv, in_=dtv, func=AF.Exp, scale=-1.0), reads=["dtv"], writes=["dtv"])
        EO("act", lambda e: e.activation(out=dtv, in_=dtv, func=AF.Ln, bias=1.0), reads=["dtv"], writes=["dtv"])
        EO("dve", lambda e: e.scalar_tensor_tensor(out=dtv, in0=dtraw, scalar=0.0, in1=dtv, op0=ALU.max, op1=ALU.add),
           reads=dk + ["dtv"], writes=["dtv"])
        EO("dve", lambda e: e.tensor_tensor(out=av.rearrange("p (c h) -> p c h", c=NCH), in0=dtv.rearrange("p (c h) -> p c h", c=NCH),
                                            in1=bcm(aneg, NCH), op=ALU.mult), reads=["dtv", "aneg"], writes=["av"])
        def fsm(e):
            last = None
            for c in range(NCH):
                a_c = av[:, c * 8:(c + 1) * 8]
                o = 136 + c * 24
                e.matmul(PE0[:, o:o + 8], lhsT=triGT, rhs=a_c, start=True, stop=True)
                e.matmul(PE0[:, o + 8:o + 16], lhsT=triLE, rhs=a_c, start=True, stop=True)
                last = e.matmul(PE0[:, o + 16:o + 24], lhsT=onesf, rhs=a_c, start=True, stop=True)
            return last
        EO("pe", fsm, reads=["av", "cst", "onesf"], writes=["PE0"])
        EO("act", lambda e: e.activation(out=exv[:].rearrange("p c k -> p (c k)"), in_=PE0[:, 136:136 + NCH * 24], func=AF.Exp),
           reads=["PE0"], writes=[f"exv{c}" for c in range(NCH)])
        EO("dve", lambda e: e.tensor_tensor(out=wend[:], in0=dtv.rearrange("p (c h) -> p c h", c=NCH), in1=exv[:, :, 0:8], op=ALU.mult),
           reads=["dtv"] + [f"exv{c}" for c in range(NCH)], writes=[f"wend{c}" for c in range(NCH)])
        if main:
            for c in range(NCH):
                EO("act", lambda e, c=c: e.activation(out=sz[:, c, :], in_=PA[c][:], func=AF.Silu), reads=[f"PA{c}"], writes=[f"sz{c}"])
        if next_s1a is not None:
            next_s1a()
        for c in range(NCH):
            def fcx(e, c=c):
                last = None
                for j in range(4):
                    for k in range(4):
                        e.matmul(PC[1][:, j * 128:(j + 1) * 128], lhsT=uT[:, j, c * 128 + k:c * 128 + k + 128], rhs=cdiag[:, j * 4 + k, :],
                                 start=(k == 0), stop=False)
                    last = e.matmul(PC[1][:, j * 128:(j + 1) * 128], lhsT=onesb, rhs=bdiag[:, j, :], start=False, stop=True)
                return last
            EO("pe", fcx, reads=uTk, writes=pk("PC1"))
            EO("act", lambda e: e.activation(out=xs[:], in_=PC[1][:], func=AF.Silu), reads=pk("PC1") + ["xs"], writes=["xs"])
            xs3 = xs[:].rearrange("p (h d) -> p h d", h=8)
            EO("dve", lambda e, c=c: e.tensor_tensor(out=xdt[:, c, :].rearrange("p (h d) -> p h d", h=8), in0=xs3,
                                                     in1=bcl(dtv[:, c * 8:(c + 1) * 8], 64), op=ALU.mult), reads=["xs", "dtv"], writes=[f"xdt{c}"])
            EO("dve", lambda e, c=c: e.tensor_tensor(out=xdtd[:, c, :].rearrange("p (h d) -> p h d", h=8), in0=xs3,
                                                     in1=bcl(wend[:, c, :], 64), op=ALU.mult), reads=["xs", f"wend{c}"], writes=[f"xdtd{c}"])
            if main:
                EO("pool", lambda e, c=c: e.tensor_tensor(out=xsD[:, c, :].rearrange("p (h d) -> p h d", h=8), in0=xs3,
                                                          in1=bcl(dskip, 64), op=ALU.mult), reads=["xs", "dskip"], writes=[f"xsD{c}"])

            def fcb(e, c=c):
                last = None
                for jj in range(2):
                    j = 4 + jj
                    for k in range(4):
                        e.matmul(PC[0][:, jj * 128:(jj + 1) * 128], lhsT=uT[:, j, c * 128 + k:c * 128 + k + 128], rhs=cdiag[:, j * 4 + k, :],
                                 start=(k == 0), stop=False)
                    last = e.matmul(PC[0][:, jj * 128:(jj + 1) * 128], lhsT=onesb, rhs=bdiag[:, j, :], start=False, stop=True)
                return last
            EO("pe", fcb, reads=uTk, writes=["PC0"])
            EO("act", lambda e, c=c: e.activation(out=Btm[:, c, :], in_=PC[0][:, 0:256], func=AF.Silu), reads=["PC0"], writes=[f"Btm{c}"])
        if main:
            for jj in range(4):
                j = 4 + jj

                def fcf(e, j=j, jj=jj):
                    last = None
                    for k in range(4):
                        last = e.matmul(PA[jj][:], lhsT=cdiag[:, j * 4 + k, :], rhs=uT[:, j, k:k + T], start=(k == 0), stop=(k == 3))
                    return last
                EO("pe", fcf, reads=uTk, writes=pk(f"PA{jj}"))
                EO("act", lambda e, j=j, jj=jj: e.activation(out=BCT[:, jj, :], in_=PA[jj][:], func=AF.Silu, bias=convb[:, j:j + 1]),
                   reads=pk(f"PA{jj}") + ["cols"], writes=[f"BCT{jj}"])
        if main:
            em.barrier()
        if stop == "A":
            return

        TPf = TP[:].rearrange("p a b -> p (a b)").bitcast(F32)

        def state_update(c, bank, bkey):
            def fst(e, c=c):
                last = None
                for g in range(2):
                    last = e.matmul(bank[:, g * 256:(g + 1) * 256], lhsT=Btm[:, c, g * 128:(g + 1) * 128], rhs=xdtd[:, c, g * 256:(g + 1) * 256],
                                    start=True, stop=True)
                return last
            EO("pe", fst, reads=[f"Btm{c}", f"xdtd{c}"], writes=[bkey])
            EO("dve", lambda e, c=c: e.tensor_tensor(out=prevT[:].rearrange("p (h d) -> p h d", h=8), in0=prevT[:].rearrange("p (h d) -> p h d", h=8),
                                                     in1=bcl(exv[:, c, 16:24], 64), op=ALU.mult), reads=["prevT", f"exv{c}"], writes=["prevT"])
            EO("dve", lambda e: e.tensor_tensor(out=prevT[:], in0=bank, in1=prevT[:], op=ALU.add), reads=[bkey, "prevT"], writes=["prevT"])
            EO("act", lambda e: e.activation(out=prevTb[:], in_=prevT[:], func=AF.Copy), reads=["prevT"], writes=["prevTb"])

        if not main:
            for c in range(NCH):
                state_update(c, PC[1][:], "PC1")
            if next_s1b is not None:
                next_s1b()
            return

        def E1(c):
            a_c = av[:, c * 8:(c + 1) * 8]
            EO("dve", lambda e, a_c=a_c: e.tensor_tensor(out=aTri[:].rearrange("p (h l) -> p h l", h=8), in0=bcm(triLE, 8), in1=bcl(a_c, 128),
                                                         op=ALU.mult), reads=["av", "cst"], writes=["aTri"])

            def scores(g):
                for blk in range(2):
                    bank = PA[blk]
                    kc0 = c * 128 + blk * 128

                    def fsc(e, g=g, blk=blk, bank=bank, kc0=kc0):
                        moff = 0 if blk == 1 else 512
                        last = None
                        for hp in range(2):
                            for jj in range(2):
                                r0 = (hp * 2 + jj) * 128
                                e.matmul(bank[:, r0:r0 + 128], lhsT=identb, rhs=maskb[:, moff:moff + 128], start=True, stop=False)
                                last = e.matmul(bank[:, r0:r0 + 128], lhsT=kTz[:, g, hp, kc0:kc0 + 128],
                                                rhs=qT[:, 2 * g + jj, c * 128:(c + 1) * 128], start=False, stop=True)
                        return last
                    EO("pe", fsc, reads=[f"qT{2 * g}", f"qT{2 * g + 1}", f"kTz{g}", "maskb", "identb"], writes=[f"PA{blk}"])
                    EO("act", lambda e, g=g, blk=blk, bank=bank: e.activation(out=PT[:, 2 * g + blk, :], in_=bank[:], func=AF.Exp, scale=0.125),
                       reads=[f"PA{blk}"], writes=[f"PT{2 * g + blk}"])
            scores(0)
            for hh in range(2):
                EO("pe", lambda e, hh=hh: e.matmul(PA[2 + hh][:], lhsT=triGT, rhs=aTri[:, hh * 512:(hh + 1) * 512], start=True, stop=True),
                   reads=["aTri", "cst"], writes=[f"PA{2 + hh}"])
                EO("act", lambda e, hh=hh: e.activation(out=Ebuf[:, hh * 512:(hh + 1) * 512], in_=PA[2 + hh][:], func=AF.Exp),
                   reads=[f"PA{2 + hh}"], writes=[f"E{hh}"])
            scores(1)

            def fcbm(e):
                last = None
                for g in range(2):
                    last = e.matmul(PE0[:, 256 + g * 128:256 + (g + 1) * 128], lhsT=BCT[:, g, c * 128:(c + 1) * 128],
                                    rhs=BCT[:, 2 + g, c * 128:(c + 1) * 128], start=True, stop=True)
                return last
            EO("pe", fcbm, reads=[f"BCT{i}" for i in range(4)], writes=["PE0"])

        def E1b(c):
            EO("dve", lambda e: e.tensor_tensor(out=CBm[:].rearrange("p (g l) -> p g l", g=2),
                                                in0=PE0[:, 256:512].rearrange("p (g l) -> p g l", g=2), in1=bcm(triLE, 2), op=ALU.mult),
               reads=["PE0", "cst"], writes=["CBm"])
            EO("dve", lambda e: e.tensor_tensor(out=MT[:].rearrange("p (g j l) -> p g j l", g=2, j=4),
                                                in0=Ebuf[:].rearrange("p (g j l) -> p g j l", g=2, j=4),
                                                in1=CBm[:].rearrange("p (g l) -> p g l", g=2).unsqueeze(2).to_broadcast([128, 2, 4, 128]),
                                                op=ALU.mult), reads=["E0", "E1", "CBm"], writes=["MT"])

        def E2a(c):
            for g in range(2):
                def fpv(e, g=g):
                    last = None
                    for i in range(4):
                        hp, jj = i % 2, i // 2
                        cb = hp * 256 + jj * 128
                        e.matmul(PC[g][:, i * 128:i * 128 + 65], lhsT=PT[:, 2 * g, cb:cb + 128], rhs=Vaug[:, c, g * 65:(g + 1) * 65],
                                 start=True, stop=False)
                        last = e.matmul(PC[g][:, i * 128:i * 128 + 65], lhsT=PT[:, 2 * g + 1, cb:cb + 128], rhs=Vaug[:, c + 1, g * 65:(g + 1) * 65],
                                        start=False, stop=True)
                    return last
                EO("pe", fpv, reads=[f"PT{2 * g}", f"PT{2 * g + 1}", f"Vaug{c}", f"Vaug{c + 1}", "Vaug_all"], writes=[f"PC{g}"])
                o3 = PC[g][:, :].rearrange("p (i d) -> p i d", i=4)
                EO("dve", lambda e, g=g, o3=o3: e.tensor_tensor(out=den[:, g * 4:(g + 1) * 4], in0=o3[:, :, 64], in1=esink[:, g * 4:(g + 1) * 4],
                                                              op=ALU.add), reads=[f"PC{g}", "esink"], writes=[f"den{g}"])
                EO("dve", lambda e, g=g: e.reciprocal(out=rden[:, g * 4:(g + 1) * 4], in_=den[:, g * 4:(g + 1) * 4]), reads=[f"den{g}"],
                   writes=[f"rden{g}"])
                EO("dve", lambda e, g=g, o3=o3: e.tensor_tensor(out=ytm[:, g * 256:(g + 1) * 256].rearrange("p (i d) -> p i d", i=4),
                                                              in0=o3[:, :, 0:64], in1=bcl(rden[:, g * 4:(g + 1) * 4], 64), op=ALU.mult),
                   reads=[f"PC{g}", f"rden{g}"], writes=[f"ytm_a{g}"])

            def fyd(e):
                last = None
                for h in range(8):
                    e.matmul(PC[0][:, h * 64:(h + 1) * 64], lhsT=identb, rhs=xsD[:, c, h * 64:(h + 1) * 64], start=True, stop=False)
                    last = e.matmul(PC[0][:, h * 64:(h + 1) * 64], lhsT=MT[:, h * 128:(h + 1) * 128], rhs=xdt[:, c, h * 64:(h + 1) * 64],
                                    start=False, stop=True)
                return last
            EO("pe", fyd, reads=["MT", f"xdt{c}", f"xsD{c}", "identb"], writes=["PC0"])

            def fyo(e):
                last = None
                for g in range(2):
                    last = e.matmul(PC[1][:, g * 256:(g + 1) * 256], lhsT=BCT[:, 2 + g, c * 128:(c + 1) * 128],
                                    rhs=prevTb[:, g * 256:(g + 1) * 256], start=True, stop=True)
                return last
            EO("pe", fyo, reads=["BCT2", "BCT3", "prevTb"], writes=["PC1"])
            state_update(c, TPf, "TP")

        def E2b(c):
            EO("dve", lambda e: e.tensor_tensor(out=yt[:].rearrange("p (h d) -> p h d", h=8),
                                                in0=PC[1][:].rearrange("p (h d) -> p h d", h=8), in1=bcl(exv[:, c, 8:16], 64), op=ALU.mult),
               reads=["PC1", f"exv{c}"], writes=["yt"])
            EO("dve", lambda e: e.tensor_tensor(out=yt[:], in0=PC[0][:], in1=yt[:], op=ALU.add), reads=["PC0", "yt"], writes=["yt"])
            EO("dve", lambda e: e.tensor_tensor(out=yt[:], in0=yt[:], in1=sz[:, c, :], op=ALU.mult), reads=["yt", f"sz{c}"], writes=["yt"])
            for g in range(2):
                EO("dve", lambda e, g=g: e.scalar_tensor_tensor(out=sqj[:, g * 256:(g + 1) * 256], in0=yt[:, g * 256:(g + 1) * 256], scalar=1.0,
                                                               in1=yt[:, g * 256:(g + 1) * 256], op0=ALU.mult, op1=ALU.mult,
                                                               accum_out=ssg[:, g:g + 1]),
                   reads=["yt", "sqj"], writes=["sqj", f"ssg{g}"])
            rstd_from_ss(ssg, rsg, 256 * EPS, ["ssg0", "ssg1"], "rsg")
            for g in range(2):
                EO("dve", lambda e, g=g: e.scalar_tensor_tensor(out=ytm[:, 512 + g * 256:512 + (g + 1) * 256], in0=yt[:, g * 256:(g + 1) * 256],
                                                               scalar=rsg[:, g:g + 1], in1=ssmw16[:, g * 256:(g + 1) * 256], op0=ALU.mult,
                                                               op1=ALU.mult), reads=["yt", "rsg", "ssmw16"], writes=[f"ytm_s{g}"])

        def E2c(c):
            def tpy(e):
                last = None
                for f in range(KD):
                    last = e.transpose(out=TP[:, f, :], in_=ytm[:, f * 128:(f + 1) * 128], identity=identb)
                return last
            EO("pe", tpy, reads=["ytm_a0", "ytm_a1", "ytm_s0", "ytm_s1", "identb"], writes=["TP"])
            EO("act", lambda e: e.activation(out=hT[:, :, c * 128:(c + 1) * 128], in_=TP[:], func=AF.Copy), reads=["TP"], writes=[f"hT{c}"])

        E1(0)
        E1b(0)
        for c in range(NCH):
            E2a(c)
            if c + 1 < NCH:
                E1(c + 1)
            E2b(c)
            if c + 1 < NCH:
                E1b(c + 1)
            E2c(c)
        if not main:
            return
        if stop == "B":
            em.barrier()
            return
        def s8a(c):
            EO("act", lambda e: e.activation(out=sqj[:], in_=x1buf[:, c, :], func=AF.Square, accum_out=ss1[:, c:c + 1]),
               reads=[f"x1_{c}", "sqj"], writes=["sqj", f"ss1_{c}"])
            rstd_from_ss(ss1[:, c:c + 1], rs1[:, c:c + 1], D * EPS, [f"ss1_{c}"], f"rs8_{c}")
            EO("act", lambda e: e.activation(out=xnB[:], in_=x1buf[:, c, :], func=AF.Identity, scale=rs1[:, c:c + 1]),
               reads=[f"x1_{c}", f"rs8_{c}", "xnB"], writes=["xnB"])

        def s8b(c):
            transpose_to_hT(xnB, ["xnB"], c, ((PC[0], "PC0"), (PC[1], "PC1")))

        for p in range(3):
            em.dma(wgus[:, p, :, :], wgu_scr[p].rearrange("p (k c) -> p k c", k=KD), writes=[f"wgus{p}"])
        for c in range(NCH):
            for dh in range(2):
                bi = (2 * c + dh) % 4

                def fop(e, c=c, dh=dh, bi=bi):
                    last = None
                    for k in range(KD):
                        last = e.matmul(PA[bi][:], lhsT=hT[:, k, c * 128:(c + 1) * 128], rhs=w_out_bf[:, k, dh * 512:(dh + 1) * 512],
                                        start=(k == 0), stop=(k == KD - 1))
                    return last
                EO("pe", fop, reads=[f"hT{c}"], writes=pk(f"PA{bi}"))
                EO("dve", lambda e, c=c, dh=dh, bi=bi: e.tensor_tensor(out=x1buf[:, c, dh * 512:(dh + 1) * 512], in0=PA[bi][:],
                                                                    in1=x1buf[:, c, dh * 512:(dh + 1) * 512], op=ALU.add),
                   reads=pk(f"PA{bi}") + [f"x1_{c}"], writes=[f"x1_{c}"])
            if c > 0:
                s8b(c - 1)
            s8a(c)
        s8b(NCH - 1)
        em.barrier()
        if stop == "OP":
            return
        for dq0 in range(2):
            em.dma(wdns[:, dq0, :, :], wdn_scr[dq0].rearrange("p (f c) -> p f c", f=NFF), writes=[f"wdns{dq0}"])
        for ffc in range(NFF):
            slot = ffc % 3
            bg, bu = PA[2 * (ffc % 2)], PA[2 * (ffc % 2) + 1]
            kg, ku = f"PA{2 * (ffc % 2)}", f"PA{2 * (ffc % 2) + 1}"

            def fup(e, slot=slot, bg=bg, bu=bu):
                last = None
                for t, bank in ((0, bg), (1, bu)):
                    for k in range(KD):
                        last = e.matmul(bank[:], lhsT=wgus[:, slot, k, t * 128:(t + 1) * 128], rhs=hT[:, k, :], start=(k == 0), stop=(k == KD - 1))
                return last
            EO("pe", fup, reads=hTk + [f"wgus{slot}"], writes=pk(kg) + pk(ku))
            if ffc + 3 < NFF:
                em.dma(wgus[:, slot, :, :], wgu_scr[ffc + 3].rearrange("p (k c) -> p k c", k=KD), writes=[f"wgus{slot}"])
            sgs = sg[:, ffc % 2, :]
            EO("act", lambda e, ffc=ffc, bg=bg, sgs=sgs: e.activation(out=sgs, in_=bg[:], func=AF.Silu, bias=bias_gu[:, ffc:ffc + 1]),
               reads=pk(kg) + [f"sg{ffc % 2}"], writes=[f"sg{ffc % 2}"])
            EO("dve", lambda e, ffc=ffc, bu=bu, sgs=sgs: e.scalar_tensor_tensor(out=actT[:, ffc, :], in0=bu[:], scalar=bias_gu[:, NFF + ffc:NFF + ffc + 1],
                                                                             in1=sgs, op0=ALU.add, op1=ALU.mult),
               reads=pk(ku) + [f"sg{ffc % 2}"], writes=[f"actT{ffc}"])
        if si + 1 < n_main:
            emit_s1a(si + 1, xown, junk=xnC, buf=xn4C)
        actk = [f"actT{f}" for f in range(NFF)]
        for dq in range(4):
            s = dq % 2
            if dq >= 2:
                em.dma(wdns[:, s, :, :], wdn_scr[dq].rearrange("p (f c) -> p f c", f=NFF), writes=[f"wdns{s}"])
            for c in range(NCH):
                reg = (dq * NCH + c) % 4
                bank = (PC[0], PC[1], PE0, PA[0])[reg][:, 0:256]
                bkey = ("PC0", "PC1", "PE0", "PA0")[reg]

                def fdn(e, s=s, c=c, bank=bank):
                    last = None
                    for f in range(NFF):
                        last = e.matmul(bank, lhsT=actT[:, f, c * 128:(c + 1) * 128], rhs=wdns[:, s, f, :], start=(f == 0), stop=(f == NFF - 1))
                    return last
                EO("pe", fdn, reads=actk + [f"wdns{s}"], writes=[bkey])
                EO("dve", lambda e, c=c, dq=dq, bank=bank: e.tensor_tensor(out=x1buf[:, c, dq * 256:(dq + 1) * 256], in0=bank,
                                                                        in1=x1buf[:, c, dq * 256:(dq + 1) * 256], op=ALU.add),
                   reads=[bkey, f"x1_{c}"], writes=[f"x1_{c}"])
        if si + 1 < n_main:
            emit_s1b(buf=xn4C, banks=((PA[2], "PA2"), (PA[3], "PA3")), alias_x1=False)
        for c in range(NCH):
            EO("act", lambda e, c=c: e.activation(out=xnC[:], in_=x1buf[:, c, :], func=AF.Square, accum_out=ss1[:, c:c + 1]),
               reads=[f"x1_{c}", "xn"], writes=["xn", f"ss1_{c}"])
        rstd_from_ss(ss1, rs1, D * EPS, [f"ss1_{c}" for c in range(NCH)], "rs1_fin")
        for c in range(NCH):
            EO("dve", lambda e, c=c: e.scalar_tensor_tensor(out=x1buf[:, c, :], in0=x1buf[:, c, :], scalar=rs1[:, c:c + 1], in1=fnw32,
                                                           op0=ALU.mult, op1=ALU.mult), reads=[f"x1_{c}", "rs1_fin", "fnw32"], writes=[f"x1_{c}"])
            em.dma(out[tok0 + c * 128:tok0 + (c + 1) * 128, :], x1buf[:, c, :], reads=[f"x1_{c}"], writes=[f"out{si}_{c}"])
        em.barrier()

    seq = [("prelast" if si == NSC - 1 else "pre", si, xprev, 0) for si in range(NSC - n_pre, NSC)]
    seq += [("main", si, xown, T + si * T) for si in range(n_main)]
    for idx, (kind, si, src, pos0) in enumerate(seq):
        s1_done = idx > 0
        nxt = nxa = nxb = None
        if kind != "main" and idx + 1 < len(seq):
            nk, nsi, nsrc, _ = seq[idx + 1]
            if nk == "main":
                nxt = (lambda nsi=nsi, nsrc=nsrc: emit_s1(nsi, nsrc, True))
            else:
                nxa = (lambda nsi=nsi, nsrc=nsrc: emit_s1a(nsi, nsrc))
                nxb = emit_s1b
        emit_sc(kind, si, src, pos0, s1_done, nxt, nxa, nxb)
    em.barrier()
    stats = em.finalize()
    return nc, stats


def _gather_cols():
    idx = []
    idx += list(range(0, 512))
    idx += [64 * h + (d + 32) % 64 for h in range(8) for d in range(64)]
    for g in range(2):
        idx += [512 + 64 * g + d for d in range(64)] * 2
    for g in range(2):
        idx += [512 + 64 * g + (d + 32) % 64 for d in range(64)] * 2
    idx += list(range(1280, 2304))
    idx += list(range(640, 768))
    idx += list(range(2304, 2312))
    idx += list(range(768, 1280))
    assert len(idx) == NW
    return np.array(idx)


def _consts():
    p = np.arange(128)
    ident = (p[:, None] == p[None, :]).astype(np.float32)
    triLE = (p[:, None] <= p[None, :]).astype(np.float32)
    triGT = (p[:, None] > p[None, :]).astype(np.float32)
    mcur = np.where(p[:, None] <= p[None, :], 0.0, NEG).astype(np.float32)
    mprev = np.where(p[:, None] > p[None, :], 0.0, NEG).astype(np.float32)
    cp = np.concatenate([ident, triLE, triGT, np.tile(mcur, (1, 4)), np.tile(mprev, (1, 4))], axis=1)
    return np.ascontiguousarray(cp.astype(np.float32))


_PROG = {}


def kernel(x, c, positions, w_ada, b_ada, norm1_w, w_in, conv_w, conv_b, dt_bias, a_log, d_skip, attn_sinks,
           ssm_norm_w, w_out, norm2_w, w_gate_up, w_down, final_norm_w):
    f32 = np.float32
    x = np.asarray(x, f32)
    if "p" not in _PROG:
        _PROG["p"] = build_program()
    nc, stats = _PROG["p"]
    w_in_g = np.ascontiguousarray(np.asarray(w_in, f32)[0][:, _gather_cols()])
    cpack = _consts()
    half = 32
    inv_freq = (10000.0 ** (-np.arange(half, dtype=np.float32) / np.float32(half))).astype(f32)
    p = np.arange(128)
    rows = np.concatenate([np.asarray(final_norm_w, f32), np.asarray(ssm_norm_w, f32)[0], np.asarray(dt_bias, f32)[0],
                           np.asarray(a_log, f32)[0], np.asarray(d_skip, f32)[0], np.asarray(attn_sinks, f32)[0]])
    rowpack = np.ascontiguousarray(np.tile(rows[None, :], (128, 1)))
    badap = np.ascontiguousarray(np.tile(np.asarray(b_ada, f32)[0][None, :], (128, 1)))
    in_maps = []
    for i in range(8):
        b, hf = i // 2, i % 2
        colp = np.zeros((128, 80), f32)
        colp[:, 0:8] = np.asarray(c, f32)[b].reshape(8, 128).T
        cw = np.asarray(conv_w, f32)[0]
        colp[:, 8:40] = cw.reshape(4, 8, 128).transpose(2, 1, 0).reshape(128, 32)
        colp[:, 40:48] = np.asarray(conv_b, f32)[0].reshape(8, 128).T
        colp[:, 48] = inv_freq[p % 32]
        colp[:, 49] = float(hf)
        colp[:, 50] = np.where((p % 64) < 32, -1.0, 1.0)
        colp[:, 51:59] = np.asarray(norm1_w, f32)[0].reshape(8, 128).T
        colp[:, 59:67] = np.asarray(norm2_w, f32)[0].reshape(8, 128).T
        colp[:, 67] = (p < 64).astype(f32)
        colp[:, 68] = (p >= 64).astype(f32)
        pos_b = np.asarray(positions)[b].astype(np.int32)
        pos_cat = np.concatenate([pos_b[SEQ_HALF - T:SEQ_HALF], pos_b[hf * SEQ_HALF:(hf + 1) * SEQ_HALF]])
        in_maps.append({
            "xprev": np.ascontiguousarray(x[b, 0:SEQ_HALF]),
            "xown": np.ascontiguousarray(x[b, hf * SEQ_HALF:(hf + 1) * SEQ_HALF]),
            "posrep": np.ascontiguousarray(np.tile(pos_cat[None, :], (128, 1))),
            "colpack": colp, "rowpack": rowpack, "bada": badap, "cpack": cpack,
            "w_ada": np.ascontiguousarray(np.asarray(w_ada, f32)[0]), "w_in": w_in_g,
            "w_out": np.ascontiguousarray(np.asarray(w_out, f32)[0]),
            "w_gu": np.ascontiguousarray(np.asarray(w_gate_up, f32)[0]),
            "w_dn": np.ascontiguousarray(np.asarray(w_down, f32)[0]),
        })
    res = run_bass_kernel_spmd(nc, in_maps, core_ids=list(range(8)))
    outp = np.empty((4, 2 * SEQ_HALF, D), f32)
    for i in range(8):
        b, hf = i // 2, i % 2
        outp[b, hf * SEQ_HALF:(hf + 1) * SEQ_HALF] = res.results[i]["out"]
    return outp
```

```python
import numpy as np
import concourse.bass as bass
import concourse.mybir as mybir
from concourse.bass_utils import run_bass_kernel_spmd

F32 = mybir.dt.float32
BF16 = mybir.dt.bfloat16
I32 = mybir.dt.int32
AF = mybir.ActivationFunctionType
ALU = mybir.AluOpType
AX = mybir.AxisListType

EPOCH = 2000
N_DMA_SEMS = 12

D = 1024
KD = 8
SEQ_HALF = 4096
T = 512
NCH = 4
NSC = SEQ_HALF // T
DFF = 2816
NFF = 22
EPS = 1e-6
CQ, CQS, CK, CKS, CXC, CV, CDT, CZ, NW = 0, 512, 1024, 1280, 1536, 2560, 2688, 2696, 3208
NFM = 20
NTM = NW - CV
NEG = -30000.0


class _Op:
    __slots__ = ("eng", "fn", "reads", "writes", "dma", "deps", "signal", "barrier")

    def __init__(self, eng, fn, reads, writes, dma, barrier=False):
        self.eng, self.fn, self.reads, self.writes, self.dma = eng, fn, reads, writes, dma
        self.deps = ()
        self.signal = False
        self.barrier = barrier


class Emitter:
    def __init__(self, nc):
        self.nc = nc
        self.ops = []
        self.engines = {"pe": nc.tensor, "act": nc.scalar, "dve": nc.vector, "pool": nc.gpsimd, "sp": nc.sync}

    def op(self, eng, fn, reads=(), writes=()):
        self.ops.append(_Op(eng, fn, tuple(reads), tuple(writes), False))

    def dma(self, out, in_, reads=(), writes=(), eng="sp"):
        self.ops.append(_Op(eng, lambda e: e.dma_start(out=out, in_=in_), tuple(reads), tuple(writes), True))

    def barrier(self):
        self.ops.append(_Op("sp", lambda e: e.nop(), (), (), False, barrier=True))

    def finalize(self):
        nc = self.nc
        ops = self.ops
        n = len(ops)
        last_writer, readers = {}, {}
        last_on_eng = {}
        dma_since = []
        cur_barrier = None
        for i, o in enumerate(ops):
            deps = set()
            if o.barrier:
                deps.update(last_on_eng.values())
                deps.update(dma_since)
                dma_since = []
                last_writer, readers = {}, {}
            else:
                for r in o.reads:
                    w = last_writer.get(r)
                    if w is not None:
                        deps.add(w)
                for w_ in o.writes:
                    w = last_writer.get(w_)
                    if w is not None:
                        deps.add(w)
                    deps.update(readers.get(w_, ()))
                if o.eng == "pe":
                    deps = {d for d in deps if not (ops[d].eng == "pe" and not ops[d].dma)}
                if cur_barrier is not None:
                    deps.add(cur_barrier)
            deps.discard(i)
            o.deps = tuple(sorted(deps))
            for d in o.deps:
                ops[d].signal = True
            if o.barrier:
                cur_barrier = i
            for w_ in o.writes:
                last_writer[w_] = i
                readers[w_] = []
            for r in o.reads:
                if r not in o.writes:
                    readers.setdefault(r, []).append(i)
            last_on_eng[o.eng] = i
            if o.dma:
                dma_since.append(i)
        eng_count = {e: 0 for e in self.engines}
        eng_sems = {e: [] for e in self.engines}
        dma_sems = [nc.alloc_semaphore(name=f"dma{i}") for i in range(N_DMA_SEMS)]
        dma_val = [0] * N_DMA_SEMS
        rr = 0
        state = {e: {} for e in self.engines}
        sig = [None] * n
        clock = [None] * n
        nwaits = 0
        for i, o in enumerate(ops):
            E = self.engines[o.eng]
            st = state[o.eng]
            for d in o.deps:
                key, val, sem, semval = sig[d]
                if st.get(key, 0) >= val:
                    continue
                E.wait_ge(sem, semval)
                nwaits += 1
                for k2, v2 in clock[d].items():
                    if st.get(k2, 0) < v2:
                        st[k2] = v2
            if o.dma:
                k = rr
                rr = (rr + 1) % N_DMA_SEMS
                key = ("d", k)
                if st.get(key, 0) < dma_val[k]:
                    E.wait_ge(dma_sems[k], dma_val[k])
                    nwaits += 1
                    st[key] = dma_val[k]
                ins = o.fn(E)
                dma_val[k] += 16
                ins.then_inc(dma_sems[k], 16)
                sig[i] = (key, dma_val[k], dma_sems[k], dma_val[k])
                clk = dict(st)
                clk[key] = dma_val[k]
                clock[i] = clk
            else:
                ins = o.fn(E)
                if o.signal:
                    c = eng_count[o.eng]
                    ep, off = divmod(c, EPOCH)
                    if ep >= len(eng_sems[o.eng]):
                        eng_sems[o.eng].append(nc.alloc_semaphore(name=f"{o.eng}{ep}"))
                    sem = eng_sems[o.eng][ep]
                    ins.then_inc(sem, 1)
                    eng_count[o.eng] = c + 1
                    key = ("e", o.eng)
                    sig[i] = (key, c + 1, sem, off + 1)
                    clk = dict(st)
                    clk[key] = c + 1
                    clock[i] = clk
        return dict(nops=n, nwaits=nwaits, counts=dict(eng_count))


def bcl(ap, n):
    return ap.unsqueeze(2).to_broadcast([ap.shape[0], ap.shape[1], n])


def bcm(ap, n):
    return ap.unsqueeze(1).to_broadcast([ap.shape[0], n, ap.shape[1]])


def build_program(n_pre=NSC, n_main=NSC, dbg=None, stop=None):
    nc = bass.Bass("TRN2", target_bir_lowering=False)

    def din(name, shape, dt=F32):
        return nc.dram_tensor(name, list(shape), dt, kind="ExternalInput").ap()

    xprev = din("xprev", [SEQ_HALF, D])
    xown = din("xown", [SEQ_HALF, D])
    posrep = din("posrep", [128, T + SEQ_HALF], I32)
    colpack = din("colpack", [128, 80])
    rowpack = din("rowpack", [128, 1568])
    bada = din("bada", [128, 6144])
    cpack = din("cpack", [128, 1408])
    w_ada = din("w_ada", [D, 6144])
    w_in = din("w_in", [D, NW])
    w_out = din("w_out", [D, D])
    w_gu = din("w_gu", [D, 2 * DFF])
    w_dn = din("w_dn", [DFF, D])
    out = nc.dram_tensor("out", [SEQ_HALF, D], F32, kind="ExternalOutput").ap()
    wgu_scr = nc.dram_tensor("wgu_scr", [NFF, 128, KD * 256], BF16, kind="Internal").ap()
    wdn_scr = nc.dram_tensor("wdn_scr", [4, 128, NFF * 256], BF16, kind="Internal").ap()
    dbg_out = {}
    if dbg:
        for nm, shp in dbg.items():
            dbg_out[nm] = nc.dram_tensor("dbg_" + nm, list(shp), F32, kind="ExternalOutput").ap()

    def sb(name, shape, dt):
        return nc.sbuf_tensor(name, list(shape), dt).__enter__()

    def psum(name, shape, dt):
        return nc.psum_tensor(name, list(shape), dt).__enter__()

    w_in_bf = sb("w_in_bf", [128, KD, NW], BF16)
    w_out_bf = sb("w_out_bf", [128, KD, D], BF16)
    cst = sb("cst", [128, 512], F32)
    identf, triLE, triGT, onesf = cst[:, 0:128], cst[:, 128:256], cst[:, 256:384], cst[:, 384:512]
    cstb = sb("cstb", [128, 256], BF16)
    identb, onesb = cstb[:, 0:128], cstb[:, 128:256]
    maskb = sb("maskb", [128, 1024], BF16)
    cdiag = sb("cdiag", [128, 32, 128], BF16)
    bdiag = sb("bdiag", [128, 8, 128], BF16)
    rows = sb("rows", [128, 1568], F32)
    fnw32, ssmw16 = rows[:, 0:1024], rows[:, 1024:1536]
    dtb, aneg, dskip, esink = rows[:, 1536:1544], rows[:, 1544:1552], rows[:, 1552:1560], rows[:, 1560:1568]
    cols = sb("cols", [128, 80], F32)
    ccol, convw, convb = cols[:, 0:8], cols[:, 8:40], cols[:, 40:48]
    invf, flag, rsign = cols[:, 48:49], cols[:, 49:50], cols[:, 50:51]
    n1wc, n2wc = cols[:, 51:59], cols[:, 59:67]
    hmask = cols[:, 67:69]
    small = sb("small", [128, 256], F32)
    g1c, g2c, sh1c, sh2c = small[:, 0:8], small[:, 8:16], small[:, 16:24], small[:, 24:32]
    nhalf = small[:, 32:33]
    ss1 = small[:, 40:44]
    rs1 = small[:, 44:48]
    ssg = small[:, 48:50]
    rsg = small[:, 50:52]
    den = small[:, 56:64]
    rden = small[:, 64:72]
    bias_fm = small[:, 80:100]
    bias_gu = small[:, 100:144]
    dtraw = small[:, 144:176]
    dtv = small[:, 176:208]
    av = small[:, 208:240]
    tmp32 = sb("tmp32", [128, 128], F32)
    exv = sb("exv", [128, NCH, 24], F32)
    wend = sb("wend", [128, NCH, 8], F32)
    bias_row = sb("bias_row", [1, NTM], BF16)
    browf = sb("browf", [1, 2, 512], F32)
    prevT = sb("prevT", [128, 512], F32)
    prevTb = sb("prevTb", [128, 512], BF16)
    x1buf = sb("x1buf", [128, NCH, D], F32)
    xstage = sb("xstage", [128, 2, D], F32)
    hT = sb("hT", [128, KD, T], BF16)
    wgus = sb("wgus", [128, 3, KD, 256], BF16)
    utail = sb("utail", [128, 8, 3], BF16)
    khalo = sb("khalo", [128, 2, 128], BF16)
    vhalo = sb("vhalo", [128, 130], BF16)
    ARENA = 61440
    arena = sb("arena", [128, ARENA // 2], BF16)

    class Lay:
        def __init__(self, base=0):
            self.off = base

        def take(self, shape, dt):
            nel = int(np.prod(shape))
            nb = nel * (4 if dt == F32 or dt == I32 else 2)
            nb_al = (nb + 63) // 64 * 64
            o = self.off
            self.off += nb_al
            assert self.off <= ARENA, (self.off, ARENA)
            ap = arena[:, o // 2:(o + nb) // 2]
            if dt != BF16:
                ap = ap.bitcast(dt)
            if len(shape) == 2:
                return ap.rearrange("p (a b) -> p a b", a=shape[0])
            if len(shape) == 3:
                return ap.rearrange("p (a b c) -> p a b c", a=shape[0], b=shape[1])
            return ap

    L = Lay()
    qT = L.take([4, T], BF16)
    kT = L.take([2, 128 + T], BF16)
    kTz = L.take([2, 2, 128 + T], BF16)
    Vaug = L.take([5, 130], BF16)
    xdt = L.take([NCH, 512], BF16)
    xdtd = L.take([NCH, 512], BF16)
    xsD = L.take([NCH, 512], BF16)
    Btm = L.take([NCH, 256], BF16)
    BCT = L.take([4, T], BF16)
    sz = L.take([NCH, 512], BF16)
    shared_end = L.off
    LA = Lay(shared_end)
    uT = LA.take([8, T + 3], BF16)
    cosT = LA.take([T], F32)
    sinT = LA.take([T], F32)
    rt1 = LA.take([T], F32)
    rt2 = LA.take([T], F32)
    xs = LA.take([512], F32)
    xn = LA.take([D], BF16)
    posi = LA.take([T], I32)
    LB = Lay(shared_end)
    PT = LB.take([4, 512], BF16)
    aTri = LB.take([1024], F32)
    Ebuf = LB.take([1024], F32)
    MT = LB.take([1024], BF16)
    CBm = LB.take([256], F32)
    yt = LB.take([512], F32)
    xnB = LB.take([D], BF16)
    sqj = LB.take([D], BF16)
    ytm = LB.take([D], BF16)
    jnk = LB.take([256], BF16)
    LC = Lay()
    actT = LC.take([NFF, T], BF16)
    wdns = LC.take([2, NFF, 256], BF16)
    xnC = LC.take([D], BF16)
    sg = LC.take([2, T], BF16)
    xn4C = arena[:, 49152 // 2:(49152 + NCH * D * 2) // 2].rearrange("p (c d) -> p c d", c=NCH)
    assert LC.off <= 49152
    LS = Lay()
    stg = LS.take([2, KD * 512], F32)
    cvo = LS.take([2, KD * 512], BF16)
    scb = LS.take([KD, 128], F32)
    rowst = LS.take([1568], F32)
    mod_lo = x1buf[:].rearrange("p a b -> p (a b)")
    mod_hi = hT[:].rearrange("p a b -> p (a b)").bitcast(F32)

    def modbc(c0, c1):
        if c1 <= 4096:
            return mod_lo[:, c0:c1]
        assert c0 >= 4096
        return mod_hi[:, c0 - 4096:c1 - 4096]

    TP = psum("TP", [128, 8, 128], BF16)
    PA = [psum(f"PA{i}", [128, 512], F32) for i in range(4)]
    PC = [psum(f"PC{i}", [128, 512], F32) for i in range(2)]
    PE0 = psum("PE0", [128, 512], F32)

    def pk(name):
        return [name]

    em = Emitter(nc)
    EO = em.op

    def dump(name, ap, reads):
        if name in dbg_out:
            em.dma(dbg_out[name], ap, reads=reads, writes=["dbg_" + name])

    em.dma(cst[:, 0:384], cpack[:, 0:384], writes=["cst"])
    em.dma(cols[:], colpack, writes=["cols"])
    em.dma(rowst[:], rowpack, writes=["rowst"])
    EO("dve", lambda e: e.memset(onesf, 1.0), writes=["onesf"])
    EO("dve", lambda e: e.memset(onesb, 1.0), writes=["onesb"])
    EO("dve", lambda e: e.memset(nhalf, -0.5), writes=["nhalf"])
    EO("dve", lambda e: e.tensor_copy(out=identb, in_=identf), reads=["cst"], writes=["identb"])
    em.dma(stg[:, 0, 0:1024], cpack[:, 384:1408], writes=["stg0"])
    EO("dve", lambda e: e.tensor_copy(out=maskb[:], in_=stg[:, 0, 0:1024]), reads=["stg0"], writes=["maskb"])
    EO("dve", lambda e: e.tensor_scalar(out=fnw32, in0=rowst[:, 0:1024], scalar1=32.0, scalar2=None, op0=ALU.mult),
       reads=["rowst"], writes=["fnw32"])
    EO("dve", lambda e: e.tensor_scalar(out=ssmw16, in0=rowst[:, 1024:1536], scalar1=16.0, scalar2=None, op0=ALU.mult),
       reads=["rowst"], writes=["ssmw16"])
    EO("dve", lambda e: e.tensor_copy(out=rows[:, 1536:1544], in_=rowst[:, 1536:1544]), reads=["rowst"], writes=["dtb"])
    EO("dve", lambda e: e.tensor_copy(out=dskip, in_=rowst[:, 1552:1560]), reads=["rowst"], writes=["dskip"])
    EO("act", lambda e: e.activation(out=aneg, in_=rowst[:, 1544:1552], func=AF.Exp), reads=["rowst"], writes=["aneg0"])
    EO("dve", lambda e: e.tensor_scalar(out=aneg, in0=aneg, scalar1=-1.0, scalar2=None, op0=ALU.mult),
       reads=["aneg0"], writes=["aneg"])
    EO("act", lambda e: e.activation(out=esink, in_=rowst[:, 1560:1568], func=AF.Exp), reads=["rowst"], writes=["esink"])
    EO("act", lambda e: e.activation(out=small[:, 240:248], in_=ccol, func=AF.Silu), reads=["cols"], writes=["sc"])
    EO("dve", lambda e: e.tensor_copy(out=scb[:], in_=bcl(small[:, 240:248], 128)), reads=["sc"], writes=["scb"])
    w_ada_v = w_ada.rearrange("(k p) c -> p k c", p=128)
    scbb = wgus[:].rearrange("p a k c -> p (a k c)")[:, 4096:5120].rearrange("p (k f) -> p k f", k=KD)
    shb = wgus[:].rearrange("p a k c -> p (a k c)")[:, 5120:5136]
    plainb = wgus[:].rearrange("p a k c -> p (a k c)")[:, 0:4096]
    EO("dve", lambda e: e.tensor_copy(out=scbb, in_=bcl(small[:, 240:248], 128)), reads=["sc"], writes=["scbb"])
    for cg in range(12):
        s = cg % 2
        em.dma(stg[:, s, :].rearrange("p (k c) -> p k c", k=KD), w_ada_v[:, :, cg * 512:(cg + 1) * 512], writes=[f"stg{s}"])
        em.dma(xstage[:, s, 0:512], bada[:, cg * 512:(cg + 1) * 512], writes=[f"xst{s}"])
        EO("act", lambda e, s=s: e.activation(out=cvo[:, s, 0:2048], in_=stg[:, s, 0:2048], func=AF.Copy), reads=[f"stg{s}"], writes=[f"cvo{s}a"])
        EO("dve", lambda e, s=s: e.tensor_copy(out=cvo[:, s, 2048:4096], in_=stg[:, s, 2048:4096]), reads=[f"stg{s}"], writes=[f"cvo{s}b"])
        bank = PA[cg % 4]

        def mmod(e, s=s, bank=bank):
            last = None
            for k in range(KD):
                last = e.matmul(bank[:], lhsT=scbb[:, k, :], rhs=cvo[:, s, k * 512:(k + 1) * 512], start=(k == 0), stop=(k == KD - 1))
            return last
        EO("pe", mmod, reads=["scbb", f"cvo{s}a", f"cvo{s}b"], writes=pk(f"PA{cg % 4}"))
        EO("dve", lambda e, s=s, bank=bank, cg=cg: e.tensor_tensor(out=modbc(cg * 512, (cg + 1) * 512), in0=bank[:],
                                                                 in1=xstage[:, s, 0:512], op=ALU.add),
           reads=pk(f"PA{cg % 4}") + [f"xst{s}"], writes=[f"mod{cg}"])
    modkeys = [f"mod{i}" for i in range(12)]

    def diag_extract(dst, c0, key):
        EO("dve", lambda e: e.tensor_tensor(out=stg[:, 0, 0:1024].rearrange("p (k f) -> p k f", k=KD),
                                            in0=modbc(c0, c0 + 1024).rearrange("p (k f) -> p k f", k=KD),
                                            in1=bcm(identf, KD), op=ALU.mult), reads=modkeys + ["cst", "stg0"], writes=["stg0"])
        EO("dve", lambda e: e.tensor_reduce(out=dst, in_=stg[:, 0, 0:1024].rearrange("p (k f) -> p k f", k=KD),
                                            axis=AX.X, op=ALU.add), reads=["stg0"], writes=[key])
    diag_extract(sh1c, 0, "sh1c")
    diag_extract(g1c, 1024, "g1c0")
    diag_extract(sh2c, 3072, "sh2c")
    diag_extract(g2c, 4096, "g2c0")
    EO("dve", lambda e: e.tensor_copy(out=shb[:, 0:8], in_=sh1c), reads=["sh1c"], writes=["shb1"])
    EO("dve", lambda e: e.tensor_copy(out=shb[:, 8:16], in_=sh2c), reads=["sh2c"], writes=["shb2"])
    for gc, nw, k0, k1 in ((g1c, n1wc, "g1c0", "g1c"), (g2c, n2wc, "g2c0", "g2c")):
        EO("dve", lambda e, gc=gc, nw=nw: e.scalar_tensor_tensor(out=gc, in0=gc, scalar=1.0, in1=nw, op0=ALU.add, op1=ALU.mult),
           reads=[k0, "cols"], writes=[k0 + "x"])
        EO("dve", lambda e, gc=gc: e.tensor_scalar(out=gc, in0=gc, scalar1=32.0, scalar2=None, op0=ALU.mult),
           reads=[k0 + "x"], writes=[k1])

    w_in_v = w_in.rearrange("(k p) c -> p k c", p=128)
    pieces = [(i * 512, 512) for i in range(5)] + [(CV, 136), (CZ, 512)]
    cvt_rr = 0
    for pi, (c0, w) in enumerate(pieces):
        s = pi % 2
        sv = stg[:, s, 0:KD * w].rearrange("p (k c) -> p k c", k=KD)
        em.dma(sv, w_in_v[:, :, c0:c0 + w], writes=[f"stg{s}"])
        pv = plainb[:, 0:KD * w].rearrange("p (k c) -> p k c", k=KD)
        EO("act", lambda e, sv=sv, pv=pv: e.activation(out=pv[:, 0:4, :], in_=sv[:, 0:4, :], func=AF.Copy), reads=[f"stg{s}", "plainb"], writes=["plainb_a"])
        EO("dve", lambda e, sv=sv, pv=pv: e.tensor_copy(out=pv[:, 4:8, :], in_=sv[:, 4:8, :]), reads=[f"stg{s}", "plainb"], writes=["plainb_b"])
        if c0 < CV:
            rb = pi % 2

            def mbr(e, pv=pv):
                last = None
                for k in range(KD):
                    last = e.matmul(PC[0][0:1, 0:512], lhsT=shb[:, k:k + 1], rhs=pv[:, k, :], start=(k == 0), stop=(k == KD - 1))
                return last
            EO("pe", mbr, reads=["plainb_a", "plainb_b", "shb1"], writes=["PC0", "plainb"])
            EO("dve", lambda e, rb=rb: e.tensor_copy(out=browf[0:1, rb, :], in_=PC[0][0:1, 0:512]), reads=["PC0"], writes=[f"browf{rb}"])

            def mbt(e, rb=rb, c0=c0):
                last = None
                for m in range(4):
                    mi = c0 // 128 + m
                    last = e.matmul(PE0[:, mi:mi + 1], lhsT=browf[0:1, rb, m * 128:(m + 1) * 128], rhs=onesf[0:1, 0:1], start=True, stop=True)
                return last
            EO("pe", mbt, reads=[f"browf{rb}", "onesf"], writes=["PE0"])
        else:
            o0 = c0 - CV
            bank = PC[0] if c0 == CV else PC[1]

            def mb2(e, pv=pv, w=w, bank=bank):
                last = None
                for k in range(KD):
                    last = e.matmul(bank[0:1, 0:w], lhsT=shb[:, k:k + 1], rhs=pv[:, k, :], start=(k == 0), stop=(k == KD - 1))
                return last
            bk = "PC0" if c0 == CV else "PC1"
            EO("pe", mb2, reads=["plainb_a", "plainb_b", "shb1"], writes=pk(bk) + ["plainb"])
            EO("dve", lambda e, w=w, bank=bank, o0=o0: e.tensor_copy(out=bias_row[0:1, o0:o0 + w], in_=bank[0:1, 0:w]),
               reads=pk(bk), writes=[f"brow{o0}"])
        for k in range(KD):
            eng = ("act", "dve")[cvt_rr % 2]
            cvt_rr += 1
            if eng == "act":
                EO("act", lambda e, k=k, sv=sv, c0=c0, w=w: e.activation(out=w_in_bf[:, k, c0:c0 + w], in_=sv[:, k, :], func=AF.Identity,
                                                                     scale=g1c[:, k:k + 1]),
                   reads=[f"stg{s}", "g1c"], writes=[f"win{pi}_{k}"])
            else:
                EO(eng, lambda e, k=k, sv=sv, c0=c0, w=w: e.tensor_scalar(out=w_in_bf[:, k, c0:c0 + w], in0=sv[:, k, :],
                                                                       scalar1=g1c[:, k:k + 1], scalar2=None, op0=ALU.mult),
                   reads=[f"stg{s}", "g1c"], writes=[f"win{pi}_{k}"])
    EO("dve", lambda e: e.tensor_copy(out=bias_fm, in_=PE0[:, 0:NFM]), reads=["PE0"], writes=["bias_fm"])

    w_out_v = w_out.rearrange("(k p) c -> p k c", p=128)
    for hh in range(2):
        s = hh
        sv = stg[:, s, :].rearrange("p (k c) -> p k c", k=KD)
        em.dma(sv, w_out_v[:, :, hh * 512:(hh + 1) * 512], writes=[f"stg{s}"])
        EO(("dve", "pool")[hh], lambda e, sv=sv, hh=hh: e.tensor_tensor(out=w_out_bf[:, :, hh * 512:(hh + 1) * 512], in0=sv,
                                                                      in1=bcm(modbc(2048 + hh * 512, 2048 + (hh + 1) * 512), KD), op=ALU.mult),
           reads=[f"stg{s}"] + modkeys, writes=[f"wout{hh}"])

    for j in range(8):
        for k in range(4):
            EO("dve", lambda e, j=j, k=k: e.tensor_scalar(out=cdiag[:, j * 4 + k, :], in0=identf,
                                                                                 scalar1=convw[:, j * 4 + k:j * 4 + k + 1], scalar2=None, op0=ALU.mult),
               reads=["cst", "cols"], writes=[f"cdiag{j}_{k}"])
        EO("dve", lambda e, j=j: e.tensor_scalar(out=bdiag[:, j, :], in0=identf, scalar1=convb[:, j:j + 1], scalar2=None, op0=ALU.mult),
           reads=["cst", "cols"], writes=[f"bdiag{j}"])

    w_gu_v = w_gu.rearrange("(k p) c -> p k c", p=128)
    def gu_load(pc):
        s = pc % 2
        sv = stg[:, s, :].rearrange("p (k c) -> p k c", k=KD)
        em.dma(sv[:, :, 0:256], w_gu_v[:, :, pc * 256:(pc + 1) * 256], writes=[f"stg{s}"])
        em.dma(sv[:, :, 256:512], w_gu_v[:, :, DFF + pc * 256:DFF + (pc + 1) * 256], writes=[f"stg{s}"])
    gu_load(0)
    for pc in range(11):
        s = pc % 2
        sv = stg[:, s, :].rearrange("p (k c) -> p k c", k=KD)
        if pc + 1 < 11:
            gu_load(pc + 1)
        rb = pc % 2
        pv = plainb.rearrange("p (k c) -> p k c", k=KD)
        EO("act", lambda e, sv=sv, pv=pv: e.activation(out=pv[:, 0:4, :], in_=sv[:, 0:4, :], func=AF.Copy), reads=[f"stg{s}", "plainb"], writes=["plainb_a"])
        EO("dve", lambda e, sv=sv, pv=pv: e.tensor_copy(out=pv[:, 4:8, :], in_=sv[:, 4:8, :]), reads=[f"stg{s}", "plainb"], writes=["plainb_b"])

        def mbr3(e, pv=pv):
            last = None
            for k in range(KD):
                last = e.matmul(PC[0][0:1, 0:512], lhsT=shb[:, 8 + k:9 + k], rhs=pv[:, k, :], start=(k == 0), stop=(k == KD - 1))
            return last
        EO("pe", mbr3, reads=["plainb_a", "plainb_b", "shb2"], writes=["PC0", "plainb"])
        EO("dve", lambda e, rb=rb: e.tensor_copy(out=browf[0:1, rb, :], in_=PC[0][0:1, 0:512]), reads=["PC0"], writes=[f"browf{rb}"])

        def mbt3(e, rb=rb, pc=pc):
            last = None
            for m in range(4):
                ffc = 2 * pc + (m % 2)
                bcol = ffc if m < 2 else NFF + ffc
                last = e.matmul(PE0[:, 64 + bcol:64 + bcol + 1], lhsT=browf[0:1, rb, m * 128:(m + 1) * 128], rhs=onesf[0:1, 0:1], start=True, stop=True)
            return last
        EO("pe", mbt3, reads=[f"browf{rb}", "onesf"], writes=["PE0"])
        cv = cvo[:, s, :].rearrange("p (k c) -> p k c", k=KD)
        for k in range(KD):
            eng = ("act", "dve")[cvt_rr % 2]
            cvt_rr += 1
            if eng == "act":
                EO("act", lambda e, k=k, sv=sv, cv=cv: e.activation(out=cv[:, k, :], in_=sv[:, k, :], func=AF.Identity, scale=g2c[:, k:k + 1]),
                   reads=[f"stg{s}", "g2c"], writes=[f"cvo{s}_{k}"])
            else:
                EO(eng, lambda e, k=k, sv=sv, cv=cv: e.tensor_scalar(out=cv[:, k, :], in0=sv[:, k, :], scalar1=g2c[:, k:k + 1], scalar2=None,
                                                                    op0=ALU.mult),
                   reads=[f"stg{s}", "g2c"], writes=[f"cvo{s}_{k}"])
        for half in range(2):
            ffc = 2 * pc + half
            dst = wgu_scr[ffc].rearrange("p (k t c) -> p k t c", k=KD, t=2)
            em.dma(dst[:, :, 0, :], cv[:, :, half * 128:(half + 1) * 128], reads=[f"cvo{s}_{k}" for k in range(KD)], writes=[f"wguscr{ffc}g"])
            em.dma(dst[:, :, 1, :], cv[:, :, 256 + half * 128:256 + (half + 1) * 128], reads=[f"cvo{s}_{k}" for k in range(KD)],
                   writes=[f"wguscr{ffc}u"])
    EO("dve", lambda e: e.tensor_copy(out=bias_gu, in_=PE0[:, 64:64 + 2 * NFF]), reads=["PE0"], writes=["bias_gu"])
    w_dn_v = w_dn.rearrange("(f p) c -> p f c", p=128)
    dn_pieces = [(dq, fh) for dq in range(4) for fh in range(2)]

    def dn_load(i):
        dq, fh = dn_pieces[i]
        s = i % 2
        sv = stg[:, s, 0:11 * 256].rearrange("p (f c) -> p f c", f=11)
        em.dma(sv, w_dn_v[:, fh * 11:(fh + 1) * 11, dq * 256:(dq + 1) * 256], writes=[f"stg{s}"])
    dn_load(0)
    for i, (dq, fh) in enumerate(dn_pieces):
        s = i % 2
        sv = stg[:, s, 0:11 * 256].rearrange("p (f c) -> p f c", f=11)
        cv = cvo[:, s, 0:11 * 256].rearrange("p (f c) -> p f c", f=11)
        if i + 1 < len(dn_pieces):
            dn_load(i + 1)
        EO(("dve", "pool")[i % 2], lambda e, sv=sv, cv=cv, dq=dq: e.tensor_tensor(
            out=cv, in0=sv, in1=bcm(modbc(5120 + dq * 256, 5120 + (dq + 1) * 256), 11), op=ALU.mult),
           reads=[f"stg{s}"] + modkeys, writes=[f"cvo{s}"] + [f"cvo{s}_{k}" for k in range(KD)])
        dst = wdn_scr[dq].rearrange("p (f c) -> p f c", f=NFF)
        em.dma(dst[:, fh * 11:(fh + 1) * 11, :], cv, reads=[f"cvo{s}"], writes=[f"wdnscr{dq}_{fh}"])
    em.barrier()

    EO("dve", lambda e: e.memset(prevT[:], 0.0), writes=["prevT"])
    EO("dve", lambda e: e.memset(prevTb[:], 0.0), writes=["prevTb"])
    EO("pool", lambda e: e.memset(utail[:], 0.0), writes=["utail"])
    EO("pool", lambda e: e.memset(khalo[:], 0.0), writes=["khalo"])
    EO("pool", lambda e: e.memset(vhalo[:], 0.0), writes=["vhalo"])

    SCRKEYS_W = [f"wguscr{f}{t}" for f in range(NFF) for t in "gu"] + [f"wdnscr{q}_{h}" for q in range(4) for h in range(2)]

    def rstd_from_ss(ssap, rsap, n_eps, rk, wk):
        EO("act", lambda e: e.activation(out=rsap, in_=ssap, func=AF.Ln, bias=float(n_eps)), reads=rk, writes=[wk + "t"])
        EO("act", lambda e: e.activation(out=rsap, in_=rsap, func=AF.Exp, scale=-0.5), reads=[wk + "t"], writes=[wk])

    def transpose_to_hT(src, src_keys, c, banks):
        for half in range(2):
            bank, bkey = banks[half]

            def tps(e, half=half, bank=bank):
                last = None
                for f4 in range(4):
                    f = half * 4 + f4
                    last = e.matmul(bank[:, f4 * 128:(f4 + 1) * 128], lhsT=src[:, f * 128:(f + 1) * 128], rhs=identb, start=True, stop=True)
                return last
            EO("pe", tps, reads=src_keys + ["identb"], writes=[bkey])
            dst = hT[:, half * 4:(half + 1) * 4, c * 128:(c + 1) * 128]
            srcv = bank[:].rearrange("p (f t) -> p f t", f=4)
            if half == 0:
                EO("act", lambda e, dst=dst, srcv=srcv: e.activation(out=dst, in_=srcv, func=AF.Copy), reads=[bkey], writes=[f"hT{c}"])
            else:
                EO("dve", lambda e, dst=dst, srcv=srcv: e.tensor_copy(out=dst, in_=srcv), reads=[bkey], writes=[f"hT{c}"])

    S1_BANKS = ((PA[0], "PA0"), (PA[1], "PA1"))

    def norm_to_hT(xsrc, xkeys, xnbuf, c, sskey):
        EO("act", lambda e: e.activation(out=xnbuf[:], in_=xsrc, func=AF.Identity, scale=rs1[:, c:c + 1]),
           reads=xkeys + [sskey, "xn"], writes=["xn"])
        transpose_to_hT(xnbuf, ["xn"], c, S1_BANKS)

    def emit_s1(si, xsrc_dram, main):
        tok0 = si * T
        for c in range(NCH):
            em.dma(xstage[:, c % 2, :], xsrc_dram[tok0 + c * 128: tok0 + (c + 1) * 128, :], writes=[f"xst{c % 2}"])
            EO("act", lambda e, c=c: e.activation(out=xn[:], in_=xstage[:, c % 2, :], func=AF.Square, accum_out=ss1[:, c:c + 1]),
               reads=[f"xst{c % 2}"], writes=["xn", f"ss1_{c}"])
            if c == 1 and main:
                em.dma(x1buf[:], xsrc_dram[tok0:tok0 + T, :].rearrange("(c p) d -> p c d", p=128), writes=[f"x1_{cc}" for cc in range(NCH)])
            if c % 2 == 1:
                cc0 = c - 1
                rstd_from_ss(ss1[:, cc0:c + 1], rs1[:, cc0:c + 1], D * EPS, [f"ss1_{cc0}", f"ss1_{c}"], f"rs1_{cc0}")
                for cc in (cc0, c):
                    norm_to_hT(xstage[:, cc % 2, :], [f"xst{cc % 2}"], xn, cc, f"rs1_{cc0}")

    xn4 = x1buf[:].rearrange("p a b -> p (a b)").bitcast(BF16)[:, 0:NCH * D].rearrange("p (c d) -> p c d", c=NCH)

    def emit_s1a(si, xsrc_dram, junk=None, buf=None):
        junk = xn if junk is None else junk
        buf = xn4 if buf is None else buf
        tok0 = si * T
        for c in range(NCH):
            em.dma(xstage[:, c % 2, :], xsrc_dram[tok0 + c * 128: tok0 + (c + 1) * 128, :], writes=[f"xst{c % 2}"])
            EO("dve", lambda e, c=c: e.scalar_tensor_tensor(out=junk[:], in0=xstage[:, c % 2, :], scalar=1.0, in1=xstage[:, c % 2, :],
                                                           op0=ALU.mult, op1=ALU.mult, accum_out=ss1[:, c:c + 1]),
               reads=[f"xst{c % 2}", "xn"], writes=["xn", f"ss1_{c}"])
            if c % 2 == 1:
                cc0 = c - 1
                rstd_from_ss(ss1[:, cc0:c + 1], rs1[:, cc0:c + 1], D * EPS, [f"ss1_{cc0}", f"ss1_{c}"], f"rs1_{cc0}")
                for cc in (cc0, c):
                    EO("dve", lambda e, cc=cc: e.tensor_scalar(out=buf[:, cc, :], in0=xstage[:, cc % 2, :], scalar1=rs1[:, cc:cc + 1], scalar2=None,
                                                              op0=ALU.mult), reads=[f"xst{cc % 2}", f"rs1_{cc0}"], writes=[f"xn4_{cc}"])

    def emit_s1b(buf=None, banks=None, alias_x1=True):
        buf = xn4 if buf is None else buf
        banks = S1_BANKS if banks is None else banks
        for c in range(NCH):
            keys = [f"xn4_{c}"] + ([f"x1_{c // 2}"] if alias_x1 else [])
            transpose_to_hT(buf[:, c, :], keys, c, banks)

    def emit_sc(kind, si, xsrc_dram, pos0, s1_done=False, next_s1=None, next_s1a=None, next_s1b=None):
        main = kind == "main"
        last_pre = kind == "prelast"
        need_k = main or last_pre
        tok0 = si * T
        hTk = [f"hT{c}" for c in range(NCH)]
        if not s1_done:
            emit_s1(si, xsrc_dram, main)
        elif main and si > 0:
            em.dma(x1buf[:], xsrc_dram[tok0:tok0 + T, :].rearrange("(c p) d -> p c d", p=128), writes=[f"x1_{cc}" for cc in range(NCH)])
        if stop == "A1":
            em.barrier()
            return
        EO("pool", lambda e: e.tensor_copy(out=uT[:, :, 0:3], in_=utail[:]), reads=["utail"], writes=["uTtail"])
        if main:
            EO("pool", lambda e: e.tensor_copy(out=kT[:, :, 0:128], in_=khalo[:]), reads=["khalo"], writes=["kThalo"])
            EO("pool", lambda e: e.memset(Vaug[:], 1.0), writes=["Vaug_all"] + [f"Vaug{i}" for i in range(5)])
            EO("pool", lambda e: e.tensor_copy(out=Vaug[:, 0, :], in_=vhalo[:]), reads=["vhalo", "Vaug_all"], writes=["Vaug0"])
            if si == 0:
                EO("dve", lambda e: e.tensor_scalar(out=uT[:, :, 0:3], in0=uT[:, :, 0:3], scalar1=flag, scalar2=None, op0=ALU.mult),
                   reads=["uTtail", "cols"], writes=["uTtail"])
                EO("dve", lambda e: e.tensor_scalar(out=Vaug[:, 0, :], in0=Vaug[:, 0, :], scalar1=flag, scalar2=None, op0=ALU.mult),
                   reads=["Vaug0", "cols"], writes=["Vaug0"])
                EO("dve", lambda e: e.tensor_scalar(out=prevT[:], in0=prevT[:], scalar1=flag, scalar2=None, op0=ALU.mult),
                   reads=["prevT", "cols"], writes=["prevT"])
                EO("dve", lambda e: e.tensor_copy(out=prevTb[:], in_=prevT[:]), reads=["prevT"], writes=["prevTb"])
        else:
            EO("pool", lambda e: e.memset(Vaug[:], 1.0), writes=["Vaug_all"] + [f"Vaug{i}" for i in range(5)])
        if need_k:
            em.dma(posi[:], posrep[:, pos0:pos0 + T], writes=["posi"])
        if stop == "A2":
            em.barrier()
            return

        def fm_group(m, bank, bkey):
            def f(e):
                last = None
                for k in range(KD):
                    last = e.matmul(bank[:], lhsT=w_in_bf[:, k, m * 128:(m + 1) * 128], rhs=hT[:, k, :], start=(k == 0), stop=(k == KD - 1))
                return last
            EO("pe", f, reads=hTk, writes=pk(bkey))

        def rope_pair(m_plain, m_sw, dst, dkey, par):
            b0, b1 = PA[2 * par], PA[2 * par + 1]
            fm_group(m_plain, b0, f"PA{2 * par}")
            fm_group(m_sw, b1, f"PA{2 * par + 1}")
            EO("dve", lambda e: e.scalar_tensor_tensor(out=rt1[:], in0=b0[:], scalar=bias_fm[:, m_plain:m_plain + 1], in1=cosT[:],
                                                      op0=ALU.add, op1=ALU.mult), reads=pk(f"PA{2 * par}") + ["cT", "rt1"], writes=["rt1"])
            EO("dve", lambda e: e.scalar_tensor_tensor(out=rt2[:], in0=b1[:], scalar=bias_fm[:, m_sw:m_sw + 1], in1=sinT[:],
                                                      op0=ALU.add, op1=ALU.mult), reads=pk(f"PA{2 * par + 1}") + ["sT", "rt2"], writes=["rt2"])
            EO("pool", lambda e: e.tensor_tensor(out=dst, in0=rt1[:], in1=rt2[:], op=ALU.add), reads=["rt1", "rt2"], writes=[dkey])

        nxc = 8 if (main or last_pre) else 6
        xbanks = ((PC[0], "PC0"), (PC[1], "PC1"), (PE0, "PE0"))

        def xgroup(j):
            m = 12 + j
            xb, xk = xbanks[j % 3]
            fm_group(m, xb, xk)
            EO("act", lambda e: e.activation(out=uT[:, j, 3:3 + T], in_=xb[:], func=AF.Identity, bias=bias_fm[:, m:m + 1]),
               reads=[xk], writes=[f"uT{j}"])
        pairs = []
        if main:
            pairs += [(j, 4 + j, qT[:, j, :], f"qT{j}") for j in range(4)]
        if need_k:
            pairs += [(8 + g, 10 + g, kT[:, g, 128:128 + T], f"kT{g}") for g in range(2)]
        for j in range(nxc):
            xgroup(j)
        if need_k:
            EO("dve", lambda e: e.tensor_copy(out=rt1[:], in_=posi[:]), reads=["posi"], writes=["rt1"])
            EO("dve", lambda e: e.tensor_scalar(out=rt1[:], in0=rt1[:], scalar1=invf, scalar2=None, op0=ALU.mult),
               reads=["rt1", "cols"], writes=["rt1"])
            for which, dst, shift in (("s", sinT, 0.0), ("c", cosT, float(np.pi / 2))):
                EO("dve", lambda e, shift=shift: e.tensor_scalar(out=rt2[:], in0=rt1[:], scalar1=shift, scalar2=float(1.0 / (2 * np.pi)),
                                                                 op0=ALU.add, op1=ALU.mult), reads=["rt1", "rt2"], writes=["rt2"])
                EO("dve", lambda e: e.tensor_copy(out=posi[:], in_=rt2[:]), reads=["rt2", "posi"], writes=["posi"])
                EO("dve", lambda e: e.tensor_copy(out=rt2[:], in_=posi[:]), reads=["posi"], writes=["rt2"])
                EO("dve", lambda e: e.tensor_scalar(out=rt2[:], in0=rt2[:], scalar1=float(-2 * np.pi), scalar2=None, op0=ALU.mult),
                   reads=["rt2"], writes=["rt2"])
                EO("dve", lambda e, shift=shift, dst=dst: e.scalar_tensor_tensor(out=dst[:], in0=rt1[:], scalar=shift, in1=rt2[:], op0=ALU.add,
                                                                              op1=ALU.add) if False else
                   e.tensor_tensor(out=dst[:], in0=rt1[:], in1=rt2[:], op=ALU.add), reads=["rt1", "rt2"], writes=[which + "T"])
                if shift != 0.0:
                    EO("dve", lambda e, dst=dst, shift=shift: e.tensor_scalar(out=dst[:], in0=dst[:], scalar1=shift, scalar2=None, op0=ALU.add),
                       reads=[which + "T"], writes=[which + "T"])
                EO("dve", lambda e, dst=dst: e.tensor_scalar(out=rt2[:], in0=dst[:], scalar1=float(np.pi), scalar2=float(-2 * np.pi),
                                                             op0=ALU.is_gt, op1=ALU.mult), reads=[which + "T", "rt2"], writes=["rt2"])
                EO("dve", lambda e, dst=dst: e.tensor_tensor(out=dst[:], in0=dst[:], in1=rt2[:], op=ALU.add), reads=[which + "T", "rt2"],
                   writes=[which + "T"])
                EO("dve", lambda e, dst=dst: e.tensor_scalar(out=rt2[:], in0=dst[:], scalar1=float(-np.pi), scalar2=float(2 * np.pi),
                                                             op0=ALU.is_lt, op1=ALU.mult), reads=[which + "T", "rt2"], writes=["rt2"])
                EO("dve", lambda e, dst=dst: e.tensor_tensor(out=dst[:], in0=dst[:], in1=rt2[:], op=ALU.add), reads=[which + "T", "rt2"],
                   writes=[which + "T"])
                EO("act", lambda e, dst=dst: e.activation(out=dst[:], in_=dst[:], func=AF.Sin), reads=[which + "T"], writes=[which + "T"])
            EO("dve", lambda e: e.tensor_scalar(out=sinT[:], in0=sinT[:], scalar1=rsign, scalar2=None, op0=ALU.mult),
               reads=["sT", "cols"], writes=["sT"])
        par = 0
        for (mp, ms, dst, dkey) in pairs:
            rope_pair(mp, ms, dst, dkey, par)
            par ^= 1
        if need_k:
            EO("pool", lambda e: e.tensor_copy(out=khalo[:], in_=kT[:, :, T:T + 128]), reads=["kT0", "kT1"], writes=["khalo"])
            if main:
                for g in range(2):
                    for hp in range(2):
                        EO("dve", lambda e, g=g, hp=hp: e.tensor_scalar(out=kTz[:, g, hp, :], in0=kT[:, g, :], scalar1=hmask[:, hp:hp + 1],
                                                                         scalar2=None, op0=ALU.mult),
                           reads=[f"kT{g}", "kThalo", "cols"], writes=[f"kTz{g}"])
        uTk = [f"uT{j}" for j in range(nxc)] + ["uTtail"]
        EO("pool", lambda e: e.tensor_copy(out=utail[:], in_=uT[:, :, T:T + 3]), reads=uTk, writes=["utail"])
        if stop == "A3":
            em.barrier()
            return
        for c in range(NCH):
            def fvd(e, c=c):
                for k in range(KD):
                    e.matmul(PE0[:, 0:136], lhsT=hT[:, k, c * 128:(c + 1) * 128], rhs=w_in_bf[:, k, CV:CV + 136], start=(k == 0), stop=False)
                return e.matmul(PE0[:, 0:136], lhsT=onesb[0:1, :], rhs=bias_row[0:1, 0:136], start=False, stop=True)
            EO("pe", fvd, reads=[f"hT{c}"], writes=["PE0"])
            EO("act", lambda e, c=c: e.activation(out=Vaug[:, c + 1, :].rearrange("p (g d) -> p g d", g=2)[:, :, 0:64],
                                                  in_=PE0[:, 0:128].rearrange("p (g d) -> p g d", g=2), func=AF.Copy),
               reads=["PE0", "Vaug_all"], writes=[f"Vaug{c + 1}"])
            EO("dve", lambda e, c=c: e.tensor_tensor(out=dtraw[:, c * 8:(c + 1) * 8], in0=PE0[:, 128:136], in1=dtb, op=ALU.add),
               reads=["PE0", "dtb"], writes=[f"dtraw{c}"])
        EO("pool", lambda e: e.tensor_copy(out=vhalo[:], in_=Vaug[:, 4, :]), reads=["Vaug4"], writes=["vhalo"])
        if next_s1 is not None:
            next_s1()
        if main:
            for c in range(NCH):
                def fz(e, c=c):
                    for k in range(KD):
                        e.matmul(PA[c][:], lhsT=hT[:, k, c * 128:(c + 1) * 128], rhs=w_in_bf[:, k, CZ:CZ + 512], start=(k == 0), stop=False)
                    return e.matmul(PA[c][:], lhsT=onesb[0:1, :], rhs=bias_row[0:1, 136:648], start=False, stop=True)
                EO("pe", fz, reads=[f"hT{c}"], writes=[f"PA{c}"])
        dk = [f"dtraw{c}" for c in range(NCH)]
        EO("dve", lambda e: e.scalar_tensor_tensor(out=dtv, in0=dtraw, scalar=-1.0, in1=dtraw, op0=ALU.mult, op1=ALU.max), reads=dk, writes=["dtv"])
        EO("act", lambda e: e.activation(out=dtv, in_=dtv, func=AF.Exp, scale=-1.0), reads=["dtv"], writes=["dtv"])
        EO("act", lambda e: e.activation(out=dtv, in_=dtv, func=AF.Ln, bias=1.0), reads=["dtv"], writes=["dtv"])
        EO("dve", lambda e: e.scalar_tensor_tensor(out=dtv, in0=dtraw, scalar=0.0, in1=dtv, op0=ALU.max, op1=ALU.add),
           reads=dk + ["dtv"], writes=["dtv"])
        EO("dve", lambda e: e.tensor_tensor(out=av.rearrange("p (c h) -> p c h", c=NCH), in0=dtv.rearrange("p (c h) -> p c h", c=NCH),
                                            in1=bcm(aneg, NCH), op=ALU.mult), reads=["dtv", "aneg"], writes=["av"])
        def fsm(e):
            last = None
            for c in range(NCH):
                a_c = av[:, c * 8:(c + 1) * 8]
                o = 136 + c * 24
                e.matmul(PE0[:, o:o + 8], lhsT=triGT, rhs=a_c, start=True, stop=True)
                e.matmul(PE0[:, o + 8:o + 16], lhsT=triLE, rhs=a_c, start=True, stop=True)
                last = e.matmul(PE0[:, o + 16:o + 24], lhsT=onesf, rhs=a_c, start=True, stop=True)
            return last
        EO("pe", fsm, reads=["av", "cst", "onesf"], writes=["PE0"])
        EO("act", lambda e: e.activation(out=exv[:].rearrange("p c k -> p (c k)"), in_=PE0[:, 136:136 + NCH * 24], func=AF.Exp),
           reads=["PE0"], writes=[f"exv{c}" for c in range(NCH)])
        EO("dve", lambda e: e.tensor_tensor(out=wend[:], in0=dtv.rearrange("p (c h) -> p c h", c=NCH), in1=exv[:, :, 0:8], op=ALU.mult),
           reads=["dtv"] + [f"exv{c}" for c in range(NCH)], writes=[f"wend{c}" for c in range(NCH)])
        if main:
            for c in range(NCH):
                EO("act", lambda e, c=c: e.activation(out=sz[:, c, :], in_=PA[c][:], func=AF.Silu), reads=[f"PA{c}"], writes=[f"sz{c}"])
        if next_s1a is not None:
            next_s1a()
        for c in range(NCH):
            def fcx(e, c=c):
                last = None
                for j in range(4):
                    for k in range(4):
                        e.matmul(PC[1][:, j * 128:(j + 1) * 128], lhsT=uT[:, j, c * 128 + k:c * 128 + k + 128], rhs=cdiag[:, j * 4 + k, :],
                                 start=(k == 0), stop=False)
                    last = e.matmul(PC[1][:, j * 128:(j + 1) * 128], lhsT=onesb, rhs=bdiag[:, j, :], start=False, stop=True)
                return last
            EO("pe", fcx, reads=uTk, writes=pk("PC1"))
            EO("act", lambda e: e.activation(out=xs[:], in_=PC[1][:], func=AF.Silu), reads=pk("PC1") + ["xs"], writes=["xs"])
            xs3 = xs[:].rearrange("p (h d) -> p h d", h=8)
            EO("dve", lambda e, c=c: e.tensor_tensor(out=xdt[:, c, :].rearrange("p (h d) -> p h d", h=8), in0=xs3,
                                                     in1=bcl(dtv[:, c * 8:(c + 1) * 8], 64), op=ALU.mult), reads=["xs", "dtv"], writes=[f"xdt{c}"])
            EO("dve", lambda e, c=c: e.tensor_tensor(out=xdtd[:, c, :].rearrange("p (h d) -> p h d", h=8), in0=xs3,
                                                     in1=bcl(wend[:, c, :], 64), op=ALU.mult), reads=["xs", f"wend{c}"], writes=[f"xdtd{c}"])
            if main:
                EO("pool", lambda e, c=c: e.tensor_tensor(out=xsD[:, c, :].rearrange("p (h d) -> p h d", h=8), in0=xs3,
                                                          in1=bcl(dskip, 64), op=ALU.mult), reads=["xs", "dskip"], writes=[f"xsD{c}"])

            def fcb(e, c=c):
                last = None
                for jj in range(2):
                    j = 4 + jj
                    for k in range(4):
                        e.matmul(PC[0][:, jj * 128:(jj + 1) * 128], lhsT=uT[:, j, c * 128 + k:c * 128 + k + 128], rhs=cdiag[:, j * 4 + k, :],
                                 start=(k == 0), stop=False)
                    last = e.matmul(PC[0][:, jj * 128:(jj + 1) * 128], lhsT=onesb, rhs=bdiag[:, j, :], start=False, stop=True)
                return last
            EO("pe", fcb, reads=uTk, writes=["PC0"])
            EO("act", lambda e, c=c: e.activation(out=Btm[:, c, :], in_=PC[0][:, 0:256], func=AF.Silu), reads=["PC0"], writes=[f"Btm{c}"])
        if main:
            for jj in range(4):
                j = 4 + jj

                def fcf(e, j=j, jj=jj):
                    last = None
                    for k in range(4):
                        last = e.matmul(PA[jj][:], lhsT=cdiag[:, j * 4 + k, :], rhs=uT[:, j, k:k + T], start=(k == 0), stop=(k == 3))
                    return last
                EO("pe", fcf, reads=uTk, writes=pk(f"PA{jj}"))
                EO("act", lambda e, j=j, jj=jj: e.activation(out=BCT[:, jj, :], in_=PA[jj][:], func=AF.Silu, bias=convb[:, j:j + 1]),
                   reads=pk(f"PA{jj}") + ["cols"], writes=[f"BCT{jj}"])
        if main:
            em.barrier()
        if stop == "A":
            return

        TPf = TP[:].rearrange("p a b -> p (a b)").bitcast(F32)

        def state_update(c, bank, bkey):
            def fst(e, c=c):
                last = None
                for g in range(2):
                    last = e.matmul(bank[:, g * 256:(g + 1) * 256], lhsT=Btm[:, c, g * 128:(g + 1) * 128], rhs=xdtd[:, c, g * 256:(g + 1) * 256],
                                    start=True, stop=True)
                return last
            EO("pe", fst, reads=[f"Btm{c}", f"xdtd{c}"], writes=[bkey])
            EO("dve", lambda e, c=c: e.tensor_tensor(out=prevT[:].rearrange("p (h d) -> p h d", h=8), in0=prevT[:].rearrange("p (h d) -> p h d", h=8),
                                                     in1=bcl(exv[:, c, 16:24], 64), op=ALU.mult), reads=["prevT", f"exv{c}"], writes=["prevT"])
            EO("dve", lambda e: e.tensor_tensor(out=prevT[:], in0=bank, in1=prevT[:], op=ALU.add), reads=[bkey, "prevT"], writes=["prevT"])
            EO("act", lambda e: e.activation(out=prevTb[:], in_=prevT[:], func=AF.Copy), reads=["prevT"], writes=["prevTb"])

        if not main:
            for c in range(NCH):
                state_update(c, PC[1][:], "PC1")
            if next_s1b is not None:
                next_s1b()
            return

        def E1(c):
            a_c = av[:, c * 8:(c + 1) * 8]
            EO("dve", lambda e, a_c=a_c: e.tensor_tensor(out=aTri[:].rearrange("p (h l) -> p h l", h=8), in0=bcm(triLE, 8), in1=bcl(a_c, 128),
                                                         op=ALU.mult), reads=["av", "cst"], writes=["aTri"])

            def scores(g):
                for blk in range(2):
                    bank = PA[blk]
                    kc0 = c * 128 + blk * 128

                    def fsc(e, g=g, blk=blk, bank=bank, kc0=kc0):
                        moff = 0 if blk == 1 else 512
                        last = None
                        for hp in range(2):
                            for jj in range(2):
                                r0 = (hp * 2 + jj) * 128
                                e.matmul(bank[:, r0:r0 + 128], lhsT=identb, rhs=maskb[:, moff:moff + 128], start=True, stop=False)
                                last = e.matmul(bank[:, r0:r0 + 128], lhsT=kTz[:, g, hp, kc0:kc0 + 128],
                                                rhs=qT[:, 2 * g + jj, c * 128:(c + 1) * 128], start=False, stop=True)
                        return last
                    EO("pe", fsc, reads=[f"qT{2 * g}", f"qT{2 * g + 1}", f"kTz{g}", "maskb", "identb"], writes=[f"PA{blk}"])
                    EO("act", lambda e, g=g, blk=blk, bank=bank: e.activation(out=PT[:, 2 * g + blk, :], in_=bank[:], func=AF.Exp, scale=0.125),
                       reads=[f"PA{blk}"], writes=[f"PT{2 * g + blk}"])
            scores(0)
            for hh in range(2):
                EO("pe", lambda e, hh=hh: e.matmul(PA[2 + hh][:], lhsT=triGT, rhs=aTri[:, hh * 512:(hh + 1) * 512], start=True, stop=True),
                   reads=["aTri", "cst"], writes=[f"PA{2 + hh}"])
                EO("act", lambda e, hh=hh: e.activation(out=Ebuf[:, hh * 512:(hh + 1) * 512], in_=PA[2 + hh][:], func=AF.Exp),
                   reads=[f"PA{2 + hh}"], writes=[f"E{hh}"])
            scores(1)

            def fcbm(e):
                last = None
                for g in range(2):
                    last = e.matmul(PE0[:, 256 + g * 128:256 + (g + 1) * 128], lhsT=BCT[:, g, c * 128:(c + 1) * 128],
                                    rhs=BCT[:, 2 + g, c * 128:(c + 1) * 128], start=True, stop=True)
                return last
            EO("pe", fcbm, reads=[f"BCT{i}" for i in range(4)], writes=["PE0"])

        def E1b(c):
            EO("dve", lambda e: e.tensor_tensor(out=CBm[:].rearrange("p (g l) -> p g l", g=2),
                                                in0=PE0[:, 256:512].rearrange("p (g l) -> p g l", g=2), in1=bcm(triLE, 2), op=ALU.mult),
               reads=["PE0", "cst"], writes=["CBm"])
            EO("dve", lambda e: e.tensor_tensor(out=MT[:].rearrange("p (g j l) -> p g j l", g=2, j=4),
                                                in0=Ebuf[:].rearrange("p (g j l) -> p g j l", g=2, j=4),
                                                in1=CBm[:].rearrange("p (g l) -> p g l", g=2).unsqueeze(2).to_broadcast([128, 2, 4, 128]),
                                                op=ALU.mult), reads=["E0", "E1", "CBm"], writes=["MT"])

        def E2a(c):
            for g in range(2):
                def fpv(e, g=g):
                    last = None
                    for i in range(4):
                        hp, jj = i % 2, i // 2
                        cb = hp * 256 + jj * 128
                        e.matmul(PC[g][:, i * 128:i * 128 + 65], lhsT=PT[:, 2 * g, cb:cb + 128], rhs=Vaug[:, c, g * 65:(g + 1) * 65],
                                 start=True, stop=False)
                        last = e.matmul(PC[g][:, i * 128:i * 128 + 65], lhsT=PT[:, 2 * g + 1, cb:cb + 128], rhs=Vaug[:, c + 1, g * 65:(g + 1) * 65],
                                        start=False, stop=True)
                    return last
                EO("pe", fpv, reads=[f"PT{2 * g}", f"PT{2 * g + 1}", f"Vaug{c}", f"Vaug{c + 1}", "Vaug_all"], writes=[f"PC{g}"])
                o3 = PC[g][:, :].rearrange("p (i d) -> p i d", i=4)
                EO("dve", lambda e, g=g, o3=o3: e.tensor_tensor(out=den[:, g * 4:(g + 1) * 4], in0=o3[:, :, 64], in1=esink[:, g * 4:(g + 1) * 4],
                                                              op=ALU.add), reads=[f"PC{g}", "esink"], writes=[f"den{g}"])
                EO("dve", lambda e, g=g: e.reciprocal(out=rden[:, g * 4:(g + 1) * 4], in_=den[:, g * 4:(g + 1) * 4]), reads=[f"den{g}"],
                   writes=[f"rden{g}"])
                EO("dve", lambda e, g=g, o3=o3: e.tensor_tensor(out=ytm[:, g * 256:(g + 1) * 256].rearrange("p (i d) -> p i d", i=4),
                                                              in0=o3[:, :, 0:64], in1=bcl(rden[:, g * 4:(g + 1) * 4], 64), op=ALU.mult),
                   reads=[f"PC{g}", f"rden{g}"], writes=[f"ytm_a{g}"])

            def fyd(e):
                last = None
                for h in range(8):
                    e.matmul(PC[0][:, h * 64:(h + 1) * 64], lhsT=identb, rhs=xsD[:, c, h * 64:(h + 1) * 64], start=True, stop=False)
                    last = e.matmul(PC[0][:, h * 64:(h + 1) * 64], lhsT=MT[:, h * 128:(h + 1) * 128], rhs=xdt[:, c, h * 64:(h + 1) * 64],
                                    start=False, stop=True)
                return last
            EO("pe", fyd, reads=["MT", f"xdt{c}", f"xsD{c}", "identb"], writes=["PC0"])

            def fyo(e):
                last = None
                for g in range(2):
                    last = e.matmul(PC[1][:, g * 256:(g + 1) * 256], lhsT=BCT[:, 2 + g, c * 128:(c + 1) * 128],
                                    rhs=prevTb[:, g * 256:(g + 1) * 256], start=True, stop=True)
                return last
            EO("pe", fyo, reads=["BCT2", "BCT3", "prevTb"], writes=["PC1"])
            state_update(c, TPf, "TP")

        def E2b(c):
            EO("dve", lambda e: e.tensor_tensor(out=yt[:].rearrange("p (h d) -> p h d", h=8),
                                                in0=PC[1][:].rearrange("p (h d) -> p h d", h=8), in1=bcl(exv[:, c, 8:16], 64), op=ALU.mult),
               reads=["PC1", f"exv{c}"], writes=["yt"])
            EO("dve", lambda e: e.tensor_tensor(out=yt[:], in0=PC[0][:], in1=yt[:], op=ALU.add), reads=["PC0", "yt"], writes=["yt"])
            EO("dve", lambda e: e.tensor_tensor(out=yt[:], in0=yt[:], in1=sz[:, c, :], op=ALU.mult), reads=["yt", f"sz{c}"], writes=["yt"])
            for g in range(2):
                EO("dve", lambda e, g=g: e.scalar_tensor_tensor(out=sqj[:, g * 256:(g + 1) * 256], in0=yt[:, g * 256:(g + 1) * 256], scalar=1.0,
                                                               in1=yt[:, g * 256:(g + 1) * 256], op0=ALU.mult, op1=ALU.mult,
                                                               accum_out=ssg[:, g:g + 1]),
                   reads=["yt", "sqj"], writes=["sqj", f"ssg{g}"])
            rstd_from_ss(ssg, rsg, 256 * EPS, ["ssg0", "ssg1"], "rsg")
            for g in range(2):
                EO("dve", lambda e, g=g: e.scalar_tensor_tensor(out=ytm[:, 512 + g * 256:512 + (g + 1) * 256], in0=yt[:, g * 256:(g + 1) * 256],
                                                               scalar=rsg[:, g:g + 1], in1=ssmw16[:, g * 256:(g + 1) * 256], op0=ALU.mult,
                                                               op1=ALU.mult), reads=["yt", "rsg", "ssmw16"], writes=[f"ytm_s{g}"])

        def E2c(c):
            def tpy(e):
                last = None
                for f in range(KD):
                    last = e.transpose(out=TP[:, f, :], in_=ytm[:, f * 128:(f + 1) * 128], identity=identb)
                return last
            EO("pe", tpy, reads=["ytm_a0", "ytm_a1", "ytm_s0", "ytm_s1", "identb"], writes=["TP"])
            EO("act", lambda e: e.activation(out=hT[:, :, c * 128:(c + 1) * 128], in_=TP[:], func=AF.Copy), reads=["TP"], writes=[f"hT{c}"])

        E1(0)
        E1b(0)
        for c in range(NCH):
            E2a(c)
            if c + 1 < NCH:
                E1(c + 1)
            E2b(c)
            if c + 1 < NCH:
                E1b(c + 1)
            E2c(c)
        if not main:
            return
        if stop == "B":
            em.barrier()
            return
        def s8a(c):
            EO("act", lambda e: e.activation(out=sqj[:], in_=x1buf[:, c, :], func=AF.Square, accum_out=ss1[:, c:c + 1]),
               reads=[f"x1_{c}", "sqj"], writes=["sqj", f"ss1_{c}"])
            rstd_from_ss(ss1[:, c:c + 1], rs1[:, c:c + 1], D * EPS, [f"ss1_{c}"], f"rs8_{c}")
            EO("act", lambda e: e.activation(out=xnB[:], in_=x1buf[:, c, :], func=AF.Identity, scale=rs1[:, c:c + 1]),
               reads=[f"x1_{c}", f"rs8_{c}", "xnB"], writes=["xnB"])

        def s8b(c):
            transpose_to_hT(xnB, ["xnB"], c, ((PC[0], "PC0"), (PC[1], "PC1")))

        for p in range(3):
            em.dma(wgus[:, p, :, :], wgu_scr[p].rearrange("p (k c) -> p k c", k=KD), writes=[f"wgus{p}"])
        for c in range(NCH):
            for dh in range(2):
                bi = (2 * c + dh) % 4

                def fop(e, c=c, dh=dh, bi=bi):
                    last = None
                    for k in range(KD):
                        last = e.matmul(PA[bi][:], lhsT=hT[:, k, c * 128:(c + 1) * 128], rhs=w_out_bf[:, k, dh * 512:(dh + 1) * 512],
                                        start=(k == 0), stop=(k == KD - 1))
                    return last
                EO("pe", fop, reads=[f"hT{c}"], writes=pk(f"PA{bi}"))
                EO("dve", lambda e, c=c, dh=dh, bi=bi: e.tensor_tensor(out=x1buf[:, c, dh * 512:(dh + 1) * 512], in0=PA[bi][:],
                                                                    in1=x1buf[:, c, dh * 512:(dh + 1) * 512], op=ALU.add),
                   reads=pk(f"PA{bi}") + [f"x1_{c}"], writes=[f"x1_{c}"])
            if c > 0:
                s8b(c - 1)
            s8a(c)
        s8b(NCH - 1)
        em.barrier()
        if stop == "OP":
            return
        for dq0 in range(2):
            em.dma(wdns[:, dq0, :, :], wdn_scr[dq0].rearrange("p (f c) -> p f c", f=NFF), writes=[f"wdns{dq0}"])
        for ffc in range(NFF):
            slot = ffc % 3
            bg, bu = PA[2 * (ffc % 2)], PA[2 * (ffc % 2) + 1]
            kg, ku = f"PA{2 * (ffc % 2)}", f"PA{2 * (ffc % 2) + 1}"

            def fup(e, slot=slot, bg=bg, bu=bu):
                last = None
                for t, bank in ((0, bg), (1, bu)):
                    for k in range(KD):
                        last = e.matmul(bank[:], lhsT=wgus[:, slot, k, t * 128:(t + 1) * 128], rhs=hT[:, k, :], start=(k == 0), stop=(k == KD - 1))
                return last
            EO("pe", fup, reads=hTk + [f"wgus{slot}"], writes=pk(kg) + pk(ku))
            if ffc + 3 < NFF:
                em.dma(wgus[:, slot, :, :], wgu_scr[ffc + 3].rearrange("p (k c) -> p k c", k=KD), writes=[f"wgus{slot}"])
            sgs = sg[:, ffc % 2, :]
            EO("act", lambda e, ffc=ffc, bg=bg, sgs=sgs: e.activation(out=sgs, in_=bg[:], func=AF.Silu, bias=bias_gu[:, ffc:ffc + 1]),
               reads=pk(kg) + [f"sg{ffc % 2}"], writes=[f"sg{ffc % 2}"])
            EO("dve", lambda e, ffc=ffc, bu=bu, sgs=sgs: e.scalar_tensor_tensor(out=actT[:, ffc, :], in0=bu[:], scalar=bias_gu[:, NFF + ffc:NFF + ffc + 1],
                                                                             in1=sgs, op0=ALU.add, op1=ALU.mult),
               reads=pk(ku) + [f"sg{ffc % 2}"], writes=[f"actT{ffc}"])
        if si + 1 < n_main:
            emit_s1a(si + 1, xown, junk=xnC, buf=xn4C)
        actk = [f"actT{f}" for f in range(NFF)]
        for dq in range(4):
            s = dq % 2
            if dq >= 2:
                em.dma(wdns[:, s, :, :], wdn_scr[dq].rearrange("p (f c) -> p f c", f=NFF), writes=[f"wdns{s}"])
            for c in range(NCH):
                reg = (dq * NCH + c) % 4
                bank = (PC[0], PC[1], PE0, PA[0])[reg][:, 0:256]
                bkey = ("PC0", "PC1", "PE0", "PA0")[reg]

                def fdn(e, s=s, c=c, bank=bank):
                    last = None
                    for f in range(NFF):
                        last = e.matmul(bank, lhsT=actT[:, f, c * 128:(c + 1) * 128], rhs=wdns[:, s, f, :], start=(f == 0), stop=(f == NFF - 1))
                    return last
                EO("pe", fdn, reads=actk + [f"wdns{s}"], writes=[bkey])
                EO("dve", lambda e, c=c, dq=dq, bank=bank: e.tensor_tensor(out=x1buf[:, c, dq * 256:(dq + 1) * 256], in0=bank,
                                                                        in1=x1buf[:, c, dq * 256:(dq + 1) * 256], op=ALU.add),
                   reads=[bkey, f"x1_{c}"], writes=[f"x1_{c}"])
        if si + 1 < n_main:
            emit_s1b(buf=xn4C, banks=((PA[2], "PA2"), (PA[3], "PA3")), alias_x1=False)
        for c in range(NCH):
            EO("act", lambda e, c=c: e.activation(out=xnC[:], in_=x1buf[:, c, :], func=AF.Square, accum_out=ss1[:, c:c + 1]),
               reads=[f"x1_{c}", "xn"], writes=["xn", f"ss1_{c}"])
        rstd_from_ss(ss1, rs1, D * EPS, [f"ss1_{c}" for c in range(NCH)], "rs1_fin")
        for c in range(NCH):
            EO("dve", lambda e, c=c: e.scalar_tensor_tensor(out=x1buf[:, c, :], in0=x1buf[:, c, :], scalar=rs1[:, c:c + 1], in1=fnw32,
                                                           op0=ALU.mult, op1=ALU.mult), reads=[f"x1_{c}", "rs1_fin", "fnw32"], writes=[f"x1_{c}"])
            em.dma(out[tok0 + c * 128:tok0 + (c + 1) * 128, :], x1buf[:, c, :], reads=[f"x1_{c}"], writes=[f"out{si}_{c}"])
        em.barrier()

    seq = [("prelast" if si == NSC - 1 else "pre", si, xprev, 0) for si in range(NSC - n_pre, NSC)]
    seq += [("main", si, xown, T + si * T) for si in range(n_main)]
    for idx, (kind, si, src, pos0) in enumerate(seq):
        s1_done = idx > 0
        nxt = nxa = nxb = None
        if kind != "main" and idx + 1 < len(seq):
            nk, nsi, nsrc, _ = seq[idx + 1]
            if nk == "main":
                nxt = (lambda nsi=nsi, nsrc=nsrc: emit_s1(nsi, nsrc, True))
            else:
                nxa = (lambda nsi=nsi, nsrc=nsrc: emit_s1a(nsi, nsrc))
                nxb = emit_s1b
        emit_sc(kind, si, src, pos0, s1_done, nxt, nxa, nxb)
    em.barrier()
    stats = em.finalize()
    return nc, stats


def _gather_cols():
    idx = []
    idx += list(range(0, 512))
    idx += [64 * h + (d + 32) % 64 for h in range(8) for d in range(64)]
    for g in range(2):
        idx += [512 + 64 * g + d for d in range(64)] * 2
    for g in range(2):
        idx += [512 + 64 * g + (d + 32) % 64 for d in range(64)] * 2
    idx += list(range(1280, 2304))
    idx += list(range(640, 768))
    idx += list(range(2304, 2312))
    idx += list(range(768, 1280))
    assert len(idx) == NW
    return np.array(idx)


def _consts():
    p = np.arange(128)
    ident = (p[:, None] == p[None, :]).astype(np.float32)
    triLE = (p[:, None] <= p[None, :]).astype(np.float32)
    triGT = (p[:, None] > p[None, :]).astype(np.float32)
    mcur = np.where(p[:, None] <= p[None, :], 0.0, NEG).astype(np.float32)
    mprev = np.where(p[:, None] > p[None, :], 0.0, NEG).astype(np.float32)
    cp = np.concatenate([ident, triLE, triGT, np.tile(mcur, (1, 4)), np.tile(mprev, (1, 4))], axis=1)
    return np.ascontiguousarray(cp.astype(np.float32))


_PROG = {}


def kernel(x, c, positions, w_ada, b_ada, norm1_w, w_in, conv_w, conv_b, dt_bias, a_log, d_skip, attn_sinks,
           ssm_norm_w, w_out, norm2_w, w_gate_up, w_down, final_norm_w):
    f32 = np.float32
    x = np.asarray(x, f32)
    if "p" not in _PROG:
        _PROG["p"] = build_program()
    nc, stats = _PROG["p"]
    w_in_g = np.ascontiguousarray(np.asarray(w_in, f32)[0][:, _gather_cols()])
    cpack = _consts()
    half = 32
    inv_freq = (10000.0 ** (-np.arange(half, dtype=np.float32) / np.float32(half))).astype(f32)
    p = np.arange(128)
    rows = np.concatenate([np.asarray(final_norm_w, f32), np.asarray(ssm_norm_w, f32)[0], np.asarray(dt_bias, f32)[0],
                           np.asarray(a_log, f32)[0], np.asarray(d_skip, f32)[0], np.asarray(attn_sinks, f32)[0]])
    rowpack = np.ascontiguousarray(np.tile(rows[None, :], (128, 1)))
    badap = np.ascontiguousarray(np.tile(np.asarray(b_ada, f32)[0][None, :], (128, 1)))
    in_maps = []
    for i in range(8):
        b, hf = i // 2, i % 2
        colp = np.zeros((128, 80), f32)
        colp[:, 0:8] = np.asarray(c, f32)[b].reshape(8, 128).T
        cw = np.asarray(conv_w, f32)[0]
        colp[:, 8:40] = cw.reshape(4, 8, 128).transpose(2, 1, 0).reshape(128, 32)
        colp[:, 40:48] = np.asarray(conv_b, f32)[0].reshape(8, 128).T
        colp[:, 48] = inv_freq[p % 32]
        colp[:, 49] = float(hf)
        colp[:, 50] = np.where((p % 64) < 32, -1.0, 1.0)
        colp[:, 51:59] = np.asarray(norm1_w, f32)[0].reshape(8, 128).T
        colp[:, 59:67] = np.asarray(norm2_w, f32)[0].reshape(8, 128).T
        colp[:, 67] = (p < 64).astype(f32)
        colp[:, 68] = (p >= 64).astype(f32)
        pos_b = np.asarray(positions)[b].astype(np.int32)
        pos_cat = np.concatenate([pos_b[SEQ_HALF - T:SEQ_HALF], pos_b[hf * SEQ_HALF:(hf + 1) * SEQ_HALF]])
        in_maps.append({
            "xprev": np.ascontiguousarray(x[b, 0:SEQ_HALF]),
            "xown": np.ascontiguousarray(x[b, hf * SEQ_HALF:(hf + 1) * SEQ_HALF]),
            "posrep": np.ascontiguousarray(np.tile(pos_cat[None, :], (128, 1))),
            "colpack": colp, "rowpack": rowpack, "bada": badap, "cpack": cpack,
            "w_ada": np.ascontiguousarray(np.asarray(w_ada, f32)[0]), "w_in": w_in_g,
            "w_out": np.ascontiguousarray(np.asarray(w_out, f32)[0]),
            "w_gu": np.ascontiguousarray(np.asarray(w_gate_up, f32)[0]),
            "w_dn": np.ascontiguousarray(np.asarray(w_down, f32)[0]),
        })
    res = run_bass_kernel_spmd(nc, in_maps, core_ids=list(range(8)))
    outp = np.empty((4, 2 * SEQ_HALF, D), f32)
    for i in range(8):
        b, hf = i // 2, i % 2
        outp[b, hf * SEQ_HALF:(hf + 1) * SEQ_HALF] = res.results[i]["out"]
    return outp
```

```python
import numpy as np
import concourse.bass as bass
import concourse.mybir as mybir
from concourse.bass_utils import run_bass_kernel_spmd

F32 = mybir.dt.float32
BF16 = mybir.dt.bfloat16
I32 = mybir.dt.int32
AF = mybir.ActivationFunctionType
ALU = mybir.AluOpType
AX = mybir.AxisListType

EPOCH = 2000
N_DMA_SEMS = 12

D = 1024
KD = 8
SEQ_HALF = 4096
T = 512
NCH = 4
NSC = SEQ_HALF // T
DFF = 2816
NFF = 22
EPS = 1e-6
CQ, CQS, CK, CKS, CXC, CV, CDT, CZ, NW = 0, 512, 1024, 1280, 1536, 2560, 2688, 2696, 3208
NFM = 20
NTM = NW - CV
NEG = -30000.0


class _Op:
    __slots__ = ("eng", "fn", "reads", "writes", "dma", "deps", "signal", "barrier")

    def __init__(self, eng, fn, reads, writes, dma, barrier=False):
        self.eng, self.fn, self.reads, self.writes, self.dma = eng, fn, reads, writes, dma
        self.deps = ()
        self.signal = False
        self.barrier = barrier


class Emitter:
    def __init__(self, nc):
        self.nc = nc
        self.ops = []
        self.engines = {"pe": nc.tensor, "act": nc.scalar, "dve": nc.vector, "pool": nc.gpsimd, "sp": nc.sync}

    def op(self, eng, fn, reads=(), writes=()):
        self.ops.append(_Op(eng, fn, tuple(reads), tuple(writes), False))

    def dma(self, out, in_, reads=(), writes=(), eng="sp"):
        self.ops.append(_Op(eng, lambda e: e.dma_start(out=out, in_=in_), tuple(reads), tuple(writes), True))

    def barrier(self):
        self.ops.append(_Op("sp", lambda e: e.nop(), (), (), False, barrier=True))

    def finalize(self):
        nc = self.nc
        ops = self.ops
        n = len(ops)
        last_writer, readers = {}, {}
        last_on_eng = {}
        dma_since = []
        cur_barrier = None
        for i, o in enumerate(ops):
            deps = set()
            if o.barrier:
                deps.update(last_on_eng.values())
                deps.update(dma_since)
                dma_since = []
                last_writer, readers = {}, {}
            else:
                for r in o.reads:
                    w = last_writer.get(r)
                    if w is not None:
                        deps.add(w)
                for w_ in o.writes:
                    w = last_writer.get(w_)
                    if w is not None:
                        deps.add(w)
                    deps.update(readers.get(w_, ()))
                if o.eng == "pe":
                    deps = {d for d in deps if not (ops[d].eng == "pe" and not ops[d].dma)}
                if cur_barrier is not None:
                    deps.add(cur_barrier)
            deps.discard(i)
            o.deps = tuple(sorted(deps))
            for d in o.deps:
                ops[d].signal = True
            if o.barrier:
                cur_barrier = i
            for w_ in o.writes:
                last_writer[w_] = i
                readers[w_] = []
            for r in o.reads:
                if r not in o.writes:
                    readers.setdefault(r, []).append(i)
            last_on_eng[o.eng] = i
            if o.dma:
                dma_since.append(i)
        eng_count = {e: 0 for e in self.engines}
        eng_sems = {e: [] for e in self.engines}
        dma_sems = [nc.alloc_semaphore(name=f"dma{i}") for i in range(N_DMA_SEMS)]
        dma_val = [0] * N_DMA_SEMS
        rr = 0
        state = {e: {} for e in self.engines}
        sig = [None] * n
        clock = [None] * n
        nwaits = 0
        for i, o in enumerate(ops):
            E = self.engines[o.eng]
            st = state[o.eng]
            for d in o.deps:
                key, val, sem, semval = sig[d]
                if st.get(key, 0) >= val:
                    continue
                E.wait_ge(sem, semval)
                nwaits += 1
                for k2, v2 in clock[d].items():
                    if st.get(k2, 0) < v2:
                        st[k2] = v2
            if o.dma:
                k = rr
                rr = (rr + 1) % N_DMA_SEMS
                key = ("d", k)
                if st.get(key, 0) < dma_val[k]:
                    E.wait_ge(dma_sems[k], dma_val[k])
                    nwaits += 1
                    st[key] = dma_val[k]
                ins = o.fn(E)
                dma_val[k] += 16
                ins.then_inc(dma_sems[k], 16)
                sig[i] = (key, dma_val[k], dma_sems[k], dma_val[k])
                clk = dict(st)
                clk[key] = dma_val[k]
                clock[i] = clk
            else:
                ins = o.fn(E)
                if o.signal:
                    c = eng_count[o.eng]
                    ep, off = divmod(c, EPOCH)
                    if ep >= len(eng_sems[o.eng]):
                        eng_sems[o.eng].append(nc.alloc_semaphore(name=f"{o.eng}{ep}"))
                    sem = eng_sems[o.eng][ep]
                    ins.then_inc(sem, 1)
                    eng_count[o.eng] = c + 1
                    key = ("e", o.eng)
                    sig[i] = (key, c + 1, sem, off + 1)
                    clk = dict(st)
                    clk[key] = c + 1
                    clock[i] = clk
        return dict(nops=n, nwaits=nwaits, counts=dict(eng_count))


def bcl(ap, n):
    return ap.unsqueeze(2).to_broadcast([ap.shape[0], ap.shape[1], n])


def bcm(ap, n):
    return ap.unsqueeze(1).to_broadcast([ap.shape[0], n, ap.shape[1]])


def build_program(n_pre=NSC, n_main=NSC, dbg=None, stop=None):
    nc = bass.Bass("TRN2", target_bir_lowering=False)

    def din(name, shape, dt=F32):
        return nc.dram_tensor(name, list(shape), dt, kind="ExternalInput").ap()

    xprev = din("xprev", [SEQ_HALF, D])
    xown = din("xown", [SEQ_HALF, D])
    posrep = din("posrep", [128, T + SEQ_HALF], I32)
    colpack = din("colpack", [128, 80])
    rowpack = din("rowpack", [128, 1568])
    bada = din("bada", [128, 6144])
    cpack = din("cpack", [128, 1408])
    w_ada = din("w_ada", [D, 6144])
    w_in = din("w_in", [D, NW])
    w_out = din("w_out", [D, D])
    w_gu = din("w_gu", [D, 2 * DFF])
    w_dn = din("w_dn", [DFF, D])
    out = nc.dram_tensor("out", [SEQ_HALF, D], F32, kind="ExternalOutput").ap()
    wgu_scr = nc.dram_tensor("wgu_scr", [NFF, 128, KD * 256], BF16, kind="Internal").ap()
    wdn_scr = nc.dram_tensor("wdn_scr", [4, 128, NFF * 256], BF16, kind="Internal").ap()
    dbg_out = {}
    if dbg:
        for nm, shp in dbg.items():
            dbg_out[nm] = nc.dram_tensor("dbg_" + nm, list(shp), F32, kind="ExternalOutput").ap()

    def sb(name, shape, dt):
        return nc.sbuf_tensor(name, list(shape), dt).__enter__()

    def psum(name, shape, dt):
        return nc.psum_tensor(name, list(shape), dt).__enter__()

    w_in_bf = sb("w_in_bf", [128, KD, NW], BF16)
    w_out_bf = sb("w_out_bf", [128, KD, D], BF16)
    cst = sb("cst", [128, 512], F32)
    identf, triLE, triGT, onesf = cst[:, 0:128], cst[:, 128:256], cst[:, 256:384], cst[:, 384:512]
    cstb = sb("cstb", [128, 256], BF16)
    identb, onesb = cstb[:, 0:128], cstb[:, 128:256]
    maskb = sb("maskb", [128, 1024], BF16)
    cdiag = sb("cdiag", [128, 32, 128], BF16)
    bdiag = sb("bdiag", [128, 8, 128], BF16)
    rows = sb("rows", [128, 1568], F32)
    fnw32, ssmw16 = rows[:, 0:1024], rows[:, 1024:1536]
    dtb, aneg, dskip, esink = rows[:, 1536:1544], rows[:, 1544:1552], rows[:, 1552:1560], rows[:, 1560:1568]
    cols = sb("cols", [128, 80], F32)
    ccol, convw, convb = cols[:, 0:8], cols[:, 8:40], cols[:, 40:48]
    invf, flag, rsign = cols[:, 48:49], cols[:, 49:50], cols[:, 50:51]
    n1wc, n2wc = cols[:, 51:59], cols[:, 59:67]
    hmask = cols[:, 67:69]
    small = sb("small", [128, 256], F32)
    g1c, g2c, sh1c, sh2c = small[:, 0:8], small[:, 8:16], small[:, 16:24], small[:, 24:32]
    nhalf = small[:, 32:33]
    ss1 = small[:, 40:44]
    rs1 = small[:, 44:48]
    ssg = small[:, 48:50]
    rsg = small[:, 50:52]
    den = small[:, 56:64]
    rden = small[:, 64:72]
    bias_fm = small[:, 80:100]
    bias_gu = small[:, 100:144]
    dtraw = small[:, 144:176]
    dtv = small[:, 176:208]
    av = small[:, 208:240]
    tmp32 = sb("tmp32", [128, 128], F32)
    exv = sb("exv", [128, NCH, 24], F32)
    wend = sb("wend", [128, NCH, 8], F32)
    bias_row = sb("bias_row", [1, NTM], BF16)
    browf = sb("browf", [1, 2, 512], F32)
    prevT = sb("prevT", [128, 512], F32)
    prevTb = sb("prevTb", [128, 512], BF16)
    x1buf = sb("x1buf", [128, NCH, D], F32)
    xstage = sb("xstage", [128, 2, D], F32)
    hT = sb("hT", [128, KD, T], BF16)
    wgus = sb("wgus", [128, 3, KD, 256], BF16)
    utail = sb("utail", [128, 8, 3], BF16)
    khalo = sb("khalo", [128, 2, 128], BF16)
    vhalo = sb("vhalo", [128, 130], BF16)
    ARENA = 61440
    arena = sb("arena", [128, ARENA // 2], BF16)

    class Lay:
        def __init__(self, base=0):
            self.off = base

        def take(self, shape, dt):
            nel = int(np.prod(shape))
            nb = nel * (4 if dt == F32 or dt == I32 else 2)
            nb_al = (nb + 63) // 64 * 64
            o = self.off
            self.off += nb_al
            assert self.off <= ARENA, (self.off, ARENA)
            ap = arena[:, o // 2:(o + nb) // 2]
            if dt != BF16:
                ap = ap.bitcast(dt)
            if len(shape) == 2:
                return ap.rearrange("p (a b) -> p a b", a=shape[0])
            if len(shape) == 3:
                return ap.rearrange("p (a b c) -> p a b c", a=shape[0], b=shape[1])
            return ap

    L = Lay()
    qT = L.take([4, T], BF16)
    kT = L.take([2, 128 + T], BF16)
    kTz = L.take([2, 2, 128 + T], BF16)
    Vaug = L.take([5, 130], BF16)
    xdt = L.take([NCH, 512], BF16)
    xdtd = L.take([NCH, 512], BF16)
    xsD = L.take([NCH, 512], BF16)
    Btm = L.take([NCH, 256], BF16)
    BCT = L.take([4, T], BF16)
    sz = L.take([NCH, 512], BF16)
    shared_end = L.off
    LA = Lay(shared_end)
    uT = LA.take([8, T + 3], BF16)
    cosT = LA.take([T], F32)
    sinT = LA.take([T], F32)
    rt1 = LA.take([T], F32)
    rt2 = LA.take([T], F32)
    xs2 = LA.take([2, 512], F32)
    xn = LA.take([D], BF16)
    posi = LA.take([T], I32)
    LB = Lay(shared_end)
    PT = LB.take([4, 512], BF16)
    aTri = LB.take([1024], F32)
    Ebuf = LB.take([1024], F32)
    MT = LB.take([1024], BF16)
    CBm = LB.take([256], F32)
    yt = LB.take([512], F32)
    xnB = LB.take([D], BF16)
    sqj = LB.take([D], BF16)
    ytm = LB.take([D], BF16)
    jnk = LB.take([256], BF16)
    LC = Lay()
    actT = LC.take([NFF, T], BF16)
    wdns = LC.take([2, NFF, 256], BF16)
    xnC = LC.take([D], BF16)
    sg = LC.take([2, T], BF16)
    xn4C = arena[:, 49152 // 2:(49152 + NCH * D * 2) // 2].rearrange("p (c d) -> p c d", c=NCH)
    assert LC.off <= 49152
    LS = Lay()
    stg = LS.take([2, KD * 512], F32)
    cvo = LS.take([2, KD * 512], BF16)
    scb = LS.take([KD, 128], F32)
    rowst = LS.take([1568], F32)
    mod_lo = x1buf[:].rearrange("p a b -> p (a b)")
    mod_hi = hT[:].rearrange("p a b -> p (a b)").bitcast(F32)

    def modbc(c0, c1):
        if c1 <= 4096:
            return mod_lo[:, c0:c1]
        assert c0 >= 4096
        return mod_hi[:, c0 - 4096:c1 - 4096]

    TP = psum("TP", [128, 8, 128], BF16)
    PA = [psum(f"PA{i}", [128, 512], F32) for i in range(4)]
    PC = [psum(f"PC{i}", [128, 512], F32) for i in range(2)]
    PE0 = psum("PE0", [128, 512], F32)

    def pk(name):
        return [name]

    em = Emitter(nc)
    EO = em.op

    def dump(name, ap, reads):
        if name in dbg_out:
            em.dma(dbg_out[name], ap, reads=reads, writes=["dbg_" + name])

    em.dma(cst[:, 0:384], cpack[:, 0:384], writes=["cst"])
    em.dma(cols[:], colpack, writes=["cols"])
    em.dma(rowst[:], rowpack, writes=["rowst"])
    EO("dve", lambda e: e.memset(onesf, 1.0), writes=["onesf"])
    EO("dve", lambda e: e.memset(onesb, 1.0), writes=["onesb"])
    EO("dve", lambda e: e.memset(nhalf, -0.5), writes=["nhalf"])
    EO("dve", lambda e: e.tensor_copy(out=identb, in_=identf), reads=["cst"], writes=["identb"])
    em.dma(stg[:, 0, 0:1024], cpack[:, 384:1408], writes=["stg0"])
    EO("dve", lambda e: e.tensor_copy(out=maskb[:], in_=stg[:, 0, 0:1024]), reads=["stg0"], writes=["maskb"])
    EO("dve", lambda e: e.tensor_scalar(out=fnw32, in0=rowst[:, 0:1024], scalar1=32.0, scalar2=None, op0=ALU.mult),
       reads=["rowst"], writes=["fnw32"])
    EO("dve", lambda e: e.tensor_scalar(out=ssmw16, in0=rowst[:, 1024:1536], scalar1=16.0, scalar2=None, op0=ALU.mult),
       reads=["rowst"], writes=["ssmw16"])
    EO("dve", lambda e: e.tensor_copy(out=rows[:, 1536:1544], in_=rowst[:, 1536:1544]), reads=["rowst"], writes=["dtb"])
    EO("dve", lambda e: e.tensor_copy(out=dskip, in_=rowst[:, 1552:1560]), reads=["rowst"], writes=["dskip"])
    EO("act", lambda e: e.activation(out=aneg, in_=rowst[:, 1544:1552], func=AF.Exp), reads=["rowst"], writes=["aneg0"])
    EO("dve", lambda e: e.tensor_scalar(out=aneg, in0=aneg, scalar1=-1.0, scalar2=None, op0=ALU.mult),
       reads=["aneg0"], writes=["aneg"])
    EO("act", lambda e: e.activation(out=esink, in_=rowst[:, 1560:1568], func=AF.Exp), reads=["rowst"], writes=["esink"])
    EO("act", lambda e: e.activation(out=small[:, 240:248], in_=ccol, func=AF.Silu), reads=["cols"], writes=["sc"])
    EO("dve", lambda e: e.tensor_copy(out=scb[:], in_=bcl(small[:, 240:248], 128)), reads=["sc"], writes=["scb"])
    w_ada_v = w_ada.rearrange("(k p) c -> p k c", p=128)
    scbb = wgus[:].rearrange("p a k c -> p (a k c)")[:, 4096:5120].rearrange("p (k f) -> p k f", k=KD)
    shb = wgus[:].rearrange("p a k c -> p (a k c)")[:, 5120:5136]
    plainb = wgus[:].rearrange("p a k c -> p (a k c)")[:, 0:4096]
    EO("dve", lambda e: e.tensor_copy(out=scbb, in_=bcl(small[:, 240:248], 128)), reads=["sc"], writes=["scbb"])
    for cg in range(12):
        s = cg % 2
        em.dma(stg[:, s, :].rearrange("p (k c) -> p k c", k=KD), w_ada_v[:, :, cg * 512:(cg + 1) * 512], writes=[f"stg{s}"])
        em.dma(xstage[:, s, 0:512], bada[:, cg * 512:(cg + 1) * 512], writes=[f"xst{s}"])
        EO("act", lambda e, s=s: e.activation(out=cvo[:, s, 0:2048], in_=stg[:, s, 0:2048], func=AF.Copy), reads=[f"stg{s}"], writes=[f"cvo{s}a"])
        EO("dve", lambda e, s=s: e.tensor_copy(out=cvo[:, s, 2048:4096], in_=stg[:, s, 2048:4096]), reads=[f"stg{s}"], writes=[f"cvo{s}b"])
        bank = PA[cg % 4]

        def mmod(e, s=s, bank=bank):
            last = None
            for k in range(KD):
                last = e.matmul(bank[:], lhsT=scbb[:, k, :], rhs=cvo[:, s, k * 512:(k + 1) * 512], start=(k == 0), stop=(k == KD - 1))
            return last
        EO("pe", mmod, reads=["scbb", f"cvo{s}a", f"cvo{s}b"], writes=pk(f"PA{cg % 4}"))
        EO("dve", lambda e, s=s, bank=bank, cg=cg: e.tensor_tensor(out=modbc(cg * 512, (cg + 1) * 512), in0=bank[:],
                                                                 in1=xstage[:, s, 0:512], op=ALU.add),
           reads=pk(f"PA{cg % 4}") + [f"xst{s}"], writes=[f"mod{cg}"])
    modkeys = [f"mod{i}" for i in range(12)]

    def diag_extract(dst, c0, key):
        EO("dve", lambda e: e.tensor_tensor(out=stg[:, 0, 0:1024].rearrange("p (k f) -> p k f", k=KD),
                                            in0=modbc(c0, c0 + 1024).rearrange("p (k f) -> p k f", k=KD),
                                            in1=bcm(identf, KD), op=ALU.mult), reads=modkeys + ["cst", "stg0"], writes=["stg0"])
        EO("dve", lambda e: e.tensor_reduce(out=dst, in_=stg[:, 0, 0:1024].rearrange("p (k f) -> p k f", k=KD),
                                            axis=AX.X, op=ALU.add), reads=["stg0"], writes=[key])
    diag_extract(sh1c, 0, "sh1c")
    diag_extract(g1c, 1024, "g1c0")
    diag_extract(sh2c, 3072, "sh2c")
    diag_extract(g2c, 4096, "g2c0")
    EO("dve", lambda e: e.tensor_copy(out=shb[:, 0:8], in_=sh1c), reads=["sh1c"], writes=["shb1"])
    EO("dve", lambda e: e.tensor_copy(out=shb[:, 8:16], in_=sh2c), reads=["sh2c"], writes=["shb2"])
    for gc, nw, k0, k1 in ((g1c, n1wc, "g1c0", "g1c"), (g2c, n2wc, "g2c0", "g2c")):
        EO("dve", lambda e, gc=gc, nw=nw: e.scalar_tensor_tensor(out=gc, in0=gc, scalar=1.0, in1=nw, op0=ALU.add, op1=ALU.mult),
           reads=[k0, "cols"], writes=[k0 + "x"])
        EO("dve", lambda e, gc=gc: e.tensor_scalar(out=gc, in0=gc, scalar1=32.0, scalar2=None, op0=ALU.mult),
           reads=[k0 + "x"], writes=[k1])

    w_in_v = w_in.rearrange("(k p) c -> p k c", p=128)
    pieces = [(i * 512, 512) for i in range(5)] + [(CV, 136), (CZ, 512)]
    cvt_rr = 0
    for pi, (c0, w) in enumerate(pieces):
        s = pi % 2
        sv = stg[:, s, 0:KD * w].rearrange("p (k c) -> p k c", k=KD)
        em.dma(sv, w_in_v[:, :, c0:c0 + w], writes=[f"stg{s}"])
        pv = plainb[:, 0:KD * w].rearrange("p (k c) -> p k c", k=KD)
        EO("act", lambda e, sv=sv, pv=pv: e.activation(out=pv[:, 0:4, :], in_=sv[:, 0:4, :], func=AF.Copy), reads=[f"stg{s}", "plainb"], writes=["plainb_a"])
        EO("dve", lambda e, sv=sv, pv=pv: e.tensor_copy(out=pv[:, 4:8, :], in_=sv[:, 4:8, :]), reads=[f"stg{s}", "plainb"], writes=["plainb_b"])
        if c0 < CV:
            rb = pi % 2

            def mbr(e, pv=pv):
                last = None
                for k in range(KD):
                    last = e.matmul(PC[0][0:1, 0:512], lhsT=shb[:, k:k + 1], rhs=pv[:, k, :], start=(k == 0), stop=(k == KD - 1))
                return last
            EO("pe", mbr, reads=["plainb_a", "plainb_b", "shb1"], writes=["PC0", "plainb"])
            EO("dve", lambda e, rb=rb: e.tensor_copy(out=browf[0:1, rb, :], in_=PC[0][0:1, 0:512]), reads=["PC0"], writes=[f"browf{rb}"])

            def mbt(e, rb=rb, c0=c0):
                last = None
                for m in range(4):
                    mi = c0 // 128 + m
                    last = e.matmul(PE0[:, mi:mi + 1], lhsT=browf[0:1, rb, m * 128:(m + 1) * 128], rhs=onesf[0:1, 0:1], start=True, stop=True)
                return last
            EO("pe", mbt, reads=[f"browf{rb}", "onesf"], writes=["PE0"])
        else:
            o0 = c0 - CV
            bank = PC[0] if c0 == CV else PC[1]

            def mb2(e, pv=pv, w=w, bank=bank):
                last = None
                for k in range(KD):
                    last = e.matmul(bank[0:1, 0:w], lhsT=shb[:, k:k + 1], rhs=pv[:, k, :], start=(k == 0), stop=(k == KD - 1))
                return last
            bk = "PC0" if c0 == CV else "PC1"
            EO("pe", mb2, reads=["plainb_a", "plainb_b", "shb1"], writes=pk(bk) + ["plainb"])
            EO("dve", lambda e, w=w, bank=bank, o0=o0: e.tensor_copy(out=bias_row[0:1, o0:o0 + w], in_=bank[0:1, 0:w]),
               reads=pk(bk), writes=[f"brow{o0}"])
        for k in range(KD):
            eng = ("act", "dve")[cvt_rr % 2]
            cvt_rr += 1
            if eng == "act":
                EO("act", lambda e, k=k, sv=sv, c0=c0, w=w: e.activation(out=w_in_bf[:, k, c0:c0 + w], in_=sv[:, k, :], func=AF.Identity,
                                                                     scale=g1c[:, k:k + 1]),
                   reads=[f"stg{s}", "g1c"], writes=[f"win{pi}_{k}"])
            else:
                EO(eng, lambda e, k=k, sv=sv, c0=c0, w=w: e.tensor_scalar(out=w_in_bf[:, k, c0:c0 + w], in0=sv[:, k, :],
                                                                       scalar1=g1c[:, k:k + 1], scalar2=None, op0=ALU.mult),
                   reads=[f"stg{s}", "g1c"], writes=[f"win{pi}_{k}"])
    EO("dve", lambda e: e.tensor_copy(out=bias_fm, in_=PE0[:, 0:NFM]), reads=["PE0"], writes=["bias_fm"])

    w_out_v = w_out.rearrange("(k p) c -> p k c", p=128)
    for hh in range(2):
        s = hh
        sv = stg[:, s, :].rearrange("p (k c) -> p k c", k=KD)
        em.dma(sv, w_out_v[:, :, hh * 512:(hh + 1) * 512], writes=[f"stg{s}"])
        EO(("dve", "pool")[hh], lambda e, sv=sv, hh=hh: e.tensor_tensor(out=w_out_bf[:, :, hh * 512:(hh + 1) * 512], in0=sv,
                                                                      in1=bcm(modbc(2048 + hh * 512, 2048 + (hh + 1) * 512), KD), op=ALU.mult),
           reads=[f"stg{s}"] + modkeys, writes=[f"wout{hh}"])

    for j in range(8):
        for k in range(4):
            EO("dve", lambda e, j=j, k=k: e.tensor_scalar(out=cdiag[:, j * 4 + k, :], in0=identf,
                                                                                 scalar1=convw[:, j * 4 + k:j * 4 + k + 1], scalar2=None, op0=ALU.mult),
               reads=["cst", "cols"], writes=[f"cdiag{j}_{k}"])
        EO("dve", lambda e, j=j: e.tensor_scalar(out=bdiag[:, j, :], in0=identf, scalar1=convb[:, j:j + 1], scalar2=None, op0=ALU.mult),
           reads=["cst", "cols"], writes=[f"bdiag{j}"])

    w_gu_v = w_gu.rearrange("(k p) c -> p k c", p=128)
    def gu_load(pc):
        s = pc % 2
        sv = stg[:, s, :].rearrange("p (k c) -> p k c", k=KD)
        em.dma(sv[:, :, 0:256], w_gu_v[:, :, pc * 256:(pc + 1) * 256], writes=[f"stg{s}"])
        em.dma(sv[:, :, 256:512], w_gu_v[:, :, DFF + pc * 256:DFF + (pc + 1) * 256], writes=[f"stg{s}"])
    gu_load(0)
    for pc in range(11):
        s = pc % 2
        sv = stg[:, s, :].rearrange("p (k c) -> p k c", k=KD)
        if pc + 1 < 11:
            gu_load(pc + 1)
        rb = pc % 2
        pv = plainb.rearrange("p (k c) -> p k c", k=KD)
        EO("act", lambda e, sv=sv, pv=pv: e.activation(out=pv[:, 0:4, :], in_=sv[:, 0:4, :], func=AF.Copy), reads=[f"stg{s}", "plainb"], writes=["plainb_a"])
        EO("dve", lambda e, sv=sv, pv=pv: e.tensor_copy(out=pv[:, 4:8, :], in_=sv[:, 4:8, :]), reads=[f"stg{s}", "plainb"], writes=["plainb_b"])

        def mbr3(e, pv=pv):
            last = None
            for k in range(KD):
                last = e.matmul(PC[0][0:1, 0:512], lhsT=shb[:, 8 + k:9 + k], rhs=pv[:, k, :], start=(k == 0), stop=(k == KD - 1))
            return last
        EO("pe", mbr3, reads=["plainb_a", "plainb_b", "shb2"], writes=["PC0", "plainb"])
        EO("dve", lambda e, rb=rb: e.tensor_copy(out=browf[0:1, rb, :], in_=PC[0][0:1, 0:512]), reads=["PC0"], writes=[f"browf{rb}"])

        def mbt3(e, rb=rb, pc=pc):
            last = None
            for m in range(4):
                ffc = 2 * pc + (m % 2)
                bcol = ffc if m < 2 else NFF + ffc
                last = e.matmul(PE0[:, 64 + bcol:64 + bcol + 1], lhsT=browf[0:1, rb, m * 128:(m + 1) * 128], rhs=onesf[0:1, 0:1], start=True, stop=True)
            return last
        EO("pe", mbt3, reads=[f"browf{rb}", "onesf"], writes=["PE0"])
        cv5 = cvo[:, s, :].rearrange("p (h k t c) -> p h k t c", h=2, k=KD, t=2)
        for k in range(KD):
            eng = ("act", "dve")[cvt_rr % 2]
            cvt_rr += 1
            src4 = sv[:, k, :].rearrange("p (t h c) -> p h t c", t=2, h=2)
            dst4 = cv5[:, :, k, :, :]
            if eng == "act":
                EO("act", lambda e, k=k, src4=src4, dst4=dst4: e.activation(out=dst4, in_=src4, func=AF.Identity, scale=g2c[:, k:k + 1]),
                   reads=[f"stg{s}", "g2c"], writes=[f"cvo{s}_{k}"])
            else:
                EO(eng, lambda e, k=k, src4=src4, dst4=dst4: e.tensor_scalar(out=dst4, in0=src4, scalar1=g2c[:, k:k + 1], scalar2=None, op0=ALU.mult),
                   reads=[f"stg{s}", "g2c"], writes=[f"cvo{s}_{k}"])
        for half in range(2):
            ffc = 2 * pc + half
            em.dma(wgu_scr[ffc], cvo[:, s, half * 2048:(half + 1) * 2048], reads=[f"cvo{s}_{k}" for k in range(KD)], writes=[f"wguscr{ffc}g"])
    EO("dve", lambda e: e.tensor_copy(out=bias_gu, in_=PE0[:, 64:64 + 2 * NFF]), reads=["PE0"], writes=["bias_gu"])
    w_dn_v = w_dn.rearrange("(f p) c -> p f c", p=128)
    dn_pieces = [(dq, fh) for dq in range(4) for fh in range(2)]

    def dn_load(i):
        dq, fh = dn_pieces[i]
        s = i % 2
        sv = stg[:, s, 0:11 * 256].rearrange("p (f c) -> p f c", f=11)
        em.dma(sv, w_dn_v[:, fh * 11:(fh + 1) * 11, dq * 256:(dq + 1) * 256], writes=[f"stg{s}"])
    dn_load(0)
    for i, (dq, fh) in enumerate(dn_pieces):
        s = i % 2
        sv = stg[:, s, 0:11 * 256].rearrange("p (f c) -> p f c", f=11)
        cv = cvo[:, s, 0:11 * 256].rearrange("p (f c) -> p f c", f=11)
        if i + 1 < len(dn_pieces):
            dn_load(i + 1)
        EO(("dve", "pool")[i % 2], lambda e, sv=sv, cv=cv, dq=dq: e.tensor_tensor(
            out=cv, in0=sv, in1=bcm(modbc(5120 + dq * 256, 5120 + (dq + 1) * 256), 11), op=ALU.mult),
           reads=[f"stg{s}"] + modkeys, writes=[f"cvo{s}"] + [f"cvo{s}_{k}" for k in range(KD)])
        dst = wdn_scr[dq].rearrange("p (f c) -> p f c", f=NFF)
        em.dma(dst[:, fh * 11:(fh + 1) * 11, :], cv, reads=[f"cvo{s}"], writes=[f"wdnscr{dq}_{fh}"])
    em.barrier()

    EO("dve", lambda e: e.memset(prevT[:], 0.0), writes=["prevT"])
    EO("dve", lambda e: e.memset(prevTb[:], 0.0), writes=["prevTb"])
    EO("pool", lambda e: e.memset(utail[:], 0.0), writes=["utail"])
    EO("pool", lambda e: e.memset(khalo[:], 0.0), writes=["khalo"])
    EO("pool", lambda e: e.memset(vhalo[:], 0.0), writes=["vhalo"])

    SCRKEYS_W = [f"wguscr{f}{t}" for f in range(NFF) for t in "gu"] + [f"wdnscr{q}_{h}" for q in range(4) for h in range(2)]

    def rstd_from_ss(ssap, rsap, n_eps, rk, wk):
        EO("act", lambda e: e.activation(out=rsap, in_=ssap, func=AF.Ln, bias=float(n_eps)), reads=rk, writes=[wk + "t"])
        EO("act", lambda e: e.activation(out=rsap, in_=rsap, func=AF.Exp, scale=-0.5), reads=[wk + "t"], writes=[wk])

    def transpose_to_hT(src, src_keys, c, banks):
        for half in range(2):
            bank, bkey = banks[half]

            def tps(e, half=half, bank=bank):
                last = None
                for f4 in range(4):
                    f = half * 4 + f4
                    last = e.matmul(bank[:, f4 * 128:(f4 + 1) * 128], lhsT=src[:, f * 128:(f + 1) * 128], rhs=identb, start=True, stop=True)
                return last
            EO("pe", tps, reads=src_keys + ["identb"], writes=[bkey])
            dst = hT[:, half * 4:(half + 1) * 4, c * 128:(c + 1) * 128]
            srcv = bank[:].rearrange("p (f t) -> p f t", f=4)
            if half == 0:
                EO("act", lambda e, dst=dst, srcv=srcv: e.activation(out=dst, in_=srcv, func=AF.Copy), reads=[bkey], writes=[f"hT{c}"])
            else:
                EO("dve", lambda e, dst=dst, srcv=srcv: e.tensor_copy(out=dst, in_=srcv), reads=[bkey], writes=[f"hT{c}"])

    S1_BANKS = ((PA[0], "PA0"), (PA[1], "PA1"))

    def norm_to_hT(xsrc, xkeys, xnbuf, c, sskey):
        EO("act", lambda e: e.activation(out=xnbuf[:], in_=xsrc, func=AF.Identity, scale=rs1[:, c:c + 1]),
           reads=xkeys + [sskey, "xn"], writes=["xn"])
        transpose_to_hT(xnbuf, ["xn"], c, S1_BANKS)

    def emit_s1(si, xsrc_dram, main):
        tok0 = si * T
        for c in range(NCH):
            em.dma(xstage[:, c % 2, :], xsrc_dram[tok0 + c * 128: tok0 + (c + 1) * 128, :], writes=[f"xst{c % 2}"])
            EO("act", lambda e, c=c: e.activation(out=xn[:], in_=xstage[:, c % 2, :], func=AF.Square, accum_out=ss1[:, c:c + 1]),
               reads=[f"xst{c % 2}"], writes=["xn", f"ss1_{c}"])
            if c == 1 and main:
                em.dma(x1buf[:], xsrc_dram[tok0:tok0 + T, :].rearrange("(c p) d -> p c d", p=128), writes=[f"x1_{cc}" for cc in range(NCH)])
            if c % 2 == 1:
                cc0 = c - 1
                rstd_from_ss(ss1[:, cc0:c + 1], rs1[:, cc0:c + 1], D * EPS, [f"ss1_{cc0}", f"ss1_{c}"], f"rs1_{cc0}")
                for cc in (cc0, c):
                    norm_to_hT(xstage[:, cc % 2, :], [f"xst{cc % 2}"], xn, cc, f"rs1_{cc0}")

    xn4 = x1buf[:].rearrange("p a b -> p (a b)").bitcast(BF16)[:, 0:NCH * D].rearrange("p (c d) -> p c d", c=NCH)

    def emit_s1a(si, xsrc_dram, junk=None, buf=None):
        junk = xn if junk is None else junk
        buf = xn4 if buf is None else buf
        tok0 = si * T
        for c in range(NCH):
            em.dma(xstage[:, c % 2, :], xsrc_dram[tok0 + c * 128: tok0 + (c + 1) * 128, :], writes=[f"xst{c % 2}"])
            EO("dve", lambda e, c=c: e.scalar_tensor_tensor(out=junk[:], in0=xstage[:, c % 2, :], scalar=1.0, in1=xstage[:, c % 2, :],
                                                           op0=ALU.mult, op1=ALU.mult, accum_out=ss1[:, c:c + 1]),
               reads=[f"xst{c % 2}", "xn"], writes=["xn", f"ss1_{c}"])
            if c % 2 == 1:
                cc0 = c - 1
                rstd_from_ss(ss1[:, cc0:c + 1], rs1[:, cc0:c + 1], D * EPS, [f"ss1_{cc0}", f"ss1_{c}"], f"rs1_{cc0}")
                for cc in (cc0, c):
                    EO("dve", lambda e, cc=cc: e.tensor_scalar(out=buf[:, cc, :], in0=xstage[:, cc % 2, :], scalar1=rs1[:, cc:cc + 1], scalar2=None,
                                                              op0=ALU.mult), reads=[f"xst{cc % 2}", f"rs1_{cc0}"], writes=[f"xn4_{cc}"])

    def emit_s1b(buf=None, banks=None, alias_x1=True):
        buf = xn4 if buf is None else buf
        banks = S1_BANKS if banks is None else banks
        for c in range(NCH):
            keys = [f"xn4_{c}"] + ([f"x1_{c // 2}"] if alias_x1 else [])
            transpose_to_hT(buf[:, c, :], keys, c, banks)

    def emit_sc(kind, si, xsrc_dram, pos0, s1_done=False, next_s1=None, next_s1a=None, next_s1b=None):
        main = kind == "main"
        last_pre = kind == "prelast"
        need_k = main or last_pre
        tok0 = si * T
        hTk = [f"hT{c}" for c in range(NCH)]
        if not s1_done:
            emit_s1(si, xsrc_dram, main)
        elif main and si > 0:
            em.dma(x1buf[:], xsrc_dram[tok0:tok0 + T, :].rearrange("(c p) d -> p c d", p=128), writes=[f"x1_{cc}" for cc in range(NCH)])
        if stop == "A1":
            em.barrier()
            return
        EO("pool", lambda e: e.tensor_copy(out=uT[:, :, 0:3], in_=utail[:]), reads=["utail"], writes=["uTtail"])
        if main:
            EO("pool", lambda e: e.tensor_copy(out=kT[:, :, 0:128], in_=khalo[:]), reads=["khalo"], writes=["kThalo"])
            EO("pool", lambda e: e.memset(Vaug[:], 1.0), writes=["Vaug_all"] + [f"Vaug{i}" for i in range(5)])
            EO("pool", lambda e: e.tensor_copy(out=Vaug[:, 0, :], in_=vhalo[:]), reads=["vhalo", "Vaug_all"], writes=["Vaug0"])
            if si == 0:
                EO("dve", lambda e: e.tensor_scalar(out=uT[:, :, 0:3], in0=uT[:, :, 0:3], scalar1=flag, scalar2=None, op0=ALU.mult),
                   reads=["uTtail", "cols"], writes=["uTtail"])
                EO("dve", lambda e: e.tensor_scalar(out=Vaug[:, 0, :], in0=Vaug[:, 0, :], scalar1=flag, scalar2=None, op0=ALU.mult),
                   reads=["Vaug0", "cols"], writes=["Vaug0"])
                EO("dve", lambda e: e.tensor_scalar(out=prevT[:], in0=prevT[:], scalar1=flag, scalar2=None, op0=ALU.mult),
                   reads=["prevT", "cols"], writes=["prevT"])
                EO("dve", lambda e: e.tensor_copy(out=prevTb[:], in_=prevT[:]), reads=["prevT"], writes=["prevTb"])
        else:
            EO("pool", lambda e: e.memset(Vaug[:], 1.0), writes=["Vaug_all"] + [f"Vaug{i}" for i in range(5)])
        if need_k:
            em.dma(posi[:], posrep[:, pos0:pos0 + T], writes=["posi"])
        if stop == "A2":
            em.barrier()
            return

        def fm_group(m, bank, bkey):
            def f(e):
                last = None
                for k in range(KD):
                    last = e.matmul(bank[:], lhsT=w_in_bf[:, k, m * 128:(m + 1) * 128], rhs=hT[:, k, :], start=(k == 0), stop=(k == KD - 1))
                return last
            EO("pe", f, reads=hTk, writes=pk(bkey))

        def rope_pair(m_plain, m_sw, dst, dkey, par):
            b0, b1 = PA[2 * par], PA[2 * par + 1]
            fm_group(m_plain, b0, f"PA{2 * par}")
            fm_group(m_sw, b1, f"PA{2 * par + 1}")
            EO("dve", lambda e: e.scalar_tensor_tensor(out=rt1[:], in0=b0[:], scalar=bias_fm[:, m_plain:m_plain + 1], in1=cosT[:],
                                                      op0=ALU.add, op1=ALU.mult), reads=pk(f"PA{2 * par}") + ["cT", "rt1"], writes=["rt1"])
            EO("dve", lambda e: e.scalar_tensor_tensor(out=rt2[:], in0=b1[:], scalar=bias_fm[:, m_sw:m_sw + 1], in1=sinT[:],
                                                      op0=ALU.add, op1=ALU.mult), reads=pk(f"PA{2 * par + 1}") + ["sT", "rt2"], writes=["rt2"])
            EO("pool", lambda e: e.tensor_tensor(out=dst, in0=rt1[:], in1=rt2[:], op=ALU.add), reads=["rt1", "rt2"], writes=[dkey])

        nxc = 8 if (main or last_pre) else 6
        xbanks = ((PC[0], "PC0"), (PC[1], "PC1"), (PE0, "PE0"))

        def xgroup(j):
            m = 12 + j
            xb, xk = xbanks[j % 3]
            fm_group(m, xb, xk)
            EO("act", lambda e: e.activation(out=uT[:, j, 3:3 + T], in_=xb[:], func=AF.Identity, bias=bias_fm[:, m:m + 1]),
               reads=[xk], writes=[f"uT{j}"])
        pairs = []
        if main:
            pairs += [(j, 4 + j, qT[:, j, :], f"qT{j}") for j in range(4)]
        if need_k:
            pairs += [(8 + g, 10 + g, kT[:, g, 128:128 + T], f"kT{g}") for g in range(2)]
        for j in range(nxc):
            xgroup(j)
        if need_k:
            EO("dve", lambda e: e.tensor_copy(out=rt1[:], in_=posi[:]), reads=["posi"], writes=["rt1"])
            EO("dve", lambda e: e.tensor_scalar(out=rt1[:], in0=rt1[:], scalar1=invf, scalar2=None, op0=ALU.mult),
               reads=["rt1", "cols"], writes=["rt1"])
            for which, dst, shift in (("s", sinT, 0.0), ("c", cosT, float(np.pi / 2))):
                EO("dve", lambda e, shift=shift: e.tensor_scalar(out=rt2[:], in0=rt1[:], scalar1=shift, scalar2=float(1.0 / (2 * np.pi)),
                                                                 op0=ALU.add, op1=ALU.mult), reads=["rt1", "rt2"], writes=["rt2"])
                EO("dve", lambda e: e.tensor_copy(out=posi[:], in_=rt2[:]), reads=["rt2", "posi"], writes=["posi"])
                EO("dve", lambda e: e.tensor_copy(out=rt2[:], in_=posi[:]), reads=["posi"], writes=["rt2"])
                EO("dve", lambda e: e.tensor_scalar(out=rt2[:], in0=rt2[:], scalar1=float(-2 * np.pi), scalar2=None, op0=ALU.mult),
                   reads=["rt2"], writes=["rt2"])
                EO("dve", lambda e, shift=shift, dst=dst: e.scalar_tensor_tensor(out=dst[:], in0=rt1[:], scalar=shift, in1=rt2[:], op0=ALU.add,
                                                                              op1=ALU.add) if False else
                   e.tensor_tensor(out=dst[:], in0=rt1[:], in1=rt2[:], op=ALU.add), reads=["rt1", "rt2"], writes=[which + "T"])
                if shift != 0.0:
                    EO("dve", lambda e, dst=dst, shift=shift: e.tensor_scalar(out=dst[:], in0=dst[:], scalar1=shift, scalar2=None, op0=ALU.add),
                       reads=[which + "T"], writes=[which + "T"])
                EO("dve", lambda e, dst=dst: e.tensor_scalar(out=rt2[:], in0=dst[:], scalar1=float(np.pi), scalar2=float(-2 * np.pi),
                                                             op0=ALU.is_gt, op1=ALU.mult), reads=[which + "T", "rt2"], writes=["rt2"])
                EO("dve", lambda e, dst=dst: e.tensor_tensor(out=dst[:], in0=dst[:], in1=rt2[:], op=ALU.add), reads=[which + "T", "rt2"],
                   writes=[which + "T"])
                EO("dve", lambda e, dst=dst: e.tensor_scalar(out=rt2[:], in0=dst[:], scalar1=float(-np.pi), scalar2=float(2 * np.pi),
                                                             op0=ALU.is_lt, op1=ALU.mult), reads=[which + "T", "rt2"], writes=["rt2"])
                EO("dve", lambda e, dst=dst: e.tensor_tensor(out=dst[:], in0=dst[:], in1=rt2[:], op=ALU.add), reads=[which + "T", "rt2"],
                   writes=[which + "T"])
                EO("act", lambda e, dst=dst: e.activation(out=dst[:], in_=dst[:], func=AF.Sin), reads=[which + "T"], writes=[which + "T"])
            EO("dve", lambda e: e.tensor_scalar(out=sinT[:], in0=sinT[:], scalar1=rsign, scalar2=None, op0=ALU.mult),
               reads=["sT", "cols"], writes=["sT"])
        par = 0
        for (mp, ms, dst, dkey) in pairs:
            rope_pair(mp, ms, dst, dkey, par)
            par ^= 1
        if need_k:
            EO("pool", lambda e: e.tensor_copy(out=khalo[:], in_=kT[:, :, T:T + 128]), reads=["kT0", "kT1"], writes=["khalo"])
            if main:
                for g in range(2):
                    for hp in range(2):
                        EO("dve", lambda e, g=g, hp=hp: e.tensor_scalar(out=kTz[:, g, hp, :], in0=kT[:, g, :], scalar1=hmask[:, hp:hp + 1],
                                                                         scalar2=None, op0=ALU.mult),
                           reads=[f"kT{g}", "kThalo", "cols"], writes=[f"kTz{g}"])
        uTk = [f"uT{j}" for j in range(nxc)] + ["uTtail"]
        EO("pool", lambda e: e.tensor_copy(out=utail[:], in_=uT[:, :, T:T + 3]), reads=uTk, writes=["utail"])
        if stop == "A3":
            em.barrier()
            return
        for c in range(NCH):
            def fvd(e, c=c):
                for k in range(KD):
                    e.matmul(PE0[:, 0:136], lhsT=hT[:, k, c * 128:(c + 1) * 128], rhs=w_in_bf[:, k, CV:CV + 136], start=(k == 0), stop=False)
                return e.matmul(PE0[:, 0:136], lhsT=onesb[0:1, :], rhs=bias_row[0:1, 0:136], start=False, stop=True)
            EO("pe", fvd, reads=[f"hT{c}"], writes=["PE0"])
            EO("act", lambda e, c=c: e.activation(out=Vaug[:, c + 1, :].rearrange("p (g d) -> p g d", g=2)[:, :, 0:64],
                                                  in_=PE0[:, 0:128].rearrange("p (g d) -> p g d", g=2), func=AF.Copy),
               reads=["PE0", "Vaug_all"], writes=[f"Vaug{c + 1}"])
            EO("dve", lambda e, c=c: e.tensor_tensor(out=dtraw[:, c * 8:(c + 1) * 8], in0=PE0[:, 128:136], in1=dtb, op=ALU.add),
               reads=["PE0", "dtb"], writes=[f"dtraw{c}"])
        EO("pool", lambda e: e.tensor_copy(out=vhalo[:], in_=Vaug[:, 4, :]), reads=["Vaug4"], writes=["vhalo"])
        if next_s1 is not None:
            next_s1()
        if main:
            for c in range(NCH):
                def fz(e, c=c):
                    for k in range(KD):
                        e.matmul(PA[c][:], lhsT=hT[:, k, c * 128:(c + 1) * 128], rhs=w_in_bf[:, k, CZ:CZ + 512], start=(k == 0), stop=False)
                    return e.matmul(PA[c][:], lhsT=onesb[0:1, :], rhs=bias_row[0:1, 136:648], start=False, stop=True)
                EO("pe", fz, reads=[f"hT{c}"], writes=[f"PA{c}"])
        dk = [f"dtraw{c}" for c in range(NCH)]
        EO("dve", lambda e: e.scalar_tensor_tensor(out=dtv, in0=dtraw, scalar=-1.0, in1=dtraw, op0=ALU.mult, op1=ALU.max), reads=dk, writes=["dtv"])
        EO("act", lambda e: e.activation(out=dtv, in_=dtv, func=AF.Exp, scale=-1.0), reads=["dtv"], writes=["dtv"])
        EO("act", lambda e: e.activation(out=dtv, in_=dtv, func=AF.Ln, bias=1.0), reads=["dtv"], writes=["dtv"])
        EO("dve", lambda e: e.scalar_tensor_tensor(out=dtv, in0=dtraw, scalar=0.0, in1=dtv, op0=ALU.max, op1=ALU.add),
           reads=dk + ["dtv"], writes=["dtv"])
        EO("dve", lambda e: e.tensor_tensor(out=av.rearrange("p (c h) -> p c h", c=NCH), in0=dtv.rearrange("p (c h) -> p c h", c=NCH),
                                            in1=bcm(aneg, NCH), op=ALU.mult), reads=["dtv", "aneg"], writes=["av"])
        def fsm(e):
            last = None
            for c in range(NCH):
                a_c = av[:, c * 8:(c + 1) * 8]
                o = 136 + c * 24
                e.matmul(PE0[:, o:o + 8], lhsT=triGT, rhs=a_c, start=True, stop=True)
                e.matmul(PE0[:, o + 8:o + 16], lhsT=triLE, rhs=a_c, start=True, stop=True)
                last = e.matmul(PE0[:, o + 16:o + 24], lhsT=onesf, rhs=a_c, start=True, stop=True)
            return last
        EO("pe", fsm, reads=["av", "cst", "onesf"], writes=["PE0"])
        EO("act", lambda e: e.activation(out=exv[:].rearrange("p c k -> p (c k)"), in_=PE0[:, 136:136 + NCH * 24], func=AF.Exp),
           reads=["PE0"], writes=[f"exv{c}" for c in range(NCH)])
        EO("dve", lambda e: e.tensor_tensor(out=wend[:], in0=dtv.rearrange("p (c h) -> p c h", c=NCH), in1=exv[:, :, 0:8], op=ALU.mult),
           reads=["dtv"] + [f"exv{c}" for c in range(NCH)], writes=[f"wend{c}" for c in range(NCH)])
        if main:
            for c in range(NCH):
                EO("act", lambda e, c=c: e.activation(out=sz[:, c, :], in_=PA[c][:], func=AF.Silu), reads=[f"PA{c}"], writes=[f"sz{c}"])
        if next_s1a is not None:
            next_s1a()
        for c in range(NCH):
            def fcx(e, c=c):
                last = None
                for j in range(4):
                    for k in range(4):
                        e.matmul(PC[1][:, j * 128:(j + 1) * 128], lhsT=uT[:, j, c * 128 + k:c * 128 + k + 128], rhs=cdiag[:, j * 4 + k, :],
                                 start=(k == 0), stop=False)
                    last = e.matmul(PC[1][:, j * 128:(j + 1) * 128], lhsT=onesb, rhs=bdiag[:, j, :], start=False, stop=True)
                return last
            EO("pe", fcx, reads=uTk, writes=pk("PC1"))
            xs = xs2[:, c % 2, :]
            xsk = f"xs{c % 2}"
            EO("act", lambda e, xs=xs: e.activation(out=xs, in_=PC[1][:], func=AF.Silu), reads=pk("PC1") + [xsk], writes=[xsk])
            xs3 = xs.rearrange("p (h d) -> p h d", h=8)
            EO("dve", lambda e, c=c, xs3=xs3: e.tensor_tensor(out=xdt[:, c, :].rearrange("p (h d) -> p h d", h=8), in0=xs3,
                                                     in1=bcl(dtv[:, c * 8:(c + 1) * 8], 64), op=ALU.mult), reads=[xsk, "dtv"], writes=[f"xdt{c}"])
            EO("dve", lambda e, c=c, xs3=xs3: e.tensor_tensor(out=xdtd[:, c, :].rearrange("p (h d) -> p h d", h=8), in0=xs3,
                                                     in1=bcl(wend[:, c, :], 64), op=ALU.mult), reads=[xsk, f"wend{c}"], writes=[f"xdtd{c}"])
            if main:
                EO("pool", lambda e, c=c, xs3=xs3: e.tensor_tensor(out=xsD[:, c, :].rearrange("p (h d) -> p h d", h=8), in0=xs3,
                                                          in1=bcl(dskip, 64), op=ALU.mult), reads=[xsk, "dskip"], writes=[f"xsD{c}"])

            def fcb(e, c=c):
                last = None
                for jj in range(2):
                    j = 4 + jj
                    for k in range(4):
                        e.matmul(PC[0][:, jj * 128:(jj + 1) * 128], lhsT=uT[:, j, c * 128 + k:c * 128 + k + 128], rhs=cdiag[:, j * 4 + k, :],
                                 start=(k == 0), stop=False)
                    last = e.matmul(PC[0][:, jj * 128:(jj + 1) * 128], lhsT=onesb, rhs=bdiag[:, j, :], start=False, stop=True)
                return last
            EO("pe", fcb, reads=uTk, writes=["PC0"])
            EO("act", lambda e, c=c: e.activation(out=Btm[:, c, :], in_=PC[0][:, 0:256], func=AF.Silu), reads=["PC0"], writes=[f"Btm{c}"])
        if main:
            for jj in range(4):
                j = 4 + jj

                def fcf(e, j=j, jj=jj):
                    last = None
                    for k in range(4):
                        last = e.matmul(PA[jj][:], lhsT=cdiag[:, j * 4 + k, :], rhs=uT[:, j, k:k + T], start=(k == 0), stop=(k == 3))
                    return last
                EO("pe", fcf, reads=uTk, writes=pk(f"PA{jj}"))
                EO("act", lambda e, j=j, jj=jj: e.activation(out=BCT[:, jj, :], in_=PA[jj][:], func=AF.Silu, bias=convb[:, j:j + 1]),
                   reads=pk(f"PA{jj}") + ["cols"], writes=[f"BCT{jj}"])
        if main:
            em.barrier()
        if stop == "A":
            return

        TPf = TP[:].rearrange("p a b -> p (a b)").bitcast(F32)

        def state_update(c, bank, bkey):
            def fst(e, c=c):
                last = None
                for g in range(2):
                    last = e.matmul(bank[:, g * 256:(g + 1) * 256], lhsT=Btm[:, c, g * 128:(g + 1) * 128], rhs=xdtd[:, c, g * 256:(g + 1) * 256],
                                    start=True, stop=True)
                return last
            EO("pe", fst, reads=[f"Btm{c}", f"xdtd{c}"], writes=[bkey])
            EO("dve", lambda e, c=c: e.tensor_tensor(out=prevT[:].rearrange("p (h d) -> p h d", h=8), in0=prevT[:].rearrange("p (h d) -> p h d", h=8),
                                                     in1=bcl(exv[:, c, 16:24], 64), op=ALU.mult), reads=["prevT", f"exv{c}"], writes=["prevT"])
            EO("dve", lambda e: e.tensor_tensor(out=prevT[:], in0=bank, in1=prevT[:], op=ALU.add), reads=[bkey, "prevT"], writes=["prevT"])
            EO("act", lambda e: e.activation(out=prevTb[:], in_=prevT[:], func=AF.Copy), reads=["prevT"], writes=["prevTb"])

        if not main:
            for c in range(NCH):
                state_update(c, PC[1][:], "PC1")
            if next_s1b is not None:
                next_s1b()
            return

        def E1(c):
            a_c = av[:, c * 8:(c + 1) * 8]
            EO("dve", lambda e, a_c=a_c: e.tensor_tensor(out=aTri[:].rearrange("p (h l) -> p h l", h=8), in0=bcm(triLE, 8), in1=bcl(a_c, 128),
                                                         op=ALU.mult), reads=["av", "cst"], writes=["aTri"])

            def scores(g):
                for blk in range(2):
                    bank = PA[blk]
                    kc0 = c * 128 + blk * 128

                    def fsc(e, g=g, blk=blk, bank=bank, kc0=kc0):
                        moff = 0 if blk == 1 else 512
                        last = None
                        for hp in range(2):
                            for jj in range(2):
                                r0 = (hp * 2 + jj) * 128
                                e.matmul(bank[:, r0:r0 + 128], lhsT=identb, rhs=maskb[:, moff:moff + 128], start=True, stop=False)
                                last = e.matmul(bank[:, r0:r0 + 128], lhsT=kTz[:, g, hp, kc0:kc0 + 128],
                                                rhs=qT[:, 2 * g + jj, c * 128:(c + 1) * 128], start=False, stop=True)
                        return last
                    EO("pe", fsc, reads=[f"qT{2 * g}", f"qT{2 * g + 1}", f"kTz{g}", "maskb", "identb"], writes=[f"PA{blk}"])
                    EO("act", lambda e, g=g, blk=blk, bank=bank: e.activation(out=PT[:, 2 * g + blk, :], in_=bank[:], func=AF.Exp, scale=0.125),
                       reads=[f"PA{blk}"], writes=[f"PT{2 * g + blk}"])
            scores(0)
            for hh in range(2):
                EO("pe", lambda e, hh=hh: e.matmul(PA[2 + hh][:], lhsT=triGT, rhs=aTri[:, hh * 512:(hh + 1) * 512], start=True, stop=True),
                   reads=["aTri", "cst"], writes=[f"PA{2 + hh}"])
                EO("act", lambda e, hh=hh: e.activation(out=Ebuf[:, hh * 512:(hh + 1) * 512], in_=PA[2 + hh][:], func=AF.Exp),
                   reads=[f"PA{2 + hh}"], writes=[f"E{hh}"])
            scores(1)

            def fcbm(e):
                last = None
                for g in range(2):
                    last = e.matmul(PE0[:, 256 + g * 128:256 + (g + 1) * 128], lhsT=BCT[:, g, c * 128:(c + 1) * 128],
                                    rhs=BCT[:, 2 + g, c * 128:(c + 1) * 128], start=True, stop=True)
                return last
            EO("pe", fcbm, reads=[f"BCT{i}" for i in range(4)], writes=["PE0"])

        def E1b(c):
            EO("dve", lambda e: e.tensor_tensor(out=CBm[:].rearrange("p (g l) -> p g l", g=2),
                                                in0=PE0[:, 256:512].rearrange("p (g l) -> p g l", g=2), in1=bcm(triLE, 2), op=ALU.mult),
               reads=["PE0", "cst"], writes=["CBm"])
            EO("dve", lambda e: e.tensor_tensor(out=MT[:].rearrange("p (g j l) -> p g j l", g=2, j=4),
                                                in0=Ebuf[:].rearrange("p (g j l) -> p g j l", g=2, j=4),
                                                in1=CBm[:].rearrange("p (g l) -> p g l", g=2).unsqueeze(2).to_broadcast([128, 2, 4, 128]),
                                                op=ALU.mult), reads=["E0", "E1", "CBm"], writes=["MT"])

        def E2a(c):
            for g in range(2):
                def fpv(e, g=g):
                    last = None
                    for i in range(4):
                        hp, jj = i % 2, i // 2
                        cb = hp * 256 + jj * 128
                        e.matmul(PC[g][:, i * 128:i * 128 + 65], lhsT=PT[:, 2 * g, cb:cb + 128], rhs=Vaug[:, c, g * 65:(g + 1) * 65],
                                 start=True, stop=False)
                        last = e.matmul(PC[g][:, i * 128:i * 128 + 65], lhsT=PT[:, 2 * g + 1, cb:cb + 128], rhs=Vaug[:, c + 1, g * 65:(g + 1) * 65],
                                        start=False, stop=True)
                    return last
                EO("pe", fpv, reads=[f"PT{2 * g}", f"PT{2 * g + 1}", f"Vaug{c}", f"Vaug{c + 1}", "Vaug_all"], writes=[f"PC{g}"])
                o3 = PC[g][:, :].rearrange("p (i d) -> p i d", i=4)
                EO("dve", lambda e, g=g, o3=o3: e.tensor_tensor(out=den[:, g * 4:(g + 1) * 4], in0=o3[:, :, 64], in1=esink[:, g * 4:(g + 1) * 4],
                                                              op=ALU.add), reads=[f"PC{g}", "esink"], writes=[f"den{g}"])
                EO("dve", lambda e, g=g: e.reciprocal(out=rden[:, g * 4:(g + 1) * 4], in_=den[:, g * 4:(g + 1) * 4]), reads=[f"den{g}"],
                   writes=[f"rden{g}"])
                EO("dve", lambda e, g=g, o3=o3: e.tensor_tensor(out=ytm[:, g * 256:(g + 1) * 256].rearrange("p (i d) -> p i d", i=4),
                                                              in0=o3[:, :, 0:64], in1=bcl(rden[:, g * 4:(g + 1) * 4], 64), op=ALU.mult),
                   reads=[f"PC{g}", f"rden{g}"], writes=[f"ytm_a{g}"])

            def fyd(e):
                last = None
                for h in range(8):
                    e.matmul(PC[0][:, h * 64:(h + 1) * 64], lhsT=identb, rhs=xsD[:, c, h * 64:(h + 1) * 64], start=True, stop=False)
                    last = e.matmul(PC[0][:, h * 64:(h + 1) * 64], lhsT=MT[:, h * 128:(h + 1) * 128], rhs=xdt[:, c, h * 64:(h + 1) * 64],
                                    start=False, stop=True)
                return last
            EO("pe", fyd, reads=["MT", f"xdt{c}", f"xsD{c}", "identb"], writes=["PC0"])

            def fyo(e):
                last = None
                for g in range(2):
                    last = e.matmul(PC[1][:, g * 256:(g + 1) * 256], lhsT=BCT[:, 2 + g, c * 128:(c + 1) * 128],
                                    rhs=prevTb[:, g * 256:(g + 1) * 256], start=True, stop=True)
                return last
            EO("pe", fyo, reads=["BCT2", "BCT3", "prevTb"], writes=["PC1"])
            state_update(c, TPf, "TP")

        def E2b(c):
            EO("dve", lambda e: e.tensor_tensor(out=yt[:].rearrange("p (h d) -> p h d", h=8),
                                                in0=PC[1][:].rearrange("p (h d) -> p h d", h=8), in1=bcl(exv[:, c, 8:16], 64), op=ALU.mult),
               reads=["PC1", f"exv{c}"], writes=["yt"])
            EO("dve", lambda e: e.tensor_tensor(out=yt[:], in0=PC[0][:], in1=yt[:], op=ALU.add), reads=["PC0", "yt"], writes=["yt"])
            EO("dve", lambda e: e.tensor_tensor(out=yt[:], in0=yt[:], in1=sz[:, c, :], op=ALU.mult), reads=["yt", f"sz{c}"], writes=["yt"])
            for g in range(2):
                EO("dve", lambda e, g=g: e.scalar_tensor_tensor(out=sqj[:, g * 256:(g + 1) * 256], in0=yt[:, g * 256:(g + 1) * 256], scalar=1.0,
                                                               in1=yt[:, g * 256:(g + 1) * 256], op0=ALU.mult, op1=ALU.mult,
                                                               accum_out=ssg[:, g:g + 1]),
                   reads=["yt", "sqj"], writes=["sqj", f"ssg{g}"])
            rstd_from_ss(ssg, rsg, 256 * EPS, ["ssg0", "ssg1"], "rsg")
            for g in range(2):
                EO("dve", lambda e, g=g: e.scalar_tensor_tensor(out=ytm[:, 512 + g * 256:512 + (g + 1) * 256], in0=yt[:, g * 256:(g + 1) * 256],
                                                               scalar=rsg[:, g:g + 1], in1=ssmw16[:, g * 256:(g + 1) * 256], op0=ALU.mult,
                                                               op1=ALU.mult), reads=["yt", "rsg", "ssmw16"], writes=[f"ytm_s{g}"])

        def E2c(c):
            def tpy(e):
                last = None
                for f in range(KD):
                    last = e.transpose(out=TP[:, f, :], in_=ytm[:, f * 128:(f + 1) * 128], identity=identb)
                return last
            EO("pe", tpy, reads=["ytm_a0", "ytm_a1", "ytm_s0", "ytm_s1", "identb"], writes=["TP"])
            EO("act", lambda e: e.activation(out=hT[:, :, c * 128:(c + 1) * 128], in_=TP[:], func=AF.Copy), reads=["TP"], writes=[f"hT{c}"])

        E1(0)
        E1b(0)
        for c in range(NCH):
            E2a(c)
            if c + 1 < NCH:
                E1(c + 1)
            E2b(c)
            if c + 1 < NCH:
                E1b(c + 1)
            E2c(c)
        if not main:
            return
        if stop == "B":
            em.barrier()
            return
        def s8a(c):
            EO("act", lambda e: e.activation(out=sqj[:], in_=x1buf[:, c, :], func=AF.Square, accum_out=ss1[:, c:c + 1]),
               reads=[f"x1_{c}", "sqj"], writes=["sqj", f"ss1_{c}"])
            rstd_from_ss(ss1[:, c:c + 1], rs1[:, c:c + 1], D * EPS, [f"ss1_{c}"], f"rs8_{c}")
            EO("act", lambda e: e.activation(out=xnB[:], in_=x1buf[:, c, :], func=AF.Identity, scale=rs1[:, c:c + 1]),
               reads=[f"x1_{c}", f"rs8_{c}", "xnB"], writes=["xnB"])

        def s8b(c):
            transpose_to_hT(xnB, ["xnB"], c, ((PC[0], "PC0"), (PC[1], "PC1")))

        for p in range(3):
            em.dma(wgus[:, p, :, :], wgu_scr[p].rearrange("p (k c) -> p k c", k=KD), writes=[f"wgus{p}"])
        for c in range(NCH):
            for dh in range(2):
                bi = (2 * c + dh) % 4

                def fop(e, c=c, dh=dh, bi=bi):
                    last = None
                    for k in range(KD):
                        last = e.matmul(PA[bi][:], lhsT=hT[:, k, c * 128:(c + 1) * 128], rhs=w_out_bf[:, k, dh * 512:(dh + 1) * 512],
                                        start=(k == 0), stop=(k == KD - 1))
                    return last
                EO("pe", fop, reads=[f"hT{c}"], writes=pk(f"PA{bi}"))
                EO("dve", lambda e, c=c, dh=dh, bi=bi: e.tensor_tensor(out=x1buf[:, c, dh * 512:(dh + 1) * 512], in0=PA[bi][:],
                                                                    in1=x1buf[:, c, dh * 512:(dh + 1) * 512], op=ALU.add),
                   reads=pk(f"PA{bi}") + [f"x1_{c}"], writes=[f"x1_{c}"])
            if c > 0:
                s8b(c - 1)
            s8a(c)
        s8b(NCH - 1)
        em.barrier()
        if stop == "OP":
            return
        for dq0 in range(2):
            em.dma(wdns[:, dq0, :, :], wdn_scr[dq0].rearrange("p (f c) -> p f c", f=NFF), writes=[f"wdns{dq0}"])
        for ffc in range(NFF):
            slot = ffc % 3
            bg, bu = PA[2 * (ffc % 2)], PA[2 * (ffc % 2) + 1]
            kg, ku = f"PA{2 * (ffc % 2)}", f"PA{2 * (ffc % 2) + 1}"

            def fup(e, slot=slot, bg=bg, bu=bu):
                last = None
                for t, bank in ((0, bg), (1, bu)):
                    for k in range(KD):
                        last = e.matmul(bank[:], lhsT=wgus[:, slot, k, t * 128:(t + 1) * 128], rhs=hT[:, k, :], start=(k == 0), stop=(k == KD - 1))
                return last
            EO("pe", fup, reads=hTk + [f"wgus{slot}"], writes=pk(kg) + pk(ku))
            if ffc + 3 < NFF:
                em.dma(wgus[:, slot, :, :], wgu_scr[ffc + 3].rearrange("p (k c) -> p k c", k=KD), writes=[f"wgus{slot}"])
            sgs = sg[:, ffc % 2, :]
            EO("act", lambda e, ffc=ffc, bg=bg, sgs=sgs: e.activation(out=sgs, in_=bg[:], func=AF.Silu, bias=bias_gu[:, ffc:ffc + 1]),
               reads=pk(kg) + [f"sg{ffc % 2}"], writes=[f"sg{ffc % 2}"])
            EO("dve", lambda e, ffc=ffc, bu=bu, sgs=sgs: e.scalar_tensor_tensor(out=actT[:, ffc, :], in0=bu[:], scalar=bias_gu[:, NFF + ffc:NFF + ffc + 1],
                                                                             in1=sgs, op0=ALU.add, op1=ALU.mult),
               reads=pk(ku) + [f"sg{ffc % 2}"], writes=[f"actT{ffc}"])
        if si + 1 < n_main:
            emit_s1a(si + 1, xown, junk=xnC, buf=xn4C)
        actk = [f"actT{f}" for f in range(NFF)]
        for dq in range(4):
            s = dq % 2
            if dq >= 2:
                em.dma(wdns[:, s, :, :], wdn_scr[dq].rearrange("p (f c) -> p f c", f=NFF), writes=[f"wdns{s}"])
            for c in range(NCH):
                reg = (dq * NCH + c) % 4
                bank = (PC[0], PC[1], PE0, PA[0])[reg][:, 0:256]
                bkey = ("PC0", "PC1", "PE0", "PA0")[reg]

                def fdn(e, s=s, c=c, bank=bank):
                    last = None
                    for f in range(NFF):
                        last = e.matmul(bank, lhsT=actT[:, f, c * 128:(c + 1) * 128], rhs=wdns[:, s, f, :], start=(f == 0), stop=(f == NFF - 1))
                    return last
                EO("pe", fdn, reads=actk + [f"wdns{s}"], writes=[bkey])
                EO("dve", lambda e, c=c, dq=dq, bank=bank: e.tensor_tensor(out=x1buf[:, c, dq * 256:(dq + 1) * 256], in0=bank,
                                                                        in1=x1buf[:, c, dq * 256:(dq + 1) * 256], op=ALU.add),
                   reads=[bkey, f"x1_{c}"], writes=[f"x1_{c}"])
                if dq == 3:
                    EO("act", lambda e, c=c: e.activation(out=xnC[:], in_=x1buf[:, c, :], func=AF.Square, accum_out=ss1[:, c:c + 1]),
                       reads=[f"x1_{c}", "xn"], writes=["xn", f"ss1_{c}"])
                    rstd_from_ss(ss1[:, c:c + 1], rs1[:, c:c + 1], D * EPS, [f"ss1_{c}"], f"rs1_fin{c}")
                    EO("dve", lambda e, c=c: e.scalar_tensor_tensor(out=x1buf[:, c, :], in0=x1buf[:, c, :], scalar=rs1[:, c:c + 1], in1=fnw32,
                                                                   op0=ALU.mult, op1=ALU.mult), reads=[f"x1_{c}", f"rs1_fin{c}", "fnw32"],
                       writes=[f"x1_{c}"])
                    em.dma(out[tok0 + c * 128:tok0 + (c + 1) * 128, :], x1buf[:, c, :], reads=[f"x1_{c}"], writes=[f"out{si}_{c}"])
        if si + 1 < n_main:
            emit_s1b(buf=xn4C, banks=((PA[2], "PA2"), (PA[3], "PA3")), alias_x1=False)
        em.barrier()

    seq = [("prelast" if si == NSC - 1 else "pre", si, xprev, 0) for si in range(NSC - n_pre, NSC)]
    seq += [("main", si, xown, T + si * T) for si in range(n_main)]
    for idx, (kind, si, src, pos0) in enumerate(seq):
        s1_done = idx > 0
        nxt = nxa = nxb = None
        if kind != "main" and idx + 1 < len(seq):
            nk, nsi, nsrc, _ = seq[idx + 1]
            if nk == "main":
                nxt = (lambda nsi=nsi, nsrc=nsrc: emit_s1(nsi, nsrc, True))
            else:
                nxa = (lambda nsi=nsi, nsrc=nsrc: emit_s1a(nsi, nsrc))
                nxb = emit_s1b
        emit_sc(kind, si, src, pos0, s1_done, nxt, nxa, nxb)
    em.barrier()
    stats = em.finalize()
    return nc, stats


def _gather_cols():
    idx = []
    idx += list(range(0, 512))
    idx += [64 * h + (d + 32) % 64 for h in range(8) for d in range(64)]
    for g in range(2):
        idx += [512 + 64 * g + d for d in range(64)] * 2
    for g in range(2):
        idx += [512 + 64 * g + (d + 32) % 64 for d in range(64)] * 2
    idx += list(range(1280, 2304))
    idx += list(range(640, 768))
    idx += list(range(2304, 2312))
    idx += list(range(768, 1280))
    assert len(idx) == NW
    return np.array(idx)


def _consts():
    p = np.arange(128)
    ident = (p[:, None] == p[None, :]).astype(np.float32)
    triLE = (p[:, None] <= p[None, :]).astype(np.float32)
    triGT = (p[:, None] > p[None, :]).astype(np.float32)
    mcur = np.where(p[:, None] <= p[None, :], 0.0, NEG).astype(np.float32)
    mprev = np.where(p[:, None] > p[None, :], 0.0, NEG).astype(np.float32)
    cp = np.concatenate([ident, triLE, triGT, np.tile(mcur, (1, 4)), np.tile(mprev, (1, 4))], axis=1)
    return np.ascontiguousarray(cp.astype(np.float32))


_PROG = {}


def kernel(x, c, positions, w_ada, b_ada, norm1_w, w_in, conv_w, conv_b, dt_bias, a_log, d_skip, attn_sinks,
           ssm_norm_w, w_out, norm2_w, w_gate_up, w_down, final_norm_w):
    f32 = np.float32
    x = np.asarray(x, f32)
    if "p" not in _PROG:
        _PROG["p"] = build_program()
    nc, stats = _PROG["p"]
    w_in_g = np.ascontiguousarray(np.asarray(w_in, f32)[0][:, _gather_cols()])
    cpack = _consts()
    half = 32
    inv_freq = (10000.0 ** (-np.arange(half, dtype=np.float32) / np.float32(half))).astype(f32)
    p = np.arange(128)
    rows = np.concatenate([np.asarray(final_norm_w, f32), np.asarray(ssm_norm_w, f32)[0], np.asarray(dt_bias, f32)[0],
                           np.asarray(a_log, f32)[0], np.asarray(d_skip, f32)[0], np.asarray(attn_sinks, f32)[0]])
    rowpack = np.ascontiguousarray(np.tile(rows[None, :], (128, 1)))
    badap = np.ascontiguousarray(np.tile(np.asarray(b_ada, f32)[0][None, :], (128, 1)))
    in_maps = []
    for i in range(8):
        b, hf = i // 2, i % 2
        colp = np.zeros((128, 80), f32)
        colp[:, 0:8] = np.asarray(c, f32)[b].reshape(8, 128).T
        cw = np.asarray(conv_w, f32)[0]
        colp[:, 8:40] = cw.reshape(4, 8, 128).transpose(2, 1, 0).reshape(128, 32)
        colp[:, 40:48] = np.asarray(conv_b, f32)[0].reshape(8, 128).T
        colp[:, 48] = inv_freq[p % 32]
        colp[:, 49] = float(hf)
        colp[:, 50] = np.where((p % 64) < 32, -1.0, 1.0)
        colp[:, 51:59] = np.asarray(norm1_w, f32)[0].reshape(8, 128).T
        colp[:, 59:67] = np.asarray(norm2_w, f32)[0].reshape(8, 128).T
        colp[:, 67] = (p < 64).astype(f32)
        colp[:, 68] = (p >= 64).astype(f32)
        pos_b = np.asarray(positions)[b].astype(np.int32)
        pos_cat = np.concatenate([pos_b[SEQ_HALF - T:SEQ_HALF], pos_b[hf * SEQ_HALF:(hf + 1) * SEQ_HALF]])
        in_maps.append({
            "xprev": np.ascontiguousarray(x[b, 0:SEQ_HALF]),
            "xown": np.ascontiguousarray(x[b, hf * SEQ_HALF:(hf + 1) * SEQ_HALF]),
            "posrep": np.ascontiguousarray(np.tile(pos_cat[None, :], (128, 1))),
            "colpack": colp, "rowpack": rowpack, "bada": badap, "cpack": cpack,
            "w_ada": np.ascontiguousarray(np.asarray(w_ada, f32)[0]), "w_in": w_in_g,
            "w_out": np.ascontiguousarray(np.asarray(w_out, f32)[0]),
            "w_gu": np.ascontiguousarray(np.asarray(w_gate_up, f32)[0]),
            "w_dn": np.ascontiguousarray(np.asarray(w_down, f32)[0]),
        })
    res = run_bass_kernel_spmd(nc, in_maps, core_ids=list(range(8)))
    outp = np.empty((4, 2 * SEQ_HALF, D), f32)
    for i in range(8):
        b, hf = i // 2, i % 2
        outp[b, hf * SEQ_HALF:(hf + 1) * SEQ_HALF] = res.results[i]["out"]
    return outp
```

```python
import numpy as np
import concourse.bass as bass
import concourse.mybir as mybir
from concourse.bass_utils import run_bass_kernel_spmd

F32 = mybir.dt.float32
BF16 = mybir.dt.bfloat16
I32 = mybir.dt.int32
AF = mybir.ActivationFunctionType
ALU = mybir.AluOpType
AX = mybir.AxisListType

EPOCH = 2000
N_DMA_SEMS = 12

D = 1024
KD = 8
SEQ_HALF = 4096
T = 512
NCH = 4
NSC = SEQ_HALF // T
DFF = 2816
NFF = 22
EPS = 1e-6
CQ, CQS, CK, CKS, CXC, CV, CDT, CZ, NW = 0, 512, 1024, 1280, 1536, 2560, 2688, 2696, 3208
NFM = 20
NTM = NW - CV
NEG = -30000.0


class _Op:
    __slots__ = ("eng", "fn", "reads", "writes", "dma", "deps", "signal", "barrier")

    def __init__(self, eng, fn, reads, writes, dma, barrier=False):
        self.eng, self.fn, self.reads, self.writes, self.dma = eng, fn, reads, writes, dma
        self.deps = ()
        self.signal = False
        self.barrier = barrier


class Emitter:
    def __init__(self, nc):
        self.nc = nc
        self.ops = []
        self.engines = {"pe": nc.tensor, "act": nc.scalar, "dve": nc.vector, "pool": nc.gpsimd, "sp": nc.sync}

    def op(self, eng, fn, reads=(), writes=()):
        self.ops.append(_Op(eng, fn, tuple(reads), tuple(writes), False))

    def dma(self, out, in_, reads=(), writes=(), eng="sp"):
        self.ops.append(_Op(eng, lambda e: e.dma_start(out=out, in_=in_), tuple(reads), tuple(writes), True))

    def barrier(self):
        self.ops.append(_Op("sp", lambda e: e.nop(), (), (), False, barrier=True))

    def finalize(self):
        nc = self.nc
        ops = self.ops
        n = len(ops)
        last_writer, readers = {}, {}
        last_on_eng = {}
        dma_since = []
        cur_barrier = None
        for i, o in enumerate(ops):
            deps = set()
            if o.barrier:
                deps.update(last_on_eng.values())
                deps.update(dma_since)
                dma_since = []
                last_writer, readers = {}, {}
            else:
                for r in o.reads:
                    w = last_writer.get(r)
                    if w is not None:
                        deps.add(w)
                for w_ in o.writes:
                    w = last_writer.get(w_)
                    if w is not None:
                        deps.add(w)
                    deps.update(readers.get(w_, ()))
                if o.eng == "pe":
                    deps = {d for d in deps if not (ops[d].eng == "pe" and not ops[d].dma)}
                if cur_barrier is not None:
                    deps.add(cur_barrier)
            deps.discard(i)
            o.deps = tuple(sorted(deps))
            for d in o.deps:
                ops[d].signal = True
            if o.barrier:
                cur_barrier = i
            for w_ in o.writes:
                last_writer[w_] = i
                readers[w_] = []
            for r in o.reads:
                if r not in o.writes:
                    readers.setdefault(r, []).append(i)
            last_on_eng[o.eng] = i
            if o.dma:
                dma_since.append(i)
        eng_count = {e: 0 for e in self.engines}
        eng_sems = {e: [] for e in self.engines}
        dma_sems = [nc.alloc_semaphore(name=f"dma{i}") for i in range(N_DMA_SEMS)]
        dma_val = [0] * N_DMA_SEMS
        rr = 0
        state = {e: {} for e in self.engines}
        sig = [None] * n
        clock = [None] * n
        nwaits = 0
        for i, o in enumerate(ops):
            E = self.engines[o.eng]
            st = state[o.eng]
            for d in o.deps:
                key, val, sem, semval = sig[d]
                if st.get(key, 0) >= val:
                    continue
                E.wait_ge(sem, semval)
                nwaits += 1
                for k2, v2 in clock[d].items():
                    if st.get(k2, 0) < v2:
                        st[k2] = v2
            if o.dma:
                k = rr
                rr = (rr + 1) % N_DMA_SEMS
                key = ("d", k)
                if st.get(key, 0) < dma_val[k]:
                    E.wait_ge(dma_sems[k], dma_val[k])
                    nwaits += 1
                    st[key] = dma_val[k]
                ins = o.fn(E)
                dma_val[k] += 16
                ins.then_inc(dma_sems[k], 16)
                sig[i] = (key, dma_val[k], dma_sems[k], dma_val[k])
                clk = dict(st)
                clk[key] = dma_val[k]
                clock[i] = clk
            else:
                ins = o.fn(E)
                if o.signal:
                    c = eng_count[o.eng]
                    ep, off = divmod(c, EPOCH)
                    if ep >= len(eng_sems[o.eng]):
                        eng_sems[o.eng].append(nc.alloc_semaphore(name=f"{o.eng}{ep}"))
                    sem = eng_sems[o.eng][ep]
                    ins.then_inc(sem, 1)
                    eng_count[o.eng] = c + 1
                    key = ("e", o.eng)
                    sig[i] = (key, c + 1, sem, off + 1)
                    clk = dict(st)
                    clk[key] = c + 1
                    clock[i] = clk
        return dict(nops=n, nwaits=nwaits, counts=dict(eng_count))


def bcl(ap, n):
    return ap.unsqueeze(2).to_broadcast([ap.shape[0], ap.shape[1], n])


def bcm(ap, n):
    return ap.unsqueeze(1).to_broadcast([ap.shape[0], n, ap.shape[1]])


def build_program(n_pre=NSC, n_main=NSC, dbg=None, stop=None):
    nc = bass.Bass("TRN2", target_bir_lowering=False)

    def din(name, shape, dt=F32):
        return nc.dram_tensor(name, list(shape), dt, kind="ExternalInput").ap()

    xprev = din("xprev", [SEQ_HALF, D])
    xown = din("xown", [SEQ_HALF, D])
    posrep = din("posrep", [128, T + SEQ_HALF], I32)
    colpack = din("colpack", [128, 80])
    rowpack = din("rowpack", [128, 1568])
    bada = din("bada", [128, 6144])
    cpack = din("cpack", [128, 1408])
    w_ada = din("w_ada", [D, 6144])
    w_in = din("w_in", [D, NW])
    w_out = din("w_out", [D, D])
    w_gu = din("w_gu", [D, 2 * DFF])
    w_dn = din("w_dn", [DFF, D])
    out = nc.dram_tensor("out", [SEQ_HALF, D], F32, kind="ExternalOutput").ap()
    wgu_scr = nc.dram_tensor("wgu_scr", [NFF, 128, KD * 256], BF16, kind="Internal").ap()
    wdn_scr = nc.dram_tensor("wdn_scr", [4, 128, NFF * 256], BF16, kind="Internal").ap()
    dbg_out = {}
    if dbg:
        for nm, shp in dbg.items():
            dbg_out[nm] = nc.dram_tensor("dbg_" + nm, list(shp), F32, kind="ExternalOutput").ap()

    def sb(name, shape, dt):
        return nc.sbuf_tensor(name, list(shape), dt).__enter__()

    def psum(name, shape, dt):
        return nc.psum_tensor(name, list(shape), dt).__enter__()

    w_in_bf = sb("w_in_bf", [128, KD, NW], BF16)
    w_out_bf = sb("w_out_bf", [128, KD, D], BF16)
    cst = sb("cst", [128, 512], F32)
    identf, triLE, triGT, onesf = cst[:, 0:128], cst[:, 128:256], cst[:, 256:384], cst[:, 384:512]
    cstb = sb("cstb", [128, 256], BF16)
    identb, onesb = cstb[:, 0:128], cstb[:, 128:256]
    maskb = sb("maskb", [128, 1024], BF16)
    cdiag = sb("cdiag", [128, 32, 128], BF16)
    bdiag = sb("bdiag", [128, 8, 128], BF16)
    rows = sb("rows", [128, 1568], F32)
    fnw32, ssmw16 = rows[:, 0:1024], rows[:, 1024:1536]
    dtb, aneg, dskip, esink = rows[:, 1536:1544], rows[:, 1544:1552], rows[:, 1552:1560], rows[:, 1560:1568]
    cols = sb("cols", [128, 80], F32)
    ccol, convw, convb = cols[:, 0:8], cols[:, 8:40], cols[:, 40:48]
    invf, flag, rsign = cols[:, 48:49], cols[:, 49:50], cols[:, 50:51]
    n1wc, n2wc = cols[:, 51:59], cols[:, 59:67]
    hmask = cols[:, 67:69]
    small = sb("small", [128, 256], F32)
    g1c, g2c, sh1c, sh2c = small[:, 0:8], small[:, 8:16], small[:, 16:24], small[:, 24:32]
    nhalf = small[:, 32:33]
    ss1 = small[:, 40:44]
    rs1 = small[:, 44:48]
    ssg = small[:, 48:50]
    rsg = small[:, 50:52]
    den = small[:, 56:64]
    rden = small[:, 64:72]
    bias_fm = small[:, 80:100]
    bias_gu = small[:, 100:144]
    dtraw = small[:, 144:176]
    dtv = small[:, 176:208]
    av = small[:, 208:240]
    tmp32 = sb("tmp32", [128, 128], F32)
    exv = sb("exv", [128, NCH, 24], F32)
    wend = sb("wend", [128, NCH, 8], F32)
    bias_row = sb("bias_row", [1, NTM], BF16)
    browf = sb("browf", [1, 2, 512], F32)
    prevT = sb("prevT", [128, 512], F32)
    prevTb = sb("prevTb", [128, 512], BF16)
    x1buf = sb("x1buf", [128, NCH, D], F32)
    xstage = sb("xstage", [128, 2, D], F32)
    hT = sb("hT", [128, KD, T], BF16)
    wgus = sb("wgus", [128, 3, KD, 256], BF16)
    utail = sb("utail", [128, 8, 3], BF16)
    khalo = sb("khalo", [128, 2, 128], BF16)
    vhalo = sb("vhalo", [128, 130], BF16)
    ARENA = 61440
    arena = sb("arena", [128, ARENA // 2], BF16)

    class Lay:
        def __init__(self, base=0):
            self.off = base

        def take(self, shape, dt):
            nel = int(np.prod(shape))
            nb = nel * (4 if dt == F32 or dt == I32 else 2)
            nb_al = (nb + 63) // 64 * 64
            o = self.off
            self.off += nb_al
            assert self.off <= ARENA, (self.off, ARENA)
            ap = arena[:, o // 2:(o + nb) // 2]
            if dt != BF16:
                ap = ap.bitcast(dt)
            if len(shape) == 2:
                return ap.rearrange("p (a b) -> p a b", a=shape[0])
            if len(shape) == 3:
                return ap.rearrange("p (a b c) -> p a b c", a=shape[0], b=shape[1])
            return ap

    L = Lay()
    qT = L.take([4, T], BF16)
    kT = L.take([2, 128 + T], BF16)
    kTz = L.take([2, 2, 128 + T], BF16)
    Vaug = L.take([5, 130], BF16)
    xdt = L.take([NCH, 512], BF16)
    xdtd = L.take([NCH, 512], BF16)
    xsD = L.take([NCH, 512], BF16)
    Btm = L.take([NCH, 256], BF16)
    BCT = L.take([4, T], BF16)
    sz = L.take([NCH, 512], BF16)
    shared_end = L.off
    LA = Lay(shared_end)
    uT = LA.take([8, T + 3], BF16)
    cosT = LA.take([T], F32)
    sinT = LA.take([T], F32)
    rt1 = LA.take([T], F32)
    rt2 = LA.take([T], F32)
    xs2 = LA.take([2, 512], F32)
    xn = LA.take([D], BF16)
    posi = LA.take([T], I32)
    LB = Lay(shared_end)
    PT = LB.take([4, 512], BF16)
    aTri = LB.take([1024], F32)
    Ebuf = LB.take([1024], F32)
    MT = LB.take([1024], BF16)
    CBm = LB.take([256], F32)
    yt = LB.take([512], F32)
    xnB = LB.take([D], BF16)
    sqj = LB.take([D], BF16)
    ytm = LB.take([D], BF16)
    jnk = LB.take([256], BF16)
    LC = Lay()
    actT = LC.take([NFF, T], BF16)
    wdns = LC.take([2, NFF, 256], BF16)
    xnC = LC.take([D], BF16)
    sg = LC.take([2, T], BF16)
    xn4C = arena[:, 49152 // 2:(49152 + NCH * D * 2) // 2].rearrange("p (c d) -> p c d", c=NCH)
    assert LC.off <= 49152
    LS = Lay()
    stg = LS.take([2, KD * 512], F32)
    cvo = LS.take([2, KD * 512], BF16)
    scb = LS.take([KD, 128], F32)
    rowst = LS.take([1568], F32)
    mod_lo = x1buf[:].rearrange("p a b -> p (a b)")
    mod_hi = hT[:].rearrange("p a b -> p (a b)").bitcast(F32)

    def modbc(c0, c1):
        if c1 <= 4096:
            return mod_lo[:, c0:c1]
        assert c0 >= 4096
        return mod_hi[:, c0 - 4096:c1 - 4096]

    TP = psum("TP", [128, 8, 128], BF16)
    TPf = TP[:].rearrange("p a b -> p (a b)").bitcast(F32)
    PA = [psum(f"PA{i}", [128, 512], F32) for i in range(4)]
    PC = [psum(f"PC{i}", [128, 512], F32) for i in range(2)]
    PE0 = psum("PE0", [128, 512], F32)

    def pk(name):
        return [name]

    em = Emitter(nc)
    EO = em.op

    def dump(name, ap, reads):
        if name in dbg_out:
            em.dma(dbg_out[name], ap, reads=reads, writes=["dbg_" + name])

    em.dma(cst[:, 0:384], cpack[:, 0:384], writes=["cst"])
    em.dma(cols[:], colpack, writes=["cols"])
    em.dma(rowst[:], rowpack, writes=["rowst"])
    EO("dve", lambda e: e.memset(onesf, 1.0), writes=["onesf"])
    EO("dve", lambda e: e.memset(onesb, 1.0), writes=["onesb"])
    EO("dve", lambda e: e.memset(nhalf, -0.5), writes=["nhalf"])
    EO("dve", lambda e: e.tensor_copy(out=identb, in_=identf), reads=["cst"], writes=["identb"])
    em.dma(stg[:, 0, 0:1024], cpack[:, 384:1408], writes=["stg0"])
    EO("dve", lambda e: e.tensor_copy(out=maskb[:], in_=stg[:, 0, 0:1024]), reads=["stg0"], writes=["maskb"])
    EO("dve", lambda e: e.tensor_scalar(out=fnw32, in0=rowst[:, 0:1024], scalar1=32.0, scalar2=None, op0=ALU.mult),
       reads=["rowst"], writes=["fnw32"])
    EO("dve", lambda e: e.tensor_scalar(out=ssmw16, in0=rowst[:, 1024:1536], scalar1=16.0, scalar2=None, op0=ALU.mult),
       reads=["rowst"], writes=["ssmw16"])
    EO("dve", lambda e: e.tensor_copy(out=rows[:, 1536:1544], in_=rowst[:, 1536:1544]), reads=["rowst"], writes=["dtb"])
    EO("dve", lambda e: e.tensor_copy(out=dskip, in_=rowst[:, 1552:1560]), reads=["rowst"], writes=["dskip"])
    EO("act", lambda e: e.activation(out=aneg, in_=rowst[:, 1544:1552], func=AF.Exp), reads=["rowst"], writes=["aneg0"])
    EO("dve", lambda e: e.tensor_scalar(out=aneg, in0=aneg, scalar1=-1.0, scalar2=None, op0=ALU.mult),
       reads=["aneg0"], writes=["aneg"])
    EO("act", lambda e: e.activation(out=esink, in_=rowst[:, 1560:1568], func=AF.Exp), reads=["rowst"], writes=["esink"])
    EO("act", lambda e: e.activation(out=small[:, 240:248], in_=ccol, func=AF.Silu), reads=["cols"], writes=["sc"])
    EO("dve", lambda e: e.tensor_copy(out=scb[:], in_=bcl(small[:, 240:248], 128)), reads=["sc"], writes=["scb"])
    w_ada_v = w_ada.rearrange("(k p) c -> p k c", p=128)
    scbb = wgus[:].rearrange("p a k c -> p (a k c)")[:, 4096:5120].rearrange("p (k f) -> p k f", k=KD)
    shb = wgus[:].rearrange("p a k c -> p (a k c)")[:, 5120:5136]
    plainb = wgus[:].rearrange("p a k c -> p (a k c)")[:, 0:4096]
    EO("dve", lambda e: e.tensor_copy(out=scbb, in_=bcl(small[:, 240:248], 128)), reads=["sc"], writes=["scbb"])
    for cg in range(12):
        s = cg % 2
        em.dma(stg[:, s, :].rearrange("p (k c) -> p k c", k=KD), w_ada_v[:, :, cg * 512:(cg + 1) * 512], writes=[f"stg{s}"])
        em.dma(xstage[:, s, 0:512], bada[:, cg * 512:(cg + 1) * 512], writes=[f"xst{s}"])
        EO("act", lambda e, s=s: e.activation(out=cvo[:, s, 0:2048], in_=stg[:, s, 0:2048], func=AF.Copy), reads=[f"stg{s}"], writes=[f"cvo{s}a"])
        EO("dve", lambda e, s=s: e.tensor_copy(out=cvo[:, s, 2048:4096], in_=stg[:, s, 2048:4096]), reads=[f"stg{s}"], writes=[f"cvo{s}b"])
        bank = PA[cg % 4]

        def mmod(e, s=s, bank=bank):
            last = None
            for k in range(KD):
                last = e.matmul(bank[:], lhsT=scbb[:, k, :], rhs=cvo[:, s, k * 512:(k + 1) * 512], start=(k == 0), stop=(k == KD - 1))
            return last
        EO("pe", mmod, reads=["scbb", f"cvo{s}a", f"cvo{s}b"], writes=pk(f"PA{cg % 4}"))
        EO("dve", lambda e, s=s, bank=bank, cg=cg: e.tensor_tensor(out=modbc(cg * 512, (cg + 1) * 512), in0=bank[:],
                                                                 in1=xstage[:, s, 0:512], op=ALU.add),
           reads=pk(f"PA{cg % 4}") + [f"xst{s}"], writes=[f"mod{cg}"])
    modkeys = [f"mod{i}" for i in range(12)]

    def diag_extract(dst, c0, key):
        EO("dve", lambda e: e.tensor_tensor(out=stg[:, 0, 0:1024].rearrange("p (k f) -> p k f", k=KD),
                                            in0=modbc(c0, c0 + 1024).rearrange("p (k f) -> p k f", k=KD),
                                            in1=bcm(identf, KD), op=ALU.mult), reads=modkeys + ["cst", "stg0"], writes=["stg0"])
        EO("dve", lambda e: e.tensor_reduce(out=dst, in_=stg[:, 0, 0:1024].rearrange("p (k f) -> p k f", k=KD),
                                            axis=AX.X, op=ALU.add), reads=["stg0"], writes=[key])
    diag_extract(sh1c, 0, "sh1c")
    diag_extract(g1c, 1024, "g1c0")
    diag_extract(sh2c, 3072, "sh2c")
    diag_extract(g2c, 4096, "g2c0")
    EO("dve", lambda e: e.tensor_copy(out=shb[:, 0:8], in_=sh1c), reads=["sh1c"], writes=["shb1"])
    EO("dve", lambda e: e.tensor_copy(out=shb[:, 8:16], in_=sh2c), reads=["sh2c"], writes=["shb2"])
    for gc, nw, k0, k1 in ((g1c, n1wc, "g1c0", "g1c"), (g2c, n2wc, "g2c0", "g2c")):
        EO("dve", lambda e, gc=gc, nw=nw: e.scalar_tensor_tensor(out=gc, in0=gc, scalar=1.0, in1=nw, op0=ALU.add, op1=ALU.mult),
           reads=[k0, "cols"], writes=[k0 + "x"])
        EO("dve", lambda e, gc=gc: e.tensor_scalar(out=gc, in0=gc, scalar1=32.0, scalar2=None, op0=ALU.mult),
           reads=[k0 + "x"], writes=[k1])

    w_in_v = w_in.rearrange("(k p) c -> p k c", p=128)
    pieces = [(i * 512, 512) for i in range(5)] + [(CV, 136), (CZ, 512)]
    cvt_rr = 0
    for pi, (c0, w) in enumerate(pieces):
        s = pi % 2
        sv = stg[:, s, 0:KD * w].rearrange("p (k c) -> p k c", k=KD)
        em.dma(sv, w_in_v[:, :, c0:c0 + w], writes=[f"stg{s}"])
        pv = plainb[:, 0:KD * w].rearrange("p (k c) -> p k c", k=KD)
        EO("act", lambda e, sv=sv, pv=pv: e.activation(out=pv[:, 0:4, :], in_=sv[:, 0:4, :], func=AF.Copy), reads=[f"stg{s}", "plainb"], writes=["plainb_a"])
        EO("dve", lambda e, sv=sv, pv=pv: e.tensor_copy(out=pv[:, 4:8, :], in_=sv[:, 4:8, :]), reads=[f"stg{s}", "plainb"], writes=["plainb_b"])
        if c0 < CV:
            rb = pi % 2

            def mbr(e, pv=pv):
                last = None
                for k in range(KD):
                    last = e.matmul(PC[0][0:1, 0:512], lhsT=shb[:, k:k + 1], rhs=pv[:, k, :], start=(k == 0), stop=(k == KD - 1))
                return last
            EO("pe", mbr, reads=["plainb_a", "plainb_b", "shb1"], writes=["PC0", "plainb"])
            EO("dve", lambda e, rb=rb: e.tensor_copy(out=browf[0:1, rb, :], in_=PC[0][0:1, 0:512]), reads=["PC0"], writes=[f"browf{rb}"])

            def mbt(e, rb=rb, c0=c0):
                last = None
                for m in range(4):
                    mi = c0 // 128 + m
                    last = e.matmul(PE0[:, mi:mi + 1], lhsT=browf[0:1, rb, m * 128:(m + 1) * 128], rhs=onesf[0:1, 0:1], start=True, stop=True)
                return last
            EO("pe", mbt, reads=[f"browf{rb}", "onesf"], writes=["PE0"])
        else:
            o0 = c0 - CV
            bank = PC[0] if c0 == CV else PC[1]

            def mb2(e, pv=pv, w=w, bank=bank):
                last = None
                for k in range(KD):
                    last = e.matmul(bank[0:1, 0:w], lhsT=shb[:, k:k + 1], rhs=pv[:, k, :], start=(k == 0), stop=(k == KD - 1))
                return last
            bk = "PC0" if c0 == CV else "PC1"
            EO("pe", mb2, reads=["plainb_a", "plainb_b", "shb1"], writes=pk(bk) + ["plainb"])
            EO("dve", lambda e, w=w, bank=bank, o0=o0: e.tensor_copy(out=bias_row[0:1, o0:o0 + w], in_=bank[0:1, 0:w]),
               reads=pk(bk), writes=[f"brow{o0}"])
        for k in range(KD):
            eng = ("act", "dve")[cvt_rr % 2]
            cvt_rr += 1
            if eng == "act":
                EO("act", lambda e, k=k, sv=sv, c0=c0, w=w: e.activation(out=w_in_bf[:, k, c0:c0 + w], in_=sv[:, k, :], func=AF.Identity,
                                                                     scale=g1c[:, k:k + 1]),
                   reads=[f"stg{s}", "g1c"], writes=[f"win{pi}_{k}"])
            else:
                EO(eng, lambda e, k=k, sv=sv, c0=c0, w=w: e.tensor_scalar(out=w_in_bf[:, k, c0:c0 + w], in0=sv[:, k, :],
                                                                       scalar1=g1c[:, k:k + 1], scalar2=None, op0=ALU.mult),
                   reads=[f"stg{s}", "g1c"], writes=[f"win{pi}_{k}"])
    EO("dve", lambda e: e.tensor_copy(out=bias_fm, in_=PE0[:, 0:NFM]), reads=["PE0"], writes=["bias_fm"])

    w_out_v = w_out.rearrange("(k p) c -> p k c", p=128)
    for hh in range(2):
        s = hh
        sv = stg[:, s, :].rearrange("p (k c) -> p k c", k=KD)
        em.dma(sv, w_out_v[:, :, hh * 512:(hh + 1) * 512], writes=[f"stg{s}"])
        EO(("dve", "pool")[hh], lambda e, sv=sv, hh=hh: e.tensor_tensor(out=w_out_bf[:, :, hh * 512:(hh + 1) * 512], in0=sv,
                                                                      in1=bcm(modbc(2048 + hh * 512, 2048 + (hh + 1) * 512), KD), op=ALU.mult),
           reads=[f"stg{s}"] + modkeys, writes=[f"wout{hh}"])

    for j in range(8):
        for k in range(4):
            EO("dve", lambda e, j=j, k=k: e.tensor_scalar(out=cdiag[:, j * 4 + k, :], in0=identf,
                                                                                 scalar1=convw[:, j * 4 + k:j * 4 + k + 1], scalar2=None, op0=ALU.mult),
               reads=["cst", "cols"], writes=[f"cdiag{j}_{k}"])
        EO("dve", lambda e, j=j: e.tensor_scalar(out=bdiag[:, j, :], in0=identf, scalar1=convb[:, j:j + 1], scalar2=None, op0=ALU.mult),
           reads=["cst", "cols"], writes=[f"bdiag{j}"])

    w_gu_v = w_gu.rearrange("(k p) c -> p k c", p=128)
    def gu_load(pc):
        s = pc % 2
        sv = stg[:, s, :].rearrange("p (k c) -> p k c", k=KD)
        em.dma(sv[:, :, 0:256], w_gu_v[:, :, pc * 256:(pc + 1) * 256], writes=[f"stg{s}"])
        em.dma(sv[:, :, 256:512], w_gu_v[:, :, DFF + pc * 256:DFF + (pc + 1) * 256], writes=[f"stg{s}"])
    gu_load(0)
    for pc in range(11):
        s = pc % 2
        sv = stg[:, s, :].rearrange("p (k c) -> p k c", k=KD)
        if pc + 1 < 11:
            gu_load(pc + 1)
        rb = pc % 2
        pv = plainb.rearrange("p (k c) -> p k c", k=KD)
        EO("act", lambda e, sv=sv, pv=pv: e.activation(out=pv[:, 0:4, :], in_=sv[:, 0:4, :], func=AF.Copy), reads=[f"stg{s}", "plainb"], writes=["plainb_a"])
        EO("dve", lambda e, sv=sv, pv=pv: e.tensor_copy(out=pv[:, 4:8, :], in_=sv[:, 4:8, :]), reads=[f"stg{s}", "plainb"], writes=["plainb_b"])

        def mbr3(e, pv=pv):
            last = None
            for k in range(KD):
                last = e.matmul(PC[0][0:1, 0:512], lhsT=shb[:, 8 + k:9 + k], rhs=pv[:, k, :], start=(k == 0), stop=(k == KD - 1))
            return last
        EO("pe", mbr3, reads=["plainb_a", "plainb_b", "shb2"], writes=["PC0", "plainb"])
        EO("dve", lambda e, rb=rb: e.tensor_copy(out=browf[0:1, rb, :], in_=PC[0][0:1, 0:512]), reads=["PC0"], writes=[f"browf{rb}"])

        def mbt3(e, rb=rb, pc=pc):
            last = None
            for m in range(4):
                ffc = 2 * pc + (m % 2)
                bcol = ffc if m < 2 else NFF + ffc
                last = e.matmul(PE0[:, 64 + bcol:64 + bcol + 1], lhsT=browf[0:1, rb, m * 128:(m + 1) * 128], rhs=onesf[0:1, 0:1], start=True, stop=True)
            return last
        EO("pe", mbt3, reads=[f"browf{rb}", "onesf"], writes=["PE0"])
        cv5 = cvo[:, s, :].rearrange("p (h k t c) -> p h k t c", h=2, k=KD, t=2)
        for k in range(KD):
            eng = ("act", "dve")[cvt_rr % 2]
            cvt_rr += 1
            src4 = sv[:, k, :].rearrange("p (t h c) -> p h t c", t=2, h=2)
            dst4 = cv5[:, :, k, :, :]
            if eng == "act":
                EO("act", lambda e, k=k, src4=src4, dst4=dst4: e.activation(out=dst4, in_=src4, func=AF.Identity, scale=g2c[:, k:k + 1]),
                   reads=[f"stg{s}", "g2c"], writes=[f"cvo{s}_{k}"])
            else:
                EO(eng, lambda e, k=k, src4=src4, dst4=dst4: e.tensor_scalar(out=dst4, in0=src4, scalar1=g2c[:, k:k + 1], scalar2=None, op0=ALU.mult),
                   reads=[f"stg{s}", "g2c"], writes=[f"cvo{s}_{k}"])
        for half in range(2):
            ffc = 2 * pc + half
            em.dma(wgu_scr[ffc], cvo[:, s, half * 2048:(half + 1) * 2048], reads=[f"cvo{s}_{k}" for k in range(KD)], writes=[f"wguscr{ffc}g"])
    EO("dve", lambda e: e.tensor_copy(out=bias_gu, in_=PE0[:, 64:64 + 2 * NFF]), reads=["PE0"], writes=["bias_gu"])
    w_dn_v = w_dn.rearrange("(f p) c -> p f c", p=128)
    dn_pieces = [(dq, fh) for dq in range(4) for fh in range(2)]

    def dn_load(i):
        dq, fh = dn_pieces[i]
        s = i % 2
        sv = stg[:, s, 0:11 * 256].rearrange("p (f c) -> p f c", f=11)
        em.dma(sv, w_dn_v[:, fh * 11:(fh + 1) * 11, dq * 256:(dq + 1) * 256], writes=[f"stg{s}"])
    dn_load(0)
    for i, (dq, fh) in enumerate(dn_pieces):
        s = i % 2
        sv = stg[:, s, 0:11 * 256].rearrange("p (f c) -> p f c", f=11)
        cv = cvo[:, s, 0:11 * 256].rearrange("p (f c) -> p f c", f=11)
        if i + 1 < len(dn_pieces):
            dn_load(i + 1)
        EO(("dve", "pool")[i % 2], lambda e, sv=sv, cv=cv, dq=dq: e.tensor_tensor(
            out=cv, in0=sv, in1=bcm(modbc(5120 + dq * 256, 5120 + (dq + 1) * 256), 11), op=ALU.mult),
           reads=[f"stg{s}"] + modkeys, writes=[f"cvo{s}"] + [f"cvo{s}_{k}" for k in range(KD)])
        dst = wdn_scr[dq].rearrange("p (f c) -> p f c", f=NFF)
        em.dma(dst[:, fh * 11:(fh + 1) * 11, :], cv, reads=[f"cvo{s}"], writes=[f"wdnscr{dq}_{fh}"])
    em.barrier()

    EO("dve", lambda e: e.memset(prevT[:], 0.0), writes=["prevT"])
    EO("dve", lambda e: e.memset(prevTb[:], 0.0), writes=["prevTb"])
    EO("pool", lambda e: e.memset(utail[:], 0.0), writes=["utail"])
    EO("pool", lambda e: e.memset(khalo[:], 0.0), writes=["khalo"])
    EO("pool", lambda e: e.memset(vhalo[:], 0.0), writes=["vhalo"])

    SCRKEYS_W = [f"wguscr{f}{t}" for f in range(NFF) for t in "gu"] + [f"wdnscr{q}_{h}" for q in range(4) for h in range(2)]

    def rstd_from_ss(ssap, rsap, n_eps, rk, wk):
        EO("act", lambda e: e.activation(out=rsap, in_=ssap, func=AF.Ln, bias=float(n_eps)), reads=rk, writes=[wk + "t"])
        EO("act", lambda e: e.activation(out=rsap, in_=rsap, func=AF.Exp, scale=-0.5), reads=[wk + "t"], writes=[wk])

    def transpose_to_hT(src, src_keys, c, banks):
        for half in range(2):
            bank, bkey = banks[(2 * c + half) % len(banks)]

            def tps(e, half=half, bank=bank):
                last = None
                for f4 in range(4):
                    f = half * 4 + f4
                    last = e.matmul(bank[:, f4 * 128:(f4 + 1) * 128], lhsT=src[:, f * 128:(f + 1) * 128], rhs=identb, start=True, stop=True)
                return last
            EO("pe", tps, reads=src_keys + ["identb"], writes=[bkey])
            dst = hT[:, half * 4:(half + 1) * 4, c * 128:(c + 1) * 128]
            srcv = bank[:].rearrange("p (f t) -> p f t", f=4)
            if half == 0:
                EO("act", lambda e, dst=dst, srcv=srcv: e.activation(out=dst, in_=srcv, func=AF.Copy), reads=[bkey], writes=[f"hT{c}"])
            else:
                EO("dve", lambda e, dst=dst, srcv=srcv: e.tensor_copy(out=dst, in_=srcv), reads=[bkey], writes=[f"hT{c}"])

    S1_BANKS = ((PA[0], "PA0"), (PA[1], "PA1"))

    def norm_to_hT(xsrc, xkeys, xnbuf, c, sskey):
        EO("act", lambda e: e.activation(out=xnbuf[:], in_=xsrc, func=AF.Identity, scale=rs1[:, c:c + 1]),
           reads=xkeys + [sskey, "xn"], writes=["xn"])
        transpose_to_hT(xnbuf, ["xn"], c, S1_BANKS)

    def emit_s1(si, xsrc_dram, main):
        tok0 = si * T
        for c in range(NCH):
            em.dma(xstage[:, c % 2, :], xsrc_dram[tok0 + c * 128: tok0 + (c + 1) * 128, :], writes=[f"xst{c % 2}"])
            EO("act", lambda e, c=c: e.activation(out=xn[:], in_=xstage[:, c % 2, :], func=AF.Square, accum_out=ss1[:, c:c + 1]),
               reads=[f"xst{c % 2}"], writes=["xn", f"ss1_{c}"])
            if c == 1 and main:
                em.dma(x1buf[:], xsrc_dram[tok0:tok0 + T, :].rearrange("(c p) d -> p c d", p=128), writes=[f"x1_{cc}" for cc in range(NCH)])
            if c % 2 == 1:
                cc0 = c - 1
                rstd_from_ss(ss1[:, cc0:c + 1], rs1[:, cc0:c + 1], D * EPS, [f"ss1_{cc0}", f"ss1_{c}"], f"rs1_{cc0}")
                for cc in (cc0, c):
                    norm_to_hT(xstage[:, cc % 2, :], [f"xst{cc % 2}"], xn, cc, f"rs1_{cc0}")

    xn4 = x1buf[:].rearrange("p a b -> p (a b)").bitcast(BF16)[:, 0:NCH * D].rearrange("p (c d) -> p c d", c=NCH)

    def emit_s1a(si, xsrc_dram, junk=None, buf=None):
        junk = xn if junk is None else junk
        buf = xn4 if buf is None else buf
        tok0 = si * T
        for c in range(NCH):
            em.dma(xstage[:, c % 2, :], xsrc_dram[tok0 + c * 128: tok0 + (c + 1) * 128, :], writes=[f"xst{c % 2}"])
            EO("dve", lambda e, c=c: e.scalar_tensor_tensor(out=junk[:], in0=xstage[:, c % 2, :], scalar=1.0, in1=xstage[:, c % 2, :],
                                                           op0=ALU.mult, op1=ALU.mult, accum_out=ss1[:, c:c + 1]),
               reads=[f"xst{c % 2}", "xn"], writes=["xn", f"ss1_{c}"])
            if c % 2 == 1:
                cc0 = c - 1
                rstd_from_ss(ss1[:, cc0:c + 1], rs1[:, cc0:c + 1], D * EPS, [f"ss1_{cc0}", f"ss1_{c}"], f"rs1_{cc0}")
                for cc in (cc0, c):
                    EO("dve", lambda e, cc=cc: e.tensor_scalar(out=buf[:, cc, :], in0=xstage[:, cc % 2, :], scalar1=rs1[:, cc:cc + 1], scalar2=None,
                                                              op0=ALU.mult), reads=[f"xst{cc % 2}", f"rs1_{cc0}"], writes=[f"xn4_{cc}"])

    def emit_s1b(buf=None, banks=None, alias_x1=True):
        buf = xn4 if buf is None else buf
        banks = S1_BANKS if banks is None else banks
        for c in range(NCH):
            keys = [f"xn4_{c}"] + ([f"x1_{c // 2}"] if alias_x1 else [])
            transpose_to_hT(buf[:, c, :], keys, c, banks)

    def emit_sc(kind, si, xsrc_dram, pos0, s1_done=False, next_s1=None, next_s1a=None, next_s1b=None):
        main = kind == "main"
        last_pre = kind == "prelast"
        need_k = main or last_pre
        tok0 = si * T
        hTk = [f"hT{c}" for c in range(NCH)]
        if not s1_done:
            emit_s1(si, xsrc_dram, main)
        elif main and si > 0:
            em.dma(x1buf[:], xsrc_dram[tok0:tok0 + T, :].rearrange("(c p) d -> p c d", p=128), writes=[f"x1_{cc}" for cc in range(NCH)])
        if stop == "A1":
            em.barrier()
            return
        EO("pool", lambda e: e.tensor_copy(out=uT[:, :, 0:3], in_=utail[:]), reads=["utail"], writes=["uTtail"])
        if main:
            EO("pool", lambda e: e.tensor_copy(out=kT[:, :, 0:128], in_=khalo[:]), reads=["khalo"], writes=["kThalo"])
            EO("pool", lambda e: e.memset(Vaug[:], 1.0), writes=["Vaug_all"] + [f"Vaug{i}" for i in range(5)])
            EO("pool", lambda e: e.tensor_copy(out=Vaug[:, 0, :], in_=vhalo[:]), reads=["vhalo", "Vaug_all"], writes=["Vaug0"])
            if si == 0:
                EO("dve", lambda e: e.tensor_scalar(out=uT[:, :, 0:3], in0=uT[:, :, 0:3], scalar1=flag, scalar2=None, op0=ALU.mult),
                   reads=["uTtail", "cols"], writes=["uTtail"])
                EO("dve", lambda e: e.tensor_scalar(out=Vaug[:, 0, :], in0=Vaug[:, 0, :], scalar1=flag, scalar2=None, op0=ALU.mult),
                   reads=["Vaug0", "cols"], writes=["Vaug0"])
                EO("dve", lambda e: e.tensor_scalar(out=prevT[:], in0=prevT[:], scalar1=flag, scalar2=None, op0=ALU.mult),
                   reads=["prevT", "cols"], writes=["prevT"])
                EO("dve", lambda e: e.tensor_copy(out=prevTb[:], in_=prevT[:]), reads=["prevT"], writes=["prevTb"])
        else:
            EO("pool", lambda e: e.memset(Vaug[:], 1.0), writes=["Vaug_all"] + [f"Vaug{i}" for i in range(5)])
        if need_k:
            em.dma(posi[:], posrep[:, pos0:pos0 + T], writes=["posi"])
        if stop == "A2":
            em.barrier()
            return

        def fm_group(m, bank, bkey):
            def f(e):
                last = None
                for k in range(KD):
                    last = e.matmul(bank[:], lhsT=w_in_bf[:, k, m * 128:(m + 1) * 128], rhs=hT[:, k, :], start=(k == 0), stop=(k == KD - 1))
                return last
            EO("pe", f, reads=hTk, writes=pk(bkey))

        def rope_pair(m_plain, m_sw, dst, dkey, par):
            b0, b1 = PA[2 * par], PA[2 * par + 1]
            fm_group(m_plain, b0, f"PA{2 * par}")
            fm_group(m_sw, b1, f"PA{2 * par + 1}")
            EO("dve", lambda e: e.scalar_tensor_tensor(out=rt1[:], in0=b0[:], scalar=bias_fm[:, m_plain:m_plain + 1], in1=cosT[:],
                                                      op0=ALU.add, op1=ALU.mult), reads=pk(f"PA{2 * par}") + ["cT", "rt1"], writes=["rt1"])
            EO("dve", lambda e: e.scalar_tensor_tensor(out=rt2[:], in0=b1[:], scalar=bias_fm[:, m_sw:m_sw + 1], in1=sinT[:],
                                                      op0=ALU.add, op1=ALU.mult), reads=pk(f"PA{2 * par + 1}") + ["sT", "rt2"], writes=["rt2"])
            EO("pool", lambda e: e.tensor_tensor(out=dst, in0=rt1[:], in1=rt2[:], op=ALU.add), reads=["rt1", "rt2"], writes=[dkey])

        nxc = 8 if (main or last_pre) else 6
        xbanks = ((PC[0], "PC0"), (PC[1], "PC1"), (PE0, "PE0"))

        def xgroup(j):
            m = 12 + j
            xb, xk = xbanks[j % 3]
            fm_group(m, xb, xk)
            EO("act", lambda e: e.activation(out=uT[:, j, 3:3 + T], in_=xb[:], func=AF.Identity, bias=bias_fm[:, m:m + 1]),
               reads=[xk], writes=[f"uT{j}"])
        pairs = []
        if main:
            pairs += [(j, 4 + j, qT[:, j, :], f"qT{j}") for j in range(4)]
        if need_k:
            pairs += [(8 + g, 10 + g, kT[:, g, 128:128 + T], f"kT{g}") for g in range(2)]
        for j in range(nxc):
            xgroup(j)
        if need_k:
            EO("dve", lambda e: e.tensor_copy(out=rt1[:], in_=posi[:]), reads=["posi"], writes=["rt1"])
            EO("dve", lambda e: e.tensor_scalar(out=rt1[:], in0=rt1[:], scalar1=invf, scalar2=None, op0=ALU.mult),
               reads=["rt1", "cols"], writes=["rt1"])
            for which, dst, shift in (("s", sinT, 0.0), ("c", cosT, float(np.pi / 2))):
                EO("dve", lambda e, shift=shift: e.tensor_scalar(out=rt2[:], in0=rt1[:], scalar1=shift, scalar2=float(1.0 / (2 * np.pi)),
                                                                 op0=ALU.add, op1=ALU.mult), reads=["rt1", "rt2"], writes=["rt2"])
                EO("dve", lambda e: e.tensor_copy(out=posi[:], in_=rt2[:]), reads=["rt2", "posi"], writes=["posi"])
                EO("dve", lambda e: e.tensor_copy(out=rt2[:], in_=posi[:]), reads=["posi"], writes=["rt2"])
                EO("dve", lambda e: e.tensor_scalar(out=rt2[:], in0=rt2[:], scalar1=float(-2 * np.pi), scalar2=None, op0=ALU.mult),
                   reads=["rt2"], writes=["rt2"])
                EO("dve", lambda e, shift=shift, dst=dst: e.scalar_tensor_tensor(out=dst[:], in0=rt1[:], scalar=shift, in1=rt2[:], op0=ALU.add,
                                                                              op1=ALU.add) if False else
                   e.tensor_tensor(out=dst[:], in0=rt1[:], in1=rt2[:], op=ALU.add), reads=["rt1", "rt2"], writes=[which + "T"])
                if shift != 0.0:
                    EO("dve", lambda e, dst=dst, shift=shift: e.tensor_scalar(out=dst[:], in0=dst[:], scalar1=shift, scalar2=None, op0=ALU.add),
                       reads=[which + "T"], writes=[which + "T"])
                EO("dve", lambda e, dst=dst: e.tensor_scalar(out=rt2[:], in0=dst[:], scalar1=float(np.pi), scalar2=float(-2 * np.pi),
                                                             op0=ALU.is_gt, op1=ALU.mult), reads=[which + "T", "rt2"], writes=["rt2"])
                EO("dve", lambda e, dst=dst: e.tensor_tensor(out=dst[:], in0=dst[:], in1=rt2[:], op=ALU.add), reads=[which + "T", "rt2"],
                   writes=[which + "T"])
                EO("dve", lambda e, dst=dst: e.tensor_scalar(out=rt2[:], in0=dst[:], scalar1=float(-np.pi), scalar2=float(2 * np.pi),
                                                             op0=ALU.is_lt, op1=ALU.mult), reads=[which + "T", "rt2"], writes=["rt2"])
                EO("dve", lambda e, dst=dst: e.tensor_tensor(out=dst[:], in0=dst[:], in1=rt2[:], op=ALU.add), reads=[which + "T", "rt2"],
                   writes=[which + "T"])
                EO("act", lambda e, dst=dst: e.activation(out=dst[:], in_=dst[:], func=AF.Sin), reads=[which + "T"], writes=[which + "T"])
            EO("dve", lambda e: e.tensor_scalar(out=sinT[:], in0=sinT[:], scalar1=rsign, scalar2=None, op0=ALU.mult),
               reads=["sT", "cols"], writes=["sT"])
        par = 0
        for (mp, ms, dst, dkey) in pairs:
            rope_pair(mp, ms, dst, dkey, par)
            par ^= 1
        if need_k:
            EO("pool", lambda e: e.tensor_copy(out=khalo[:], in_=kT[:, :, T:T + 128]), reads=["kT0", "kT1"], writes=["khalo"])
            if main:
                for g in range(2):
                    for hp in range(2):
                        EO("dve", lambda e, g=g, hp=hp: e.tensor_scalar(out=kTz[:, g, hp, :], in0=kT[:, g, :], scalar1=hmask[:, hp:hp + 1],
                                                                         scalar2=None, op0=ALU.mult),
                           reads=[f"kT{g}", "kThalo", "cols"], writes=[f"kTz{g}"])
        uTk = [f"uT{j}" for j in range(nxc)] + ["uTtail"]
        EO("pool", lambda e: e.tensor_copy(out=utail[:], in_=uT[:, :, T:T + 3]), reads=uTk, writes=["utail"])
        if stop == "A3":
            em.barrier()
            return
        tmb = ((PC[0][:], "PC0"), (PC[1][:], "PC1"), (PE0[:], "PE0"), (TPf, "TP"))
        for c in range(NCH):
            tb, tk = tmb[c]

            def fvd(e, c=c, tb=tb):
                for k in range(KD):
                    e.matmul(tb[:, 0:136], lhsT=hT[:, k, c * 128:(c + 1) * 128], rhs=w_in_bf[:, k, CV:CV + 136], start=(k == 0), stop=False)
                return e.matmul(tb[:, 0:136], lhsT=onesb[0:1, :], rhs=bias_row[0:1, 0:136], start=False, stop=True)
            EO("pe", fvd, reads=[f"hT{c}"], writes=[tk])
            EO("act", lambda e, c=c, tb=tb: e.activation(out=Vaug[:, c + 1, :].rearrange("p (g d) -> p g d", g=2)[:, :, 0:64],
                                                         in_=tb[:, 0:128].rearrange("p (g d) -> p g d", g=2), func=AF.Copy),
               reads=[tk, "Vaug_all"], writes=[f"Vaug{c + 1}"])
            EO("dve", lambda e, c=c, tb=tb: e.tensor_tensor(out=dtraw[:, c * 8:(c + 1) * 8], in0=tb[:, 128:136], in1=dtb, op=ALU.add),
               reads=[tk, "dtb"], writes=[f"dtraw{c}"])
        EO("pool", lambda e: e.tensor_copy(out=vhalo[:], in_=Vaug[:, 4, :]), reads=["Vaug4"], writes=["vhalo"])
        if next_s1 is not None:
            next_s1()
        if main:
            for c in range(NCH):
                def fz(e, c=c):
                    for k in range(KD):
                        e.matmul(PA[c][:], lhsT=hT[:, k, c * 128:(c + 1) * 128], rhs=w_in_bf[:, k, CZ:CZ + 512], start=(k == 0), stop=False)
                    return e.matmul(PA[c][:], lhsT=onesb[0:1, :], rhs=bias_row[0:1, 136:648], start=False, stop=True)
                EO("pe", fz, reads=[f"hT{c}"], writes=[f"PA{c}"])
        dk = [f"dtraw{c}" for c in range(NCH)]
        EO("dve", lambda e: e.scalar_tensor_tensor(out=dtv, in0=dtraw, scalar=-1.0, in1=dtraw, op0=ALU.mult, op1=ALU.max), reads=dk, writes=["dtv"])
        EO("act", lambda e: e.activation(out=dtv, in_=dtv, func=AF.Exp, scale=-1.0), reads=["dtv"], writes=["dtv"])
        EO("act", lambda e: e.activation(out=dtv, in_=dtv, func=AF.Ln, bias=1.0), reads=["dtv"], writes=["dtv"])
        EO("dve", lambda e: e.scalar_tensor_tensor(out=dtv, in0=dtraw, scalar=0.0, in1=dtv, op0=ALU.max, op1=ALU.add),
           reads=dk + ["dtv"], writes=["dtv"])
        EO("dve", lambda e: e.tensor_tensor(out=av.rearrange("p (c h) -> p c h", c=NCH), in0=dtv.rearrange("p (c h) -> p c h", c=NCH),
                                            in1=bcm(aneg, NCH), op=ALU.mult), reads=["dtv", "aneg"], writes=["av"])
        def fsm(e):
            last = None
            for c in range(NCH):
                a_c = av[:, c * 8:(c + 1) * 8]
                o = 136 + c * 24
                e.matmul(PE0[:, o:o + 8], lhsT=triGT, rhs=a_c, start=True, stop=True)
                e.matmul(PE0[:, o + 8:o + 16], lhsT=triLE, rhs=a_c, start=True, stop=True)
                last = e.matmul(PE0[:, o + 16:o + 24], lhsT=onesf, rhs=a_c, start=True, stop=True)
            return last
        EO("pe", fsm, reads=["av", "cst", "onesf"], writes=["PE0"])
        EO("act", lambda e: e.activation(out=exv[:].rearrange("p c k -> p (c k)"), in_=PE0[:, 136:136 + NCH * 24], func=AF.Exp),
           reads=["PE0"], writes=[f"exv{c}" for c in range(NCH)])
        EO("dve", lambda e: e.tensor_tensor(out=wend[:], in0=dtv.rearrange("p (c h) -> p c h", c=NCH), in1=exv[:, :, 0:8], op=ALU.mult),
           reads=["dtv"] + [f"exv{c}" for c in range(NCH)], writes=[f"wend{c}" for c in range(NCH)])
        if main:
            for c in range(NCH):
                EO("act", lambda e, c=c: e.activation(out=sz[:, c, :], in_=PA[c][:], func=AF.Silu), reads=[f"PA{c}"], writes=[f"sz{c}"])
        if next_s1a is not None:
            next_s1a()
        for c in range(NCH):
            cxb, cxk = ((PC[1], "PC1"), (PE0, "PE0"))[c % 2]
            cbb, cbk = ((PC[0], "PC0"), (TPf, "TP"))[c % 2]

            def fcx(e, c=c, cxb=cxb):
                last = None
                for j in range(4):
                    for k in range(4):
                        e.matmul(cxb[:, j * 128:(j + 1) * 128], lhsT=uT[:, j, c * 128 + k:c * 128 + k + 128], rhs=cdiag[:, j * 4 + k, :],
                                 start=(k == 0), stop=False)
                    last = e.matmul(cxb[:, j * 128:(j + 1) * 128], lhsT=onesb, rhs=bdiag[:, j, :], start=False, stop=True)
                return last
            EO("pe", fcx, reads=uTk, writes=[cxk])
            xs = xs2[:, c % 2, :]
            xsk = f"xs{c % 2}"
            EO("act", lambda e, xs=xs, cxb=cxb: e.activation(out=xs, in_=cxb[:, 0:512], func=AF.Silu), reads=[cxk, xsk], writes=[xsk])
            xs3 = xs.rearrange("p (h d) -> p h d", h=8)
            EO("dve", lambda e, c=c, xs3=xs3: e.tensor_tensor(out=xdt[:, c, :].rearrange("p (h d) -> p h d", h=8), in0=xs3,
                                                     in1=bcl(dtv[:, c * 8:(c + 1) * 8], 64), op=ALU.mult), reads=[xsk, "dtv"], writes=[f"xdt{c}"])
            EO("dve", lambda e, c=c, xs3=xs3: e.tensor_tensor(out=xdtd[:, c, :].rearrange("p (h d) -> p h d", h=8), in0=xs3,
                                                     in1=bcl(wend[:, c, :], 64), op=ALU.mult), reads=[xsk, f"wend{c}"], writes=[f"xdtd{c}"])
            if main:
                EO("pool", lambda e, c=c, xs3=xs3: e.tensor_tensor(out=xsD[:, c, :].rearrange("p (h d) -> p h d", h=8), in0=xs3,
                                                          in1=bcl(dskip, 64), op=ALU.mult), reads=[xsk, "dskip"], writes=[f"xsD{c}"])

            def fcb(e, c=c, cbb=cbb):
                last = None
                for jj in range(2):
                    j = 4 + jj
                    for k in range(4):
                        e.matmul(cbb[:, jj * 128:(jj + 1) * 128], lhsT=uT[:, j, c * 128 + k:c * 128 + k + 128], rhs=cdiag[:, j * 4 + k, :],
                                 start=(k == 0), stop=False)
                    last = e.matmul(cbb[:, jj * 128:(jj + 1) * 128], lhsT=onesb, rhs=bdiag[:, j, :], start=False, stop=True)
                return last
            EO("pe", fcb, reads=uTk, writes=[cbk])
            EO("act", lambda e, c=c, cbb=cbb: e.activation(out=Btm[:, c, :], in_=cbb[:, 0:256], func=AF.Silu), reads=[cbk], writes=[f"Btm{c}"])
        if main:
            for jj in range(4):
                j = 4 + jj

                def fcf(e, j=j, jj=jj):
                    last = None
                    for k in range(4):
                        last = e.matmul(PA[jj][:], lhsT=cdiag[:, j * 4 + k, :], rhs=uT[:, j, k:k + T], start=(k == 0), stop=(k == 3))
                    return last
                EO("pe", fcf, reads=uTk, writes=pk(f"PA{jj}"))
                EO("act", lambda e, j=j, jj=jj: e.activation(out=BCT[:, jj, :], in_=PA[jj][:], func=AF.Silu, bias=convb[:, j:j + 1]),
                   reads=pk(f"PA{jj}") + ["cols"], writes=[f"BCT{jj}"])
        if main:
            em.barrier()
        if stop == "A":
            return


        def state_update(c, bank, bkey):
            def fst(e, c=c):
                last = None
                for g in range(2):
                    last = e.matmul(bank[:, g * 256:(g + 1) * 256], lhsT=Btm[:, c, g * 128:(g + 1) * 128], rhs=xdtd[:, c, g * 256:(g + 1) * 256],
                                    start=True, stop=True)
                return last
            EO("pe", fst, reads=[f"Btm{c}", f"xdtd{c}"], writes=[bkey])
            EO("dve", lambda e, c=c: e.tensor_tensor(out=prevT[:].rearrange("p (h d) -> p h d", h=8), in0=prevT[:].rearrange("p (h d) -> p h d", h=8),
                                                     in1=bcl(exv[:, c, 16:24], 64), op=ALU.mult), reads=["prevT", f"exv{c}"], writes=["prevT"])
            EO("dve", lambda e: e.tensor_tensor(out=prevT[:], in0=bank, in1=prevT[:], op=ALU.add), reads=[bkey, "prevT"], writes=["prevT"])
            EO("act", lambda e: e.activation(out=prevTb[:], in_=prevT[:], func=AF.Copy), reads=["prevT"], writes=["prevTb"])

        if not main:
            for c in range(NCH):
                state_update(c, PC[1][:], "PC1")
            if next_s1b is not None:
                next_s1b()
            return

        def E1(c):
            a_c = av[:, c * 8:(c + 1) * 8]
            EO("dve", lambda e, a_c=a_c: e.tensor_tensor(out=aTri[:].rearrange("p (h l) -> p h l", h=8), in0=bcm(triLE, 8), in1=bcl(a_c, 128),
                                                         op=ALU.mult), reads=["av", "cst"], writes=["aTri"])

            def scores(g):
                for blk in range(2):
                    bank = PA[blk]
                    kc0 = c * 128 + blk * 128

                    def fsc(e, g=g, blk=blk, bank=bank, kc0=kc0):
                        moff = 0 if blk == 1 else 512
                        last = None
                        for hp in range(2):
                            for jj in range(2):
                                r0 = (hp * 2 + jj) * 128
                                e.matmul(bank[:, r0:r0 + 128], lhsT=identb, rhs=maskb[:, moff:moff + 128], start=True, stop=False)
                                last = e.matmul(bank[:, r0:r0 + 128], lhsT=kTz[:, g, hp, kc0:kc0 + 128],
                                                rhs=qT[:, 2 * g + jj, c * 128:(c + 1) * 128], start=False, stop=True)
                        return last
                    EO("pe", fsc, reads=[f"qT{2 * g}", f"qT{2 * g + 1}", f"kTz{g}", "maskb", "identb"], writes=[f"PA{blk}"])
                    EO("act", lambda e, g=g, blk=blk, bank=bank: e.activation(out=PT[:, 2 * g + blk, :], in_=bank[:], func=AF.Exp, scale=0.125),
                       reads=[f"PA{blk}"], writes=[f"PT{2 * g + blk}"])
            scores(0)
            for hh in range(2):
                EO("pe", lambda e, hh=hh: e.matmul(PA[2 + hh][:], lhsT=triGT, rhs=aTri[:, hh * 512:(hh + 1) * 512], start=True, stop=True),
                   reads=["aTri", "cst"], writes=[f"PA{2 + hh}"])
                EO("act", lambda e, hh=hh: e.activation(out=Ebuf[:, hh * 512:(hh + 1) * 512], in_=PA[2 + hh][:], func=AF.Exp),
                   reads=[f"PA{2 + hh}"], writes=[f"E{hh}"])
            scores(1)

            def fcbm(e):
                last = None
                for g in range(2):
                    last = e.matmul(PE0[:, 256 + g * 128:256 + (g + 1) * 128], lhsT=BCT[:, g, c * 128:(c + 1) * 128],
                                    rhs=BCT[:, 2 + g, c * 128:(c + 1) * 128], start=True, stop=True)
                return last
            EO("pe", fcbm, reads=[f"BCT{i}" for i in range(4)], writes=["PE0"])

        def E1b(c):
            EO("dve", lambda e: e.tensor_tensor(out=CBm[:].rearrange("p (g l) -> p g l", g=2),
                                                in0=PE0[:, 256:512].rearrange("p (g l) -> p g l", g=2), in1=bcm(triLE, 2), op=ALU.mult),
               reads=["PE0", "cst"], writes=["CBm"])
            EO("dve", lambda e: e.tensor_tensor(out=MT[:].rearrange("p (g j l) -> p g j l", g=2, j=4),
                                                in0=Ebuf[:].rearrange("p (g j l) -> p g j l", g=2, j=4),
                                                in1=CBm[:].rearrange("p (g l) -> p g l", g=2).unsqueeze(2).to_broadcast([128, 2, 4, 128]),
                                                op=ALU.mult), reads=["E0", "E1", "CBm"], writes=["MT"])

        def E2a(c):
            for g in range(2):
                def fpv(e, g=g):
                    last = None
                    for i in range(4):
                        hp, jj = i % 2, i // 2
                        cb = hp * 256 + jj * 128
                        e.matmul(PC[g][:, i * 128:i * 128 + 65], lhsT=PT[:, 2 * g, cb:cb + 128], rhs=Vaug[:, c, g * 65:(g + 1) * 65],
                                 start=True, stop=False)
                        last = e.matmul(PC[g][:, i * 128:i * 128 + 65], lhsT=PT[:, 2 * g + 1, cb:cb + 128], rhs=Vaug[:, c + 1, g * 65:(g + 1) * 65],
                                        start=False, stop=True)
                    return last
                EO("pe", fpv, reads=[f"PT{2 * g}", f"PT{2 * g + 1}", f"Vaug{c}", f"Vaug{c + 1}", "Vaug_all"], writes=[f"PC{g}"])
                o3 = PC[g][:, :].rearrange("p (i d) -> p i d", i=4)
                EO("dve", lambda e, g=g, o3=o3: e.tensor_tensor(out=den[:, g * 4:(g + 1) * 4], in0=o3[:, :, 64], in1=esink[:, g * 4:(g + 1) * 4],
                                                              op=ALU.add), reads=[f"PC{g}", "esink"], writes=[f"den{g}"])
                EO("dve", lambda e, g=g: e.reciprocal(out=rden[:, g * 4:(g + 1) * 4], in_=den[:, g * 4:(g + 1) * 4]), reads=[f"den{g}"],
                   writes=[f"rden{g}"])
                EO("dve", lambda e, g=g, o3=o3: e.tensor_tensor(out=ytm[:, g * 256:(g + 1) * 256].rearrange("p (i d) -> p i d", i=4),
                                                              in0=o3[:, :, 0:64], in1=bcl(rden[:, g * 4:(g + 1) * 4], 64), op=ALU.mult),
                   reads=[f"PC{g}", f"rden{g}"], writes=[f"ytm_a{g}"])

            def fyd(e):
                last = None
                for h in range(8):
                    e.matmul(PC[0][:, h * 64:(h + 1) * 64], lhsT=identb, rhs=xsD[:, c, h * 64:(h + 1) * 64], start=True, stop=False)
                    last = e.matmul(PC[0][:, h * 64:(h + 1) * 64], lhsT=MT[:, h * 128:(h + 1) * 128], rhs=xdt[:, c, h * 64:(h + 1) * 64],
                                    start=False, stop=True)
                return last
            EO("pe", fyd, reads=["MT", f"xdt{c}", f"xsD{c}", "identb"], writes=["PC0"])

            def fyo(e):
                last = None
                for g in range(2):
                    last = e.matmul(PC[1][:, g * 256:(g + 1) * 256], lhsT=BCT[:, 2 + g, c * 128:(c + 1) * 128],
                                    rhs=prevTb[:, g * 256:(g + 1) * 256], start=True, stop=True)
                return last
            EO("pe", fyo, reads=["BCT2", "BCT3", "prevTb"], writes=["PC1"])
            state_update(c, TPf, "TP")

        def E2b(c):
            EO("dve", lambda e: e.tensor_tensor(out=yt[:].rearrange("p (h d) -> p h d", h=8),
                                                in0=PC[1][:].rearrange("p (h d) -> p h d", h=8), in1=bcl(exv[:, c, 8:16], 64), op=ALU.mult),
               reads=["PC1", f"exv{c}"], writes=["yt"])
            EO("dve", lambda e: e.tensor_tensor(out=yt[:], in0=PC[0][:], in1=yt[:], op=ALU.add), reads=["PC0", "yt"], writes=["yt"])
            EO("dve", lambda e: e.tensor_tensor(out=yt[:], in0=yt[:], in1=sz[:, c, :], op=ALU.mult), reads=["yt", f"sz{c}"], writes=["yt"])
            for g in range(2):
                EO("dve", lambda e, g=g: e.scalar_tensor_tensor(out=sqj[:, g * 256:(g + 1) * 256], in0=yt[:, g * 256:(g + 1) * 256], scalar=1.0,
                                                               in1=yt[:, g * 256:(g + 1) * 256], op0=ALU.mult, op1=ALU.mult,
                                                               accum_out=ssg[:, g:g + 1]),
                   reads=["yt", "sqj"], writes=["sqj", f"ssg{g}"])
            rstd_from_ss(ssg, rsg, 256 * EPS, ["ssg0", "ssg1"], "rsg")
            for g in range(2):
                EO("dve", lambda e, g=g: e.scalar_tensor_tensor(out=ytm[:, 512 + g * 256:512 + (g + 1) * 256], in0=yt[:, g * 256:(g + 1) * 256],
                                                               scalar=rsg[:, g:g + 1], in1=ssmw16[:, g * 256:(g + 1) * 256], op0=ALU.mult,
                                                               op1=ALU.mult), reads=["yt", "rsg", "ssmw16"], writes=[f"ytm_s{g}"])

        def E2c(c):
            def tpy(e):
                last = None
                for f in range(KD):
                    last = e.transpose(out=TP[:, f, :], in_=ytm[:, f * 128:(f + 1) * 128], identity=identb)
                return last
            EO("pe", tpy, reads=["ytm_a0", "ytm_a1", "ytm_s0", "ytm_s1", "identb"], writes=["TP"])
            EO("act", lambda e: e.activation(out=hT[:, :, c * 128:(c + 1) * 128], in_=TP[:], func=AF.Copy), reads=["TP"], writes=[f"hT{c}"])

        E1(0)
        E1b(0)
        for c in range(NCH):
            E2a(c)
            if c + 1 < NCH:
                E1(c + 1)
            E2b(c)
            if c + 1 < NCH:
                E1b(c + 1)
            E2c(c)
        if not main:
            return
        if stop == "B":
            em.barrier()
            return
        def s8a(c):
            EO("act", lambda e: e.activation(out=sqj[:], in_=x1buf[:, c, :], func=AF.Square, accum_out=ss1[:, c:c + 1]),
               reads=[f"x1_{c}", "sqj"], writes=["sqj", f"ss1_{c}"])
            rstd_from_ss(ss1[:, c:c + 1], rs1[:, c:c + 1], D * EPS, [f"ss1_{c}"], f"rs8_{c}")
            EO("act", lambda e: e.activation(out=xnB[:], in_=x1buf[:, c, :], func=AF.Identity, scale=rs1[:, c:c + 1]),
               reads=[f"x1_{c}", f"rs8_{c}", "xnB"], writes=["xnB"])

        def s8b(c):
            transpose_to_hT(xnB, ["xnB"], c, ((PC[0], "PC0"), (PC[1], "PC1")))

        for p in range(3):
            em.dma(wgus[:, p, :, :], wgu_scr[p].rearrange("p (k c) -> p k c", k=KD), writes=[f"wgus{p}"])
        for c in range(NCH):
            for dh in range(2):
                bi = (2 * c + dh) % 4

                def fop(e, c=c, dh=dh, bi=bi):
                    last = None
                    for k in range(KD):
                        last = e.matmul(PA[bi][:], lhsT=hT[:, k, c * 128:(c + 1) * 128], rhs=w_out_bf[:, k, dh * 512:(dh + 1) * 512],
                                        start=(k == 0), stop=(k == KD - 1))
                    return last
                EO("pe", fop, reads=[f"hT{c}"], writes=pk(f"PA{bi}"))
                EO("dve", lambda e, c=c, dh=dh, bi=bi: e.tensor_tensor(out=x1buf[:, c, dh * 512:(dh + 1) * 512], in0=PA[bi][:],
                                                                    in1=x1buf[:, c, dh * 512:(dh + 1) * 512], op=ALU.add),
                   reads=pk(f"PA{bi}") + [f"x1_{c}"], writes=[f"x1_{c}"])
            if c > 0:
                s8b(c - 1)
            s8a(c)
        s8b(NCH - 1)
        em.barrier()
        if stop == "OP":
            return
        for dq0 in range(2):
            em.dma(wdns[:, dq0, :, :], wdn_scr[dq0].rearrange("p (f c) -> p f c", f=NFF), writes=[f"wdns{dq0}"])
        for ffc in range(NFF):
            slot = ffc % 3
            bg, bu = PA[2 * (ffc % 2)], PA[2 * (ffc % 2) + 1]
            kg, ku = f"PA{2 * (ffc % 2)}", f"PA{2 * (ffc % 2) + 1}"

            def fup(e, slot=slot, bg=bg, bu=bu):
                last = None
                for t, bank in ((0, bg), (1, bu)):
                    for k in range(KD):
                        last = e.matmul(bank[:], lhsT=wgus[:, slot, k, t * 128:(t + 1) * 128], rhs=hT[:, k, :], start=(k == 0), stop=(k == KD - 1))
                return last
            EO("pe", fup, reads=hTk + [f"wgus{slot}"], writes=pk(kg) + pk(ku))
            if ffc + 3 < NFF:
                em.dma(wgus[:, slot, :, :], wgu_scr[ffc + 3].rearrange("p (k c) -> p k c", k=KD), writes=[f"wgus{slot}"])
            sgs = sg[:, ffc % 2, :]
            EO("act", lambda e, ffc=ffc, bg=bg, sgs=sgs: e.activation(out=sgs, in_=bg[:], func=AF.Silu, bias=bias_gu[:, ffc:ffc + 1]),
               reads=pk(kg) + [f"sg{ffc % 2}"], writes=[f"sg{ffc % 2}"])
            EO("dve", lambda e, ffc=ffc, bu=bu, sgs=sgs: e.scalar_tensor_tensor(out=actT[:, ffc, :], in0=bu[:], scalar=bias_gu[:, NFF + ffc:NFF + ffc + 1],
                                                                             in1=sgs, op0=ALU.add, op1=ALU.mult),
               reads=pk(ku) + [f"sg{ffc % 2}"], writes=[f"actT{ffc}"])
        if si + 1 < n_main:
            emit_s1a(si + 1, xown, junk=xnC, buf=xn4C)
        actk = [f"actT{f}" for f in range(NFF)]
        for dq in range(4):
            s = dq % 2
            if dq >= 2:
                em.dma(wdns[:, s, :, :], wdn_scr[dq].rearrange("p (f c) -> p f c", f=NFF), writes=[f"wdns{s}"])
            for c in range(NCH):
                reg = (dq * NCH + c) % 4
                bank = (PC[0], PC[1], PE0, PA[0])[reg][:, 0:256]
                bkey = ("PC0", "PC1", "PE0", "PA0")[reg]

                def fdn(e, s=s, c=c, bank=bank):
                    last = None
                    for f in range(NFF):
                        last = e.matmul(bank, lhsT=actT[:, f, c * 128:(c + 1) * 128], rhs=wdns[:, s, f, :], start=(f == 0), stop=(f == NFF - 1))
                    return last
                EO("pe", fdn, reads=actk + [f"wdns{s}"], writes=[bkey])
                EO("dve", lambda e, c=c, dq=dq, bank=bank: e.tensor_tensor(out=x1buf[:, c, dq * 256:(dq + 1) * 256], in0=bank,
                                                                        in1=x1buf[:, c, dq * 256:(dq + 1) * 256], op=ALU.add),
                   reads=[bkey, f"x1_{c}"], writes=[f"x1_{c}"])
                if dq == 3:
                    EO("act", lambda e, c=c: e.activation(out=xnC[:], in_=x1buf[:, c, :], func=AF.Square, accum_out=ss1[:, c:c + 1]),
                       reads=[f"x1_{c}", "xn"], writes=["xn", f"ss1_{c}"])
                    rstd_from_ss(ss1[:, c:c + 1], rs1[:, c:c + 1], D * EPS, [f"ss1_{c}"], f"rs1_fin{c}")
                    EO("dve", lambda e, c=c: e.scalar_tensor_tensor(out=x1buf[:, c, :], in0=x1buf[:, c, :], scalar=rs1[:, c:c + 1], in1=fnw32,
                                                                   op0=ALU.mult, op1=ALU.mult), reads=[f"x1_{c}", f"rs1_fin{c}", "fnw32"],
                       writes=[f"x1_{c}"])
                    em.dma(out[tok0 + c * 128:tok0 + (c + 1) * 128, :], x1buf[:, c, :], reads=[f"x1_{c}"], writes=[f"out{si}_{c}"])
            if dq == 1 and si + 1 < n_main:
                emit_s1b(buf=xn4C, banks=((PA[2], "PA2"), (PA[3], "PA3"), (PA[1], "PA1")), alias_x1=False)
        em.barrier()

    seq = [("prelast" if si == NSC - 1 else "pre", si, xprev, 0) for si in range(NSC - n_pre, NSC)]
    seq += [("main", si, xown, T + si * T) for si in range(n_main)]
    for idx, (kind, si, src, pos0) in enumerate(seq):
        s1_done = idx > 0
        nxt = nxa = nxb = None
        if kind != "main" and idx + 1 < len(seq):
            nk, nsi, nsrc, _ = seq[idx + 1]
            if nk == "main":
                nxt = (lambda nsi=nsi, nsrc=nsrc: emit_s1(nsi, nsrc, True))
            else:
                nxa = (lambda nsi=nsi, nsrc=nsrc: emit_s1a(nsi, nsrc))
                nxb = emit_s1b
        emit_sc(kind, si, src, pos0, s1_done, nxt, nxa, nxb)
    em.barrier()
    stats = em.finalize()
    return nc, stats


def _gather_cols():
    idx = []
    idx += list(range(0, 512))
    idx += [64 * h + (d + 32) % 64 for h in range(8) for d in range(64)]
    for g in range(2):
        idx += [512 + 64 * g + d for d in range(64)] * 2
    for g in range(2):
        idx += [512 + 64 * g + (d + 32) % 64 for d in range(64)] * 2
    idx += list(range(1280, 2304))
    idx += list(range(640, 768))
    idx += list(range(2304, 2312))
    idx += list(range(768, 1280))
    assert len(idx) == NW
    return np.array(idx)


def _consts():
    p = np.arange(128)
    ident = (p[:, None] == p[None, :]).astype(np.float32)
    triLE = (p[:, None] <= p[None, :]).astype(np.float32)
    triGT = (p[:, None] > p[None, :]).astype(np.float32)
    mcur = np.where(p[:, None] <= p[None, :], 0.0, NEG).astype(np.float32)
    mprev = np.where(p[:, None] > p[None, :], 0.0, NEG).astype(np.float32)
    cp = np.concatenate([ident, triLE, triGT, np.tile(mcur, (1, 4)), np.tile(mprev, (1, 4))], axis=1)
    return np.ascontiguousarray(cp.astype(np.float32))


_PROG = {}


def kernel(x, c, positions, w_ada, b_ada, norm1_w, w_in, conv_w, conv_b, dt_bias, a_log, d_skip, attn_sinks,
           ssm_norm_w, w_out, norm2_w, w_gate_up, w_down, final_norm_w):
    f32 = np.float32
    x = np.asarray(x, f32)
    if "p" not in _PROG:
        _PROG["p"] = build_program()
    nc, stats = _PROG["p"]
    w_in_g = np.ascontiguousarray(np.asarray(w_in, f32)[0][:, _gather_cols()])
    cpack = _consts()
    half = 32
    inv_freq = (10000.0 ** (-np.arange(half, dtype=np.float32) / np.float32(half))).astype(f32)
    p = np.arange(128)
    rows = np.concatenate([np.asarray(final_norm_w, f32), np.asarray(ssm_norm_w, f32)[0], np.asarray(dt_bias, f32)[0],
                           np.asarray(a_log, f32)[0], np.asarray(d_skip, f32)[0], np.asarray(attn_sinks, f32)[0]])
    rowpack = np.ascontiguousarray(np.tile(rows[None, :], (128, 1)))
    badap = np.ascontiguousarray(np.tile(np.asarray(b_ada, f32)[0][None, :], (128, 1)))
    in_maps = []
    for i in range(8):
        b, hf = i // 2, i % 2
        colp = np.zeros((128, 80), f32)
        colp[:, 0:8] = np.asarray(c, f32)[b].reshape(8, 128).T
        cw = np.asarray(conv_w, f32)[0]
        colp[:, 8:40] = cw.reshape(4, 8, 128).transpose(2, 1, 0).reshape(128, 32)
        colp[:, 40:48] = np.asarray(conv_b, f32)[0].reshape(8, 128).T
        colp[:, 48] = inv_freq[p % 32]
        colp[:, 49] = float(hf)
        colp[:, 50] = np.where((p % 64) < 32, -1.0, 1.0)
        colp[:, 51:59] = np.asarray(norm1_w, f32)[0].reshape(8, 128).T
        colp[:, 59:67] = np.asarray(norm2_w, f32)[0].reshape(8, 128).T
        colp[:, 67] = (p < 64).astype(f32)
        colp[:, 68] = (p >= 64).astype(f32)
        pos_b = np.asarray(positions)[b].astype(np.int32)
        pos_cat = np.concatenate([pos_b[SEQ_HALF - T:SEQ_HALF], pos_b[hf * SEQ_HALF:(hf + 1) * SEQ_HALF]])
        in_maps.append({
            "xprev": np.ascontiguousarray(x[b, 0:SEQ_HALF]),
            "xown": np.ascontiguousarray(x[b, hf * SEQ_HALF:(hf + 1) * SEQ_HALF]),
            "posrep": np.ascontiguousarray(np.tile(pos_cat[None, :], (128, 1))),
            "colpack": colp, "rowpack": rowpack, "bada": badap, "cpack": cpack,
            "w_ada": np.ascontiguousarray(np.asarray(w_ada, f32)[0]), "w_in": w_in_g,
            "w_out": np.ascontiguousarray(np.asarray(w_out, f32)[0]),
            "w_gu": np.ascontiguousarray(np.asarray(w_gate_up, f32)[0]),
            "w_dn": np.ascontiguousarray(np.asarray(w_down, f32)[0]),
        })
    res = run_bass_kernel_spmd(nc, in_maps, core_ids=list(range(8)))
    outp = np.empty((4, 2 * SEQ_HALF, D), f32)
    for i in range(8):
        b, hf = i // 2, i % 2
        outp[b, hf * SEQ_HALF:(hf + 1) * SEQ_HALF] = res.results[i]["out"]
    return outp
```

```python
import numpy as np
import concourse.bass as bass
import concourse.mybir as mybir
from concourse.bass_utils import run_bass_kernel_spmd

F32 = mybir.dt.float32
BF16 = mybir.dt.bfloat16
I32 = mybir.dt.int32
AF = mybir.ActivationFunctionType
ALU = mybir.AluOpType
AX = mybir.AxisListType

EPOCH = 2000
N_DMA_SEMS = 12

D = 1024
KD = 8
SEQ_HALF = 4096
T = 512
NCH = 4
NSC = SEQ_HALF // T
DFF = 2816
NFF = 22
EPS = 1e-6
CQ, CQS, CK, CKS, CXC, CV, CDT, CZ, NW = 0, 512, 1024, 1280, 1536, 2560, 2688, 2696, 3208
NFM = 20
NTM = NW - CV
NEG = -30000.0


class _Op:
    __slots__ = ("eng", "fn", "reads", "writes", "dma", "deps", "signal", "barrier")

    def __init__(self, eng, fn, reads, writes, dma, barrier=False):
        self.eng, self.fn, self.reads, self.writes, self.dma = eng, fn, reads, writes, dma
        self.deps = ()
        self.signal = False
        self.barrier = barrier


class Emitter:
    def __init__(self, nc):
        self.nc = nc
        self.ops = []
        self.engines = {"pe": nc.tensor, "act": nc.scalar, "dve": nc.vector, "pool": nc.gpsimd, "sp": nc.sync}

    def op(self, eng, fn, reads=(), writes=()):
        self.ops.append(_Op(eng, fn, tuple(reads), tuple(writes), False))

    def dma(self, out, in_, reads=(), writes=(), eng="sp"):
        self.ops.append(_Op(eng, lambda e: e.dma_start(out=out, in_=in_), tuple(reads), tuple(writes), True))

    def barrier(self):
        self.ops.append(_Op("sp", lambda e: e.nop(), (), (), False, barrier=True))

    def finalize(self):
        nc = self.nc
        ops = self.ops
        n = len(ops)
        last_writer, readers = {}, {}
        last_on_eng = {}
        dma_since = []
        cur_barrier = None
        for i, o in enumerate(ops):
            deps = set()
            if o.barrier:
                deps.update(last_on_eng.values())
                deps.update(dma_since)
                dma_since = []
                last_writer, readers = {}, {}
            else:
                for r in o.reads:
                    w = last_writer.get(r)
                    if w is not None:
                        deps.add(w)
                for w_ in o.writes:
                    w = last_writer.get(w_)
                    if w is not None:
                        deps.add(w)
                    deps.update(readers.get(w_, ()))
                if o.eng == "pe":
                    deps = {d for d in deps if not (ops[d].eng == "pe" and not ops[d].dma)}
                if cur_barrier is not None:
                    deps.add(cur_barrier)
            deps.discard(i)
            o.deps = tuple(sorted(deps))
            for d in o.deps:
                ops[d].signal = True
            if o.barrier:
                cur_barrier = i
            for w_ in o.writes:
                last_writer[w_] = i
                readers[w_] = []
            for r in o.reads:
                if r not in o.writes:
                    readers.setdefault(r, []).append(i)
            last_on_eng[o.eng] = i
            if o.dma:
                dma_since.append(i)
        eng_count = {e: 0 for e in self.engines}
        eng_sems = {e: [] for e in self.engines}
        dma_sems = [nc.alloc_semaphore(name=f"dma{i}") for i in range(N_DMA_SEMS)]
        dma_val = [0] * N_DMA_SEMS
        rr = 0
        state = {e: {} for e in self.engines}
        sig = [None] * n
        clock = [None] * n
        nwaits = 0
        for i, o in enumerate(ops):
            E = self.engines[o.eng]
            st = state[o.eng]
            for d in o.deps:
                key, val, sem, semval = sig[d]
                if st.get(key, 0) >= val:
                    continue
                E.wait_ge(sem, semval)
                nwaits += 1
                for k2, v2 in clock[d].items():
                    if st.get(k2, 0) < v2:
                        st[k2] = v2
            if o.dma:
                k = rr
                rr = (rr + 1) % N_DMA_SEMS
                key = ("d", k)
                if st.get(key, 0) < dma_val[k]:
                    E.wait_ge(dma_sems[k], dma_val[k])
                    nwaits += 1
                    st[key] = dma_val[k]
                ins = o.fn(E)
                dma_val[k] += 16
                ins.then_inc(dma_sems[k], 16)
                sig[i] = (key, dma_val[k], dma_sems[k], dma_val[k])
                clk = dict(st)
                clk[key] = dma_val[k]
                clock[i] = clk
            else:
                ins = o.fn(E)
                if o.signal:
                    c = eng_count[o.eng]
                    ep, off = divmod(c, EPOCH)
                    if ep >= len(eng_sems[o.eng]):
                        eng_sems[o.eng].append(nc.alloc_semaphore(name=f"{o.eng}{ep}"))
                    sem = eng_sems[o.eng][ep]
                    ins.then_inc(sem, 1)
                    eng_count[o.eng] = c + 1
                    key = ("e", o.eng)
                    sig[i] = (key, c + 1, sem, off + 1)
                    clk = dict(st)
                    clk[key] = c + 1
                    clock[i] = clk
        return dict(nops=n, nwaits=nwaits, counts=dict(eng_count))


def bcl(ap, n):
    return ap.unsqueeze(2).to_broadcast([ap.shape[0], ap.shape[1], n])


def bcm(ap, n):
    return ap.unsqueeze(1).to_broadcast([ap.shape[0], n, ap.shape[1]])


def build_program(n_pre=NSC, n_main=NSC, dbg=None, stop=None):
    nc = bass.Bass("TRN2", target_bir_lowering=False)

    def din(name, shape, dt=F32):
        return nc.dram_tensor(name, list(shape), dt, kind="ExternalInput").ap()

    xprev = din("xprev", [SEQ_HALF, D])
    xown = din("xown", [SEQ_HALF, D])
    posrep = din("posrep", [128, T + SEQ_HALF], I32)
    colpack = din("colpack", [128, 80])
    rowpack = din("rowpack", [128, 1568])
    bada = din("bada", [128, 6144])
    cpack = din("cpack", [128, 1408])
    w_ada = din("w_ada", [D, 6144])
    w_in = din("w_in", [D, NW])
    w_out = din("w_out", [D, D])
    w_gu = din("w_gu", [D, 2 * DFF])
    w_dn = din("w_dn", [DFF, D])
    out = nc.dram_tensor("out", [SEQ_HALF, D], F32, kind="ExternalOutput").ap()
    wgu_scr = nc.dram_tensor("wgu_scr", [NFF, 128, KD * 256], BF16, kind="Internal").ap()
    wdn_scr = nc.dram_tensor("wdn_scr", [4, 128, NFF * 256], BF16, kind="Internal").ap()
    dbg_out = {}
    if dbg:
        for nm, shp in dbg.items():
            dbg_out[nm] = nc.dram_tensor("dbg_" + nm, list(shp), F32, kind="ExternalOutput").ap()

    def sb(name, shape, dt):
        return nc.sbuf_tensor(name, list(shape), dt).__enter__()

    def psum(name, shape, dt):
        return nc.psum_tensor(name, list(shape), dt).__enter__()

    w_in_bf = sb("w_in_bf", [128, KD, NW], BF16)
    w_out_bf = sb("w_out_bf", [128, KD, D], BF16)
    cst = sb("cst", [128, 512], F32)
    identf, triLE, triGT, onesf = cst[:, 0:128], cst[:, 128:256], cst[:, 256:384], cst[:, 384:512]
    cstb = sb("cstb", [128, 256], BF16)
    identb, onesb = cstb[:, 0:128], cstb[:, 128:256]
    maskb = sb("maskb", [128, 1024], BF16)
    cdiag = sb("cdiag", [128, 32, 128], BF16)
    bdiag = sb("bdiag", [128, 8, 128], BF16)
    rows = sb("rows", [128, 1568], F32)
    fnw32, ssmw16 = rows[:, 0:1024], rows[:, 1024:1536]
    dtb, aneg, dskip, esink = rows[:, 1536:1544], rows[:, 1544:1552], rows[:, 1552:1560], rows[:, 1560:1568]
    cols = sb("cols", [128, 80], F32)
    ccol, convw, convb = cols[:, 0:8], cols[:, 8:40], cols[:, 40:48]
    invf, flag, rsign = cols[:, 48:49], cols[:, 49:50], cols[:, 50:51]
    n1wc, n2wc = cols[:, 51:59], cols[:, 59:67]
    hmask = cols[:, 67:69]
    small = sb("small", [128, 256], F32)
    g1c, g2c, sh1c, sh2c = small[:, 0:8], small[:, 8:16], small[:, 16:24], small[:, 24:32]
    nhalf = small[:, 32:33]
    ss1 = small[:, 40:44]
    rs1 = small[:, 44:48]
    ssg = small[:, 48:50]
    rsg = small[:, 50:52]
    den = small[:, 56:64]
    rden = small[:, 64:72]
    bias_fm = small[:, 80:100]
    bias_gu = small[:, 100:144]
    dtraw = small[:, 144:176]
    dtv = small[:, 176:208]
    av = small[:, 208:240]
    tmp32 = sb("tmp32", [128, 128], F32)
    exv = sb("exv", [128, NCH, 24], F32)
    wend = sb("wend", [128, NCH, 8], F32)
    bias_row = sb("bias_row", [1, NTM], BF16)
    browf = sb("browf", [1, 2, 512], F32)
    prevT = sb("prevT", [128, 512], F32)
    prevTb = sb("prevTb", [128, 512], BF16)
    x1buf = sb("x1buf", [128, NCH, D], F32)
    xstage = sb("xstage", [128, 2, D], F32)
    hT = sb("hT", [128, KD, T], BF16)
    wgus = sb("wgus", [128, 3, KD, 256], BF16)
    utail = sb("utail", [128, 8, 3], BF16)
    khalo = sb("khalo", [128, 2, 128], BF16)
    vhalo = sb("vhalo", [128, 130], BF16)
    ARENA = 61440
    arena = sb("arena", [128, ARENA // 2], BF16)

    class Lay:
        def __init__(self, base=0):
            self.off = base

        def take(self, shape, dt):
            nel = int(np.prod(shape))
            nb = nel * (4 if dt == F32 or dt == I32 else 2)
            nb_al = (nb + 63) // 64 * 64
            o = self.off
            self.off += nb_al
            assert self.off <= ARENA, (self.off, ARENA)
            ap = arena[:, o // 2:(o + nb) // 2]
            if dt != BF16:
                ap = ap.bitcast(dt)
            if len(shape) == 2:
                return ap.rearrange("p (a b) -> p a b", a=shape[0])
            if len(shape) == 3:
                return ap.rearrange("p (a b c) -> p a b c", a=shape[0], b=shape[1])
            return ap

    L = Lay()
    qT = L.take([4, T], BF16)
    kT = L.take([2, 128 + T], BF16)
    kTz = L.take([2, 2, 128 + T], BF16)
    Vaug = L.take([5, 130], BF16)
    xdt = L.take([NCH, 512], BF16)
    xdtd = L.take([NCH, 512], BF16)
    xsD = L.take([NCH, 512], BF16)
    Btm = L.take([NCH, 256], BF16)
    BCT = L.take([4, T], BF16)
    sz = L.take([NCH, 512], BF16)
    shared_end = L.off
    LA = Lay(shared_end)
    uT = LA.take([8, T + 3], BF16)
    cosT = LA.take([T], F32)
    sinT = LA.take([T], F32)
    rt1 = LA.take([T], F32)
    rt2 = LA.take([T], F32)
    xs2 = LA.take([2, 512], F32)
    xn = LA.take([D], BF16)
    posi = LA.take([T], I32)
    LB = Lay(shared_end)
    PT = LB.take([4, 512], BF16)
    aTri = LB.take([1024], F32)
    Ebuf = LB.take([1024], F32)
    MT = LB.take([1024], BF16)
    CBm = LB.take([256], F32)
    yt = LB.take([512], F32)
    xnB = LB.take([D], BF16)
    sqj = LB.take([D], BF16)
    ytm = LB.take([D], BF16)
    jnk = LB.take([256], BF16)
    LC = Lay()
    actT = LC.take([NFF, T], BF16)
    wdns = LC.take([2, NFF, 256], BF16)
    xnC = LC.take([D], BF16)
    sg = LC.take([2, T], BF16)
    xn4C = arena[:, 49152 // 2:(49152 + NCH * D * 2) // 2].rearrange("p (c d) -> p c d", c=NCH)
    assert LC.off <= 49152
    LS = Lay()
    stg = LS.take([2, KD * 512], F32)
    cvo = LS.take([2, KD * 512], BF16)
    scb = LS.take([KD, 128], F32)
    rowst = LS.take([1568], F32)
    mod_lo = x1buf[:].rearrange("p a b -> p (a b)")
    mod_hi = hT[:].rearrange("p a b -> p (a b)").bitcast(F32)

    def modbc(c0, c1):
        if c1 <= 4096:
            return mod_lo[:, c0:c1]
        assert c0 >= 4096
        return mod_hi[:, c0 - 4096:c1 - 4096]

    TP = psum("TP", [128, 8, 128], BF16)
    TPf = TP[:].rearrange("p a b -> p (a b)").bitcast(F32)
    PA = [psum(f"PA{i}", [128, 512], F32) for i in range(4)]
    PC = [psum(f"PC{i}", [128, 512], F32) for i in range(2)]
    PE0 = psum("PE0", [128, 512], F32)

    def pk(name):
        return [name]

    em = Emitter(nc)
    EO = em.op

    def dump(name, ap, reads):
        if name in dbg_out:
            em.dma(dbg_out[name], ap, reads=reads, writes=["dbg_" + name])

    em.dma(cst[:, 0:384], cpack[:, 0:384], writes=["cst"])
    em.dma(cols[:], colpack, writes=["cols"])
    em.dma(rowst[:], rowpack, writes=["rowst"])
    EO("dve", lambda e: e.memset(onesf, 1.0), writes=["onesf"])
    EO("dve", lambda e: e.memset(onesb, 1.0), writes=["onesb"])
    EO("dve", lambda e: e.memset(nhalf, -0.5), writes=["nhalf"])
    EO("dve", lambda e: e.tensor_copy(out=identb, in_=identf), reads=["cst"], writes=["identb"])
    em.dma(stg[:, 0, 0:1024], cpack[:, 384:1408], writes=["stg0"])
    EO("dve", lambda e: e.tensor_copy(out=maskb[:], in_=stg[:, 0, 0:1024]), reads=["stg0"], writes=["maskb"])
    EO("dve", lambda e: e.tensor_scalar(out=fnw32, in0=rowst[:, 0:1024], scalar1=32.0, scalar2=None, op0=ALU.mult),
       reads=["rowst"], writes=["fnw32"])
    EO("dve", lambda e: e.tensor_scalar(out=ssmw16, in0=rowst[:, 1024:1536], scalar1=16.0, scalar2=None, op0=ALU.mult),
       reads=["rowst"], writes=["ssmw16"])
    EO("dve", lambda e: e.tensor_copy(out=rows[:, 1536:1544], in_=rowst[:, 1536:1544]), reads=["rowst"], writes=["dtb"])
    EO("dve", lambda e: e.tensor_copy(out=dskip, in_=rowst[:, 1552:1560]), reads=["rowst"], writes=["dskip"])
    EO("act", lambda e: e.activation(out=aneg, in_=rowst[:, 1544:1552], func=AF.Exp), reads=["rowst"], writes=["aneg0"])
    EO("dve", lambda e: e.tensor_scalar(out=aneg, in0=aneg, scalar1=-1.0, scalar2=None, op0=ALU.mult),
       reads=["aneg0"], writes=["aneg"])
    EO("act", lambda e: e.activation(out=esink, in_=rowst[:, 1560:1568], func=AF.Exp), reads=["rowst"], writes=["esink"])
    EO("act", lambda e: e.activation(out=small[:, 240:248], in_=ccol, func=AF.Silu), reads=["cols"], writes=["sc"])
    EO("dve", lambda e: e.tensor_copy(out=scb[:], in_=bcl(small[:, 240:248], 128)), reads=["sc"], writes=["scb"])
    w_ada_v = w_ada.rearrange("(k p) c -> p k c", p=128)
    scbb = wgus[:].rearrange("p a k c -> p (a k c)")[:, 4096:5120].rearrange("p (k f) -> p k f", k=KD)
    shb = wgus[:].rearrange("p a k c -> p (a k c)")[:, 5120:5136]
    plainb = wgus[:].rearrange("p a k c -> p (a k c)")[:, 0:4096]
    EO("dve", lambda e: e.tensor_copy(out=scbb, in_=bcl(small[:, 240:248], 128)), reads=["sc"], writes=["scbb"])
    for cg in range(12):
        s = cg % 2
        em.dma(stg[:, s, :].rearrange("p (k c) -> p k c", k=KD), w_ada_v[:, :, cg * 512:(cg + 1) * 512], writes=[f"stg{s}"])
        em.dma(xstage[:, s, 0:512], bada[:, cg * 512:(cg + 1) * 512], writes=[f"xst{s}"])
        EO("act", lambda e, s=s: e.activation(out=cvo[:, s, 0:2048], in_=stg[:, s, 0:2048], func=AF.Copy), reads=[f"stg{s}"], writes=[f"cvo{s}a"])
        EO("dve", lambda e, s=s: e.tensor_copy(out=cvo[:, s, 2048:4096], in_=stg[:, s, 2048:4096]), reads=[f"stg{s}"], writes=[f"cvo{s}b"])
        bank = PA[cg % 4]

        def mmod(e, s=s, bank=bank):
            last = None
            for k in range(KD):
                last = e.matmul(bank[:], lhsT=scbb[:, k, :], rhs=cvo[:, s, k * 512:(k + 1) * 512], start=(k == 0), stop=(k == KD - 1))
            return last
        EO("pe", mmod, reads=["scbb", f"cvo{s}a", f"cvo{s}b"], writes=pk(f"PA{cg % 4}"))
        EO("dve", lambda e, s=s, bank=bank, cg=cg: e.tensor_tensor(out=modbc(cg * 512, (cg + 1) * 512), in0=bank[:],
                                                                 in1=xstage[:, s, 0:512], op=ALU.add),
           reads=pk(f"PA{cg % 4}") + [f"xst{s}"], writes=[f"mod{cg}"])
    modkeys = [f"mod{i}" for i in range(12)]

    def diag_extract(dst, c0, key):
        EO("dve", lambda e: e.tensor_tensor(out=stg[:, 0, 0:1024].rearrange("p (k f) -> p k f", k=KD),
                                            in0=modbc(c0, c0 + 1024).rearrange("p (k f) -> p k f", k=KD),
                                            in1=bcm(identf, KD), op=ALU.mult), reads=modkeys + ["cst", "stg0"], writes=["stg0"])
        EO("dve", lambda e: e.tensor_reduce(out=dst, in_=stg[:, 0, 0:1024].rearrange("p (k f) -> p k f", k=KD),
                                            axis=AX.X, op=ALU.add), reads=["stg0"], writes=[key])
    diag_extract(sh1c, 0, "sh1c")
    diag_extract(g1c, 1024, "g1c0")
    diag_extract(sh2c, 3072, "sh2c")
    diag_extract(g2c, 4096, "g2c0")
    EO("dve", lambda e: e.tensor_copy(out=shb[:, 0:8], in_=sh1c), reads=["sh1c"], writes=["shb1"])
    EO("dve", lambda e: e.tensor_copy(out=shb[:, 8:16], in_=sh2c), reads=["sh2c"], writes=["shb2"])
    for gc, nw, k0, k1 in ((g1c, n1wc, "g1c0", "g1c"), (g2c, n2wc, "g2c0", "g2c")):
        EO("dve", lambda e, gc=gc, nw=nw: e.scalar_tensor_tensor(out=gc, in0=gc, scalar=1.0, in1=nw, op0=ALU.add, op1=ALU.mult),
           reads=[k0, "cols"], writes=[k0 + "x"])
        EO("dve", lambda e, gc=gc: e.tensor_scalar(out=gc, in0=gc, scalar1=32.0, scalar2=None, op0=ALU.mult),
           reads=[k0 + "x"], writes=[k1])

    w_in_v = w_in.rearrange("(k p) c -> p k c", p=128)
    pieces = [(i * 512, 512) for i in range(5)] + [(CV, 136), (CZ, 512)]
    cvt_rr = 0
    for pi, (c0, w) in enumerate(pieces):
        s = pi % 2
        sv = stg[:, s, 0:KD * w].rearrange("p (k c) -> p k c", k=KD)
        em.dma(sv, w_in_v[:, :, c0:c0 + w], writes=[f"stg{s}"])
        pv = plainb[:, 0:KD * w].rearrange("p (k c) -> p k c", k=KD)
        EO("act", lambda e, sv=sv, pv=pv: e.activation(out=pv[:, 0:4, :], in_=sv[:, 0:4, :], func=AF.Copy), reads=[f"stg{s}", "plainb"], writes=["plainb_a"])
        EO("dve", lambda e, sv=sv, pv=pv: e.tensor_copy(out=pv[:, 4:8, :], in_=sv[:, 4:8, :]), reads=[f"stg{s}", "plainb"], writes=["plainb_b"])
        if c0 < CV:
            rb = pi % 2

            def mbr(e, pv=pv):
                last = None
                for k in range(KD):
                    last = e.matmul(PC[0][0:1, 0:512], lhsT=shb[:, k:k + 1], rhs=pv[:, k, :], start=(k == 0), stop=(k == KD - 1))
                return last
            EO("pe", mbr, reads=["plainb_a", "plainb_b", "shb1"], writes=["PC0", "plainb"])
            EO("dve", lambda e, rb=rb: e.tensor_copy(out=browf[0:1, rb, :], in_=PC[0][0:1, 0:512]), reads=["PC0"], writes=[f"browf{rb}"])

            def mbt(e, rb=rb, c0=c0):
                last = None
                for m in range(4):
                    mi = c0 // 128 + m
                    last = e.matmul(PE0[:, mi:mi + 1], lhsT=browf[0:1, rb, m * 128:(m + 1) * 128], rhs=onesf[0:1, 0:1], start=True, stop=True)
                return last
            EO("pe", mbt, reads=[f"browf{rb}", "onesf"], writes=["PE0"])
        else:
            o0 = c0 - CV
            bank = PC[0] if c0 == CV else PC[1]

            def mb2(e, pv=pv, w=w, bank=bank):
                last = None
                for k in range(KD):
                    last = e.matmul(bank[0:1, 0:w], lhsT=shb[:, k:k + 1], rhs=pv[:, k, :], start=(k == 0), stop=(k == KD - 1))
                return last
            bk = "PC0" if c0 == CV else "PC1"
            EO("pe", mb2, reads=["plainb_a", "plainb_b", "shb1"], writes=pk(bk) + ["plainb"])
            EO("dve", lambda e, w=w, bank=bank, o0=o0: e.tensor_copy(out=bias_row[0:1, o0:o0 + w], in_=bank[0:1, 0:w]),
               reads=pk(bk), writes=[f"brow{o0}"])
        for k in range(KD):
            eng = ("act", "dve")[cvt_rr % 2]
            cvt_rr += 1
            if eng == "act":
                EO("act", lambda e, k=k, sv=sv, c0=c0, w=w: e.activation(out=w_in_bf[:, k, c0:c0 + w], in_=sv[:, k, :], func=AF.Identity,
                                                                     scale=g1c[:, k:k + 1]),
                   reads=[f"stg{s}", "g1c"], writes=[f"win{pi}_{k}"])
            else:
                EO(eng, lambda e, k=k, sv=sv, c0=c0, w=w: e.tensor_scalar(out=w_in_bf[:, k, c0:c0 + w], in0=sv[:, k, :],
                                                                       scalar1=g1c[:, k:k + 1], scalar2=None, op0=ALU.mult),
                   reads=[f"stg{s}", "g1c"], writes=[f"win{pi}_{k}"])
    EO("dve", lambda e: e.tensor_copy(out=bias_fm, in_=PE0[:, 0:NFM]), reads=["PE0"], writes=["bias_fm"])

    w_out_v = w_out.rearrange("(k p) c -> p k c", p=128)
    for hh in range(2):
        s = hh
        sv = stg[:, s, :].rearrange("p (k c) -> p k c", k=KD)
        em.dma(sv, w_out_v[:, :, hh * 512:(hh + 1) * 512], writes=[f"stg{s}"])
        EO(("dve", "pool")[hh], lambda e, sv=sv, hh=hh: e.tensor_tensor(out=w_out_bf[:, :, hh * 512:(hh + 1) * 512], in0=sv,
                                                                      in1=bcm(modbc(2048 + hh * 512, 2048 + (hh + 1) * 512), KD), op=ALU.mult),
           reads=[f"stg{s}"] + modkeys, writes=[f"wout{hh}"])

    for j in range(8):
        for k in range(4):
            EO("dve", lambda e, j=j, k=k: e.tensor_scalar(out=cdiag[:, j * 4 + k, :], in0=identf,
                                                                                 scalar1=convw[:, j * 4 + k:j * 4 + k + 1], scalar2=None, op0=ALU.mult),
               reads=["cst", "cols"], writes=[f"cdiag{j}_{k}"])
        EO("dve", lambda e, j=j: e.tensor_scalar(out=bdiag[:, j, :], in0=identf, scalar1=convb[:, j:j + 1], scalar2=None, op0=ALU.mult),
           reads=["cst", "cols"], writes=[f"bdiag{j}"])

    w_gu_v = w_gu.rearrange("(k p) c -> p k c", p=128)
    def gu_load(pc):
        s = pc % 2
        sv = stg[:, s, :].rearrange("p (k c) -> p k c", k=KD)
        em.dma(sv[:, :, 0:256], w_gu_v[:, :, pc * 256:(pc + 1) * 256], writes=[f"stg{s}"])
        em.dma(sv[:, :, 256:512], w_gu_v[:, :, DFF + pc * 256:DFF + (pc + 1) * 256], writes=[f"stg{s}"])
    gu_load(0)
    for pc in range(11):
        s = pc % 2
        sv = stg[:, s, :].rearrange("p (k c) -> p k c", k=KD)
        if pc + 1 < 11:
            gu_load(pc + 1)
        rb = pc % 2
        pv = plainb.rearrange("p (k c) -> p k c", k=KD)
        EO("act", lambda e, sv=sv, pv=pv: e.activation(out=pv[:, 0:4, :], in_=sv[:, 0:4, :], func=AF.Copy), reads=[f"stg{s}", "plainb"], writes=["plainb_a"])
        EO("dve", lambda e, sv=sv, pv=pv: e.tensor_copy(out=pv[:, 4:8, :], in_=sv[:, 4:8, :]), reads=[f"stg{s}", "plainb"], writes=["plainb_b"])

        def mbr3(e, pv=pv):
            last = None
            for k in range(KD):
                last = e.matmul(PC[0][0:1, 0:512], lhsT=shb[:, 8 + k:9 + k], rhs=pv[:, k, :], start=(k == 0), stop=(k == KD - 1))
            return last
        EO("pe", mbr3, reads=["plainb_a", "plainb_b", "shb2"], writes=["PC0", "plainb"])
        EO("dve", lambda e, rb=rb: e.tensor_copy(out=browf[0:1, rb, :], in_=PC[0][0:1, 0:512]), reads=["PC0"], writes=[f"browf{rb}"])

        def mbt3(e, rb=rb, pc=pc):
            last = None
            for m in range(4):
                ffc = 2 * pc + (m % 2)
                bcol = ffc if m < 2 else NFF + ffc
                last = e.matmul(PE0[:, 64 + bcol:64 + bcol + 1], lhsT=browf[0:1, rb, m * 128:(m + 1) * 128], rhs=onesf[0:1, 0:1], start=True, stop=True)
            return last
        EO("pe", mbt3, reads=[f"browf{rb}", "onesf"], writes=["PE0"])
        cv5 = cvo[:, s, :].rearrange("p (h k t c) -> p h k t c", h=2, k=KD, t=2)
        for k in range(KD):
            eng = ("act", "dve")[cvt_rr % 2]
            cvt_rr += 1
            src4 = sv[:, k, :].rearrange("p (t h c) -> p h t c", t=2, h=2)
            dst4 = cv5[:, :, k, :, :]
            if eng == "act":
                EO("act", lambda e, k=k, src4=src4, dst4=dst4: e.activation(out=dst4, in_=src4, func=AF.Identity, scale=g2c[:, k:k + 1]),
                   reads=[f"stg{s}", "g2c"], writes=[f"cvo{s}_{k}"])
            else:
                EO(eng, lambda e, k=k, src4=src4, dst4=dst4: e.tensor_scalar(out=dst4, in0=src4, scalar1=g2c[:, k:k + 1], scalar2=None, op0=ALU.mult),
                   reads=[f"stg{s}", "g2c"], writes=[f"cvo{s}_{k}"])
        for half in range(2):
            ffc = 2 * pc + half
            em.dma(wgu_scr[ffc], cvo[:, s, half * 2048:(half + 1) * 2048], reads=[f"cvo{s}_{k}" for k in range(KD)], writes=[f"wguscr{ffc}g"])
    EO("dve", lambda e: e.tensor_copy(out=bias_gu, in_=PE0[:, 64:64 + 2 * NFF]), reads=["PE0"], writes=["bias_gu"])
    w_dn_v = w_dn.rearrange("(f p) c -> p f c", p=128)
    dn_pieces = [(dq, fh) for dq in range(4) for fh in range(2)]

    def dn_load(i):
        dq, fh = dn_pieces[i]
        s = i % 2
        sv = stg[:, s, 0:11 * 256].rearrange("p (f c) -> p f c", f=11)
        em.dma(sv, w_dn_v[:, fh * 11:(fh + 1) * 11, dq * 256:(dq + 1) * 256], writes=[f"stg{s}"])
    dn_load(0)
    for i, (dq, fh) in enumerate(dn_pieces):
        s = i % 2
        sv = stg[:, s, 0:11 * 256].rearrange("p (f c) -> p f c", f=11)
        cv = cvo[:, s, 0:11 * 256].rearrange("p (f c) -> p f c", f=11)
        if i + 1 < len(dn_pieces):
            dn_load(i + 1)
        EO(("dve", "pool")[i % 2], lambda e, sv=sv, cv=cv, dq=dq: e.tensor_tensor(
            out=cv, in0=sv, in1=bcm(modbc(5120 + dq * 256, 5120 + (dq + 1) * 256), 11), op=ALU.mult),
           reads=[f"stg{s}"] + modkeys, writes=[f"cvo{s}"] + [f"cvo{s}_{k}" for k in range(KD)])
        dst = wdn_scr[dq].rearrange("p (f c) -> p f c", f=NFF)
        em.dma(dst[:, fh * 11:(fh + 1) * 11, :], cv, reads=[f"cvo{s}"], writes=[f"wdnscr{dq}_{fh}"])
    em.barrier()

    EO("dve", lambda e: e.memset(prevT[:], 0.0), writes=["prevT"])
    EO("dve", lambda e: e.memset(prevTb[:], 0.0), writes=["prevTb"])
    EO("pool", lambda e: e.memset(utail[:], 0.0), writes=["utail"])
    EO("pool", lambda e: e.memset(khalo[:], 0.0), writes=["khalo"])
    EO("pool", lambda e: e.memset(vhalo[:], 0.0), writes=["vhalo"])

    SCRKEYS_W = [f"wguscr{f}{t}" for f in range(NFF) for t in "gu"] + [f"wdnscr{q}_{h}" for q in range(4) for h in range(2)]

    def rstd_from_ss(ssap, rsap, n_eps, rk, wk):
        EO("act", lambda e: e.activation(out=rsap, in_=ssap, func=AF.Ln, bias=float(n_eps)), reads=rk, writes=[wk + "t"])
        EO("act", lambda e: e.activation(out=rsap, in_=rsap, func=AF.Exp, scale=-0.5), reads=[wk + "t"], writes=[wk])

    def transpose_to_hT(src, src_keys, c, banks):
        for half in range(2):
            bank, bkey = banks[(2 * c + half) % len(banks)]

            def tps(e, half=half, bank=bank):
                last = None
                for f4 in range(4):
                    f = half * 4 + f4
                    last = e.matmul(bank[:, f4 * 128:(f4 + 1) * 128], lhsT=src[:, f * 128:(f + 1) * 128], rhs=identb, start=True, stop=True)
                return last
            EO("pe", tps, reads=src_keys + ["identb"], writes=[bkey])
            dst = hT[:, half * 4:(half + 1) * 4, c * 128:(c + 1) * 128]
            srcv = bank[:].rearrange("p (f t) -> p f t", f=4)
            if half == 0:
                EO("act", lambda e, dst=dst, srcv=srcv: e.activation(out=dst, in_=srcv, func=AF.Copy), reads=[bkey], writes=[f"hT{c}"])
            else:
                EO("dve", lambda e, dst=dst, srcv=srcv: e.tensor_copy(out=dst, in_=srcv), reads=[bkey], writes=[f"hT{c}"])

    S1_BANKS = ((PA[0], "PA0"), (PA[1], "PA1"))

    def norm_to_hT(xsrc, xkeys, xnbuf, c, sskey):
        EO("act", lambda e: e.activation(out=xnbuf[:], in_=xsrc, func=AF.Identity, scale=rs1[:, c:c + 1]),
           reads=xkeys + [sskey, "xn"], writes=["xn"])
        transpose_to_hT(xnbuf, ["xn"], c, S1_BANKS)

    def emit_s1(si, xsrc_dram, main):
        tok0 = si * T
        for c in range(NCH):
            em.dma(xstage[:, c % 2, :], xsrc_dram[tok0 + c * 128: tok0 + (c + 1) * 128, :], writes=[f"xst{c % 2}"])
            EO("act", lambda e, c=c: e.activation(out=xn[:], in_=xstage[:, c % 2, :], func=AF.Square, accum_out=ss1[:, c:c + 1]),
               reads=[f"xst{c % 2}"], writes=["xn", f"ss1_{c}"])
            if c == 1 and main:
                em.dma(x1buf[:], xsrc_dram[tok0:tok0 + T, :].rearrange("(c p) d -> p c d", p=128), writes=[f"x1_{cc}" for cc in range(NCH)])
            if c % 2 == 1:
                cc0 = c - 1
                rstd_from_ss(ss1[:, cc0:c + 1], rs1[:, cc0:c + 1], D * EPS, [f"ss1_{cc0}", f"ss1_{c}"], f"rs1_{cc0}")
                for cc in (cc0, c):
                    norm_to_hT(xstage[:, cc % 2, :], [f"xst{cc % 2}"], xn, cc, f"rs1_{cc0}")

    xn4 = x1buf[:].rearrange("p a b -> p (a b)").bitcast(BF16)[:, 0:NCH * D].rearrange("p (c d) -> p c d", c=NCH)

    def emit_s1a(si, xsrc_dram, junk=None, buf=None):
        junk = xn if junk is None else junk
        buf = xn4 if buf is None else buf
        tok0 = si * T
        for c in range(NCH):
            em.dma(xstage[:, c % 2, :], xsrc_dram[tok0 + c * 128: tok0 + (c + 1) * 128, :], writes=[f"xst{c % 2}"])
            EO("dve", lambda e, c=c: e.scalar_tensor_tensor(out=junk[:], in0=xstage[:, c % 2, :], scalar=1.0, in1=xstage[:, c % 2, :],
                                                           op0=ALU.mult, op1=ALU.mult, accum_out=ss1[:, c:c + 1]),
               reads=[f"xst{c % 2}", "xn"], writes=["xn", f"ss1_{c}"])
            if c % 2 == 1:
                cc0 = c - 1
                rstd_from_ss(ss1[:, cc0:c + 1], rs1[:, cc0:c + 1], D * EPS, [f"ss1_{cc0}", f"ss1_{c}"], f"rs1_{cc0}")
                for cc in (cc0, c):
                    EO("dve", lambda e, cc=cc: e.tensor_scalar(out=buf[:, cc, :], in0=xstage[:, cc % 2, :], scalar1=rs1[:, cc:cc + 1], scalar2=None,
                                                              op0=ALU.mult), reads=[f"xst{cc % 2}", f"rs1_{cc0}"], writes=[f"xn4_{cc}"])

    def emit_s1b(buf=None, banks=None, alias_x1=True):
        buf = xn4 if buf is None else buf
        banks = S1_BANKS if banks is None else banks
        for c in range(NCH):
            keys = [f"xn4_{c}"] + ([f"x1_{c // 2}"] if alias_x1 else [])
            transpose_to_hT(buf[:, c, :], keys, c, banks)

    def emit_sc(kind, si, xsrc_dram, pos0, s1_done=False, next_s1=None, next_s1a=None, next_s1b=None):
        main = kind == "main"
        last_pre = kind == "prelast"
        need_k = main or last_pre
        tok0 = si * T
        hTk = [f"hT{c}" for c in range(NCH)]
        if not s1_done:
            emit_s1(si, xsrc_dram, main)
        elif main and si > 0:
            em.dma(x1buf[:], xsrc_dram[tok0:tok0 + T, :].rearrange("(c p) d -> p c d", p=128), writes=[f"x1_{cc}" for cc in range(NCH)])
        if stop == "A1":
            em.barrier()
            return
        EO("pool", lambda e: e.tensor_copy(out=uT[:, :, 0:3], in_=utail[:]), reads=["utail"], writes=["uTtail"])
        if main:
            EO("pool", lambda e: e.tensor_copy(out=kT[:, :, 0:128], in_=khalo[:]), reads=["khalo"], writes=["kThalo"])
            EO("pool", lambda e: e.memset(Vaug[:], 1.0), writes=["Vaug_all"] + [f"Vaug{i}" for i in range(5)])
            EO("pool", lambda e: e.tensor_copy(out=Vaug[:, 0, :], in_=vhalo[:]), reads=["vhalo", "Vaug_all"], writes=["Vaug0"])
            if si == 0:
                EO("dve", lambda e: e.tensor_scalar(out=uT[:, :, 0:3], in0=uT[:, :, 0:3], scalar1=flag, scalar2=None, op0=ALU.mult),
                   reads=["uTtail", "cols"], writes=["uTtail"])
                EO("dve", lambda e: e.tensor_scalar(out=Vaug[:, 0, :], in0=Vaug[:, 0, :], scalar1=flag, scalar2=None, op0=ALU.mult),
                   reads=["Vaug0", "cols"], writes=["Vaug0"])
                EO("dve", lambda e: e.tensor_scalar(out=prevT[:], in0=prevT[:], scalar1=flag, scalar2=None, op0=ALU.mult),
                   reads=["prevT", "cols"], writes=["prevT"])
                EO("dve", lambda e: e.tensor_copy(out=prevTb[:], in_=prevT[:]), reads=["prevT"], writes=["prevTb"])
        else:
            EO("pool", lambda e: e.memset(Vaug[:], 1.0), writes=["Vaug_all"] + [f"Vaug{i}" for i in range(5)])
        if need_k:
            em.dma(posi[:], posrep[:, pos0:pos0 + T], writes=["posi"])
        if stop == "A2":
            em.barrier()
            return

        def fm_group(m, bank, bkey):
            def f(e):
                last = None
                for k in range(KD):
                    last = e.matmul(bank[:], lhsT=w_in_bf[:, k, m * 128:(m + 1) * 128], rhs=hT[:, k, :], start=(k == 0), stop=(k == KD - 1))
                return last
            EO("pe", f, reads=hTk, writes=pk(bkey))

        def rope_pair(m_plain, m_sw, dst, dkey, par):
            b0, b1 = PA[2 * par], PA[2 * par + 1]
            fm_group(m_plain, b0, f"PA{2 * par}")
            fm_group(m_sw, b1, f"PA{2 * par + 1}")
            EO("dve", lambda e: e.scalar_tensor_tensor(out=rt1[:], in0=b0[:], scalar=bias_fm[:, m_plain:m_plain + 1], in1=cosT[:],
                                                      op0=ALU.add, op1=ALU.mult), reads=pk(f"PA{2 * par}") + ["cT", "rt1"], writes=["rt1"])
            EO("dve", lambda e: e.scalar_tensor_tensor(out=rt2[:], in0=b1[:], scalar=bias_fm[:, m_sw:m_sw + 1], in1=sinT[:],
                                                      op0=ALU.add, op1=ALU.mult), reads=pk(f"PA{2 * par + 1}") + ["sT", "rt2"], writes=["rt2"])
            EO("pool", lambda e: e.tensor_tensor(out=dst, in0=rt1[:], in1=rt2[:], op=ALU.add), reads=["rt1", "rt2"], writes=[dkey])

        nxc = 8 if (main or last_pre) else 6
        xbanks = ((PC[0], "PC0"), (PC[1], "PC1"), (PE0, "PE0"))

        def xgroup(j):
            m = 12 + j
            xb, xk = xbanks[j % 3]
            fm_group(m, xb, xk)
            EO("act", lambda e: e.activation(out=uT[:, j, 3:3 + T], in_=xb[:], func=AF.Identity, bias=bias_fm[:, m:m + 1]),
               reads=[xk], writes=[f"uT{j}"])
        pairs = []
        if main:
            pairs += [(j, 4 + j, qT[:, j, :], f"qT{j}") for j in range(4)]
        if need_k:
            pairs += [(8 + g, 10 + g, kT[:, g, 128:128 + T], f"kT{g}") for g in range(2)]
        for j in range(nxc):
            xgroup(j)
        if need_k:
            EO("dve", lambda e: e.tensor_copy(out=rt1[:], in_=posi[:]), reads=["posi"], writes=["rt1"])
            EO("dve", lambda e: e.tensor_scalar(out=rt1[:], in0=rt1[:], scalar1=invf, scalar2=None, op0=ALU.mult),
               reads=["rt1", "cols"], writes=["rt1"])
            for which, dst, shift in (("s", sinT, 0.0), ("c", cosT, float(np.pi / 2))):
                EO("dve", lambda e, shift=shift: e.tensor_scalar(out=rt2[:], in0=rt1[:], scalar1=shift, scalar2=float(1.0 / (2 * np.pi)),
                                                                 op0=ALU.add, op1=ALU.mult), reads=["rt1", "rt2"], writes=["rt2"])
                EO("dve", lambda e: e.tensor_copy(out=posi[:], in_=rt2[:]), reads=["rt2", "posi"], writes=["posi"])
                EO("dve", lambda e: e.tensor_copy(out=rt2[:], in_=posi[:]), reads=["posi"], writes=["rt2"])
                EO("dve", lambda e: e.tensor_scalar(out=rt2[:], in0=rt2[:], scalar1=float(-2 * np.pi), scalar2=None, op0=ALU.mult),
                   reads=["rt2"], writes=["rt2"])
                EO("dve", lambda e, shift=shift, dst=dst: e.scalar_tensor_tensor(out=dst[:], in0=rt1[:], scalar=shift, in1=rt2[:], op0=ALU.add,
                                                                              op1=ALU.add) if False else
                   e.tensor_tensor(out=dst[:], in0=rt1[:], in1=rt2[:], op=ALU.add), reads=["rt1", "rt2"], writes=[which + "T"])
                if shift != 0.0:
                    EO("dve", lambda e, dst=dst, shift=shift: e.tensor_scalar(out=dst[:], in0=dst[:], scalar1=shift, scalar2=None, op0=ALU.add),
                       reads=[which + "T"], writes=[which + "T"])
                EO("dve", lambda e, dst=dst: e.tensor_scalar(out=rt2[:], in0=dst[:], scalar1=float(np.pi), scalar2=float(-2 * np.pi),
                                                             op0=ALU.is_gt, op1=ALU.mult), reads=[which + "T", "rt2"], writes=["rt2"])
                EO("dve", lambda e, dst=dst: e.tensor_tensor(out=dst[:], in0=dst[:], in1=rt2[:], op=ALU.add), reads=[which + "T", "rt2"],
                   writes=[which + "T"])
                EO("dve", lambda e, dst=dst: e.tensor_scalar(out=rt2[:], in0=dst[:], scalar1=float(-np.pi), scalar2=float(2 * np.pi),
                                                             op0=ALU.is_lt, op1=ALU.mult), reads=[which + "T", "rt2"], writes=["rt2"])
                EO("dve", lambda e, dst=dst: e.tensor_tensor(out=dst[:], in0=dst[:], in1=rt2[:], op=ALU.add), reads=[which + "T", "rt2"],
                   writes=[which + "T"])
                EO("act", lambda e, dst=dst: e.activation(out=dst[:], in_=dst[:], func=AF.Sin), reads=[which + "T"], writes=[which + "T"])
            EO("dve", lambda e: e.tensor_scalar(out=sinT[:], in0=sinT[:], scalar1=rsign, scalar2=None, op0=ALU.mult),
               reads=["sT", "cols"], writes=["sT"])
        par = 0
        for (mp, ms, dst, dkey) in pairs:
            rope_pair(mp, ms, dst, dkey, par)
            par ^= 1
        if need_k:
            EO("pool", lambda e: e.tensor_copy(out=khalo[:], in_=kT[:, :, T:T + 128]), reads=["kT0", "kT1"], writes=["khalo"])
            if main:
                for g in range(2):
                    for hp in range(2):
                        EO("dve", lambda e, g=g, hp=hp: e.tensor_scalar(out=kTz[:, g, hp, :], in0=kT[:, g, :], scalar1=hmask[:, hp:hp + 1],
                                                                         scalar2=None, op0=ALU.mult),
                           reads=[f"kT{g}", "kThalo", "cols"], writes=[f"kTz{g}"])
        uTk = [f"uT{j}" for j in range(nxc)] + ["uTtail"]
        EO("pool", lambda e: e.tensor_copy(out=utail[:], in_=uT[:, :, T:T + 3]), reads=uTk, writes=["utail"])
        if stop == "A3":
            em.barrier()
            return
        tmb = ((PC[0][:], "PC0"), (PC[1][:], "PC1"), (PE0[:], "PE0"), (TPf, "TP"))
        for c in range(NCH):
            tb, tk = tmb[c]

            def fvd(e, c=c, tb=tb):
                for k in range(KD):
                    e.matmul(tb[:, 0:136], lhsT=hT[:, k, c * 128:(c + 1) * 128], rhs=w_in_bf[:, k, CV:CV + 136], start=(k == 0), stop=False)
                return e.matmul(tb[:, 0:136], lhsT=onesb[0:1, :], rhs=bias_row[0:1, 0:136], start=False, stop=True)
            EO("pe", fvd, reads=[f"hT{c}"], writes=[tk])
            EO("act", lambda e, c=c, tb=tb: e.activation(out=Vaug[:, c + 1, :].rearrange("p (g d) -> p g d", g=2)[:, :, 0:64],
                                                         in_=tb[:, 0:128].rearrange("p (g d) -> p g d", g=2), func=AF.Copy),
               reads=[tk, "Vaug_all"], writes=[f"Vaug{c + 1}"])
            EO("dve", lambda e, c=c, tb=tb: e.tensor_tensor(out=dtraw[:, c * 8:(c + 1) * 8], in0=tb[:, 128:136], in1=dtb, op=ALU.add),
               reads=[tk, "dtb"], writes=[f"dtraw{c}"])
        EO("pool", lambda e: e.tensor_copy(out=vhalo[:], in_=Vaug[:, 4, :]), reads=["Vaug4"], writes=["vhalo"])
        if next_s1 is not None:
            next_s1()
        if main:
            for c in range(NCH):
                def fz(e, c=c):
                    for k in range(KD):
                        e.matmul(PA[c][:], lhsT=hT[:, k, c * 128:(c + 1) * 128], rhs=w_in_bf[:, k, CZ:CZ + 512], start=(k == 0), stop=False)
                    return e.matmul(PA[c][:], lhsT=onesb[0:1, :], rhs=bias_row[0:1, 136:648], start=False, stop=True)
                EO("pe", fz, reads=[f"hT{c}"], writes=[f"PA{c}"])
        dk = [f"dtraw{c}" for c in range(NCH)]
        EO("dve", lambda e: e.scalar_tensor_tensor(out=dtv, in0=dtraw, scalar=-1.0, in1=dtraw, op0=ALU.mult, op1=ALU.max), reads=dk, writes=["dtv"])
        EO("act", lambda e: e.activation(out=dtv, in_=dtv, func=AF.Exp, scale=-1.0), reads=["dtv"], writes=["dtv"])
        EO("act", lambda e: e.activation(out=dtv, in_=dtv, func=AF.Ln, bias=1.0), reads=["dtv"], writes=["dtv"])
        EO("dve", lambda e: e.scalar_tensor_tensor(out=dtv, in0=dtraw, scalar=0.0, in1=dtv, op0=ALU.max, op1=ALU.add),
           reads=dk + ["dtv"], writes=["dtv"])
        EO("dve", lambda e: e.tensor_tensor(out=av.rearrange("p (c h) -> p c h", c=NCH), in0=dtv.rearrange("p (c h) -> p c h", c=NCH),
                                            in1=bcm(aneg, NCH), op=ALU.mult), reads=["dtv", "aneg"], writes=["av"])
        def fsm(e):
            last = None
            for c in range(NCH):
                a_c = av[:, c * 8:(c + 1) * 8]
                o = 136 + c * 24
                e.matmul(PE0[:, o:o + 8], lhsT=triGT, rhs=a_c, start=True, stop=True)
                e.matmul(PE0[:, o + 8:o + 16], lhsT=triLE, rhs=a_c, start=True, stop=True)
                last = e.matmul(PE0[:, o + 16:o + 24], lhsT=onesf, rhs=a_c, start=True, stop=True)
            return last
        EO("pe", fsm, reads=["av", "cst", "onesf"], writes=["PE0"])
        EO("act", lambda e: e.activation(out=exv[:].rearrange("p c k -> p (c k)"), in_=PE0[:, 136:136 + NCH * 24], func=AF.Exp),
           reads=["PE0"], writes=[f"exv{c}" for c in range(NCH)])
        EO("dve", lambda e: e.tensor_tensor(out=wend[:], in0=dtv.rearrange("p (c h) -> p c h", c=NCH), in1=exv[:, :, 0:8], op=ALU.mult),
           reads=["dtv"] + [f"exv{c}" for c in range(NCH)], writes=[f"wend{c}" for c in range(NCH)])
        if main:
            for c in range(NCH):
                EO("act", lambda e, c=c: e.activation(out=sz[:, c, :], in_=PA[c][:], func=AF.Silu), reads=[f"PA{c}"], writes=[f"sz{c}"])
        for c in range(NCH):
            cxb, cxk = ((PC[1], "PC1"), (PE0, "PE0"))[c % 2]
            cbb, cbk = ((PC[0], "PC0"), (TPf, "TP"))[c % 2]

            def fcx(e, c=c, cxb=cxb):
                last = None
                for j in range(4):
                    for k in range(4):
                        e.matmul(cxb[:, j * 128:(j + 1) * 128], lhsT=uT[:, j, c * 128 + k:c * 128 + k + 128], rhs=cdiag[:, j * 4 + k, :],
                                 start=(k == 0), stop=False)
                    last = e.matmul(cxb[:, j * 128:(j + 1) * 128], lhsT=onesb, rhs=bdiag[:, j, :], start=False, stop=True)
                return last
            EO("pe", fcx, reads=uTk, writes=[cxk])
            xs = xs2[:, c % 2, :]
            xsk = f"xs{c % 2}"
            EO("act", lambda e, xs=xs, cxb=cxb: e.activation(out=xs, in_=cxb[:, 0:512], func=AF.Silu), reads=[cxk, xsk], writes=[xsk])
            xs3 = xs.rearrange("p (h d) -> p h d", h=8)
            EO("dve", lambda e, c=c, xs3=xs3: e.tensor_tensor(out=xdt[:, c, :].rearrange("p (h d) -> p h d", h=8), in0=xs3,
                                                     in1=bcl(dtv[:, c * 8:(c + 1) * 8], 64), op=ALU.mult), reads=[xsk, "dtv"], writes=[f"xdt{c}"])
            EO("dve", lambda e, c=c, xs3=xs3: e.tensor_tensor(out=xdtd[:, c, :].rearrange("p (h d) -> p h d", h=8), in0=xs3,
                                                     in1=bcl(wend[:, c, :], 64), op=ALU.mult), reads=[xsk, f"wend{c}"], writes=[f"xdtd{c}"])
            if main:
                EO("pool", lambda e, c=c, xs3=xs3: e.tensor_tensor(out=xsD[:, c, :].rearrange("p (h d) -> p h d", h=8), in0=xs3,
                                                          in1=bcl(dskip, 64), op=ALU.mult), reads=[xsk, "dskip"], writes=[f"xsD{c}"])

            def fcb(e, c=c, cbb=cbb):
                last = None
                for jj in range(2):
                    j = 4 + jj
                    for k in range(4):
                        e.matmul(cbb[:, jj * 128:(jj + 1) * 128], lhsT=uT[:, j, c * 128 + k:c * 128 + k + 128], rhs=cdiag[:, j * 4 + k, :],
                                 start=(k == 0), stop=False)
                    last = e.matmul(cbb[:, jj * 128:(jj + 1) * 128], lhsT=onesb, rhs=bdiag[:, j, :], start=False, stop=True)
                return last
            EO("pe", fcb, reads=uTk, writes=[cbk])
            EO("act", lambda e, c=c, cbb=cbb: e.activation(out=Btm[:, c, :], in_=cbb[:, 0:256], func=AF.Silu), reads=[cbk], writes=[f"Btm{c}"])
        if next_s1a is not None:
            next_s1a()
        if main:
            for jj in range(4):
                j = 4 + jj

                def fcf(e, j=j, jj=jj):
                    last = None
                    for k in range(4):
                        last = e.matmul(PA[jj][:], lhsT=cdiag[:, j * 4 + k, :], rhs=uT[:, j, k:k + T], start=(k == 0), stop=(k == 3))
                    return last
                EO("pe", fcf, reads=uTk, writes=pk(f"PA{jj}"))
                EO("act", lambda e, j=j, jj=jj: e.activation(out=BCT[:, jj, :], in_=PA[jj][:], func=AF.Silu, bias=convb[:, j:j + 1]),
                   reads=pk(f"PA{jj}") + ["cols"], writes=[f"BCT{jj}"])
        if main:
            em.barrier()
        if stop == "A":
            return


        def state_update(c, bank, bkey, copy_b=True):
            def fst(e, c=c):
                last = None
                for g in range(2):
                    last = e.matmul(bank[:, g * 256:(g + 1) * 256], lhsT=Btm[:, c, g * 128:(g + 1) * 128], rhs=xdtd[:, c, g * 256:(g + 1) * 256],
                                    start=True, stop=True)
                return last
            EO("pe", fst, reads=[f"Btm{c}", f"xdtd{c}"], writes=[bkey])
            EO("dve", lambda e, c=c: e.tensor_tensor(out=prevT[:].rearrange("p (h d) -> p h d", h=8), in0=prevT[:].rearrange("p (h d) -> p h d", h=8),
                                                     in1=bcl(exv[:, c, 16:24], 64), op=ALU.mult), reads=["prevT", f"exv{c}"], writes=["prevT"])
            EO("dve", lambda e: e.tensor_tensor(out=prevT[:], in0=bank, in1=prevT[:], op=ALU.add), reads=[bkey, "prevT"], writes=["prevT"])
            if copy_b:
                EO("act", lambda e: e.activation(out=prevTb[:], in_=prevT[:], func=AF.Copy), reads=["prevT"], writes=["prevTb"])

        if not main:
            for c in range(NCH):
                sb_, sk_ = tmb[c]
                state_update(c, sb_, sk_, copy_b=False)
            if next_s1b is not None:
                next_s1b()
            return

        def E1(c):
            a_c = av[:, c * 8:(c + 1) * 8]
            EO("dve", lambda e, a_c=a_c: e.tensor_tensor(out=aTri[:].rearrange("p (h l) -> p h l", h=8), in0=bcm(triLE, 8), in1=bcl(a_c, 128),
                                                         op=ALU.mult), reads=["av", "cst"], writes=["aTri"])

            def scores(g):
                for blk in range(2):
                    bank = PA[blk]
                    kc0 = c * 128 + blk * 128

                    def fsc(e, g=g, blk=blk, bank=bank, kc0=kc0):
                        moff = 0 if blk == 1 else 512
                        last = None
                        for hp in range(2):
                            for jj in range(2):
                                r0 = (hp * 2 + jj) * 128
                                e.matmul(bank[:, r0:r0 + 128], lhsT=identb, rhs=maskb[:, moff:moff + 128], start=True, stop=False)
                                last = e.matmul(bank[:, r0:r0 + 128], lhsT=kTz[:, g, hp, kc0:kc0 + 128],
                                                rhs=qT[:, 2 * g + jj, c * 128:(c + 1) * 128], start=False, stop=True)
                        return last
                    EO("pe", fsc, reads=[f"qT{2 * g}", f"qT{2 * g + 1}", f"kTz{g}", "maskb", "identb"], writes=[f"PA{blk}"])
                    EO("act", lambda e, g=g, blk=blk, bank=bank: e.activation(out=PT[:, 2 * g + blk, :], in_=bank[:], func=AF.Exp, scale=0.125),
                       reads=[f"PA{blk}"], writes=[f"PT{2 * g + blk}"])
            scores(0)
            for hh in range(2):
                EO("pe", lambda e, hh=hh: e.matmul(PA[2 + hh][:], lhsT=triGT, rhs=aTri[:, hh * 512:(hh + 1) * 512], start=True, stop=True),
                   reads=["aTri", "cst"], writes=[f"PA{2 + hh}"])
                EO("act", lambda e, hh=hh: e.activation(out=Ebuf[:, hh * 512:(hh + 1) * 512], in_=PA[2 + hh][:], func=AF.Exp),
                   reads=[f"PA{2 + hh}"], writes=[f"E{hh}"])
            scores(1)

            def fcbm(e):
                last = None
                for g in range(2):
                    last = e.matmul(PE0[:, 256 + g * 128:256 + (g + 1) * 128], lhsT=BCT[:, g, c * 128:(c + 1) * 128],
                                    rhs=BCT[:, 2 + g, c * 128:(c + 1) * 128], start=True, stop=True)
                return last
            EO("pe", fcbm, reads=[f"BCT{i}" for i in range(4)], writes=["PE0"])

        def E1b(c):
            EO("dve", lambda e: e.tensor_tensor(out=CBm[:].rearrange("p (g l) -> p g l", g=2),
                                                in0=PE0[:, 256:512].rearrange("p (g l) -> p g l", g=2), in1=bcm(triLE, 2), op=ALU.mult),
               reads=["PE0", "cst"], writes=["CBm"])
            EO("dve", lambda e: e.tensor_tensor(out=MT[:].rearrange("p (g j l) -> p g j l", g=2, j=4),
                                                in0=Ebuf[:].rearrange("p (g j l) -> p g j l", g=2, j=4),
                                                in1=CBm[:].rearrange("p (g l) -> p g l", g=2).unsqueeze(2).to_broadcast([128, 2, 4, 128]),
                                                op=ALU.mult), reads=["E0", "E1", "CBm"], writes=["MT"])

        def E2a(c):
            for g in range(2):
                def fpv(e, g=g):
                    last = None
                    for i in range(4):
                        hp, jj = i % 2, i // 2
                        cb = hp * 256 + jj * 128
                        e.matmul(PC[g][:, i * 128:i * 128 + 65], lhsT=PT[:, 2 * g, cb:cb + 128], rhs=Vaug[:, c, g * 65:(g + 1) * 65],
                                 start=True, stop=False)
                        last = e.matmul(PC[g][:, i * 128:i * 128 + 65], lhsT=PT[:, 2 * g + 1, cb:cb + 128], rhs=Vaug[:, c + 1, g * 65:(g + 1) * 65],
                                        start=False, stop=True)
                    return last
                EO("pe", fpv, reads=[f"PT{2 * g}", f"PT{2 * g + 1}", f"Vaug{c}", f"Vaug{c + 1}", "Vaug_all"], writes=[f"PC{g}"])
                o3 = PC[g][:, :].rearrange("p (i d) -> p i d", i=4)
                EO("dve", lambda e, g=g, o3=o3: e.tensor_tensor(out=den[:, g * 4:(g + 1) * 4], in0=o3[:, :, 64], in1=esink[:, g * 4:(g + 1) * 4],
                                                              op=ALU.add), reads=[f"PC{g}", "esink"], writes=[f"den{g}"])
                EO("dve", lambda e, g=g: e.reciprocal(out=rden[:, g * 4:(g + 1) * 4], in_=den[:, g * 4:(g + 1) * 4]), reads=[f"den{g}"],
                   writes=[f"rden{g}"])
                EO("dve", lambda e, g=g, o3=o3: e.tensor_tensor(out=ytm[:, g * 256:(g + 1) * 256].rearrange("p (i d) -> p i d", i=4),
                                                              in0=o3[:, :, 0:64], in1=bcl(rden[:, g * 4:(g + 1) * 4], 64), op=ALU.mult),
                   reads=[f"PC{g}", f"rden{g}"], writes=[f"ytm_a{g}"])

            def fyd(e):
                last = None
                for h in range(8):
                    e.matmul(PC[0][:, h * 64:(h + 1) * 64], lhsT=identb, rhs=xsD[:, c, h * 64:(h + 1) * 64], start=True, stop=False)
                    last = e.matmul(PC[0][:, h * 64:(h + 1) * 64], lhsT=MT[:, h * 128:(h + 1) * 128], rhs=xdt[:, c, h * 64:(h + 1) * 64],
                                    start=False, stop=True)
                return last
            EO("pe", fyd, reads=["MT", f"xdt{c}", f"xsD{c}", "identb"], writes=["PC0"])

            def fyo(e):
                last = None
                for g in range(2):
                    last = e.matmul(PC[1][:, g * 256:(g + 1) * 256], lhsT=BCT[:, 2 + g, c * 128:(c + 1) * 128],
                                    rhs=prevTb[:, g * 256:(g + 1) * 256], start=True, stop=True)
                return last
            EO("pe", fyo, reads=["BCT2", "BCT3", "prevTb"], writes=["PC1"])
            state_update(c, TPf, "TP")

        def E2b(c):
            EO("dve", lambda e: e.tensor_tensor(out=yt[:].rearrange("p (h d) -> p h d", h=8),
                                                in0=PC[1][:].rearrange("p (h d) -> p h d", h=8), in1=bcl(exv[:, c, 8:16], 64), op=ALU.mult),
               reads=["PC1", f"exv{c}"], writes=["yt"])
            EO("dve", lambda e: e.tensor_tensor(out=yt[:], in0=PC[0][:], in1=yt[:], op=ALU.add), reads=["PC0", "yt"], writes=["yt"])
            EO("dve", lambda e: e.tensor_tensor(out=yt[:], in0=yt[:], in1=sz[:, c, :], op=ALU.mult), reads=["yt", f"sz{c}"], writes=["yt"])
            for g in range(2):
                EO("dve", lambda e, g=g: e.scalar_tensor_tensor(out=sqj[:, g * 256:(g + 1) * 256], in0=yt[:, g * 256:(g + 1) * 256], scalar=1.0,
                                                               in1=yt[:, g * 256:(g + 1) * 256], op0=ALU.mult, op1=ALU.mult,
                                                               accum_out=ssg[:, g:g + 1]),
                   reads=["yt", "sqj"], writes=["sqj", f"ssg{g}"])
            rstd_from_ss(ssg, rsg, 256 * EPS, ["ssg0", "ssg1"], "rsg")
            for g in range(2):
                EO("dve", lambda e, g=g: e.scalar_tensor_tensor(out=ytm[:, 512 + g * 256:512 + (g + 1) * 256], in0=yt[:, g * 256:(g + 1) * 256],
                                                               scalar=rsg[:, g:g + 1], in1=ssmw16[:, g * 256:(g + 1) * 256], op0=ALU.mult,
                                                               op1=ALU.mult), reads=["yt", "rsg", "ssmw16"], writes=[f"ytm_s{g}"])

        def E2c(c):
            def tpy(e):
                last = None
                for f in range(KD):
                    last = e.transpose(out=TP[:, f, :], in_=ytm[:, f * 128:(f + 1) * 128], identity=identb)
                return last
            EO("pe", tpy, reads=["ytm_a0", "ytm_a1", "ytm_s0", "ytm_s1", "identb"], writes=["TP"])
            EO("act", lambda e: e.activation(out=hT[:, :, c * 128:(c + 1) * 128], in_=TP[:], func=AF.Copy), reads=["TP"], writes=[f"hT{c}"])

        E1(0)
        E1b(0)
        for c in range(NCH):
            E2a(c)
            if c + 1 < NCH:
                E1(c + 1)
            E2b(c)
            if c + 1 < NCH:
                E1b(c + 1)
            E2c(c)
        if not main:
            return
        if stop == "B":
            em.barrier()
            return
        def s8a(c):
            EO("act", lambda e: e.activation(out=sqj[:], in_=x1buf[:, c, :], func=AF.Square, accum_out=ss1[:, c:c + 1]),
               reads=[f"x1_{c}", "sqj"], writes=["sqj", f"ss1_{c}"])
            rstd_from_ss(ss1[:, c:c + 1], rs1[:, c:c + 1], D * EPS, [f"ss1_{c}"], f"rs8_{c}")
            EO("act", lambda e: e.activation(out=xnB[:], in_=x1buf[:, c, :], func=AF.Identity, scale=rs1[:, c:c + 1]),
               reads=[f"x1_{c}", f"rs8_{c}", "xnB"], writes=["xnB"])

        def s8b(c):
            transpose_to_hT(xnB, ["xnB"], c, ((PC[0], "PC0"), (PC[1], "PC1")))

        for p in range(3):
            em.dma(wgus[:, p, :, :], wgu_scr[p].rearrange("p (k c) -> p k c", k=KD), writes=[f"wgus{p}"])
        for c in range(NCH):
            for dh in range(2):
                bi = (2 * c + dh) % 4

                def fop(e, c=c, dh=dh, bi=bi):
                    last = None
                    for k in range(KD):
                        last = e.matmul(PA[bi][:], lhsT=hT[:, k, c * 128:(c + 1) * 128], rhs=w_out_bf[:, k, dh * 512:(dh + 1) * 512],
                                        start=(k == 0), stop=(k == KD - 1))
                    return last
                EO("pe", fop, reads=[f"hT{c}"], writes=pk(f"PA{bi}"))
                EO("dve", lambda e, c=c, dh=dh, bi=bi: e.tensor_tensor(out=x1buf[:, c, dh * 512:(dh + 1) * 512], in0=PA[bi][:],
                                                                    in1=x1buf[:, c, dh * 512:(dh + 1) * 512], op=ALU.add),
                   reads=pk(f"PA{bi}") + [f"x1_{c}"], writes=[f"x1_{c}"])
            if c > 0:
                s8b(c - 1)
            s8a(c)
        s8b(NCH - 1)
        em.barrier()
        if stop == "OP":
            return
        for dq0 in range(2):
            em.dma(wdns[:, dq0, :, :], wdn_scr[dq0].rearrange("p (f c) -> p f c", f=NFF), writes=[f"wdns{dq0}"])
        for ffc in range(NFF):
            slot = ffc % 3
            bg, bu = PA[2 * (ffc % 2)], PA[2 * (ffc % 2) + 1]
            kg, ku = f"PA{2 * (ffc % 2)}", f"PA{2 * (ffc % 2) + 1}"

            def fup(e, slot=slot, bg=bg, bu=bu):
                last = None
                for t, bank in ((0, bg), (1, bu)):
                    for k in range(KD):
                        last = e.matmul(bank[:], lhsT=wgus[:, slot, k, t * 128:(t + 1) * 128], rhs=hT[:, k, :], start=(k == 0), stop=(k == KD - 1))
                return last
            EO("pe", fup, reads=hTk + [f"wgus{slot}"], writes=pk(kg) + pk(ku))
            if ffc + 3 < NFF:
                em.dma(wgus[:, slot, :, :], wgu_scr[ffc + 3].rearrange("p (k c) -> p k c", k=KD), writes=[f"wgus{slot}"])
            sgs = sg[:, ffc % 2, :]
            EO("act", lambda e, ffc=ffc, bg=bg, sgs=sgs: e.activation(out=sgs, in_=bg[:], func=AF.Silu, bias=bias_gu[:, ffc:ffc + 1]),
               reads=pk(kg) + [f"sg{ffc % 2}"], writes=[f"sg{ffc % 2}"])
            EO("dve", lambda e, ffc=ffc, bu=bu, sgs=sgs: e.scalar_tensor_tensor(out=actT[:, ffc, :], in0=bu[:], scalar=bias_gu[:, NFF + ffc:NFF + ffc + 1],
                                                                             in1=sgs, op0=ALU.add, op1=ALU.mult),
               reads=pk(ku) + [f"sg{ffc % 2}"], writes=[f"actT{ffc}"])
        if si + 1 < n_main:
            emit_s1a(si + 1, xown, junk=xnC, buf=xn4C)
        actk = [f"actT{f}" for f in range(NFF)]
        for dq in range(4):
            s = dq % 2
            if dq >= 2:
                em.dma(wdns[:, s, :, :], wdn_scr[dq].rearrange("p (f c) -> p f c", f=NFF), writes=[f"wdns{s}"])
            for c in range(NCH):
                reg = (dq * NCH + c) % 4
                bank = (PC[0], PC[1], PE0, PA[0])[reg][:, 0:256]
                bkey = ("PC0", "PC1", "PE0", "PA0")[reg]

                def fdn(e, s=s, c=c, bank=bank):
                    last = None
                    for f in range(NFF):
                        last = e.matmul(bank, lhsT=actT[:, f, c * 128:(c + 1) * 128], rhs=wdns[:, s, f, :], start=(f == 0), stop=(f == NFF - 1))
                    return last
                EO("pe", fdn, reads=actk + [f"wdns{s}"], writes=[bkey])
                EO("dve", lambda e, c=c, dq=dq, bank=bank: e.tensor_tensor(out=x1buf[:, c, dq * 256:(dq + 1) * 256], in0=bank,
                                                                        in1=x1buf[:, c, dq * 256:(dq + 1) * 256], op=ALU.add),
                   reads=[bkey, f"x1_{c}"], writes=[f"x1_{c}"])
                if dq == 3:
                    EO("act", lambda e, c=c: e.activation(out=xnC[:], in_=x1buf[:, c, :], func=AF.Square, accum_out=ss1[:, c:c + 1]),
                       reads=[f"x1_{c}", "xn"], writes=["xn", f"ss1_{c}"])
                    rstd_from_ss(ss1[:, c:c + 1], rs1[:, c:c + 1], D * EPS, [f"ss1_{c}"], f"rs1_fin{c}")
                    EO("dve", lambda e, c=c: e.scalar_tensor_tensor(out=x1buf[:, c, :], in0=x1buf[:, c, :], scalar=rs1[:, c:c + 1], in1=fnw32,
                                                                   op0=ALU.mult, op1=ALU.mult), reads=[f"x1_{c}", f"rs1_fin{c}", "fnw32"],
                       writes=[f"x1_{c}"])
                    em.dma(out[tok0 + c * 128:tok0 + (c + 1) * 128, :], x1buf[:, c, :], reads=[f"x1_{c}"], writes=[f"out{si}_{c}"])
            if dq == 1 and si + 1 < n_main:
                emit_s1b(buf=xn4C, banks=((PA[2], "PA2"), (PA[3], "PA3"), (PA[1], "PA1")), alias_x1=False)
        em.barrier()

    seq = [("prelast" if si == NSC - 1 else "pre", si, xprev, 0) for si in range(NSC - n_pre, NSC)]
    seq += [("main", si, xown, T + si * T) for si in range(n_main)]
    for idx, (kind, si, src, pos0) in enumerate(seq):
        s1_done = idx > 0
        nxt = nxa = nxb = None
        if kind != "main" and idx + 1 < len(seq):
            nk, nsi, nsrc, _ = seq[idx + 1]
            if nk == "main":
                nxt = (lambda nsi=nsi, nsrc=nsrc: emit_s1(nsi, nsrc, True))
            else:
                nxa = (lambda nsi=nsi, nsrc=nsrc: emit_s1a(nsi, nsrc))
                nxb = emit_s1b
        emit_sc(kind, si, src, pos0, s1_done, nxt, nxa, nxb)
    em.barrier()
    stats = em.finalize()
    return nc, stats


def _gather_cols():
    idx = []
    idx += list(range(0, 512))
    idx += [64 * h + (d + 32) % 64 for h in range(8) for d in range(64)]
    for g in range(2):
        idx += [512 + 64 * g + d for d in range(64)] * 2
    for g in range(2):
        idx += [512 + 64 * g + (d + 32) % 64 for d in range(64)] * 2
    idx += list(range(1280, 2304))
    idx += list(range(640, 768))
    idx += list(range(2304, 2312))
    idx += list(range(768, 1280))
    assert len(idx) == NW
    return np.array(idx)


def _consts():
    p = np.arange(128)
    ident = (p[:, None] == p[None, :]).astype(np.float32)
    triLE = (p[:, None] <= p[None, :]).astype(np.float32)
    triGT = (p[:, None] > p[None, :]).astype(np.float32)
    mcur = np.where(p[:, None] <= p[None, :], 0.0, NEG).astype(np.float32)
    mprev = np.where(p[:, None] > p[None, :], 0.0, NEG).astype(np.float32)
    cp = np.concatenate([ident, triLE, triGT, np.tile(mcur, (1, 4)), np.tile(mprev, (1, 4))], axis=1)
    return np.ascontiguousarray(cp.astype(np.float32))


_PROG = {}


def kernel(x, c, positions, w_ada, b_ada, norm1_w, w_in, conv_w, conv_b, dt_bias, a_log, d_skip, attn_sinks,
           ssm_norm_w, w_out, norm2_w, w_gate_up, w_down, final_norm_w):
    f32 = np.float32
    x = np.asarray(x, f32)
    if "p" not in _PROG:
        _PROG["p"] = build_program()
    nc, stats = _PROG["p"]
    w_in_g = np.ascontiguousarray(np.asarray(w_in, f32)[0][:, _gather_cols()])
    cpack = _consts()
    half = 32
    inv_freq = (10000.0 ** (-np.arange(half, dtype=np.float32) / np.float32(half))).astype(f32)
    p = np.arange(128)
    rows = np.concatenate([np.asarray(final_norm_w, f32), np.asarray(ssm_norm_w, f32)[0], np.asarray(dt_bias, f32)[0],
                           np.asarray(a_log, f32)[0], np.asarray(d_skip, f32)[0], np.asarray(attn_sinks, f32)[0]])
    rowpack = np.ascontiguousarray(np.tile(rows[None, :], (128, 1)))
    badap = np.ascontiguousarray(np.tile(np.asarray(b_ada, f32)[0][None, :], (128, 1)))
    in_maps = []
    for i in range(8):
        b, hf = i // 2, i % 2
        colp = np.zeros((128, 80), f32)
        colp[:, 0:8] = np.asarray(c, f32)[b].reshape(8, 128).T
        cw = np.asarray(conv_w, f32)[0]
        colp[:, 8:40] = cw.reshape(4, 8, 128).transpose(2, 1, 0).reshape(128, 32)
        colp[:, 40:48] = np.asarray(conv_b, f32)[0].reshape(8, 128).T
        colp[:, 48] = inv_freq[p % 32]
        colp[:, 49] = float(hf)
        colp[:, 50] = np.where((p % 64) < 32, -1.0, 1.0)
        colp[:, 51:59] = np.asarray(norm1_w, f32)[0].reshape(8, 128).T
        colp[:, 59:67] = np.asarray(norm2_w, f32)[0].reshape(8, 128).T
        colp[:, 67] = (p < 64).astype(f32)
        colp[:, 68] = (p >= 64).astype(f32)
        pos_b = np.asarray(positions)[b].astype(np.int32)
        pos_cat = np.concatenate([pos_b[SEQ_HALF - T:SEQ_HALF], pos_b[hf * SEQ_HALF:(hf + 1) * SEQ_HALF]])
        in_maps.append({
            "xprev": np.ascontiguousarray(x[b, 0:SEQ_HALF]),
            "xown": np.ascontiguousarray(x[b, hf * SEQ_HALF:(hf + 1) * SEQ_HALF]),
            "posrep": np.ascontiguousarray(np.tile(pos_cat[None, :], (128, 1))),
            "colpack": colp, "rowpack": rowpack, "bada": badap, "cpack": cpack,
            "w_ada": np.ascontiguousarray(np.asarray(w_ada, f32)[0]), "w_in": w_in_g,
            "w_out": np.ascontiguousarray(np.asarray(w_out, f32)[0]),
            "w_gu": np.ascontiguousarray(np.asarray(w_gate_up, f32)[0]),
            "w_dn": np.ascontiguousarray(np.asarray(w_down, f32)[0]),
        })
    res = run_bass_kernel_spmd(nc, in_maps, core_ids=list(range(8)))
    outp = np.empty((4, 2 * SEQ_HALF, D), f32)
    for i in range(8):
        b, hf = i // 2, i % 2
        outp[b, hf * SEQ_HALF:(hf + 1) * SEQ_HALF] = res.results[i]["out"]
    return outp
```
